# Optimizing a Trainium2 kernel written in Bass

```python
import math
import numpy as np
import jax
import jax.numpy as jnp
from jax import lax

D_MODEL = 2048
BATCH = 16
SEQ = 256
DEPTH = 2
DEC_BATCH = 4
DEC_SEQ = 2048
PAST_LEN = 512

GRID_W = 64
CHUNK = 64
D_FF = 5632
N_MOD = 9
NORM_EPS = 1e-6
MIX_OUT = 2048
STATE_INIT_SCALE = 0.3
ROPE_BASE = 10000.0

GLA_HEADS = 4
GLA_DK = 128
GLA_DV = 256
GLA_LOWRANK = 16
GLA_GATE_NORM = 16.0
GLA_QK = GLA_HEADS * GLA_DK
GLA_V = GLA_HEADS * GLA_DV

RET_HEADS = 4
RET_DK = 128
RET_DV = 256
RET_QK = RET_HEADS * RET_DK
RET_V = RET_HEADS * RET_DV
RET_DECAY_EXP_FWD = 5.0
RET_DECAY_EXP_BWD = 5.5

GDN_HEADS = 8
GDN_DK = 128
GDN_DV = 128
GDN_QK = GDN_HEADS * GDN_DK
GDN_V = GDN_HEADS * GDN_DV
GDN_QKV = 2 * GDN_QK + GDN_V
CONV_K = 5
GDN_IN = GDN_QKV + GDN_V + 4 * GDN_HEADS

RWKV_HEADS = 16
RWKV_N = 64
RWKV_C = RWKV_HEADS * RWKV_N
RWKV_DECAY_LORA = 64
RWKV_AAA_LORA = 64
RWKV_GATE_LORA = 128
RWKV_GN_EPS = 64e-5
RWKV_IN = 3 * RWKV_C + 2 * RWKV_DECAY_LORA + 2 * RWKV_AAA_LORA + RWKV_GATE_LORA

L0_SIZES = (GLA_QK, GLA_QK, GLA_V, GLA_V, GLA_LOWRANK, GLA_LOWRANK, RET_QK, RET_QK, RET_V, RET_V)
L0_IN = 2 * GLA_QK + 2 * GLA_V + 2 * GLA_LOWRANK + 2 * RET_QK + 2 * RET_V
GDN_REST_SIZES = (GDN_V, GDN_HEADS, GDN_HEADS, GDN_HEADS, GDN_HEADS)
RWKV_SIZES = (RWKV_C, RWKV_C, RWKV_C, RWKV_DECAY_LORA, RWKV_DECAY_LORA, RWKV_AAA_LORA, RWKV_AAA_LORA, RWKV_GATE_LORA)
L1_IN = GDN_IN + RWKV_IN

kernel_name = 'bidir_hybrid_gla_ret_gdn_rwkv7_prefix_dit'


def split_sizes(z, sizes):
    return jnp.split(z, np.cumsum(sizes)[:-1].tolist(), axis=-1)


def rmsnorm(x, g):
    x32 = x.astype(jnp.float32)
    y = x32 * lax.rsqrt(jnp.mean(x32 * x32, axis=-1, keepdims=True) + NORM_EPS) * g
    return y.astype(x.dtype)


def modulate(x, g, shift, scale):
    return rmsnorm(x, g) * (1 + scale) + shift


def adaln(cond, w_mod, b_mod, dtype):
    m = jax.nn.silu(cond.astype(jnp.float32)) @ w_mod + b_mod
    return [t[:, None, :].astype(dtype) for t in jnp.split(m, N_MOD, axis=-1)]


def swiglu(h, wg, wu, wd):
    return (jax.nn.silu(h @ wg) * (h @ wu)) @ wd


def heads(x, n):
    b, t, _ = x.shape
    return x.reshape(b, t, n, -1).transpose(0, 2, 1, 3)


def head_scalars(x):
    return jnp.swapaxes(x, 1, 2)


def merge_heads(x):
    b, h, t, d = x.shape
    return x.transpose(0, 2, 1, 3).reshape(b, t, h * d)


def head_norm(x, eps, center):
    if center:
        x = x - jnp.mean(x, axis=-1, keepdims=True)
    return x * lax.rsqrt(jnp.mean(x * x, axis=-1, keepdims=True) + eps)


def l2norm(x):
    return x * lax.rsqrt(jnp.sum(x * x, axis=-1, keepdims=True) + NORM_EPS)


def to_chunks(x):
    b, h, t = x.shape[:3]
    return jnp.moveaxis(x.reshape((b, h, t // CHUNK, CHUNK) + x.shape[3:]), 2, 0)


def from_chunks(y):
    y = jnp.moveaxis(y, 0, 2)
    b, h, n, c = y.shape[:4]
    return y.reshape((b, h, n * c) + y.shape[4:])


def causal_masks():
    lower = jnp.tril(jnp.ones((CHUNK, CHUNK), dtype=bool))
    strict = jnp.tril(jnp.ones((CHUNK, CHUNK), dtype=bool), -1)
    return lower, strict


def gla_chunk_scan(q, k, v, log_a, s0):
    lower, _ = causal_masks()

    def step(S, inp):
        qc, kc, vc, lc = inp
        G = jnp.cumsum(lc, axis=-2)
        rel = jnp.exp(jnp.where(lower[:, :, None], G[..., :, None, :] - G[..., None, :, :], -jnp.inf))
        att = jnp.einsum('bhik,bhjk,bhijk->bhij', qc, kc, rel)
        o = att @ vc + (qc * jnp.exp(G)) @ S
        G_end = G[..., -1:, :]
        S = jnp.swapaxes(jnp.exp(G_end), -1, -2) * S + jnp.swapaxes(kc * jnp.exp(G_end - G), -1, -2) @ vc
        return S, o

    S, o = lax.scan(step, s0.astype(jnp.float32), tuple(to_chunks(t) for t in (q, k, v, log_a)))
    return from_chunks(o), S


def retention_chunk_scan(q, k, v, s0, log_g):
    lower, _ = causal_masks()
    idx = jnp.arange(CHUNK, dtype=jnp.float32)
    lg = log_g[:, None, None]
    dist = jnp.where(lower, idx[:, None] - idx[None, :], 0.0)
    dmat = jnp.where(lower, jnp.exp(dist * lg), 0.0)
    q_dec = jnp.exp((idx + 1.0)[:, None] * lg)
    k_dec = jnp.exp((CHUNK - 1.0 - idx)[:, None] * lg)
    c_dec = jnp.exp(CHUNK * lg)

    def step(S, inp):
        qc, kc, vc = inp
        o = ((qc @ jnp.swapaxes(kc, -1, -2)) * dmat) @ vc + (qc * q_dec) @ S
        S = c_dec * S + jnp.swapaxes(kc * k_dec, -1, -2) @ vc
        return S, o

    S, o = lax.scan(step, s0.astype(jnp.float32), tuple(to_chunks(t) for t in (q, k, v)))
    return from_chunks(o), S


def gdn_chunk_scan(q, k, v, log_a, beta, s0):
    lower, strict = causal_masks()
    eye = jnp.eye(CHUNK, dtype=jnp.float32)
    dv = v.shape[-1]

    def step(S, inp):
        qc, kc, vc, lc, bc = inp
        G = jnp.cumsum(lc, axis=-1)
        rel = jnp.exp(jnp.where(lower, G[..., :, None] - G[..., None, :], -jnp.inf))
        kkt = kc @ jnp.swapaxes(kc, -1, -2)
        lmat = jnp.where(strict, bc[..., :, None] * rel * kkt, 0.0) + eye
        rhs = jnp.concatenate([bc[..., None] * vc, (bc * jnp.exp(G))[..., None] * kc], axis=-1)
        sol = lax.linalg.triangular_solve(lmat, rhs, left_side=True, lower=True)
        u = sol[..., :dv] - sol[..., dv:] @ S
        o = (qc * jnp.exp(G)[..., None]) @ S + ((qc @ jnp.swapaxes(kc, -1, -2)) * rel) @ u
        S = (jnp.exp(G[..., -1])[..., None, None] * S
             + jnp.swapaxes(kc * jnp.exp(G[..., -1:] - G)[..., None], -1, -2) @ u)
        return S, o

    S, o = lax.scan(step, s0.astype(jnp.float32), tuple(to_chunks(t) for t in (q, k, v, log_a, beta)))
    return from_chunks(o), S


def rwkv7_scan(r, log_w, k, v, a, b, s0):
    def step(S, inp):
        r_t, lw_t, k_t, v_t, a_t, b_t = inp
        sa = jnp.einsum('bhk,bhkv->bhv', a_t, S)
        S = jnp.exp(lw_t)[..., None] * S + b_t[..., :, None] * sa[..., None, :] + k_t[..., :, None] * v_t[..., None, :]
        return S, jnp.einsum('bhk,bhkv->bhv', r_t, S)

    xs = tuple(jnp.moveaxis(t, 2, 0) for t in (r, log_w, k, v, a, b))
    S, y = lax.scan(step, s0.astype(jnp.float32), xs)
    return jnp.moveaxis(y, 0, 2), S


def bidir(scan_fn, args_f, args_b, s0_f, s0_b, consts_f=(), consts_b=()):
    o_f, s_f = scan_fn(*args_f, s0_f, *consts_f)
    o_b, s_b = scan_fn(*(jnp.flip(t, axis=2) for t in args_b), s0_b, *consts_b)
    return o_f + jnp.flip(o_b, axis=2), s_f, s_b


def grid_rotary(x):
    t, dk = x.shape[2], x.shape[3]
    rows = t // GRID_W
    row = jnp.broadcast_to(jnp.arange(rows, dtype=jnp.float32)[:, None], (rows, GRID_W)).reshape(t)
    col = jnp.broadcast_to(jnp.arange(GRID_W, dtype=jnp.float32)[None, :], (rows, GRID_W)).reshape(t)
    quarter = dk // 4
    inv = ROPE_BASE ** (-jnp.arange(quarter, dtype=jnp.float32) / quarter)
    ang = jnp.concatenate([row[:, None] * inv, col[:, None] * inv], axis=-1)
    cos, sin = jnp.cos(ang), jnp.sin(ang)
    x1, x2 = x[..., :dk // 2], x[..., dk // 2:]
    return jnp.concatenate([x1 * cos - x2 * sin, x1 * sin + x2 * cos], axis=-1)


def retention_log_decay(exp0):
    h = jnp.arange(RET_HEADS, dtype=jnp.float32)
    return jnp.log1p(-jnp.power(2.0, -(exp0 + h)))


def centred_dwconv(x, w):
    return lax.conv_general_dilated(
        x, w[:, None, :].astype(x.dtype), window_strides=(1,),
        padding=((CONV_K // 2, CONV_K // 2),), dimension_numbers=('NWC', 'WIO', 'NWC'),
        feature_group_count=x.shape[-1])


def centred_shift_mix(z, mu):
    zp = jnp.pad(z, ((0, 0), (1, 1), (0, 0)))
    shifted = 0.5 * (zp[:, :-2] + zp[:, 2:])
    return z + mu * (shifted - z)


def mixer_gla_ret(h, states, latent, w_in, w_out, gk_up_f, gk_b_f, gk_up_b, gk_b_b, gla_norm, ret_norm):
    b = h.shape[0]
    z = (h @ w_in).astype(jnp.float32)
    gq, gk, gv, gg, gdf, gdb, rq, rk, rv, rg = split_sizes(z, L0_SIZES)
    if states is None:
        zg = jnp.zeros((b, GLA_HEADS, GLA_DK, GLA_DV), jnp.float32)
        zr = jnp.zeros((b, RET_HEADS, RET_DK, RET_DV), jnp.float32)
        states = (zg, zg, zr, zr)
    q = heads(gq, GLA_HEADS) * GLA_DK ** -0.5
    k = heads(gk, GLA_HEADS)
    v = heads(gv, GLA_HEADS)
    la_f = heads(jax.nn.log_sigmoid(gdf @ gk_up_f + gk_b_f) / GLA_GATE_NORM, GLA_HEADS)
    la_b = heads(jax.nn.log_sigmoid(gdb @ gk_up_b + gk_b_b) / GLA_GATE_NORM, GLA_HEADS)
    o, s_gf, s_gb = bidir(gla_chunk_scan, (q, k, v, la_f), (q, k, v, la_b), states[0], states[1])
    o_gla = merge_heads(head_norm(o, NORM_EPS, False) * gla_norm) * jax.nn.silu(gg)
    q = heads(rq, RET_HEADS)
    k = heads(rk, RET_HEADS)
    if latent:
        q, k = grid_rotary(q), grid_rotary(k)
    q = q * RET_DK ** -0.5
    v = heads(rv, RET_HEADS)
    o, s_rf, s_rb = bidir(retention_chunk_scan, (q, k, v), (q, k, v), states[2], states[3],
                          (retention_log_decay(RET_DECAY_EXP_FWD),), (retention_log_decay(RET_DECAY_EXP_BWD),))
    o_ret = merge_heads(head_norm(o, NORM_EPS, True) * ret_norm) * jax.nn.silu(rg)
    y = jnp.concatenate([o_gla, o_ret], axis=-1).astype(h.dtype) @ w_out
    return y, (s_gf, s_gb, s_rf, s_rb)


def mixer_gdn_rwkv(h, states, w_in, w_out, conv_w, A_log_f, dt_bias_f, A_log_b, dt_bias_b, gdn_norm,
                   mu, w0_f, w2_f, a0_f, a2_f, w0_b, w2_b, a0_b, a2_b, g2, k_k, k_a, r_k, ln_w, ln_b):
    b = h.shape[0]
    z = (h @ w_in).astype(jnp.float32)
    z_gdn, z_rwkv = z[..., :GDN_IN], z[..., GDN_IN:]
    if states is None:
        zd = jnp.zeros((b, GDN_HEADS, GDN_DK, GDN_DV), jnp.float32)
        zw = jnp.zeros((b, RWKV_HEADS, RWKV_N, RWKV_N), jnp.float32)
        states = (zd, zd, zw, zw)
    qkv = jax.nn.silu(centred_dwconv(z_gdn[..., :GDN_QKV], conv_w))
    gq, gk, gv = split_sizes(qkv, (GDN_QK, GDN_QK, GDN_V))
    gg, a_f, a_b, b_f, b_b = split_sizes(z_gdn[..., GDN_QKV:], GDN_REST_SIZES)
    q = l2norm(heads(gq, GDN_HEADS)) * GDN_DK ** -0.5
    k = l2norm(heads(gk, GDN_HEADS))
    v = heads(gv, GDN_HEADS)
    la_f = -jnp.exp(A_log_f)[:, None] * jax.nn.softplus(head_scalars(a_f) + dt_bias_f[:, None])
    la_b = -jnp.exp(A_log_b)[:, None] * jax.nn.softplus(head_scalars(a_b) + dt_bias_b[:, None])
    beta_f = jax.nn.sigmoid(head_scalars(b_f))
    beta_b = jax.nn.sigmoid(head_scalars(b_b))
    o, s_df, s_db = bidir(gdn_chunk_scan, (q, k, v, la_f, beta_f), (q, k, v, la_b, beta_b), states[0], states[1])
    o_gdn = merge_heads(head_norm(o, NORM_EPS, False) * gdn_norm) * jax.nn.silu(gg)
    zr = centred_shift_mix(z_rwkv, mu)
    r, kr, vr, wd_f, wd_b, ad_f, ad_b, gd = split_sizes(zr, RWKV_SIZES)
    rh = heads(r, RWKV_HEADS)
    vh = heads(vr, RWKV_HEADS)
    kk = l2norm(heads(kr * k_k, RWKV_HEADS))

    def rwkv_direction(w0, wd, w2, a0, ad, a2):
        w = -jax.nn.softplus(-(w0 + jnp.tanh(wd) @ w2)) - 0.5
        a = jax.nn.sigmoid(a0 + ad @ a2)
        kd = heads(kr * (1.0 + (a - 1.0) * k_a), RWKV_HEADS)
        bonus = jnp.sum(rh * kd * r_k[:, None, :], axis=-1, keepdims=True) * vh
        args = (rh, -jnp.exp(heads(w, RWKV_HEADS)), kd, vh, -kk, kk * heads(a, RWKV_HEADS))
        return args, bonus

    args_f, bonus_f = rwkv_direction(w0_f, wd_f, w2_f, a0_f, ad_f, a2_f)
    args_b, bonus_b = rwkv_direction(w0_b, wd_b, w2_b, a0_b, ad_b, a2_b)
    o, s_wf, s_wb = bidir(rwkv7_scan, args_f, args_b, states[2], states[3])
    y_rwkv = merge_heads(head_norm(o, RWKV_GN_EPS, True)) * ln_w + ln_b + merge_heads(bonus_f + bonus_b)
    y_rwkv = y_rwkv * (jax.nn.sigmoid(gd) @ g2)
    y = jnp.concatenate([o_gdn, y_rwkv], axis=-1).astype(h.dtype) @ w_out
    return y, (s_df, s_db, s_wf, s_wb)


def setup_inputs(seed: int = 0) -> dict:
    key = jax.random.key(seed)
    ks = iter(jax.random.split(key, 128))

    def nrm(shape, scale=1.0):
        return scale * jax.random.normal(next(ks), shape, jnp.float32)

    def gain(n):
        return 1.0 + 0.05 * nrm((n,))

    def unif(shape, lo, hi):
        return jax.random.uniform(next(ks), shape, jnp.float32, lo, hi)

    D = D_MODEL
    p = {}
    p['x_prompt'] = nrm((BATCH, SEQ, D))
    p['x_sample'] = nrm((DEC_BATCH, DEC_SEQ, D))
    p['c'] = nrm((DEC_BATCH, D))
    p['c_ctx'] = nrm((D,))
    state_shapes = (('l0_gla', (GLA_HEADS, GLA_DK, GLA_DV)), ('l0_ret', (RET_HEADS, RET_DK, RET_DV)),
                    ('l1_gdn', (GDN_HEADS, GDN_DK, GDN_DV)), ('l1_rwkv', (RWKV_HEADS, RWKV_N, RWKV_N)))
    for name, shp in state_shapes:
        for d in ('fwd', 'bwd'):
            p['state_' + name + '_' + d] = nrm((DEC_BATCH,) + shp, STATE_INIT_SCALE)

    def add_common(l, n_in):
        pre = 'l' + str(l) + '_'
        p[pre + 'w_mod'] = nrm((D, N_MOD * D), 0.5 * D ** -0.5)
        p[pre + 'b_mod'] = nrm((N_MOD * D,), 0.02)
        for i in (1, 2, 3):
            p[pre + 'norm' + str(i)] = gain(D)
        for f in ('ffn1', 'ffn2'):
            p[pre + f + '_wg'] = nrm((D, D_FF), D ** -0.5)
            p[pre + f + '_wu'] = nrm((D, D_FF), D ** -0.5)
            p[pre + f + '_wd'] = nrm((D_FF, D), D_FF ** -0.5)
        p[pre + 'w_in'] = nrm((D, n_in), D ** -0.5)
        p[pre + 'w_out'] = nrm((MIX_OUT, D), MIX_OUT ** -0.5)

    add_common(0, L0_IN)
    for d in ('fwd', 'bwd'):
        p['l0_gla_gk_up_' + d] = nrm((GLA_LOWRANK, GLA_QK), GLA_LOWRANK ** -0.5)
        p['l0_gla_gk_b_' + d] = nrm((GLA_QK,), 0.1)
    p['l0_gla_norm'] = gain(GLA_DV)
    p['l0_ret_norm'] = gain(RET_DV)

    add_common(1, L1_IN)
    p['l1_gdn_conv'] = nrm((CONV_K, GDN_QKV), CONV_K ** -0.5)
    for d in ('fwd', 'bwd'):
        p['l1_gdn_A_log_' + d] = jnp.log(unif((GDN_HEADS,), 1.0, 16.0))
        dt = jnp.exp(unif((GDN_HEADS,), math.log(1e-3), math.log(1e-1)))
        p['l1_gdn_dt_bias_' + d] = dt + jnp.log(-jnp.expm1(-dt))
    p['l1_gdn_norm'] = gain(GDN_DV)
    p['l1_rwkv_mu'] = unif((RWKV_IN,), 0.0, 1.0)
    for d in ('fwd', 'bwd'):
        p['l1_rwkv_w0_' + d] = -1.0 + 0.5 * nrm((RWKV_C,))
        p['l1_rwkv_w2_' + d] = nrm((RWKV_DECAY_LORA, RWKV_C), 0.1)
        p['l1_rwkv_a0_' + d] = nrm((RWKV_C,), 0.1)
        p['l1_rwkv_a2_' + d] = nrm((RWKV_AAA_LORA, RWKV_C), RWKV_AAA_LORA ** -0.5)
    p['l1_rwkv_g2'] = nrm((RWKV_GATE_LORA, RWKV_C), RWKV_GATE_LORA ** -0.5)
    p['l1_rwkv_k_k'] = 0.85 + 0.05 * nrm((RWKV_C,))
    p['l1_rwkv_k_a'] = 1.0 + 0.05 * nrm((RWKV_C,))
    p['l1_rwkv_r_k'] = nrm((RWKV_HEADS, RWKV_N), 0.1)
    p['l1_rwkv_ln_w'] = gain(RWKV_C)
    p['l1_rwkv_ln_b'] = nrm((RWKV_C,), 0.02)
    p['final_norm'] = gain(D)
    return p


def reference(x_prompt, x_sample, c, c_ctx,
              state_l0_gla_fwd, state_l0_gla_bwd, state_l0_ret_fwd, state_l0_ret_bwd,
              state_l1_gdn_fwd, state_l1_gdn_bwd, state_l1_rwkv_fwd, state_l1_rwkv_bwd,
              l0_w_mod, l0_b_mod, l0_norm1, l0_norm2, l0_norm3,
              l0_ffn1_wg, l0_ffn1_wu, l0_ffn1_wd, l0_ffn2_wg, l0_ffn2_wu, l0_ffn2_wd, l0_w_in, l0_w_out,
              l0_gla_gk_up_fwd, l0_gla_gk_b_fwd, l0_gla_gk_up_bwd, l0_gla_gk_b_bwd, l0_gla_norm, l0_ret_norm,
              l1_w_mod, l1_b_mod, l1_norm1, l1_norm2, l1_norm3,
              l1_ffn1_wg, l1_ffn1_wu, l1_ffn1_wd, l1_ffn2_wg, l1_ffn2_wu, l1_ffn2_wd, l1_w_in, l1_w_out,
              l1_gdn_conv, l1_gdn_A_log_fwd, l1_gdn_dt_bias_fwd, l1_gdn_A_log_bwd, l1_gdn_dt_bias_bwd, l1_gdn_norm,
              l1_rwkv_mu, l1_rwkv_w0_fwd, l1_rwkv_w2_fwd, l1_rwkv_a0_fwd, l1_rwkv_a2_fwd,
              l1_rwkv_w0_bwd, l1_rwkv_w2_bwd, l1_rwkv_a0_bwd, l1_rwkv_a2_bwd,
              l1_rwkv_g2, l1_rwkv_k_k, l1_rwkv_k_a, l1_rwkv_r_k, l1_rwkv_ln_w, l1_rwkv_ln_b,
              final_norm):
    common = (
        (l0_w_mod, l0_b_mod, (l0_norm1, l0_norm2, l0_norm3), (l0_ffn1_wg, l0_ffn1_wu, l0_ffn1_wd),
         (l0_ffn2_wg, l0_ffn2_wu, l0_ffn2_wd), l0_w_in, l0_w_out),
        (l1_w_mod, l1_b_mod, (l1_norm1, l1_norm2, l1_norm3), (l1_ffn1_wg, l1_ffn1_wu, l1_ffn1_wd),
         (l1_ffn2_wg, l1_ffn2_wu, l1_ffn2_wd), l1_w_in, l1_w_out),
    )
    mixer_params = (
        (l0_gla_gk_up_fwd, l0_gla_gk_b_fwd, l0_gla_gk_up_bwd, l0_gla_gk_b_bwd, l0_gla_norm, l0_ret_norm),
        (l1_gdn_conv, l1_gdn_A_log_fwd, l1_gdn_dt_bias_fwd, l1_gdn_A_log_bwd, l1_gdn_dt_bias_bwd, l1_gdn_norm,
         l1_rwkv_mu, l1_rwkv_w0_fwd, l1_rwkv_w2_fwd, l1_rwkv_a0_fwd, l1_rwkv_a2_fwd,
         l1_rwkv_w0_bwd, l1_rwkv_w2_bwd, l1_rwkv_a0_bwd, l1_rwkv_a2_bwd,
         l1_rwkv_g2, l1_rwkv_k_k, l1_rwkv_k_a, l1_rwkv_r_k, l1_rwkv_ln_w, l1_rwkv_ln_b),
    )
    caches = (
        (state_l0_gla_fwd, state_l0_gla_bwd, state_l0_ret_fwd, state_l0_ret_bwd),
        (state_l1_gdn_fwd, state_l1_gdn_bwd, state_l1_rwkv_fwd, state_l1_rwkv_bwd),
    )
    xp, xs = x_prompt, x_sample
    new_states = []
    for layer in range(DEPTH):
        w_mod, b_mod, norms, ffn1, ffn2, w_in, w_out = common[layer]
        outs = []
        for x, cond, cache, latent in ((xp, c_ctx[None, :], None, False), (xs, c, caches[layer], True)):
            sh1, sc1, g1, sh2, sc2, g2, sh3, sc3, g3 = adaln(cond, w_mod, b_mod, x.dtype)
            x = x + 0.5 * g1 * swiglu(modulate(x, norms[0], sh1, sc1), *ffn1)
            hm = modulate(x, norms[1], sh2, sc2)
            if layer % 2 == 0:
                m, st = mixer_gla_ret(hm, cache, latent, w_in, w_out, *mixer_params[layer])
            else:
                m, st = mixer_gdn_rwkv(hm, cache, w_in, w_out, *mixer_params[layer])
            x = x + g2 * m
            x = x + 0.5 * g3 * swiglu(modulate(x, norms[2], sh3, sc3), *ffn2)
            outs.append((x, st))
        (xp, st_prompt), (xs, _) = outs
        new_states.extend(st_prompt)
    y_prompt = rmsnorm(xp, final_norm)
    y_sample = rmsnorm(xs, final_norm)
    return (y_prompt, y_sample, *new_states)
```

```python
import contextlib
import numpy as np
import concourse.bass as bass
import concourse.mybir as mybir
from concourse.bass_utils import run_bass_kernel_spmd

F32 = mybir.dt.float32
BF16 = mybir.dt.bfloat16
U8 = mybir.dt.uint8
ALU = mybir.AluOpType
AF = mybir.ActivationFunctionType
AX = mybir.AxisListType

D = 2048
NCH = 16
T = 2048
TT = 512
NT = T // TT
DFF = 5632
NF = DFF // 128
C = 64
NCHUNK = T // C
SEG = 256
NSEG = T // SEG
EPS = 1e-6
L0_IN = 6176
L1_IN = 7584


class Buf:
    __slots__ = ("name", "last_w", "readers", "last_dma", "dma_sem", "dma_cnt")

    def __init__(self, name):
        self.name = name
        self.last_w = None
        self.readers = []
        self.last_dma = None
        self.dma_sem = None
        self.dma_cnt = 0


class Op:
    __slots__ = ("eng", "fn", "deps", "is_dma", "sem", "val", "signal")

    def __init__(self, eng, fn, is_dma):
        self.eng = eng
        self.fn = fn
        self.deps = []
        self.is_dma = is_dma
        self.sem = None
        self.val = 0
        self.signal = is_dma


class Prog:
    ENGS = ("pe", "act", "dve", "pool", "sp")

    def __init__(self, nc):
        self.nc = nc
        self.ops = {e: [] for e in self.ENGS}
        self.nops = 0
        self.dma_bufs = []
        self.barrier_deps = {e: [] for e in self.ENGS}
        self.all_bufs = []

    def buf(self, name):
        b = Buf(name)
        return b

    def _track(self, op, reads, writes):
        deps = op.deps
        for b in reads:
            if b.last_w is not None:
                deps.append(b.last_w)
            b.readers.append(op)
        for b in writes:
            if b.last_w is not None:
                deps.append(b.last_w)
            deps.extend(b.readers)
            b.last_w = op
            b.readers = []
        bd = self.barrier_deps[op.eng]
        if bd:
            deps.extend(bd)
            self.barrier_deps[op.eng] = []

    def c(self, eng, fn, reads=(), writes=()):
        op = Op(eng, fn, False)
        self._track(op, reads, writes)
        self.ops[eng].append(op)
        self.nops += 1
        return op

    def dma(self, eng, out_ap, in_ap, sbuf, reads=(), writes=()):
        def fn(e, out_ap=out_ap, in_ap=in_ap):
            return e.dma_start(out=out_ap, in_=in_ap)
        op = Op(eng, fn, True)
        self._track(op, reads, writes)
        if sbuf.last_dma is not None:
            op.deps.append(sbuf.last_dma)
        sbuf.last_dma = op
        if sbuf.dma_sem is None:
            sbuf.dma_sem = "pending"
            self.dma_bufs.append(sbuf)
        sbuf.dma_cnt += 1
        op.sem = sbuf
        op.val = 16 * sbuf.dma_cnt
        self.ops[eng].append(op)
        self.nops += 1
        return op

    def barrier(self):
        lasts = []
        for e in self.ENGS:
            for op in reversed(self.ops[e]):
                if not op.is_dma:
                    lasts.append(op)
                    break
        for b in self.dma_bufs:
            if b.last_dma is not None:
                lasts.append(b.last_dma)
        for e in self.ENGS:
            self.barrier_deps[e] = list(lasts)

    def emit(self, final_wait_ops=()):
        nc = self.nc
        for e in self.ENGS:
            for op in self.ops[e]:
                for d in op.deps:
                    if d is not op:
                        d.signal = True
        for op in final_wait_ops:
            op.signal = True
        with contextlib.ExitStack() as st:
            esem = {}
            for e in ("pe", "act", "dve", "pool"):
                esem[e] = st.enter_context(nc.semaphore("s_" + e))
            for i, b in enumerate(self.dma_bufs):
                b.dma_sem = st.enter_context(nc.semaphore("d%d_%s" % (i, b.name)))
            for e in self.ENGS:
                k = 0
                for op in self.ops[e]:
                    if op.is_dma:
                        op.sem = op.sem.dma_sem
                    elif op.signal:
                        k += 1
                        op.sem = esem[e]
                        op.val = k
            block = st.enter_context(nc.Block())

            def run(e, eng):
                waited = {}
                for op in self.ops[e]:
                    need = {}
                    for d in op.deps:
                        if d is op or not d.signal:
                            continue
                        s = d.sem
                        if waited.get(s.num, 0) >= d.val:
                            continue
                        if need.get(s.num, (None, 0))[1] < d.val:
                            need[s.num] = (s, d.val)
                    for num, (s, v) in need.items():
                        eng.wait_ge(s, v)
                        waited[num] = v
                    ins = op.fn(eng)
                    if op.signal:
                        ins.then_inc(op.sem, 16 if op.is_dma else 1)
                if e == "sp":
                    for op in final_wait_ops:
                        if waited.get(op.sem.num, 0) < op.val:
                            eng.wait_ge(op.sem, op.val)
                            waited[op.sem.num] = op.val

            @block.tensor
            def _(eng):
                run("pe", eng)

            @block.scalar
            def _(eng):
                run("act", eng)

            @block.vector
            def _(eng):
                run("dve", eng)

            @block.gpsimd
            def _(eng):
                run("pool", eng)

            @block.sync
            def _(eng):
                run("sp", eng)


class Arena:
    def __init__(self, nc, nbytes):
        self.t = nc.alloc_sbuf_tensor("arena", [128, nbytes], U8)
        self.nbytes = nbytes
        self.off = 0
        self.peak = 0

    def mark(self):
        return self.off

    def release(self, m):
        self.off = m

    def alloc(self, name, shape, dt=F32, parts=None):
        esz = 4 if dt == F32 else 2
        n = int(np.prod(shape[1:])) * esz
        o = self.off
        self.off += (n + 63) // 64 * 64
        self.peak = max(self.peak, self.off)
        assert self.off <= self.nbytes, "SBUF arena overflow %s %d" % (name, self.off)
        ap = self.t[0:shape[0], o:o + n].bitcast(dt)
        if len(shape) == 3:
            ap = ap.rearrange("p (a b) -> p a b", a=shape[1])
        elif len(shape) == 4:
            ap = ap.rearrange("p (a b c) -> p a b c", a=shape[1], b=shape[2])
        return ap


class Slots:
    def __init__(self, arena, name, n, shape, dt):
        self.aps = [arena.alloc("%s%d" % (name, i), shape, dt) for i in range(n)]
        self.bufs = [Buf("%s%d" % (name, i)) for i in range(n)]
        self.i = 0
        self.n = n

    def next(self):
        i = self.i
        self.i = (i + 1) % self.n
        return self.aps[i], self.bufs[i]


class Builder:
    def __init__(self, dbg=None):
        self.dbg = dbg or {}
        self.nc = bass.Bass("TRN2", target_bir_lowering=False)
        self.P = Prog(self.nc)
        self.inp = {}
        self.out = {}
        self.stores = []
        self.arena = Arena(self.nc, 206 * 1024)
        nc = self.nc
        self.banks = [nc.alloc_psum_tensor("bank%d" % i, [128, 512], F32) for i in range(8)]
        self.bank_b = [Buf("bank%d" % i) for i in range(8)]
        self.scr = {}
        self.shared = {}
        self.pcount = 0

    def buf(self, key):
        if key not in self.shared:
            self.shared[key] = Buf(key)
        return self.shared[key]

    def phase(self):
        self.pcount = 0

    def pbuf(self):
        b = self.buf("ph%d" % self.pcount)
        self.pcount += 1
        return b

    def dma_in(self, dst_ap, src_ap, owner, writes, reads=(), eng="sp"):
        return self.P.dma(eng, dst_ap, src_ap, owner, reads=reads, writes=writes)

    def dma_out(self, dst_ap, src_ap, owner, reads, writes=(), eng="sp", final=False):
        op = self.P.dma(eng, dst_ap, src_ap, owner, reads=reads, writes=writes)
        if final:
            self.stores.append(op)
        return op

    def din(self, name, shape, dt=F32):
        ap = self.nc.dram_tensor(name, list(shape), dt, kind="ExternalInput").ap()
        self.inp[name] = ap
        return ap

    def dout(self, name, shape, dt=F32):
        ap = self.nc.dram_tensor(name, list(shape), dt, kind="ExternalOutput").ap()
        self.out[name] = ap
        return ap

    def dscr(self, name, shape, dt=F32):
        if name in self.dbg.get("dump", ()):
            ap = self.nc.dram_tensor(name, list(shape), dt, kind="ExternalOutput").ap()
            self.out[name] = ap
        else:
            ap = self.nc.dram_tensor(name, list(shape), dt).ap()
        self.scr[name] = (ap, Buf(name))
        return ap, self.scr[name][1]

    def load(self, dst_ap, src_ap, dst_buf, reads=(), eng="sp"):
        return self.P.dma(eng, dst_ap, src_ap, dst_buf, reads=reads, writes=[dst_buf])

    def store(self, dst_ap, src_ap, src_buf, writes=(), eng="sp", final=False):
        op = self.P.dma(eng, dst_ap, src_ap, src_buf, reads=[src_buf], writes=writes)
        if final:
            self.stores.append(op)
        return op


CONST_COLS = {}
SCAN_STOP = [99]


def make_consts():
    cols = []
    off = 0

    def add(name, arr):
        nonlocal off
        a = np.zeros((128, arr.shape[1]), np.float32)
        a[:arr.shape[0]] = arr
        cols.append(a)
        CONST_COLS[name] = (off, arr.shape[1])
        off += arr.shape[1]
    idx = np.arange(C)
    add("ident", np.eye(128, dtype=np.float32))
    for d in ("f", "b"):
        if d == "f":
            before = idx[:, None] <= idx[None, :]
            sbefore = idx[:, None] < idx[None, :]
        else:
            before = idx[:, None] >= idx[None, :]
            sbefore = idx[:, None] > idx[None, :]
        after = ~before
        U = before.astype(np.float32)
        Us = sbefore.astype(np.float32)
        add("UU_" + d, np.concatenate([U, Us], axis=1))
        add("UR_" + d, np.concatenate([after, after], axis=1).astype(np.float32))
        half = np.concatenate([U, Us], axis=1)
        add("MASK_" + d, np.concatenate([half, half], axis=0))
        add("MASKN_" + d, Us.T.copy())
        add("MASKI_" + d, U)
        add("MNEG_" + d, (np.concatenate([half, half], axis=0) - 1.0) * 30000.0)
    return np.concatenate(cols, axis=1)


def scan_pass(B, name, H, dk, dv, lowrank, d, src, s0_ap, o_dst, o_dst_b, st_out, flags, flags_b,
              consts, consts_b, nchunks=NCHUNK, chunks_per_seg=SEG // C, scalar_decay=False):
    nc, P, A = B.nc, B.P, B.arena
    m0 = A.mark()
    hg = 4 if lowrank else 2
    ngroups = H // hg
    W_ = 256 if lowrank else 128
    NP = 128 if lowrank else 64

    def cst(nm, rows, c0=0, c1=None):
        o, n = CONST_COLS[nm]
        c1 = n if c1 is None else c1
        return consts[0:rows, o + c0:o + c1]
    ident = cst("ident", 128)
    UU = cst("UU_" + d, 64)
    UR = cst("UR_" + d, 64) if lowrank else cst("UR_" + d, 64, 0, 64)
    MASK = cst("MASK_" + d, 128)
    MASKN = cst("MASKN_" + d, 64)
    MASKI = cst("MASKI_" + d, 64)
    last = C - 1 if d == "f" else 0

    nb = 2
    la_t = [A.alloc(name + "la%d" % i, [64, H * dk]) for i in range(nb)]
    q_t = [A.alloc(name + "q%d" % i, [64, H * dk]) for i in range(nb)]
    la_b = [B.buf("sc_la%d" % i) for i in range(nb)]
    q_b = [B.buf("sc_q%d" % i) for i in range(nb)]
    if lowrank:
        a_t = [A.alloc(name + "a%d" % i, [64, H * dk]) for i in range(nb)]
        a_b = [B.buf("sc_a%d" % i) for i in range(nb)]
        k0_t = [A.alloc(name + "k0%d" % i, [64, H * dk]) for i in range(nb)]
        k0_b = [B.buf("sc_k0%d" % i) for i in range(nb)]
    bk_t = [A.alloc(name + "bk%d" % i, [NP, H * dk]) for i in range(nb)]
    bk_b = [B.buf("sc_bk%d" % i) for i in range(nb)]
    vs_t = [A.alloc(name + "vs%d" % i, [NP, H * dv]) for i in range(nb)]
    vs_b = [B.buf("sc_vs%d" % i) for i in range(nb)]
    E = A.alloc(name + "E", [dk, hg, W_]); E_b = Buf(name + "E")
    eH = A.alloc(name + "eH", [NP, hg * dk]); eH_b = Buf(name + "eH")
    LRf = A.alloc(name + "LR", [128, hg, W_]); LR_b = Buf(name + "LR")
    LR = LRf[0:dk]
    BKh = A.alloc(name + "BKh", [NP, hg * dk]); BKh_b = Buf(name + "BKh")
    BW = 128 if lowrank else 64
    BLK = A.alloc(name + "BLK", [NP, hg, BW]); BLK_b = Buf(name + "BLK")
    if lowrank:
        PQ = [A.alloc(name + "PQ%d" % i, [64, hg, 128]) for i in range(2)]
        PQ_b = [Buf(name + "PQ%d" % i) for i in range(2)]
        Rt = A.alloc(name + "R", [64, hg, 64]); R_b = Buf(name + "R")
        R1s = A.alloc(name + "R1s", [64, hg * dv]); R1s_b = Buf(name + "R1s")
    if scalar_decay:
        RAW = A.alloc(name + "RAW", [dk, hg, 256]); RAW_b = Buf(name + "RAW")
        DEC = A.alloc(name + "DEC", [128, hg, 128]); DEC_b = Buf(name + "DEC")
        hcol = A.alloc(name + "hcol", [128, hg]); gend = A.alloc(name + "gend", [128, hg]); hg_b = Buf(name + "hgend")
        MNEG = cst("MNEG_" + d, 128)
    Ost = [A.alloc(name + "Ost%d" % i, [64, H * dv]) for i in range(2)]
    Ost_b = [B.buf("sc_Ost%d" % i) for i in range(2)]
    Sf = A.alloc(name + "S", [128, H * dv]); S_b = [B.buf("sc_S%d" % g) for g in range(ngroups)]
    S = Sf[0:dk]
    Sst = A.alloc(name + "Sst", [dk, H * dv]); Sst_b = B.buf("sc_Sst")
    bank, bb = B.banks, B.bank_b

    if dk < 128:
        P.c("pool", lambda e: e.memset(Sf, 0.0), writes=[S_b[0]])
        P.c("pool", lambda e: e.memset(LRf, 0.0), writes=[LR_b])
    B.load(S, s0_ap, S_b[0])
    for g in range(1, ngroups):
        S_b[g].last_w = S_b[0].last_w

    order = list(range(nchunks)) if d == "f" else list(range(nchunks - 1, -1, -1))

    def issue_loads(ci):
        c = order[ci]
        i = ci % nb
        r0, r1 = c * C, (c + 1) * C
        B.load(la_t[i], src["la"][0][r0:r1, :], la_b[i], reads=[src["la"][1]])
        B.load(q_t[i], src["q"][0][r0:r1, :], q_b[i], reads=[src["q"][1]])
        if lowrank:
            B.load(a_t[i], src["a"][0][r0:r1, :], a_b[i], reads=[src["a"][1]])
            B.load(k0_t[i], src["k"][0][r0:r1, :], k0_b[i], reads=[src["k"][1]])
            B.load(bk_t[i][0:64, :], src["b"][0][r0:r1, :], bk_b[i], reads=[src["b"][1]])
            B.load(bk_t[i][64:128, :], src["k"][0][r0:r1, :], bk_b[i], reads=[src["k"][1]])
            B.load(vs_t[i][64:128, :], src["v"][0][r0:r1, :], vs_b[i], reads=[src["v"][1]])
        else:
            B.load(bk_t[i], src["k"][0][r0:r1, :], bk_b[i], reads=[src["k"][1]])
            B.load(vs_t[i], src["v"][0][r0:r1, :], vs_b[i], reads=[src["v"][1]])

    def chunk_body(ci):
        c = order[ci]
        i = ci % nb
        la, q, bk, vs = la_t[i], q_t[i], bk_t[i], vs_t[i]
        seg_start = (ci % chunks_per_seg == 0) and ci > 0
        seg_end = (ci % chunks_per_seg == chunks_per_seg - 1)
        seg = c // chunks_per_seg
        ost, ost_b = Ost[ci % 2], Ost_b[ci % 2]
        if seg_start:
            for g in range(ngroups):
                gs = slice(g * hg * dv, (g + 1) * hg * dv)
                P.c("pool", lambda e, gs=gs: e.tensor_scalar_mul(out=S[:, gs], in0=S[:, gs], scalar1=flags[0:dk, 0:1]),
                    reads=[S_b[g], flags_b], writes=[S_b[g]])
        def group_body(g):
            heads = list(range(g * hg, (g + 1) * hg))
            gk = slice(g * hg * dk, (g + 1) * hg * dk)
            gv = slice(g * hg * dv, (g + 1) * hg * dv)
            def mm_cum(e):
                ins = None
                for hi, h in enumerate(heads):
                    ins = e.matmul(bank[0][0:dk, hi * 128:(hi + 1) * 128], la[:, h * dk:(h + 1) * dk], UU, start=True, stop=True)
                return ins
            P.c("pe", mm_cum, reads=[la_b[i], consts_b], writes=[bb[0]])

            def mm_h2(e):
                ins = None
                for hi, h in enumerate(heads):
                    ins = e.matmul(bank[1][0:NP, hi * dk:(hi + 1) * dk], UR, la[:, h * dk:(h + 1) * dk], start=True, stop=True)
                return ins
            P.c("pe", mm_h2, reads=[la_b[i], consts_b], writes=[bb[1]])
            cumv = bank[0][0:dk, 0:hg * 128].rearrange("p (a b) -> p a b", a=hg)
            if scalar_decay:
                P.c("act", lambda e: e.activation(out=E[:, :, 0:128], in_=cumv, func=AF.Exp), reads=[bb[0]], writes=[E_b])
            elif lowrank:
                P.c("act", lambda e: e.activation(out=E[:, :, 0:128], in_=cumv, func=AF.Exp), reads=[bb[0]], writes=[E_b])
                P.c("act", lambda e: e.activation(out=E[:, :, 128:192], in_=cumv[:, :, 0:64], func=AF.Exp, scale=-1.0), reads=[bb[0]], writes=[E_b])
                P.c("act", lambda e: e.activation(out=E[:, :, 192:256], in_=cumv[:, :, 0:64], func=AF.Exp, scale=-1.0), reads=[bb[0]], writes=[E_b])
            else:
                P.c("act", lambda e: e.activation(out=E[:, :, 0:64], in_=cumv[:, :, 0:64], func=AF.Exp), reads=[bb[0]], writes=[E_b])
                P.c("act", lambda e: e.activation(out=E[:, :, 64:128], in_=cumv[:, :, 0:64], func=AF.Exp, scale=-1.0), reads=[bb[0]], writes=[E_b])
            P.c("act", lambda e: e.activation(out=eH, in_=bank[1][0:NP, 0:hg * dk], func=AF.Exp), reads=[bb[1]], writes=[eH_b])
            def mm_tr(e):
                ins = None
                for hi, h in enumerate(heads):
                    hs = slice(h * dk, (h + 1) * dk)
                    if lowrank:
                        bnk = bank[2 + hi // 2]
                        o = (hi % 2) * 256
                        e.transpose(bnk[0:dk, o:o + 64], q[:, hs], ident[0:64, 0:64])
                        e.transpose(bnk[0:dk, o + 64:o + 128], a_t[i][:, hs], ident[0:64, 0:64])
                        e.transpose(bnk[0:dk, o + 128:o + 192], bk[0:64, hs], ident[0:64, 0:64])
                        ins = e.transpose(bnk[0:dk, o + 192:o + 256], k0_t[i][:, hs], ident[0:64, 0:64])
                    else:
                        o = hi * 128
                        e.transpose(bank[2][0:dk, o:o + 64], q[:, hs], ident[0:64, 0:64])
                        ins = e.transpose(bank[2][0:dk, o + 64:o + 128], bk[:, hs], ident[0:64, 0:64])
                return ins
            rds = [q_b[i], bk_b[i], consts_b] + ([a_b[i], k0_b[i]] if lowrank else [])
            P.c("pe", mm_tr, reads=rds, writes=[bb[2], bb[3]] if lowrank else [bb[2]])
            if scalar_decay:
                for half in range(2):
                    trv = bank[2 + half][0:dk, :].rearrange("p (a b) -> p a b", a=2)
                    P.c("act", lambda e, half=half, trv=trv: e.activation(out=RAW[:, 2 * half:2 * half + 2, :], in_=trv, func=AF.Copy),
                        reads=[bb[2 + half]], writes=[RAW_b])
                    P.c("dve", lambda e, half=half: e.tensor_tensor(out=LR[:, 2 * half:2 * half + 2, 0:128], in0=RAW[:, 2 * half:2 * half + 2, 0:128], in1=E[:, 2 * half:2 * half + 2, 0:128], op=ALU.mult),
                        reads=[RAW_b, E_b], writes=[LR_b])
                h2v = bank[1][:, 0:hg * dk].rearrange("p (a b) -> p a b", a=hg)
                P.c("act", lambda e, h2v=h2v: e.activation(out=hcol.unsqueeze(2), in_=h2v[:, :, 0:1], func=AF.Copy), reads=[bb[1]], writes=[hg_b])
                P.c("act", lambda e: e.activation(out=gend.unsqueeze(2), in_=cumv[:, :, last:last + 1], func=AF.Copy), reads=[bb[0]], writes=[hg_b])
                P.c("dve", lambda e: e.tensor_tensor(out=hcol, in0=hcol, in1=gend, op=ALU.subtract), reads=[hg_b], writes=[hg_b])
                for hi in range(hg):
                    P.c("act", lambda e, hi=hi: e.activation(out=DEC[:, hi, :], in_=bank[0][:, hi * 128:(hi + 1) * 128], func=AF.Identity, bias=hcol[:, hi:hi + 1]),
                        reads=[bb[0], hg_b], writes=[DEC_b])
                P.c("pool", lambda e: e.tensor_tensor(out=DEC, in0=DEC, in1=MNEG.unsqueeze(1).to_broadcast([128, hg, 128]), op=ALU.add),
                    reads=[DEC_b, consts_b], writes=[DEC_b])
                P.c("act", lambda e: e.activation(out=DEC, in_=DEC, func=AF.Exp), reads=[DEC_b], writes=[DEC_b])
            elif lowrank:
                for half in range(2):
                    trv = bank[2 + half][0:dk, :].rearrange("p (a b) -> p a b", a=2)
                    P.c("dve", lambda e, half=half, trv=trv: e.tensor_tensor(out=LR[:, 2 * half:2 * half + 2, :], in0=trv, in1=E[:, 2 * half:2 * half + 2, :], op=ALU.mult),
                        reads=[bb[2 + half], E_b], writes=[LR_b])
            else:
                trv = bank[2][0:dk, 0:hg * 128].rearrange("p (a b) -> p a b", a=hg)
                P.c("dve", lambda e, trv=trv: e.tensor_tensor(out=LR, in0=trv, in1=E, op=ALU.mult), reads=[bb[2], E_b], writes=[LR_b])
            P.c("pool", lambda e: e.tensor_tensor(out=BKh, in0=bk[:, gk], in1=eH, op=ALU.mult), reads=[bk_b[i], eH_b], writes=[BKh_b])
            if SCAN_STOP[0] <= 1:
                return
            def mm_blk(e):
                ins = None
                for hi in range(hg):
                    if scalar_decay:
                        ins = e.matmul(bank[0][:, hi * 128:(hi + 1) * 128], RAW[:, hi, 128:256], RAW[:, hi, 0:128], start=True, stop=True)
                    elif lowrank:
                        ins = e.matmul(bank[0][:, hi * 128:(hi + 1) * 128], LR[:, hi, 128:256], LR[:, hi, 0:128], start=True, stop=True)
                    else:
                        ins = e.matmul(bank[0][0:64, hi * 64:(hi + 1) * 64], LR[:, hi, 64:128], LR[:, hi, 0:64], start=True, stop=True)
                return ins
            P.c("pe", mm_blk, reads=[LR_b] + ([RAW_b, DEC_b, hg_b] if scalar_decay else []), writes=[bb[0]])
            if scalar_decay:
                blkv = bank[0][:, :].rearrange("p (a b) -> p a b", a=hg)
                P.c("dve", lambda e, blkv=blkv: e.tensor_tensor(out=BLK, in0=blkv, in1=DEC, op=ALU.mult),
                    reads=[bb[0], DEC_b], writes=[BLK_b])
            elif lowrank:
                blkv = bank[0][:, :].rearrange("p (a b) -> p a b", a=hg)
                P.c("dve", lambda e, blkv=blkv: e.tensor_tensor(out=BLK, in0=blkv, in1=MASK.unsqueeze(1).to_broadcast([128, hg, 128]), op=ALU.mult),
                    reads=[bb[0], consts_b], writes=[BLK_b])
            else:
                blkv = bank[0][0:64, 0:hg * 64].rearrange("p (a b) -> p a b", a=hg)
                P.c("dve", lambda e, blkv=blkv: e.tensor_tensor(out=BLK, in0=blkv, in1=MASKI.unsqueeze(1).to_broadcast([64, hg, 64]), op=ALU.mult),
                    reads=[bb[0], consts_b], writes=[BLK_b])
            if SCAN_STOP[0] <= 2:
                return
            if lowrank:
                def mm_nab(e):
                    ins = None
                    for hi in range(hg):
                        if scalar_decay:
                            ins = e.transpose(bank[1][0:64, hi * 64:(hi + 1) * 64], BLK[0:64, hi, 64:128], ident[0:64, 0:64])
                        else:
                            ins = e.matmul(bank[1][0:64, hi * 64:(hi + 1) * 64], LR[:, hi, 64:128], LR[:, hi, 128:192], start=True, stop=True)
                    return ins
                P.c("pe", mm_nab, reads=[LR_b, BLK_b, consts_b], writes=[bb[1]])
                nabv = bank[1][0:64, 0:hg * 64].rearrange("p (a b) -> p a b", a=hg)
                P.c("dve", lambda e, nabv=nabv: e.tensor_tensor(out=PQ[0][:, :, 64:128], in0=nabv, in1=MASKN.unsqueeze(1).to_broadcast([64, hg, 64]), op=ALU.mult),
                    reads=[bb[1], consts_b], writes=[PQ_b[0]])
                P.c("pool", lambda e: e.tensor_copy(out=PQ[0][:, :, 0:64], in_=BLK[0:64, :, 64:128]), reads=[BLK_b], writes=[PQ_b[0]])
                P.c("pool", lambda e: e.tensor_tensor(out=Rt, in0=BLK[0:64, :, 64:128], in1=ident[0:64, 0:64].unsqueeze(1).to_broadcast([64, hg, 64]), op=ALU.add),
                    reads=[BLK_b, consts_b], writes=[R_b])
                for lev in range(1, 6):
                    pa, pb_ = PQ[(lev - 1) % 2], PQ[lev % 2]
                    pa_b, pb_b = PQ_b[(lev - 1) % 2], PQ_b[lev % 2]

                    def mm_sq(e, pa=pa):
                        ins = None
                        for hi in range(hg):
                            e.matmul(bank[2][0:64, hi * 128:hi * 128 + 64], pa[:, hi, 64:128], pa[:, hi, 0:64], start=True, stop=True)
                            ins = e.matmul(bank[2][0:64, hi * 128 + 64:hi * 128 + 128], pa[:, hi, 0:64], pa[:, hi, 64:128], start=True, stop=True)
                        return ins
                    P.c("pe", mm_sq, reads=[pa_b], writes=[bb[2]])
                    pqv = bank[2][0:64, :].rearrange("p (a b) -> p a b", a=hg)
                    P.c("act", lambda e, pb_=pb_, pqv=pqv: e.activation(out=pb_, in_=pqv, func=AF.Copy), reads=[bb[2]], writes=[pb_b])

                    def mm_r(e, pb_=pb_):
                        ins = None
                        for hi in range(hg):
                            ins = e.matmul(bank[1][0:64, hi * 64:(hi + 1) * 64], pb_[:, hi, 64:128], Rt[:, hi, :], start=True, stop=True)
                        return ins
                    P.c("pe", mm_r, reads=[pb_b, R_b], writes=[bb[1]])
                    rupv = bank[1][0:64, 0:hg * 64].rearrange("p (a b) -> p a b", a=hg)
                    P.c("dve", lambda e, rupv=rupv: e.tensor_tensor(out=Rt, in0=rupv, in1=Rt, op=ALU.add), reads=[bb[1], R_b], writes=[R_b])
            if SCAN_STOP[0] <= 3:
                return
            if lowrank:
                def mm_r1(e):
                    ins = None
                    for hi, h in enumerate(heads):
                        e.matmul(bank[4][0:64, hi * dv:(hi + 1) * dv], LRf[:, hi, 64:128], Sf[:, h * dv:(h + 1) * dv], start=True, stop=False)
                        ins = e.matmul(bank[4][0:64, hi * dv:(hi + 1) * dv], BLK[64:128, hi, 64:128], vs[64:128, h * dv:(h + 1) * dv], start=False, stop=True)
                    return ins
                P.c("pe", mm_r1, reads=[LR_b, S_b[g], BLK_b, vs_b[i]], writes=[bb[4]])
                P.c("act", lambda e: e.activation(out=R1s, in_=bank[4][0:64, 0:hg * dv], func=AF.Copy), reads=[bb[4]], writes=[R1s_b])

                def mm_sa(e):
                    ins = None
                    for hi in range(hg):
                        ins = e.matmul(bank[5][0:64, hi * dv:(hi + 1) * dv], Rt[:, hi, :], R1s[:, hi * dv:(hi + 1) * dv], start=True, stop=True)
                    return ins
                P.c("pe", mm_sa, reads=[R_b, R1s_b], writes=[bb[5]])
                P.c("dve", lambda e: e.tensor_copy(out=vs[0:64, gv], in_=bank[5][0:64, 0:hg * dv]), reads=[bb[5]], writes=[vs_b[i]])

                def mm_o(e):
                    ins = None
                    for hi, h in enumerate(heads):
                        e.matmul(bank[6][0:64, hi * dv:(hi + 1) * dv], LRf[:, hi, 0:64], Sf[:, h * dv:(h + 1) * dv], start=True, stop=False)
                        ins = e.matmul(bank[6][0:64, hi * dv:(hi + 1) * dv], BLK[:, hi, 0:64], vs[:, h * dv:(h + 1) * dv], start=False, stop=True)
                    return ins
                P.c("pe", mm_o, reads=[LR_b, S_b[g], BLK_b, vs_b[i]], writes=[bb[6]])
            else:
                def mm_o(e):
                    ins = None
                    for hi, h in enumerate(heads):
                        e.matmul(bank[6][0:64, hi * dv:(hi + 1) * dv], LRf[:, hi, 0:64], Sf[:, h * dv:(h + 1) * dv], start=True, stop=False)
                        ins = e.matmul(bank[6][0:64, hi * dv:(hi + 1) * dv], BLK[:, hi, :], vs[:, h * dv:(h + 1) * dv], start=False, stop=True)
                    return ins
                P.c("pe", mm_o, reads=[LR_b, S_b[g], BLK_b, vs_b[i]], writes=[bb[6]])
            P.c("act", lambda e, ost=ost: e.activation(out=ost[:, gv], in_=bank[6][0:64, 0:hg * dv], func=AF.Copy), reads=[bb[6]], writes=[ost_b])

            def mm_sd(e):
                ins = None
                for hi, h in enumerate(heads):
                    ins = e.matmul(bank[7][0:dk, hi * dv:(hi + 1) * dv], BKh[:, hi * dk:(hi + 1) * dk], vs[:, h * dv:(h + 1) * dv], start=True, stop=True)
                return ins
            P.c("pe", mm_sd, reads=[BKh_b, vs_b[i]], writes=[bb[7]])
            Sg = S[:, gv].rearrange("p (a b) -> p a b", a=hg)
            egend = E[:, :, last:last + 1].to_broadcast([dk, hg, dv])
            P.c("dve", lambda e, Sg=Sg, egend=egend: e.tensor_tensor(out=Sg, in0=Sg, in1=egend, op=ALU.mult), reads=[S_b[g], E_b], writes=[S_b[g]])
            P.c("dve", lambda e: e.tensor_tensor(out=S[:, gv], in0=S[:, gv], in1=bank[7][0:dk, 0:hg * dv], op=ALU.add), reads=[S_b[g], bb[7]], writes=[S_b[g]])
        for g in range(ngroups):
            group_body(g)
        B.store(o_dst[c * C:(c + 1) * C, :], ost, ost_b, writes=[o_dst_b])
        if seg_end and st_out is not None:
            P.c("act", lambda e: e.activation(out=Sst, in_=S, func=AF.Copy), reads=S_b, writes=[Sst_b])
            B.store(st_out[seg], Sst, Sst_b, final=True)

    issue_loads(0)
    for ci in range(nchunks):
        if ci + 1 < nchunks:
            issue_loads(ci + 1)
        chunk_body(ci)
    P.barrier()
    A.release(m0)


def build_program(dbg=None):
    B = Builder(dbg)
    nc, P, A = B.nc, B.P, B.arena
    dbg = B.dbg
    nlayers = dbg.get("nlayers", 2)

    xT_in = B.din("xT", [D, T])
    cond_in = B.din("cond", [128, NCH])
    W = {}
    for l in range(2):
        p = "l%d_" % l
        W[p + "w_mod"] = B.din(p + "w_mod", [D, 9 * D])
        W[p + "b_mod"] = B.din(p + "b_mod", [128, 9 * NCH])
        W[p + "norms"] = B.din(p + "norms", [128, 3 * NCH])
        for f in ("ffn1", "ffn2"):
            W[p + f + "_wg"] = B.din(p + f + "_wg", [D, DFF])
            W[p + f + "_wu"] = B.din(p + f + "_wu", [D, DFF])
            W[p + f + "_wd"] = B.din(p + f + "_wd", [DFF, D])
        W[p + "w_in"] = B.din(p + "w_in", [D, L0_IN if l == 0 else L1_IN])
        W[p + "w_out"] = B.din(p + "w_out", [D, D])
    fin_norm = B.din("final_norm", [128, NCH])
    yT_out = B.dout("yT", [D, T])
    consts_np = make_consts()
    consts_in = B.din("consts", consts_np.shape)
    flags_in = B.din("flags", [128, 2])
    tmask_in = B.din("tmask", [T, 4])
    rot_in = B.din("rot", [T, 128])
    s0_in = {"gla": B.din("s0_gla", [2, 128, 1024]), "ret": B.din("s0_ret", [2, 128, 1024]),
             "gdn": B.din("s0_gdn", [2, 128, 1024]), "rwkv": B.din("s0_rwkv", [2, 64, 1024])}
    st_out = {"gla": B.dout("st_gla", [2, NSEG, 128, 1024]), "ret": B.dout("st_ret", [2, NSEG, 128, 1024]),
              "gdn": B.dout("st_gdn", [2, NSEG, 128, 1024]), "rwkv": B.dout("st_rwkv", [2, NSEG, 64, 1024])}
    SP_ = {}
    for nm, shp in (("l0_gk_up", [16, 1024]), ("l0_gk_b", [1, 1024]), ("l0_gla_norm", [1, 256]), ("l0_ret_norm", [1, 256]),
                    ("ret_la_f", [T, 512]), ("ret_la_b", [T, 512]),
                    ("l1_conv", [5, 3072]), ("l1_dtb", [1, 16]), ("l1_alog", [1, 16]), ("l1_gdn_norm", [1, 128]),
                    ("l1_mu", [1, 3456]), ("l1_w0", [1, 2048]), ("l1_a0", [1, 2048]), ("l1_w2", [64, 2048]),
                    ("l1_a2", [64, 2048]), ("l1_g2", [128, 1024]), ("l1_kk", [1, 1024]), ("l1_ka", [1, 1024]),
                    ("l1_rk", [1, 1024]), ("l1_lnw", [1, 1024]), ("l1_lnb", [1, 1024])):
        SP_[nm] = B.din(nm, shp)

    ones_bf = A.alloc("ones_bf", [128, 128], BF16)
    ones_b = Buf("ones_bf")
    P.c("pool", lambda e: e.memset(ones_bf, 1.0), writes=[ones_b])
    mods = [A.alloc("mods%d" % l, [128, 9 * NCH]) for l in range(2)]
    mods_b = [Buf("mods%d" % l) for l in range(2)]
    norms = [A.alloc("norms%d" % l, [128, 3 * NCH]) for l in range(2)]
    norms_b = [Buf("norms%d" % l) for l in range(2)]
    modA = [A.alloc("modA%d" % l, [128, 3 * NCH]) for l in range(2)]
    modG = [A.alloc("modG%d" % l, [128, 3 * NCH]) for l in range(2)]
    modc_b = [Buf("modc%d" % l) for l in range(2)]
    finw = A.alloc("finw", [128, NCH])
    finw_b = Buf("finw")
    B.load(finw, fin_norm, finw_b)
    consts = A.alloc("consts", list(consts_np.shape))
    consts_b = Buf("consts")
    B.load(consts, consts_in, consts_b)
    flags = A.alloc("flags", [128, 2])
    flags_b = Buf("flags")
    B.load(flags, flags_in, flags_b)
    ones_f = A.alloc("ones_f", [1, 128])
    ones_f_b = Buf("ones_f")
    P.c("pool", lambda e: e.memset(ones_f, 1.0), writes=[ones_f_b])
    ident = consts[:, CONST_COLS["ident"][0]:CONST_COLS["ident"][0] + 128]

    def phase_mods(l):
        p = "l%d_" % l
        m0 = A.mark()
        cond = A.alloc("cond", [128, NCH])
        cond_b = B.buf("cond")
        scond = A.alloc("scond", [128, NCH], BF16)
        scond_b = Buf("scond")
        bmod = A.alloc("bmod", [128, 9 * NCH])
        bmod_b = B.buf("bmod")
        B.load(cond, cond_in, cond_b)
        B.load(bmod, W[p + "b_mod"], bmod_b)
        B.load(norms[l], W[p + "norms"], norms_b[l])
        P.c("act", lambda e: e.activation(out=scond, in_=cond, func=AF.Silu), reads=[cond_b], writes=[scond_b])
        wsl = Slots(A, "wmod", 3, [128, NCH, 512], BF16)
        wsl.bufs = [B.buf("wmod%d" % i) for i in range(3)]
        wsrc = W[p + "w_mod"].rearrange("(k p) n -> p k n", p=128)
        ps = B.banks[0]
        ps_b = B.bank_b[0]
        ngrp = 9 * D // 512
        for g in range(ngrp):
            wt, wt_b = wsl.next()
            B.load(wt, wsrc[:, :, g * 512:(g + 1) * 512], wt_b, eng="pool")

            def mm(e, wt=wt, g=g):
                ins = None
                for cidx in range(4):
                    col = g * 4 + cidx
                    for k in range(NCH):
                        ins = e.matmul(ps[:, col:col + 1], wt[:, k, cidx * 128:(cidx + 1) * 128],
                                       scond[:, k:k + 1], start=(k == 0), stop=(k == NCH - 1))
                return ins
            P.c("pe", mm, reads=[wt_b, scond_b], writes=[ps_b])
        P.c("dve", lambda e: e.tensor_tensor(out=mods[l], in0=ps[:, 0:9 * NCH], in1=bmod, op=ALU.add),
            reads=[ps_b, bmod_b], writes=[mods_b[l]])
        for i in range(3):
            sc = mods[l][:, (3 * i + 1) * NCH:(3 * i + 2) * NCH]
            g = mods[l][:, (3 * i + 2) * NCH:(3 * i + 3) * NCH]
            gam = norms[l][:, i * NCH:(i + 1) * NCH]
            P.c("dve", lambda e, sc=sc, gam=gam, i=i: e.scalar_tensor_tensor(
                out=modA[l][:, i * NCH:(i + 1) * NCH], in0=sc, scalar=1.0, in1=gam, op0=ALU.add, op1=ALU.mult),
                reads=[mods_b[l], norms_b[l]], writes=[modc_b[l]])
            fac = 1.0 if i == 1 else 0.5
            P.c("dve", lambda e, g=g, i=i, fac=fac: e.tensor_scalar_mul(
                out=modG[l][:, i * NCH:(i + 1) * NCH], in0=g, scalar1=fac),
                reads=[mods_b[l]], writes=[modc_b[l]])
        P.barrier()
        A.release(m0)

    def tt(eng, out, in0, in1, op, r, w):
        return P.c(eng, lambda e: e.tensor_tensor(out=out, in0=in0, in1=in1, op=op), reads=r, writes=w)

    def stt(eng, out, in0, scalar, in1, op0, op1, r, w):
        eng = "dve"
        return P.c(eng, lambda e: e.scalar_tensor_tensor(out=out, in0=in0, scalar=scalar, in1=in1, op0=op0, op1=op1), reads=r, writes=w)

    def tsm(eng, out, in0, s1, r, w):
        return P.c(eng, lambda e: e.tensor_scalar_mul(out=out, in0=in0, scalar1=s1), reads=r, writes=w)

    def tcopy(eng, out, in_, r, w):
        return P.c(eng, lambda e: e.tensor_copy(out=out, in_=in_), reads=r, writes=w)

    def actf(out, in_, func, r, w, scale=1.0, bias=None):
        def fn(e):
            if bias is None:
                return e.activation(out=out, in_=in_, func=func, scale=scale)
            return e.activation(out=out, in_=in_, func=func, scale=scale, bias=bias)
        return P.c("act", fn, reads=r, writes=w)

    def red(eng, out, in_, r, w):
        return P.c(eng, lambda e: e.tensor_reduce(out=out, in_=in_, axis=AX.X, op=ALU.add), reads=r, writes=w)

    def rstd_small(t, r_b, sc, eps):
        actf(t, t, AF.Ln, [r_b], [r_b], scale=sc, bias=eps)
        actf(t, t, AF.Exp, [r_b], [r_b], scale=-0.5)

    xs_scr, _ = B.dscr("xs", [D, T])
    xs_tile_b = [Buf("xs_t%d" % i) for i in range(NT)]
    oT_scr, oT_b = B.dscr("oT", [D, T], BF16)
    SC = {}
    for nm, shp in (("gla_q", [T, 512]), ("gla_k", [T, 512]), ("gla_v", [T, 1024]), ("gla_g", [T, 1024]),
                    ("gla_la_f", [T, 512]), ("gla_la_b", [T, 512]),
                    ("ret_q", [T, 512]), ("ret_k", [T, 512]), ("ret_v", [T, 1024]), ("ret_g", [T, 1024]),
                    ("OA_f", [T, 1024]), ("OA_b", [T, 1024]), ("OB_f", [T, 1024]), ("OB_b", [T, 1024]),
                    ("zq", [T + 4, 3072]), ("gdn_g", [T, 1024]), ("zr", [T + 4, 3456]), ("ab", [T, 32]),
                    ("gdn_q", [T, 1024]), ("gdn_a", [T, 1024]), ("gdn_v", [T, 1024]),
                    ("gdn_la_f", [T, 1024]), ("gdn_la_b", [T, 1024]), ("gdn_k_f", [T, 1024]), ("gdn_k_b", [T, 1024]),
                    ("gdn_b_f", [T, 1024]), ("gdn_b_b", [T, 1024]),
                    ("rw_q", [T, 1024]), ("rw_v", [T, 1024]), ("rw_a", [T, 1024]),
                    ("rw_la_f", [T, 1024]), ("rw_la_b", [T, 1024]), ("rw_k_f", [T, 1024]), ("rw_k_b", [T, 1024]),
                    ("rw_b_f", [T, 1024]), ("rw_b_b", [T, 1024]), ("rw_gate", [T, 1024]), ("rw_bonus", [T, 1024])):
        SC[nm] = B.dscr(nm, shp)
    SC["ret_la_f"] = (SP_["ret_la_f"], Buf("ret_la_f"))
    SC["ret_la_b"] = (SP_["ret_la_b"], Buf("ret_la_b"))

    class TileCtx:
        pass

    def tile_alloc():
        tc = TileCtx()
        tc.x = A.alloc("xtile", [128, NCH, TT])
        tc.x_b = [Buf("xtile%d" % j) for j in range(NCH)]
        tc.x_own = [B.buf("xown%d" % g) for g in range(4)]
        tc.h = A.alloc("htile", [128, NCH, TT], BF16)
        tc.h_b = [Buf("htile%d" % j) for j in range(NCH)]
        tc.act = A.alloc("acttile", [128, NF, TT], BF16)
        tc.act_b = [Buf("act%d" % j) for j in range(NF)]
        tc.sq = A.alloc("sq", [128, 2, TT], BF16)
        tc.sq_b = [Buf("sq0"), Buf("sq1")]
        tc.rstd = A.alloc("rstd", [128, TT])
        tc.rstd_b = Buf("rstd")
        tc.tmp = A.alloc("tmpx", [128, 2, TT])
        tc.tmp_b = [Buf("tmpx0"), Buf("tmpx1")]
        tc.sg = A.alloc("sg", [128, 2, TT])
        tc.sg_b = [Buf("sg0"), Buf("sg1")]
        tc.wgu = Slots(A, "wgu", 2, [128, NCH * 512], BF16)
        tc.wgu.bufs = [B.buf("wgu%d" % i) for i in range(2)]
        tc.wd = Slots(A, "wd", 2, [128, 11, 512], BF16)
        tc.wd.bufs = [B.buf("wd%d" % i) for i in range(2)]
        tc.stg = [A.alloc("stg%d" % i, [128, 512]) for i in range(4)]
        tc.stg_b = [B.buf("stg%d" % i) for i in range(4)]
        tc.stg_i = 0
        tc.rot = A.alloc("rot_t", [128, 128])
        tc.rot_b = B.buf("rot_t")
        tc.rx = A.alloc("rot_x", [128, 4, 128])
        tc.rx_b = Buf("rot_x")
        tc.rtmp = [A.alloc("rot_tmp%d" % i, [128, 4, 64]) for i in range(4)]
        tc.rtmp_b = [Buf("rot_tmp%d" % i) for i in range(4)]
        tc.wtail = A.alloc("wtail", [128, NCH, 32], BF16)
        tc.wtail_b = B.buf("wtail")
        tc.gdT = A.alloc("gdT", [16, 2, TT])
        tc.gdT_b = Buf("gdT")
        tc.gkup = A.alloc("gkup", [16, 2, 512])
        tc.gkb = A.alloc("gkb", [1, 2, 512])
        tc.gk_b = B.buf("gkparams")
        B.dma_in(tc.gkup, SP_["l0_gk_up"].rearrange("k (d c) -> k d c", d=2), tc.gk_b, [tc.gk_b])
        B.dma_in(tc.gkb, SP_["l0_gk_b"].rearrange("k (d c) -> k d c", d=2), tc.gk_b, [tc.gk_b])
        return tc

    def next_stage(tc):
        i = tc.stg_i
        tc.stg_i = (i + 1) % 4
        return tc.stg[i], tc.stg_b[i]

    def x_load(tc, src, t, src_bufs=()):
        srcv = src.rearrange("(j p) t -> p j t", p=128)
        for g in range(4):
            B.dma_in(tc.x[:, 4 * g:4 * g + 4, :], srcv[:, 4 * g:4 * g + 4, t * TT:(t + 1) * TT], tc.x_own[g],
                     tc.x_b[4 * g:4 * g + 4], reads=src_bufs)

    def x_store(tc, dst, t, dst_bufs=(), final=False):
        dstv = dst.rearrange("(j p) t -> p j t", p=128)
        for g in range(4):
            B.dma_out(dstv[:, 4 * g:4 * g + 4, t * TT:(t + 1) * TT], tc.x[:, 4 * g:4 * g + 4, :], tc.x_own[g],
                      tc.x_b[4 * g:4 * g + 4], writes=dst_bufs, final=final)

    def rms_stats(tc):
        ps = B.banks[7]
        ps_b = B.bank_b[7]
        for j in range(NCH):
            s = j % 2
            if j % 2 == 0:
                P.c("act", lambda e, j=j, s=s: e.activation(out=tc.sq[:, s, :], in_=tc.x[:, j, :], func=AF.Square),
                    reads=[tc.x_b[j]], writes=[tc.sq_b[s]])
            else:
                P.c("pool", lambda e, j=j, s=s: e.tensor_tensor(out=tc.sq[:, s, :], in0=tc.x[:, j, :], in1=tc.x[:, j, :], op=ALU.mult),
                    reads=[tc.x_b[j]], writes=[tc.sq_b[s]])
            P.c("pe", lambda e, j=j, s=s: e.matmul(ps[:, :], ones_bf, tc.sq[:, s, :], start=(j == 0), stop=(j == NCH - 1)),
                reads=[tc.sq_b[s], ones_b], writes=[ps_b])
        actf(tc.rstd, ps[:, :], AF.Ln, [ps_b], [tc.rstd_b], scale=1.0 / D, bias=EPS)
        actf(tc.rstd, tc.rstd, AF.Exp, [tc.rstd_b], [tc.rstd_b], scale=-0.5)

    def modulate(tc, l, i):
        rms_stats(tc)
        for j in range(NCH):
            s = j % 2
            a_col = modA[l][:, i * NCH + j:i * NCH + j + 1]
            sh_col = mods[l][:, (3 * i) * NCH + j:(3 * i) * NCH + j + 1]
            stt("dve", tc.tmp[:, s, :], tc.x[:, j, :], a_col, tc.rstd, ALU.mult, ALU.mult,
                [tc.x_b[j], tc.rstd_b, modc_b[l]], [tc.tmp_b[s]])
            actf(tc.h[:, j, :], tc.tmp[:, s, :], AF.Identity, [tc.tmp_b[s], mods_b[l]], [tc.h_b[j]], bias=sh_col)

    def ffn(tc, l, which, i):
        p = "l%d_ffn%d_" % (l, which)
        wg_src = W[p + "wg"].rearrange("(k p) n -> p k n", p=128)
        wu_src = W[p + "wu"].rearrange("(k p) n -> p k n", p=128)
        wd_src = W[p + "wd"].rearrange("(f p) n -> p f n", p=128)
        for g in range(NF // 2):
            wflat, wt_b = tc.wgu.next()
            wt = wflat.rearrange("p (a k c) -> p a k c", a=2, k=NCH)
            B.load(wt[:, 0], wg_src[:, :, g * 256:(g + 1) * 256], wt_b, eng="pool")
            B.load(wt[:, 1], wu_src[:, :, g * 256:(g + 1) * 256], wt_b, eng="pool")
            for ci in range(2):
                f = g * 2 + ci
                pg, pg_b = B.banks[(f % 2) * 2], B.bank_b[(f % 2) * 2]
                pu, pu_b = B.banks[(f % 2) * 2 + 1], B.bank_b[(f % 2) * 2 + 1]

                def mm(e, wt=wt, ci=ci, which_w=0, ps=pg):
                    ins = None
                    for k in range(NCH):
                        ins = e.matmul(ps[:, :], wt[:, which_w, k, ci * 128:(ci + 1) * 128], tc.h[:, k, :],
                                       start=(k == 0), stop=(k == NCH - 1))
                    return ins
                P.c("pe", lambda e, mm=mm, wt=wt, ci=ci, pg=pg: mm(e, wt, ci, 0, pg), reads=[wt_b] + tc.h_b, writes=[pg_b])
                P.c("pe", lambda e, mm=mm, wt=wt, ci=ci, pu=pu: mm(e, wt, ci, 1, pu), reads=[wt_b] + tc.h_b, writes=[pu_b])
                s = f % 2
                actf(tc.sg[:, s, :], pg[:, :], AF.Silu, [pg_b], [tc.sg_b[s]])
                tt("dve", tc.act[:, f, :], pu[:, :], tc.sg[:, s, :], ALU.mult, [pu_b, tc.sg_b[s]], [tc.act_b[f]])
        for dg in range(4):
            pbanks = [4 + q for q in range(4)]
            for part in range(4):
                wt, wt_b = tc.wd.next()
                B.load(wt, wd_src[:, part * 11:(part + 1) * 11, dg * 512:(dg + 1) * 512], wt_b, eng="pool")
                for q in range(4):
                    def mm(e, wt=wt, part=part, q=q):
                        ins = None
                        for fi in range(11):
                            f = part * 11 + fi
                            ins = e.matmul(B.banks[pbanks[q]][:, :], wt[:, fi, q * 128:(q + 1) * 128], tc.act[:, f, :],
                                           start=(f == 0), stop=(f == NF - 1))
                        return ins
                    P.c("pe", mm, reads=[wt_b] + tc.act_b[part * 11:(part + 1) * 11], writes=[B.bank_b[pbanks[q]]])
            for q in range(4):
                j = dg * 4 + q
                gcol = modG[l][:, i * NCH + j:i * NCH + j + 1]
                stt("dve", tc.x[:, j, :], B.banks[pbanks[q]][:, :], gcol, tc.x[:, j, :], ALU.mult, ALU.add,
                    [B.bank_b[pbanks[q]], modc_b[l], tc.x_b[j]], [tc.x_b[j]])

    def final_norm(tc):
        rms_stats(tc)
        for j in range(NCH):
            stt("dve", tc.x[:, j, :], tc.x[:, j, :], finw[:, j:j + 1], tc.rstd, ALU.mult, ALU.mult,
                [tc.x_b[j], tc.rstd_b, finw_b], [tc.x_b[j]])

    def wout(tc, l, t):
        own = B.buf("oTload")
        B.dma_in(tc.act[:, 0:NCH, :], oT_scr.rearrange("(k p) t -> p k t", p=128)[:, :, t * TT:(t + 1) * TT], own,
                 tc.act_b[0:NCH], reads=[oT_b])
        w_src = W["l%d_w_out" % l].rearrange("(k p) n -> p k n", p=128)
        for dg in range(4):
            wflat, wt_b = tc.wgu.next()
            wv = wflat.rearrange("p (k c) -> p k c", k=NCH)
            B.load(wv, w_src[:, :, dg * 512:(dg + 1) * 512], wt_b, eng="pool")
            for q in range(4):
                j = dg * 4 + q
                ps, ps_b = B.banks[q], B.bank_b[q]

                def mm(e, wv=wv, q=q, ps=ps):
                    ins = None
                    for k in range(NCH):
                        ins = e.matmul(ps[:, :], wv[:, k, q * 128:(q + 1) * 128], tc.act[:, k, :], start=(k == 0), stop=(k == NCH - 1))
                    return ins
                P.c("pe", mm, reads=[wt_b] + tc.act_b[0:NCH], writes=[ps_b])
                gcol = modG[l][:, NCH + j:NCH + j + 1]
                stt("dve", tc.x[:, j, :], ps[:, :], gcol, tc.x[:, j, :], ALU.mult, ALU.add,
                    [ps_b, modc_b[l], tc.x_b[j]], [tc.x_b[j]])

    QS = 128 ** -0.5

    def proj(tc, l, t, plan, ncols_tail):
        w_src = W["l%d_w_in" % l].rearrange("(k p) n -> p k n", p=128)
        for cg in range(len(plan)):
            dst, row_off, col_off, kind, scale = plan[cg]
            dst_ap, dst_b = SC[dst]
            wflat, wt_b = tc.wgu.next()
            wv = wflat.rearrange("p (k c) -> p k c", k=NCH)
            B.load(wv, w_src[:, :, cg * 512:(cg + 1) * 512], wt_b, eng="pool")
            for tb in range(4):
                ps, ps_b = B.banks[(cg * 4 + tb) % 4], B.bank_b[(cg * 4 + tb) % 4]
                r0 = t * TT + tb * 128

                def mm(e, wv=wv, tb=tb, ps=ps):
                    ins = None
                    for k in range(NCH):
                        ins = e.matmul(ps[:, :], tc.h[:, k, tb * 128:(tb + 1) * 128], wv[:, k, :], start=(k == 0), stop=(k == NCH - 1))
                    return ins
                P.c("pe", mm, reads=[wt_b] + tc.h_b, writes=[ps_b])
                st, st_b = next_stage(tc)
                if kind == "copy":
                    if (cg + tb) % 2 == 0:
                        actf(st, ps[:, :], AF.Identity, [ps_b], [st_b], scale=scale)
                    else:
                        tsm("dve", st, ps[:, :], scale, [ps_b], [st_b])
                elif kind == "silu":
                    actf(st, ps[:, :], AF.Silu, [ps_b], [st_b])
                else:
                    B.dma_in(tc.rot, rot_in[r0:r0 + 128, :], tc.rot_b, [tc.rot_b])
                    cosB = tc.rot[:, 0:64].unsqueeze(1).to_broadcast([128, 4, 64])
                    sinB = tc.rot[:, 64:128].unsqueeze(1).to_broadcast([128, 4, 64])
                    psv = ps[:, :].rearrange("p (h c) -> p h c", h=4)
                    actf(tc.rx, psv, AF.Identity, [ps_b], [tc.rx_b], scale=scale)
                    sv = st.rearrange("p (h c) -> p h c", h=4)
                    tt("dve", tc.rtmp[0], tc.rx[:, :, 0:64], cosB, ALU.mult, [tc.rx_b, tc.rot_b], [tc.rtmp_b[0]])
                    tt("pool", tc.rtmp[1], tc.rx[:, :, 64:128], sinB, ALU.mult, [tc.rx_b, tc.rot_b], [tc.rtmp_b[1]])
                    tt("dve", sv[:, :, 0:64], tc.rtmp[0], tc.rtmp[1], ALU.subtract, [tc.rtmp_b[0], tc.rtmp_b[1]], [st_b])
                    tt("pool", tc.rtmp[2], tc.rx[:, :, 0:64], sinB, ALU.mult, [tc.rx_b, tc.rot_b], [tc.rtmp_b[2]])
                    tt("dve", tc.rtmp[3], tc.rx[:, :, 64:128], cosB, ALU.mult, [tc.rx_b, tc.rot_b], [tc.rtmp_b[3]])
                    tt("pool", sv[:, :, 64:128], tc.rtmp[2], tc.rtmp[3], ALU.add, [tc.rtmp_b[2], tc.rtmp_b[3]], [st_b])
                B.dma_out(dst_ap[row_off + r0:row_off + r0 + 128, col_off:col_off + 512], st, st_b, [st_b], writes=[dst_b])
        c0 = len(plan) * 512
        if l == 0:
            B.load(tc.wtail, w_src[:, :, c0:c0 + 32], tc.wtail_b, eng="pool")
            for d_ in range(2):
                ps, ps_b = B.banks[d_], B.bank_b[d_]

                def mm(e, d_=d_, ps=ps):
                    ins = None
                    for k in range(NCH):
                        ins = e.matmul(ps[0:16, :], tc.wtail[:, k, d_ * 16:(d_ + 1) * 16], tc.h[:, k, :], start=(k == 0), stop=(k == NCH - 1))
                    return ins
                P.c("pe", mm, reads=[tc.wtail_b] + tc.h_b, writes=[ps_b])
                actf(tc.gdT[:, d_, :], ps[0:16, :], AF.Copy, [ps_b], [tc.gdT_b])
            for d_ in range(2):
                dst_ap, dst_b = SC["gla_la_f" if d_ == 0 else "gla_la_b"]
                for tb in range(4):
                    ps, ps_b = B.banks[2 + (tb % 2)], B.bank_b[2 + (tb % 2)]
                    r0 = t * TT + tb * 128

                    def mm(e, d_=d_, tb=tb, ps=ps):
                        e.matmul(ps[:, :], tc.gdT[0:16, d_, tb * 128:(tb + 1) * 128], tc.gkup[0:16, d_, :], start=True, stop=False)
                        return e.matmul(ps[:, :], ones_f[0:1, 0:128], tc.gkb[0:1, d_, :], start=False, stop=True)
                    P.c("pe", mm, reads=[tc.gdT_b, tc.gk_b, ones_f_b], writes=[ps_b])
                    st, st_b = next_stage(tc)
                    actf(st, ps[:, :], AF.Exp, [ps_b], [st_b], scale=-1.0)
                    actf(st, st, AF.Ln, [st_b], [st_b], bias=1.0)
                    tsm("dve", st, st, -1.0 / 16.0, [st_b], [st_b])
                    B.dma_out(dst_ap[r0:r0 + 128, :], st, st_b, [st_b], writes=[dst_b])
        else:
            wflat, wt_b = tc.wgu.next()
            wv = wflat[:, 0:NCH * 416].rearrange("p (k c) -> p k c", k=NCH)
            B.load(wv, w_src[:, :, c0:c0 + 416], wt_b, eng="pool")
            for tb in range(4):
                ps, ps_b = B.banks[tb % 4], B.bank_b[tb % 4]
                r0 = t * TT + tb * 128

                def mm(e, wv=wv, tb=tb, ps=ps):
                    ins = None
                    for k in range(NCH):
                        ins = e.matmul(ps[:, 0:416], tc.h[:, k, tb * 128:(tb + 1) * 128], wv[:, k, :], start=(k == 0), stop=(k == NCH - 1))
                    return ins
                P.c("pe", mm, reads=[wt_b] + tc.h_b, writes=[ps_b])
                st, st_b = next_stage(tc)
                actf(st[:, 0:416], ps[:, 0:416], AF.Copy, [ps_b], [st_b])
                B.dma_out(SC["zr"][0][2 + r0:2 + r0 + 128, 3072:3456], st[:, 0:384], st_b, [st_b], writes=[SC["zr"][1]])
                B.dma_out(SC["ab"][0][r0:r0 + 128, :], st[:, 384:416], st_b, [st_b], writes=[SC["ab"][1]])

    PLAN0 = [("gla_q", 0, 0, "copy", QS), ("gla_k", 0, 0, "copy", 1.0), ("gla_v", 0, 0, "copy", 1.0), ("gla_v", 0, 512, "copy", 1.0),
             ("gla_g", 0, 0, "silu", 1.0), ("gla_g", 0, 512, "silu", 1.0),
             ("ret_q", 0, 0, "rot", QS), ("ret_k", 0, 0, "rot", 1.0), ("ret_v", 0, 0, "copy", 1.0), ("ret_v", 0, 512, "copy", 1.0),
             ("ret_g", 0, 0, "silu", 1.0), ("ret_g", 0, 512, "silu", 1.0)]
    PLAN1 = [("zq", 2, 512 * i, "copy", 1.0) for i in range(6)] + [("gdn_g", 0, 0, "silu", 1.0), ("gdn_g", 0, 512, "silu", 1.0)] + \
            [("zr", 2, 512 * i, "copy", 1.0) for i in range(6)]

    def post_phase(l):
        B.phase()
        m0 = A.mark()
        specs = [("OA", 4, 256, False, EPS), ("OB", 4, 256, True, EPS)] if l == 0 else \
                [("OA", 8, 128, False, EPS), ("OB", 16, 64, True, 64e-5)]
        oall = A.alloc("oall", [128, D])
        oall_b = Buf("oall")
        oTst = A.alloc("oTst", [128, NCH, 128], BF16)
        oTst_b = B.pbuf()
        Of = [A.alloc("Of%d" % i, [128, 1024]) for i in range(2)]
        Ob = [A.alloc("Ob%d" % i, [128, 1024]) for i in range(2)]
        Gt = [A.alloc("Gt%d" % i, [128, 1024]) for i in range(2)]
        Of_b = [B.pbuf() for i in range(2)]
        Ob_b = [B.pbuf() for i in range(2)]
        Gt_b = [B.pbuf() for i in range(2)]
        sqt = A.alloc("sqt", [128, 1024])
        sqt_b = Buf("sqt")
        ss = A.alloc("ss", [128, 16])
        ss_b = Buf("ss")
        sm = A.alloc("sm", [128, 16])
        sm_b = Buf("sm")
        pown = B.pbuf()
        if l == 0:
            nwA = A.alloc("nwA", [128, 256])
            nwB = A.alloc("nwB", [128, 256])
            B.dma_in(nwA, SP_["l0_gla_norm"].partition_broadcast(128), pown, [pown])
            B.dma_in(nwB, SP_["l0_ret_norm"].partition_broadcast(128), pown, [pown])
            gates = ["gla_g", "ret_g"]
        else:
            nwA = A.alloc("nwA", [128, 128])
            lnw = A.alloc("lnw", [128, 1024])
            lnb = A.alloc("lnb", [128, 1024])
            B.dma_in(nwA, SP_["l1_gdn_norm"].partition_broadcast(128), pown, [pown])
            B.dma_in(lnw, SP_["l1_lnw"].partition_broadcast(128), pown, [pown])
            B.dma_in(lnb, SP_["l1_lnb"].partition_broadcast(128), pown, [pown])
            bon = [A.alloc("bon%d" % i, [128, 1024]) for i in range(2)]
            bon_b = [B.pbuf() for i in range(2)]
            gates = ["gdn_g", "rw_gate"]
        for tb in range(T // 128):
            r0 = tb * 128
            for mi, (onm, Hh, dvv, center, eps) in enumerate(specs):
                i = (tb * 2 + mi) % 2
                B.dma_in(Of[i], SC[onm + "_f"][0][r0:r0 + 128, :], Of_b[i], [Of_b[i]], reads=[SC[onm + "_f"][1]])
                B.dma_in(Ob[i], SC[onm + "_b"][0][r0:r0 + 128, :], Ob_b[i], [Ob_b[i]], reads=[SC[onm + "_b"][1]])
                B.dma_in(Gt[i], SC[gates[mi]][0][r0:r0 + 128, :], Gt_b[i], [Gt_b[i]], reads=[SC[gates[mi]][1]])
                o = Of[i]
                o_b = Of_b[i]
                o3 = o.rearrange("p (h c) -> p h c", h=Hh)
                tt("dve", o, Of[i], Ob[i], ALU.add, [Of_b[i], Ob_b[i]], [o_b])
                if center:
                    red("dve", sm[:, 0:Hh], o3, [o_b], [sm_b])
                    tsm("dve", sm[:, 0:Hh], sm[:, 0:Hh], -1.0 / dvv, [sm_b], [sm_b])
                    tt("pool", o3, o3, sm[:, 0:Hh].unsqueeze(2).to_broadcast([128, Hh, dvv]), ALU.add, [o_b, sm_b], [o_b])
                tt("pool", sqt, o, o, ALU.mult, [o_b], [sqt_b])
                red("dve", ss[:, 0:Hh], sqt.rearrange("p (h c) -> p h c", h=Hh), [sqt_b], [ss_b])
                rstd_small(ss[:, 0:Hh], ss_b, 1.0 / dvv, eps)
                tt("dve", o3, o3, ss[:, 0:Hh].unsqueeze(2).to_broadcast([128, Hh, dvv]), ALU.mult, [o_b, ss_b], [o_b])
                dsto = oall[:, mi * 1024:(mi + 1) * 1024]
                if l == 1 and mi == 1:
                    B.dma_in(bon[0], SC["rw_bonus"][0][r0:r0 + 128, :], bon_b[0], [bon_b[0]], reads=[SC["rw_bonus"][1]])
                    tt("pool", o, o, lnw, ALU.mult, [o_b, pown], [o_b])
                    tt("dve", o, o, lnb, ALU.add, [o_b, pown], [o_b])
                    tt("pool", o, o, bon[0], ALU.add, [o_b, bon_b[0]], [o_b])
                else:
                    nw = nwA if mi == 0 else nwB
                    tt("pool", o3, o3, nw.unsqueeze(1).to_broadcast([128, Hh, dvv]), ALU.mult, [o_b, pown], [o_b])
                tt("dve", dsto, o, Gt[i], ALU.mult, [o_b, Gt_b[i]], [oall_b])
            for bq in range(4):
                ps, ps_b = B.banks[bq], B.bank_b[bq]

                def mmt(e, bq=bq, ps=ps):
                    ins = None
                    for kk_ in range(4):
                        k = bq * 4 + kk_
                        ins = e.transpose(ps[:, kk_ * 128:(kk_ + 1) * 128], oall[:, k * 128:(k + 1) * 128], ident)
                    return ins
                P.c("pe", mmt, reads=[oall_b, consts_b], writes=[ps_b])
                dst = oTst[:, bq * 4:(bq + 1) * 4, :]
                srcv = ps[:, :].rearrange("p (a b) -> p a b", a=4)
                if bq % 2 == 0:
                    actf(dst, srcv, AF.Copy, [ps_b], [oTst_b])
                else:
                    tcopy("dve", dst, srcv, [ps_b], [oTst_b])
            B.dma_out(oT_scr.rearrange("(k p) t -> p k t", p=128)[:, :, r0:r0 + 128], oTst, oTst_b, [oTst_b], writes=[oT_b])
        P.barrier()
        A.release(m0)

    def pre1_gdn():
        B.phase()
        m0 = A.mark()
        own = B.pbuf()
        CW = A.alloc("CW", [128, 5, 3072])
        for j in range(5):
            B.dma_in(CW[:, j, :], SP_["l1_conv"][j:j + 1, :].partition_broadcast(128), own, [own])
        dtb = A.alloc("dtb", [128, 16])
        negA = A.alloc("negA", [128, 16])
        B.dma_in(dtb, SP_["l1_dtb"].partition_broadcast(128), own, [own])
        B.dma_in(negA, SP_["l1_alog"].partition_broadcast(128), own, [own])
        actf(negA, negA, AF.Exp, [own], [own])
        tsm("dve", negA, negA, -1.0, [own], [own])
        Z = [[A.alloc("Z%d_%d" % (s_, j), [128, 1024]) for j in range(5)] for s_ in range(2)]
        Z_b = [[B.pbuf() for j in range(5)] for s_ in range(2)]
        tm = A.alloc("tm", [128, 4]); tm_b = B.pbuf()
        ab = A.alloc("abt", [128, 32]); ab_b = B.pbuf()
        sm16 = [A.alloc("sm16_%d" % i, [128, 16]) for i in range(5)]
        sm_b = Buf("sm16")
        acc = A.alloc("acc", [128, 1024]); acc_b = Buf("acc")
        tmpc = A.alloc("tmpc", [128, 1024]); tmpc_b = Buf("tmpc")
        part_t = [A.alloc("part%d" % i, [128, 1024]) for i in range(3)]
        part_b = [B.pbuf() for i in range(3)]
        sq = A.alloc("sqg", [128, 1024]); sq_b = Buf("sqg")
        ssq = A.alloc("ssq", [128, 8]); ssq_b = Buf("ssq")
        ssk = A.alloc("ssk", [128, 8]); ssk_b = Buf("ssk")
        outs = {nm: (A.alloc("o_" + nm, [128, 1024]), B.pbuf()) for nm in ("la_f", "la_b", "k_f", "k_b", "b_f", "b_b")}
        la, beta, ela, nbe, xsm = sm16
        for tb in range(T // 128):
            r0 = tb * 128
            B.dma_in(tm, tmask_in[r0:r0 + 128, :], tm_b, [tm_b])
            B.dma_in(ab, SC["ab"][0][r0:r0 + 128, :], ab_b, [ab_b], reads=[SC["ab"][1]])
            tt("dve", xsm, ab[:, 0:16], dtb, ALU.add, [ab_b, own], [sm_b])
            actf(xsm, xsm, AF.Exp, [sm_b], [sm_b])
            actf(xsm, xsm, AF.Ln, [sm_b], [sm_b], bias=1.0)
            tt("dve", la, xsm, negA, ALU.mult, [sm_b, own], [sm_b])
            actf(beta, ab[:, 16:32], AF.Sigmoid, [ab_b], [sm_b])
            actf(ela, la, AF.Exp, [sm_b], [sm_b])
            stt("dve", nbe, beta, -1.0, ela, ALU.mult, ALU.mult, [sm_b], [sm_b])
            for part in range(3):
                s_ = (tb * 3 + part) % 2
                for j in range(5):
                    B.dma_in(Z[s_][j], SC["zq"][0][r0 + j:r0 + j + 128, part * 1024:(part + 1) * 1024], Z_b[s_][j], [Z_b[s_][j]],
                             reads=[SC["zq"][1]])
                pc = slice(part * 1024, (part + 1) * 1024)
                tt("dve", acc, Z[s_][2], CW[:, 2, pc], ALU.mult, [Z_b[s_][2], own], [acc_b])
                for j, mi in ((0, 0), (1, 1), (3, 2), (4, 3)):
                    stt("pool", tmpc, Z[s_][j], tm[:, mi:mi + 1], CW[:, j, pc], ALU.mult, ALU.mult, [Z_b[s_][j], tm_b, own], [tmpc_b])
                    tt("dve", acc, acc, tmpc, ALU.add, [acc_b, tmpc_b], [acc_b])
                actf(part_t[part], acc, AF.Silu, [acc_b], [part_b[part]])
            qt, kt, vt = part_t
            q3 = qt.rearrange("p (h c) -> p h c", h=8)
            k3 = kt.rearrange("p (h c) -> p h c", h=8)
            tt("pool", sq, qt, qt, ALU.mult, [part_b[0]], [sq_b])
            red("dve", ssq, sq.rearrange("p (h c) -> p h c", h=8), [sq_b], [ssq_b])
            rstd_small(ssq, ssq_b, 1.0, EPS)
            tsm("dve", ssq, ssq, QS, [ssq_b], [ssq_b])
            tt("dve", q3, q3, ssq.unsqueeze(2).to_broadcast([128, 8, 128]), ALU.mult, [part_b[0], ssq_b], [part_b[0]])
            tt("pool", sq, kt, kt, ALU.mult, [part_b[1]], [sq_b])
            red("dve", ssk, sq.rearrange("p (h c) -> p h c", h=8), [sq_b], [ssk_b])
            rstd_small(ssk, ssk_b, 1.0, EPS)
            tt("dve", k3, k3, ssk.unsqueeze(2).to_broadcast([128, 8, 128]), ALU.mult, [part_b[1], ssk_b], [part_b[1]])
            B.dma_out(SC["gdn_q"][0][r0:r0 + 128, :], qt, part_b[0], [part_b[0]], writes=[SC["gdn_q"][1]])
            B.dma_out(SC["gdn_a"][0][r0:r0 + 128, :], kt, part_b[1], [part_b[1]], writes=[SC["gdn_a"][1]])
            B.dma_out(SC["gdn_v"][0][r0:r0 + 128, :], vt, part_b[2], [part_b[2]], writes=[SC["gdn_v"][1]])
            for di, dn in enumerate(("f", "b")):
                hs = slice(di * 8, (di + 1) * 8)
                t_la, b_la = outs["la_" + dn]
                t_k, b_k = outs["k_" + dn]
                t_b, b_b = outs["b_" + dn]
                tcopy("pool", t_la.rearrange("p (h c) -> p h c", h=8), la[:, hs].unsqueeze(2).to_broadcast([128, 8, 128]), [sm_b], [b_la])
                tt("dve", t_k.rearrange("p (h c) -> p h c", h=8), k3, beta[:, hs].unsqueeze(2).to_broadcast([128, 8, 128]), ALU.mult,
                   [part_b[1], sm_b], [b_k])
                tt("pool", t_b.rearrange("p (h c) -> p h c", h=8), k3, nbe[:, hs].unsqueeze(2).to_broadcast([128, 8, 128]), ALU.mult,
                   [part_b[1], sm_b], [b_b])
                B.dma_out(SC["gdn_la_" + dn][0][r0:r0 + 128, :], t_la, b_la, [b_la], writes=[SC["gdn_la_" + dn][1]])
                B.dma_out(SC["gdn_k_" + dn][0][r0:r0 + 128, :], t_k, b_k, [b_k], writes=[SC["gdn_k_" + dn][1]])
                B.dma_out(SC["gdn_b_" + dn][0][r0:r0 + 128, :], t_b, b_b, [b_b], writes=[SC["gdn_b_" + dn][1]])
        P.barrier()
        A.release(m0)

    def pre1_rwkv():
        B.phase()
        m0 = A.mark()
        own = B.pbuf()
        MU = A.alloc("MU", [128, 3456])
        B.dma_in(MU, SP_["l1_mu"].partition_broadcast(128), own, [own])
        KKw = A.alloc("KKw", [128, 1024]); KAw = A.alloc("KAw", [128, 1024]); RKw = A.alloc("RKw", [128, 1024])
        B.dma_in(KKw, SP_["l1_kk"].partition_broadcast(128), own, [own])
        B.dma_in(KAw, SP_["l1_ka"].partition_broadcast(128), own, [own])
        B.dma_in(RKw, SP_["l1_rk"].partition_broadcast(128), own, [own])
        w0 = A.alloc("w0", [1, 2, 1024]); a0 = A.alloc("a0", [1, 2, 1024])
        B.dma_in(w0, SP_["l1_w0"].rearrange("k (d c) -> k d c", d=2), own, [own])
        B.dma_in(a0, SP_["l1_a0"].rearrange("k (d c) -> k d c", d=2), own, [own])
        w2 = A.alloc("w2", [64, 2, 1024]); a2 = A.alloc("a2", [64, 2, 1024]); g2 = A.alloc("g2", [128, 1024])
        B.dma_in(w2, SP_["l1_w2"].rearrange("k (d c) -> k d c", d=2), own, [own])
        B.dma_in(a2, SP_["l1_a2"].rearrange("k (d c) -> k d c", d=2), own, [own])
        B.dma_in(g2, SP_["l1_g2"], own, [own])
        Zs = [A.alloc("Zs%d" % j, [128, 3456]) for j in range(3)]
        Zs_b = [B.pbuf() for j in range(3)]
        zr = A.alloc("zrt", [128, 3456]); zr_b = B.pbuf()
        tm = A.alloc("tm", [128, 4]); tm_b = B.pbuf()
        th = A.alloc("th", [128, 256]); th_b = Buf("th")
        sgd = A.alloc("sgd", [128, 128]); sgd_b = Buf("sgd")
        LT = A.alloc("LT", [64, 4, 128]); LT_b = Buf("LT")
        sgT = A.alloc("sgT", [128, 128]); sgT_b = Buf("sgT")
        kk = A.alloc("kkt", [128, 1024]); kk_b = Buf("kkt")
        na = A.alloc("nat", [128, 1024]); na_b = B.pbuf()
        sq = A.alloc("sqr", [128, 1024]); sq_b = Buf("sqr")
        ss = A.alloc("ssr", [128, 16]); ss_b = Buf("ssr")
        bs = A.alloc("bsr", [128, 16]); bs_b = Buf("bsr")
        gate = A.alloc("gatet", [128, 1024]); gate_b = B.pbuf()
        bonus = A.alloc("bonust", [128, 1024]); bonus_b = B.pbuf()
        Ad = A.alloc("Adt", [128, 1024]); Ad_b = Buf("Adt")
        u = A.alloc("ut", [128, 1024]); u_b = Buf("ut")
        outs = {nm: (A.alloc("o_" + nm, [128, 1024]), B.pbuf()) for nm in ("la_f", "la_b", "k_f", "k_b", "b_f", "b_b")}
        for tb in range(T // 128):
            r0 = tb * 128
            B.dma_in(tm, tmask_in[r0:r0 + 128, :], tm_b, [tm_b])
            for j in range(3):
                B.dma_in(Zs[j], SC["zr"][0][r0 + 1 + j:r0 + 1 + j + 128, :], Zs_b[j], [Zs_b[j]], reads=[SC["zr"][1]])
            tsm("pool", zr, Zs[0], tm[:, 1:2], [Zs_b[0], tm_b], [zr_b])
            stt("dve", zr, Zs[2], tm[:, 2:3], zr, ALU.mult, ALU.add, [Zs_b[2], tm_b, zr_b], [zr_b])
            stt("pool", zr, zr, 0.5, Zs[1], ALU.mult, ALU.subtract, [zr_b, Zs_b[1]], [zr_b])
            tt("dve", zr, zr, MU, ALU.mult, [zr_b, own], [zr_b])
            tt("pool", zr, zr, Zs[1], ALU.add, [zr_b, Zs_b[1]], [zr_b])
            r_ = zr[:, 0:1024]; kr = zr[:, 1024:2048]; vr = zr[:, 2048:3072]
            rws = dbg.get("rw_stop", 99)
            if rws <= 1:
                break
            actf(th[:, 0:128], zr[:, 3072:3200], AF.Tanh, [zr_b], [th_b])
            tcopy("dve", th[:, 128:256], zr[:, 3200:3328], [zr_b], [th_b])
            actf(sgd, zr[:, 3328:3456], AF.Sigmoid, [zr_b], [sgd_b])
            ps, ps_b = B.banks[0], B.bank_b[0]

            def mmt(e, ps=ps):
                for q in range(4):
                    e.transpose(ps[0:64, q * 128:(q + 1) * 128], th[:, q * 64:(q + 1) * 64], ident)
                return e.transpose(B.banks[1][:, 0:128], sgd, ident)
            P.c("pe", mmt, reads=[th_b, sgd_b, consts_b], writes=[ps_b, B.bank_b[1]])
            tcopy("dve", LT, ps[0:64, :].rearrange("p (a b) -> p a b", a=4), [ps_b], [LT_b])
            actf(sgT, B.banks[1][:, 0:128], AF.Copy, [B.bank_b[1]], [sgT_b])
            if rws <= 2:
                break
            tt("dve", kk, kr, KKw, ALU.mult, [zr_b, own], [kk_b])
            tt("pool", sq, kk, kk, ALU.mult, [kk_b], [sq_b])
            red("dve", ss, sq.rearrange("p (h c) -> p h c", h=16), [sq_b], [ss_b])
            rstd_small(ss, ss_b, 1.0, EPS)
            kk3 = kk.rearrange("p (h c) -> p h c", h=16)
            tt("dve", kk3, kk3, ss.unsqueeze(2).to_broadcast([128, 16, 64]), ALU.mult, [kk_b, ss_b], [kk_b])
            tsm("pool", na, kk, -1.0, [kk_b], [na_b])
            B.dma_out(SC["rw_a"][0][r0:r0 + 128, :], na, na_b, [na_b], writes=[SC["rw_a"][1]])
            B.dma_out(SC["rw_q"][0][r0:r0 + 128, :], r_, zr_b, [zr_b], writes=[SC["rw_q"][1]])
            B.dma_out(SC["rw_v"][0][r0:r0 + 128, :], vr, zr_b, [zr_b], writes=[SC["rw_v"][1]])
            if rws <= 3:
                break
            for half in range(2):
                pg, pg_b = B.banks[2 + half], B.bank_b[2 + half]
                P.c("pe", lambda e, half=half, pg=pg: e.matmul(pg[:, :], sgT, g2[:, half * 512:(half + 1) * 512], start=True, stop=True),
                    reads=[sgT_b, own], writes=[pg_b])
                actf(gate[:, half * 512:(half + 1) * 512], pg[:, :], AF.Copy, [pg_b], [gate_b])
            B.dma_out(SC["rw_gate"][0][r0:r0 + 128, :], gate, gate_b, [gate_b], writes=[SC["rw_gate"][1]])
            if rws <= 4:
                break
            for di, dn in enumerate(("f", "b")):
                t_la, b_la = outs["la_" + dn]
                t_k, b_k = outs["k_" + dn]
                t_b, b_b = outs["b_" + dn]
                for half in range(2):
                    hc = slice(half * 512, (half + 1) * 512)
                    pw, pw_b = B.banks[4 + half], B.bank_b[4 + half]

                    def mmw(e, di=di, hc=hc, pw=pw):
                        e.matmul(pw[:, :], LT[0:64, di, :], w2[0:64, di, hc], start=True, stop=False)
                        return e.matmul(pw[:, :], ones_f[0:1, 0:128], w0[0:1, di, hc], start=False, stop=True)
                    P.c("pe", mmw, reads=[LT_b, own, ones_f_b], writes=[pw_b])
                    actf(t_la[:, hc], pw[:, :], AF.Exp, [pw_b], [b_la], scale=-1.0)
                    pa, pa_b = B.banks[6 + half], B.bank_b[6 + half]

                    def mma(e, di=di, hc=hc, pa=pa):
                        e.matmul(pa[:, :], LT[0:64, 2 + di, :], a2[0:64, di, hc], start=True, stop=False)
                        return e.matmul(pa[:, :], ones_f[0:1, 0:128], a0[0:1, di, hc], start=False, stop=True)
                    P.c("pe", mma, reads=[LT_b, own, ones_f_b], writes=[pa_b])
                    actf(Ad[:, hc], pa[:, :], AF.Sigmoid, [pa_b], [Ad_b])
                actf(t_la, t_la, AF.Ln, [b_la], [b_la], bias=1.0)
                actf(t_la, t_la, AF.Exp, [b_la], [b_la], scale=-1.0, bias=-0.5)
                tsm("dve", t_la, t_la, -1.0, [b_la], [b_la])
                B.dma_out(SC["rw_la_" + dn][0][r0:r0 + 128, :], t_la, b_la, [b_la], writes=[SC["rw_la_" + dn][1]])
                stt("dve", u, Ad, -1.0, KAw, ALU.add, ALU.mult, [Ad_b, own], [u_b])
                tt("pool", u, u, kr, ALU.mult, [u_b, zr_b], [u_b])
                tt("dve", t_k, u, kr, ALU.add, [u_b, zr_b], [b_k])
                B.dma_out(SC["rw_k_" + dn][0][r0:r0 + 128, :], t_k, b_k, [b_k], writes=[SC["rw_k_" + dn][1]])
                tt("pool", t_b, kk, Ad, ALU.mult, [kk_b, Ad_b], [b_b])
                B.dma_out(SC["rw_b_" + dn][0][r0:r0 + 128, :], t_b, b_b, [b_b], writes=[SC["rw_b_" + dn][1]])
                tt("dve", u, r_, t_k, ALU.mult, [zr_b, b_k], [u_b])
                tt("pool", u, u, RKw, ALU.mult, [u_b, own], [u_b])
                red("dve", bs, u.rearrange("p (h c) -> p h c", h=16), [u_b], [bs_b])
                v3 = vr.rearrange("p (h c) -> p h c", h=16)
                bsB = bs.unsqueeze(2).to_broadcast([128, 16, 64])
                if di == 0:
                    tt("dve", bonus.rearrange("p (h c) -> p h c", h=16), v3, bsB, ALU.mult, [zr_b, bs_b], [bonus_b])
                else:
                    tt("dve", u.rearrange("p (h c) -> p h c", h=16), v3, bsB, ALU.mult, [zr_b, bs_b], [u_b])
                    tt("pool", bonus, bonus, u, ALU.add, [bonus_b, u_b], [bonus_b])
            B.dma_out(SC["rw_bonus"][0][r0:r0 + 128, :], bonus, bonus_b, [bonus_b], writes=[SC["rw_bonus"][1]])
            if rws <= 5:
                break
        P.barrier()
        A.release(m0)

    def scans(l):
        specs = []
        if l == 0:
            for di, dn in enumerate(("f", "b")):
                specs.append(("gla", 4, 128, 256, False, dn, di, {"q": "gla_q", "k": "gla_k", "v": "gla_v", "la": "gla_la_" + dn}, "OA_" + dn))
                specs.append(("ret", 4, 128, 256, False, dn, di, {"q": "ret_q", "k": "ret_k", "v": "ret_v", "la": "ret_la_" + dn}, "OB_" + dn))
        else:
            for di, dn in enumerate(("f", "b")):
                specs.append(("gdn", 8, 128, 128, True, dn, di, {"q": "gdn_q", "k": "gdn_k_" + dn, "v": "gdn_v", "la": "gdn_la_" + dn,
                                                                 "a": "gdn_a", "b": "gdn_b_" + dn}, "OA_" + dn))
                specs.append(("rwkv", 16, 64, 64, True, dn, di, {"q": "rw_q", "k": "rw_k_" + dn, "v": "rw_v", "la": "rw_la_" + dn,
                                                                 "a": "rw_a", "b": "rw_b_" + dn}, "OB_" + dn))
        only = dbg.get("only_scans")
        for nm, H, dk, dv, lowrank, dn, di, srcn, onm in specs:
            if only is not None and nm not in only:
                continue
            src = {k_: SC[v_] for k_, v_ in srcn.items()}
            scan_pass(B, nm + dn, H, dk, dv, lowrank, dn, src, s0_in[nm][di], SC[onm][0], SC[onm][1], st_out[nm][di],
                      flags, flags_b, consts, consts_b, scalar_decay=(nm == "gdn"))

    nl = dbg.get("nlayers", 2)
    for l in range(2):
        phase_mods(l)
    m0 = A.mark()
    zt = A.alloc("zerot", [2, 3456])
    zt_b = B.buf("zerot")
    P.c("pool", lambda e: e.memset(zt, 0.0), writes=[zt_b])
    for nm, w_ in (("zq", 3072), ("zr", 3456)):
        B.dma_out(SC[nm][0][0:2, :], zt[:, 0:w_], zt_b, [zt_b], writes=[SC[nm][1]])
        B.dma_out(SC[nm][0][T + 2:T + 4, :], zt[:, 0:w_], zt_b, [zt_b], writes=[SC[nm][1]])
    P.barrier()
    A.release(m0)

    ntiles = dbg.get("ntiles", NT)
    stop = dbg.get("stop", 99)

    def finish():
        P.emit(final_wait_ops=B.stores)
        return B
    if stop <= 0:
        return finish()
    B.phase()
    m0 = A.mark()
    tc = tile_alloc()
    for t in range(ntiles):
        x_load(tc, xT_in, t)
        modulate(tc, 0, 0)
        ffn(tc, 0, 1, 0)
        modulate(tc, 0, 1)
        proj(tc, 0, t, PLAN0, 32)
        x_store(tc, xs_scr, t, dst_bufs=[xs_tile_b[t]])
    P.barrier()
    A.release(m0)
    if stop <= 1:
        return finish()
    scans(0)
    if stop <= 2:
        return finish()
    post_phase(0)
    if stop <= 3:
        return finish()
    B.phase()
    m0 = A.mark()
    tc = tile_alloc()
    for t in range(ntiles):
        x_load(tc, xs_scr, t, src_bufs=[xs_tile_b[t]])
        wout(tc, 0, t)
        modulate(tc, 0, 2)
        ffn(tc, 0, 2, 2)
        modulate(tc, 1, 0)
        ffn(tc, 1, 1, 0)
        modulate(tc, 1, 1)
        proj(tc, 1, t, PLAN1, 416)
        x_store(tc, xs_scr, t, dst_bufs=[xs_tile_b[t]])
    P.barrier()
    A.release(m0)
    if stop <= 4:
        return finish()
    pre1_gdn()
    if stop <= 5:
        return finish()
    pre1_rwkv()
    if stop <= 6:
        return finish()
    scans(1)
    if stop <= 7:
        return finish()
    post_phase(1)
    if stop <= 8:
        return finish()
    B.phase()
    m0 = A.mark()
    tc = tile_alloc()
    for t in range(ntiles):
        x_load(tc, xs_scr, t, src_bufs=[xs_tile_b[t]])
        wout(tc, 1, t)
        modulate(tc, 1, 2)
        ffn(tc, 1, 2, 2)
        final_norm(tc)
        x_store(tc, yT_out, t, final=True)
    A.release(m0)
    P.emit(final_wait_ops=B.stores)
    return B


def fm_vec(v):
    v = np.asarray(v, np.float32)
    return np.ascontiguousarray(v.reshape(-1, 128).T)


def row(v):
    return np.ascontiguousarray(np.asarray(v, np.float32).reshape(1, -1))


def host_weights(inp):
    f32 = lambda a: np.ascontiguousarray(np.asarray(a), dtype=np.float32)
    Wd = {}
    for l in range(2):
        p = "l%d_" % l
        Wd[p + "w_mod"] = f32(inp[p + "w_mod"])
        Wd[p + "b_mod"] = fm_vec(inp[p + "b_mod"])
        Wd[p + "norms"] = np.ascontiguousarray(np.concatenate([fm_vec(inp[p + "norm%d" % i]) for i in (1, 2, 3)], axis=1))
        for f in ("ffn1", "ffn2"):
            for w in ("wg", "wu", "wd"):
                Wd[p + f + "_" + w] = f32(inp[p + f + "_" + w])
        Wd[p + "w_out"] = f32(inp[p + "w_out"])
    perm0 = np.concatenate([np.arange(0, 3072), np.arange(3104, 6176), np.arange(3072, 3104)])
    perm1 = np.concatenate([np.arange(0, 4096), np.arange(4128, 7584), np.arange(4096, 4128)])
    Wd["l0_w_in"] = np.ascontiguousarray(np.asarray(inp["l0_w_in"], np.float32)[:, perm0])
    Wd["l1_w_in"] = np.ascontiguousarray(np.asarray(inp["l1_w_in"], np.float32)[:, perm1])
    Wd["final_norm"] = fm_vec(inp["final_norm"])
    Wd["consts"] = make_consts()
    Wd["l0_gk_up"] = np.ascontiguousarray(np.concatenate([f32(inp["l0_gla_gk_up_fwd"]), f32(inp["l0_gla_gk_up_bwd"])], axis=1))
    Wd["l0_gk_b"] = np.ascontiguousarray(np.concatenate([row(inp["l0_gla_gk_b_fwd"]), row(inp["l0_gla_gk_b_bwd"])], axis=1))
    Wd["l0_gla_norm"] = row(inp["l0_gla_norm"])
    Wd["l0_ret_norm"] = row(inp["l0_ret_norm"])
    for nm, e0 in (("ret_la_f", 5.0), ("ret_la_b", 5.5)):
        h = np.arange(4, dtype=np.float32)
        lg = np.log1p(-np.power(np.float32(2.0), -(np.float32(e0) + h))).astype(np.float32)
        Wd[nm] = np.ascontiguousarray(np.broadcast_to(np.repeat(lg, 128)[None, :], (T, 512)).astype(np.float32))
    Wd["l1_conv"] = f32(inp["l1_gdn_conv"])
    Wd["l1_dtb"] = np.ascontiguousarray(np.concatenate([row(inp["l1_gdn_dt_bias_fwd"]), row(inp["l1_gdn_dt_bias_bwd"])], axis=1))
    Wd["l1_alog"] = np.ascontiguousarray(np.concatenate([row(inp["l1_gdn_A_log_fwd"]), row(inp["l1_gdn_A_log_bwd"])], axis=1))
    Wd["l1_gdn_norm"] = row(inp["l1_gdn_norm"])
    Wd["l1_mu"] = row(inp["l1_rwkv_mu"])
    Wd["l1_w0"] = np.ascontiguousarray(np.concatenate([row(inp["l1_rwkv_w0_fwd"]), row(inp["l1_rwkv_w0_bwd"])], axis=1))
    Wd["l1_a0"] = np.ascontiguousarray(np.concatenate([row(inp["l1_rwkv_a0_fwd"]), row(inp["l1_rwkv_a0_bwd"])], axis=1))
    Wd["l1_w2"] = np.ascontiguousarray(np.concatenate([f32(inp["l1_rwkv_w2_fwd"]), f32(inp["l1_rwkv_w2_bwd"])], axis=1))
    Wd["l1_a2"] = np.ascontiguousarray(np.concatenate([f32(inp["l1_rwkv_a2_fwd"]), f32(inp["l1_rwkv_a2_bwd"])], axis=1))
    Wd["l1_g2"] = f32(inp["l1_rwkv_g2"])
    Wd["l1_kk"] = row(inp["l1_rwkv_k_k"])
    Wd["l1_ka"] = row(inp["l1_rwkv_k_a"])
    Wd["l1_rk"] = row(inp["l1_rwkv_r_k"])
    Wd["l1_lnw"] = row(inp["l1_rwkv_ln_w"])
    Wd["l1_lnb"] = row(inp["l1_rwkv_ln_b"])
    return Wd


ST_NAMES = (("gla", "l0_gla", 4, 128, 256), ("ret", "l0_ret", 4, 128, 256), ("gdn", "l1_gdn", 8, 128, 128), ("rwkv", "l1_rwkv", 16, 64, 64))


def core_inputs(inp, core):
    m = {}
    sample = core < 4
    tpos = np.arange(T)
    if sample:
        x = np.asarray(inp["x_sample"][core], np.float32)
        cond = np.asarray(inp["c"][core], np.float32)
        pos = tpos
        seglen = T
    else:
        x = np.zeros((T, D), np.float32)
        for s in range(4):
            x[s * SEG:(s + 1) * SEG] = np.asarray(inp["x_prompt"][4 * (core - 4) + s], np.float32)
        cond = np.asarray(inp["c_ctx"], np.float32)
        pos = tpos % SEG
        seglen = SEG
    m["xT"] = np.ascontiguousarray(x.T)
    m["cond"] = fm_vec(cond)
    fl = np.zeros((128, 2), np.float32)
    fl[:, 0] = 1.0 if sample else 0.0
    m["flags"] = fl
    tm = np.zeros((T, 4), np.float32)
    for i, sft in enumerate((-2, -1, 1, 2)):
        tm[:, i] = ((pos + sft >= 0) & (pos + sft < seglen)).astype(np.float32)
    m["tmask"] = tm
    rot = np.zeros((T, 128), np.float32)
    if sample:
        inv = (np.float32(10000.0) ** (-np.arange(32, dtype=np.float32) / np.float32(32))).astype(np.float32)
        rowp = (tpos // 64).astype(np.float32)
        colp = (tpos % 64).astype(np.float32)
        ang = np.concatenate([rowp[:, None] * inv[None, :], colp[:, None] * inv[None, :]], axis=1).astype(np.float32)
        rot[:, 0:64] = np.cos(ang)
        rot[:, 64:128] = np.sin(ang)
    else:
        rot[:, 0:64] = 1.0
    m["rot"] = rot
    for nm, key, H, dk, dv in ST_NAMES:
        s0 = np.zeros((2, dk, H * dv), np.float32)
        if sample:
            for di, dn in enumerate(("fwd", "bwd")):
                st = np.asarray(inp["state_%s_%s" % (key, dn)][core], np.float32)
                s0[di] = st.transpose(1, 0, 2).reshape(dk, H * dv)
        m["s0_" + nm] = s0
    return m


_CACHE = {}


def kernel(**inputs):
    if "prog" not in _CACHE:
        _CACHE["prog"] = build_program()
    Bd = _CACHE["prog"]
    Wd = host_weights(inputs)
    in_maps = []
    for core in range(8):
        m = dict(Wd)
        m.update(core_inputs(inputs, core))
        in_maps.append({k: m[k] for k in Bd.inp})
    res = run_bass_kernel_spmd(Bd.nc, in_maps, core_ids=list(range(8)))
    r = res.results
    y_prompt = np.zeros((16, SEG, D), np.float32)
    y_sample = np.zeros((4, T, D), np.float32)
    for core in range(4):
        y_sample[core] = np.asarray(r[core]["yT"]).T
    for core in range(4, 8):
        yt = np.asarray(r[core]["yT"]).T
        for s in range(4):
            y_prompt[4 * (core - 4) + s] = yt[s * SEG:(s + 1) * SEG]
    outs = [y_prompt, y_sample]
    for nm, key, H, dk, dv in ST_NAMES:
        for di in range(2):
            st = np.zeros((16, H, dk, dv), np.float32)
            for core in range(4, 8):
                so = np.asarray(r[core]["st_" + nm])
                for s in range(4):
                    st[4 * (core - 4) + s] = so[di, s].reshape(dk, H, dv).transpose(1, 0, 2)
            outs.append(st)
    return tuple(outs)
```

```python
import contextlib
import numpy as np
import concourse.bass as bass
import concourse.mybir as mybir
from concourse.bass_utils import run_bass_kernel_spmd

F32 = mybir.dt.float32
BF16 = mybir.dt.bfloat16
U8 = mybir.dt.uint8
ALU = mybir.AluOpType
AF = mybir.ActivationFunctionType
AX = mybir.AxisListType

D = 2048
NCH = 16
T = 2048
TT = 512
NT = T // TT
DFF = 5632
NF = DFF // 128
C = 64
NCHUNK = T // C
SEG = 256
NSEG = T // SEG
EPS = 1e-6
L0_IN = 6176
L1_IN = 7584


class Buf:
    __slots__ = ("name", "last_w", "readers", "last_dma", "dma_sem", "dma_cnt")

    def __init__(self, name):
        self.name = name
        self.last_w = None
        self.readers = []
        self.last_dma = None
        self.dma_sem = None
        self.dma_cnt = 0


class Op:
    __slots__ = ("eng", "fn", "deps", "is_dma", "sem", "val", "signal")

    def __init__(self, eng, fn, is_dma):
        self.eng = eng
        self.fn = fn
        self.deps = []
        self.is_dma = is_dma
        self.sem = None
        self.val = 0
        self.signal = is_dma


class Prog:
    ENGS = ("pe", "act", "dve", "pool", "sp")

    def __init__(self, nc):
        self.nc = nc
        self.ops = {e: [] for e in self.ENGS}
        self.nops = 0
        self.dma_bufs = []
        self.barrier_deps = {e: [] for e in self.ENGS}
        self.all_bufs = []
        self.streams = None
        self.cur_stream = None
        self.stream_bdeps = None

    def begin_streams(self, n):
        self.streams = [[] for _ in range(n)]
        self.stream_bdeps = [{e: list(self.barrier_deps[e]) for e in self.ENGS} for _ in range(n)]
        self.barrier_deps = {e: [] for e in self.ENGS}

    def set_stream(self, k):
        self.cur_stream = k

    def end_streams(self):
        lists = self.streams
        n = max(len(l) for l in lists)
        for i in range(n):
            for l in lists:
                if i < len(l):
                    self.ops[l[i].eng].append(l[i])
        self.streams = None
        self.cur_stream = None
        self.stream_bdeps = None

    def _append(self, op):
        if self.cur_stream is not None:
            self.streams[self.cur_stream].append(op)
        else:
            self.ops[op.eng].append(op)
        self.nops += 1

    def buf(self, name):
        b = Buf(name)
        return b

    def _track(self, op, reads, writes):
        deps = op.deps
        for b in reads:
            if b.last_w is not None:
                deps.append(b.last_w)
            b.readers.append(op)
        for b in writes:
            if b.last_w is not None:
                deps.append(b.last_w)
            deps.extend(b.readers)
            b.last_w = op
            b.readers = []
        bdd = self.barrier_deps if self.cur_stream is None else self.stream_bdeps[self.cur_stream]
        bd = bdd[op.eng]
        if bd:
            deps.extend(bd)
            bdd[op.eng] = []

    def c(self, eng, fn, reads=(), writes=()):
        op = Op(eng, fn, False)
        self._track(op, reads, writes)
        self._append(op)
        return op

    def dma(self, eng, out_ap, in_ap, sbuf, reads=(), writes=()):
        def fn(e, out_ap=out_ap, in_ap=in_ap):
            return e.dma_start(out=out_ap, in_=in_ap)
        op = Op(eng, fn, True)
        self._track(op, reads, writes)
        if sbuf.last_dma is not None:
            op.deps.append(sbuf.last_dma)
        sbuf.last_dma = op
        if sbuf.dma_sem is None:
            sbuf.dma_sem = "pending"
            self.dma_bufs.append(sbuf)
        sbuf.dma_cnt += 1
        op.sem = sbuf
        op.val = 16 * sbuf.dma_cnt
        self._append(op)
        return op

    def barrier(self):
        lasts = []
        for e in self.ENGS:
            for op in reversed(self.ops[e]):
                if not op.is_dma:
                    lasts.append(op)
                    break
        for b in self.dma_bufs:
            if b.last_dma is not None:
                lasts.append(b.last_dma)
        for e in self.ENGS:
            self.barrier_deps[e] = list(lasts)

    def emit(self, final_wait_ops=()):
        nc = self.nc
        for e in self.ENGS:
            for op in self.ops[e]:
                for d in op.deps:
                    if d is not op:
                        d.signal = True
        for op in final_wait_ops:
            op.signal = True
        with contextlib.ExitStack() as st:
            esem = {}
            for e in ("pe", "act", "dve", "pool"):
                esem[e] = st.enter_context(nc.semaphore("s_" + e))
            for i, b in enumerate(self.dma_bufs):
                b.dma_sem = st.enter_context(nc.semaphore("d%d_%s" % (i, b.name)))
            for e in self.ENGS:
                k = 0
                for op in self.ops[e]:
                    if op.is_dma:
                        op.sem = op.sem.dma_sem
                    elif op.signal:
                        k += 1
                        op.sem = esem[e]
                        op.val = k
            block = st.enter_context(nc.Block())

            def run(e, eng):
                waited = {}
                for op in self.ops[e]:
                    need = {}
                    for d in op.deps:
                        if d is op or not d.signal:
                            continue
                        s = d.sem
                        if waited.get(s.num, 0) >= d.val:
                            continue
                        if need.get(s.num, (None, 0))[1] < d.val:
                            need[s.num] = (s, d.val)
                    for num, (s, v) in need.items():
                        eng.wait_ge(s, v)
                        waited[num] = v
                    ins = op.fn(eng)
                    if op.signal:
                        ins.then_inc(op.sem, 16 if op.is_dma else 1)
                if e == "sp":
                    for op in final_wait_ops:
                        if waited.get(op.sem.num, 0) < op.val:
                            eng.wait_ge(op.sem, op.val)
                            waited[op.sem.num] = op.val

            @block.tensor
            def _(eng):
                run("pe", eng)

            @block.scalar
            def _(eng):
                run("act", eng)

            @block.vector
            def _(eng):
                run("dve", eng)

            @block.gpsimd
            def _(eng):
                run("pool", eng)

            @block.sync
            def _(eng):
                run("sp", eng)


class Arena:
    def __init__(self, nc, nbytes):
        self.t = nc.alloc_sbuf_tensor("arena", [128, nbytes], U8)
        self.nbytes = nbytes
        self.off = 0
        self.peak = 0

    def mark(self):
        return self.off

    def release(self, m):
        self.off = m

    def alloc(self, name, shape, dt=F32, parts=None):
        esz = 4 if dt == F32 else 2
        n = int(np.prod(shape[1:])) * esz
        o = self.off
        self.off += (n + 63) // 64 * 64
        self.peak = max(self.peak, self.off)
        assert self.off <= self.nbytes, "SBUF arena overflow %s %d" % (name, self.off)
        ap = self.t[0:shape[0], o:o + n].bitcast(dt)
        if len(shape) == 3:
            ap = ap.rearrange("p (a b) -> p a b", a=shape[1])
        elif len(shape) == 4:
            ap = ap.rearrange("p (a b c) -> p a b c", a=shape[1], b=shape[2])
        return ap


class Slots:
    def __init__(self, arena, name, n, shape, dt):
        self.aps = [arena.alloc("%s%d" % (name, i), shape, dt) for i in range(n)]
        self.bufs = [Buf("%s%d" % (name, i)) for i in range(n)]
        self.i = 0
        self.n = n

    def next(self):
        i = self.i
        self.i = (i + 1) % self.n
        return self.aps[i], self.bufs[i]


class Builder:
    def __init__(self, dbg=None):
        self.dbg = dbg or {}
        self.nc = bass.Bass("TRN2", target_bir_lowering=False)
        self.P = Prog(self.nc)
        self.inp = {}
        self.out = {}
        self.stores = []
        self.arena = Arena(self.nc, 206 * 1024)
        nc = self.nc
        self.banks = [nc.alloc_psum_tensor("bank%d" % i, [128, 512], F32) for i in range(8)]
        self.bank_b = [Buf("bank%d" % i) for i in range(8)]
        self.scr = {}
        self.shared = {}
        self.pcount = 0

    def buf(self, key):
        if key not in self.shared:
            self.shared[key] = Buf(key)
        return self.shared[key]

    def phase(self):
        self.pcount = 0

    def pbuf(self):
        b = self.buf("ph%d" % self.pcount)
        self.pcount += 1
        return b

    def dma_in(self, dst_ap, src_ap, owner, writes, reads=(), eng="sp"):
        return self.P.dma(eng, dst_ap, src_ap, owner, reads=reads, writes=writes)

    def dma_out(self, dst_ap, src_ap, owner, reads, writes=(), eng="sp", final=False):
        op = self.P.dma(eng, dst_ap, src_ap, owner, reads=reads, writes=writes)
        if final:
            self.stores.append(op)
        return op

    def din(self, name, shape, dt=F32):
        ap = self.nc.dram_tensor(name, list(shape), dt, kind="ExternalInput").ap()
        self.inp[name] = ap
        return ap

    def dout(self, name, shape, dt=F32):
        ap = self.nc.dram_tensor(name, list(shape), dt, kind="ExternalOutput").ap()
        self.out[name] = ap
        return ap

    def dscr(self, name, shape, dt=F32):
        if name in self.dbg.get("dump", ()):
            ap = self.nc.dram_tensor(name, list(shape), dt, kind="ExternalOutput").ap()
            self.out[name] = ap
        else:
            ap = self.nc.dram_tensor(name, list(shape), dt).ap()
        self.scr[name] = (ap, Buf(name))
        return ap, self.scr[name][1]

    def load(self, dst_ap, src_ap, dst_buf, reads=(), eng="sp"):
        return self.P.dma(eng, dst_ap, src_ap, dst_buf, reads=reads, writes=[dst_buf])

    def store(self, dst_ap, src_ap, src_buf, writes=(), eng="sp", final=False):
        op = self.P.dma(eng, dst_ap, src_ap, src_buf, reads=[src_buf], writes=writes)
        if final:
            self.stores.append(op)
        return op


CONST_COLS = {}
SCAN_STOP = [99]


def make_consts():
    cols = []
    off = 0

    def add(name, arr):
        nonlocal off
        a = np.zeros((128, arr.shape[1]), np.float32)
        a[:arr.shape[0]] = arr
        cols.append(a)
        CONST_COLS[name] = (off, arr.shape[1])
        off += arr.shape[1]
    idx = np.arange(C)
    add("ident", np.eye(128, dtype=np.float32))
    for d in ("f", "b"):
        if d == "f":
            before = idx[:, None] <= idx[None, :]
            sbefore = idx[:, None] < idx[None, :]
        else:
            before = idx[:, None] >= idx[None, :]
            sbefore = idx[:, None] > idx[None, :]
        after = ~before
        U = before.astype(np.float32)
        Us = sbefore.astype(np.float32)
        add("UU_" + d, np.concatenate([U, Us], axis=1))
        add("UR_" + d, np.concatenate([after, after], axis=1).astype(np.float32))
        half = np.concatenate([U, Us], axis=1)
        add("MASK_" + d, np.concatenate([half, half], axis=0))
        add("MASKN_" + d, Us.T.copy())
        add("MASKI_" + d, U)
        add("MNEG_" + d, (np.concatenate([half, half], axis=0) - 1.0) * 30000.0)
    return np.concatenate(cols, axis=1)


def scan_pass(B, name, H, dk, dv, lowrank, d, src, s0_ap, o_dst, o_dst_b, st_out, flags, flags_b,
              consts, consts_b, nchunks=NCHUNK, chunks_per_seg=SEG // C, scalar_decay=False, slot=0, pbanks=None,
              standalone=True):
    nc, P, A = B.nc, B.P, B.arena
    m0 = A.mark()
    hg = 4 if lowrank else 2
    ngroups = H // hg
    W_ = 256 if lowrank else 128
    NP = 128 if lowrank else 64

    def cst(nm, rows, c0=0, c1=None):
        o, n = CONST_COLS[nm]
        c1 = n if c1 is None else c1
        return consts[0:rows, o + c0:o + c1]
    ident = cst("ident", 128)
    UU = cst("UU_" + d, 64)
    UR = cst("UR_" + d, 64) if lowrank else cst("UR_" + d, 64, 0, 64)
    MASK = cst("MASK_" + d, 128)
    MASKN = cst("MASKN_" + d, 64)
    MASKI = cst("MASKI_" + d, 64)
    last = C - 1 if d == "f" else 0

    nb = 2
    la_t = [A.alloc(name + "la%d" % i, [64, H * dk]) for i in range(nb)]
    q_t = [A.alloc(name + "q%d" % i, [64, H * dk]) for i in range(nb)]
    la_b = [B.buf("sc%d_" % slot + "la%d" % i) for i in range(nb)]
    q_b = [B.buf("sc%d_" % slot + "q%d" % i) for i in range(nb)]
    if lowrank:
        a_t = [A.alloc(name + "a%d" % i, [64, H * dk]) for i in range(nb)]
        a_b = [B.buf("sc%d_" % slot + "a%d" % i) for i in range(nb)]
        k0_t = [A.alloc(name + "k0%d" % i, [64, H * dk]) for i in range(nb)]
        k0_b = [B.buf("sc%d_" % slot + "k0%d" % i) for i in range(nb)]
    bk_t = [A.alloc(name + "bk%d" % i, [NP, H * dk]) for i in range(nb)]
    bk_b = [B.buf("sc%d_" % slot + "bk%d" % i) for i in range(nb)]
    vs_t = [A.alloc(name + "vs%d" % i, [NP, H * dv]) for i in range(nb)]
    vs_b = [B.buf("sc%d_" % slot + "vs%d" % i) for i in range(nb)]
    E = A.alloc(name + "E", [dk, hg, W_]); E_b = Buf(name + "E")
    eH = A.alloc(name + "eH", [NP, hg * dk]); eH_b = Buf(name + "eH")
    LRf = A.alloc(name + "LR", [128, hg, W_]); LR_b = Buf(name + "LR")
    LR = LRf[0:dk]
    BKh = A.alloc(name + "BKh", [NP, hg * dk]); BKh_b = Buf(name + "BKh")
    BW = 128 if lowrank else 64
    BLK = A.alloc(name + "BLK", [NP, hg, BW]); BLK_b = Buf(name + "BLK")
    if lowrank:
        PQ = [A.alloc(name + "PQ%d" % i, [64, hg, 128]) for i in range(2)]
        PQ_b = [Buf(name + "PQ%d" % i) for i in range(2)]
        Rt = A.alloc(name + "R", [64, hg, 64]); R_b = Buf(name + "R")
        R1s = A.alloc(name + "R1s", [64, hg * dv]); R1s_b = Buf(name + "R1s")
    if scalar_decay:
        RAW = A.alloc(name + "RAW", [dk, hg, 256]); RAW_b = Buf(name + "RAW")
        DEC = A.alloc(name + "DEC", [128, hg, 128]); DEC_b = Buf(name + "DEC")
        hcol = A.alloc(name + "hcol", [128, hg]); gend = A.alloc(name + "gend", [128, hg]); hg_b = Buf(name + "hgend")
        MNEG = cst("MNEG_" + d, 128)
    Ost = [A.alloc(name + "Ost%d" % i, [64, H * dv]) for i in range(2)]
    Ost_b = [B.buf("sc%d_" % slot + "Ost%d" % i) for i in range(2)]
    Sf = A.alloc(name + "S", [128, H * dv]); S_b = [B.buf("sc%d_" % slot + "S%d" % g) for g in range(ngroups)]
    S = Sf[0:dk]
    Sst = A.alloc(name + "Sst", [dk, H * dv]); Sst_b = B.buf("sc%d_" % slot + "Sst")
    if pbanks is None:
        bank, bb = B.banks, B.bank_b
    else:
        b0, b1, b2, b3 = pbanks
        lmap = [b0, b1, b2, b3, b3, b1, b0, b2]
        bank = [B.banks[m] for m in lmap]
        bb = [B.bank_b[m] for m in lmap]

    if dk < 128:
        P.c("pool", lambda e: e.memset(Sf, 0.0), writes=[S_b[0]])
        P.c("pool", lambda e: e.memset(LRf, 0.0), writes=[LR_b])
    B.load(S, s0_ap, S_b[0])
    for g in range(1, ngroups):
        S_b[g].last_w = S_b[0].last_w

    order = list(range(nchunks)) if d == "f" else list(range(nchunks - 1, -1, -1))

    def issue_loads(ci):
        c = order[ci]
        i = ci % nb
        r0, r1 = c * C, (c + 1) * C
        B.load(la_t[i], src["la"][0][r0:r1, :], la_b[i], reads=[src["la"][1]])
        B.load(q_t[i], src["q"][0][r0:r1, :], q_b[i], reads=[src["q"][1]])
        if lowrank:
            B.load(a_t[i], src["a"][0][r0:r1, :], a_b[i], reads=[src["a"][1]])
            B.load(k0_t[i], src["k"][0][r0:r1, :], k0_b[i], reads=[src["k"][1]])
            B.load(bk_t[i][0:64, :], src["b"][0][r0:r1, :], bk_b[i], reads=[src["b"][1]])
            B.load(bk_t[i][64:128, :], src["k"][0][r0:r1, :], bk_b[i], reads=[src["k"][1]])
            B.load(vs_t[i][64:128, :], src["v"][0][r0:r1, :], vs_b[i], reads=[src["v"][1]])
        else:
            B.load(bk_t[i], src["k"][0][r0:r1, :], bk_b[i], reads=[src["k"][1]])
            B.load(vs_t[i], src["v"][0][r0:r1, :], vs_b[i], reads=[src["v"][1]])

    def chunk_body(ci):
        c = order[ci]
        i = ci % nb
        la, q, bk, vs = la_t[i], q_t[i], bk_t[i], vs_t[i]
        seg_start = (ci % chunks_per_seg == 0) and ci > 0
        seg_end = (ci % chunks_per_seg == chunks_per_seg - 1)
        seg = c // chunks_per_seg
        ost, ost_b = Ost[ci % 2], Ost_b[ci % 2]
        if seg_start:
            for g in range(ngroups):
                gs = slice(g * hg * dv, (g + 1) * hg * dv)
                P.c("pool", lambda e, gs=gs: e.tensor_scalar_mul(out=S[:, gs], in0=S[:, gs], scalar1=flags[0:dk, 0:1]),
                    reads=[S_b[g], flags_b], writes=[S_b[g]])
        def group_body(g):
            heads = list(range(g * hg, (g + 1) * hg))
            gk = slice(g * hg * dk, (g + 1) * hg * dk)
            gv = slice(g * hg * dv, (g + 1) * hg * dv)
            def mm_cum(e):
                ins = None
                for hi, h in enumerate(heads):
                    ins = e.matmul(bank[0][0:dk, hi * 128:(hi + 1) * 128], la[:, h * dk:(h + 1) * dk], UU, start=True, stop=True)
                return ins
            P.c("pe", mm_cum, reads=[la_b[i], consts_b], writes=[bb[0]])

            def mm_h2(e):
                ins = None
                for hi, h in enumerate(heads):
                    ins = e.matmul(bank[1][0:NP, hi * dk:(hi + 1) * dk], UR, la[:, h * dk:(h + 1) * dk], start=True, stop=True)
                return ins
            P.c("pe", mm_h2, reads=[la_b[i], consts_b], writes=[bb[1]])
            cumv = bank[0][0:dk, 0:hg * 128].rearrange("p (a b) -> p a b", a=hg)
            if scalar_decay:
                P.c("act", lambda e: e.activation(out=E[:, :, 0:128], in_=cumv, func=AF.Exp), reads=[bb[0]], writes=[E_b])
            elif lowrank:
                P.c("act", lambda e: e.activation(out=E[:, :, 0:128], in_=cumv, func=AF.Exp), reads=[bb[0]], writes=[E_b])
                P.c("act", lambda e: e.activation(out=E[:, :, 128:192], in_=cumv[:, :, 0:64], func=AF.Exp, scale=-1.0), reads=[bb[0]], writes=[E_b])
                P.c("act", lambda e: e.activation(out=E[:, :, 192:256], in_=cumv[:, :, 0:64], func=AF.Exp, scale=-1.0), reads=[bb[0]], writes=[E_b])
            else:
                P.c("act", lambda e: e.activation(out=E[:, :, 0:64], in_=cumv[:, :, 0:64], func=AF.Exp), reads=[bb[0]], writes=[E_b])
                P.c("act", lambda e: e.activation(out=E[:, :, 64:128], in_=cumv[:, :, 0:64], func=AF.Exp, scale=-1.0), reads=[bb[0]], writes=[E_b])
            P.c("act", lambda e: e.activation(out=eH, in_=bank[1][0:NP, 0:hg * dk], func=AF.Exp), reads=[bb[1]], writes=[eH_b])
            def mm_tr(e):
                ins = None
                for hi, h in enumerate(heads):
                    hs = slice(h * dk, (h + 1) * dk)
                    if lowrank:
                        bnk = bank[2 + hi // 2]
                        o = (hi % 2) * 256
                        e.transpose(bnk[0:dk, o:o + 64], q[:, hs], ident[0:64, 0:64])
                        e.transpose(bnk[0:dk, o + 64:o + 128], a_t[i][:, hs], ident[0:64, 0:64])
                        e.transpose(bnk[0:dk, o + 128:o + 192], bk[0:64, hs], ident[0:64, 0:64])
                        ins = e.transpose(bnk[0:dk, o + 192:o + 256], k0_t[i][:, hs], ident[0:64, 0:64])
                    else:
                        o = hi * 128
                        e.transpose(bank[2][0:dk, o:o + 64], q[:, hs], ident[0:64, 0:64])
                        ins = e.transpose(bank[2][0:dk, o + 64:o + 128], bk[:, hs], ident[0:64, 0:64])
                return ins
            rds = [q_b[i], bk_b[i], consts_b] + ([a_b[i], k0_b[i]] if lowrank else [])
            P.c("pe", mm_tr, reads=rds, writes=[bb[2], bb[3]] if lowrank else [bb[2]])
            if scalar_decay:
                for half in range(2):
                    trv = bank[2 + half][0:dk, :].rearrange("p (a b) -> p a b", a=2)
                    P.c("act", lambda e, half=half, trv=trv: e.activation(out=RAW[:, 2 * half:2 * half + 2, :], in_=trv, func=AF.Copy),
                        reads=[bb[2 + half]], writes=[RAW_b])
                    P.c("dve", lambda e, half=half: e.tensor_tensor(out=LR[:, 2 * half:2 * half + 2, 0:128], in0=RAW[:, 2 * half:2 * half + 2, 0:128], in1=E[:, 2 * half:2 * half + 2, 0:128], op=ALU.mult),
                        reads=[RAW_b, E_b], writes=[LR_b])
                h2v = bank[1][:, 0:hg * dk].rearrange("p (a b) -> p a b", a=hg)
                P.c("act", lambda e, h2v=h2v: e.activation(out=hcol.unsqueeze(2), in_=h2v[:, :, 0:1], func=AF.Copy), reads=[bb[1]], writes=[hg_b])
                P.c("act", lambda e: e.activation(out=gend.unsqueeze(2), in_=cumv[:, :, last:last + 1], func=AF.Copy), reads=[bb[0]], writes=[hg_b])
                P.c("dve", lambda e: e.tensor_tensor(out=hcol, in0=hcol, in1=gend, op=ALU.subtract), reads=[hg_b], writes=[hg_b])
                for hi in range(hg):
                    P.c("act", lambda e, hi=hi: e.activation(out=DEC[:, hi, :], in_=bank[0][:, hi * 128:(hi + 1) * 128], func=AF.Identity, bias=hcol[:, hi:hi + 1]),
                        reads=[bb[0], hg_b], writes=[DEC_b])
                P.c("pool", lambda e: e.tensor_tensor(out=DEC, in0=DEC, in1=MNEG.unsqueeze(1).to_broadcast([128, hg, 128]), op=ALU.add),
                    reads=[DEC_b, consts_b], writes=[DEC_b])
                P.c("act", lambda e: e.activation(out=DEC, in_=DEC, func=AF.Exp), reads=[DEC_b], writes=[DEC_b])
            elif lowrank:
                for half in range(2):
                    trv = bank[2 + half][0:dk, :].rearrange("p (a b) -> p a b", a=2)
                    P.c("dve", lambda e, half=half, trv=trv: e.tensor_tensor(out=LR[:, 2 * half:2 * half + 2, :], in0=trv, in1=E[:, 2 * half:2 * half + 2, :], op=ALU.mult),
                        reads=[bb[2 + half], E_b], writes=[LR_b])
            else:
                trv = bank[2][0:dk, 0:hg * 128].rearrange("p (a b) -> p a b", a=hg)
                P.c("dve", lambda e, trv=trv: e.tensor_tensor(out=LR, in0=trv, in1=E, op=ALU.mult), reads=[bb[2], E_b], writes=[LR_b])
            P.c("pool", lambda e: e.tensor_tensor(out=BKh, in0=bk[:, gk], in1=eH, op=ALU.mult), reads=[bk_b[i], eH_b], writes=[BKh_b])
            if SCAN_STOP[0] <= 1:
                return
            def mm_blk(e):
                ins = None
                for hi in range(hg):
                    if scalar_decay:
                        ins = e.matmul(bank[0][:, hi * 128:(hi + 1) * 128], RAW[:, hi, 128:256], RAW[:, hi, 0:128], start=True, stop=True)
                    elif lowrank:
                        ins = e.matmul(bank[0][:, hi * 128:(hi + 1) * 128], LR[:, hi, 128:256], LR[:, hi, 0:128], start=True, stop=True)
                    else:
                        ins = e.matmul(bank[0][0:64, hi * 64:(hi + 1) * 64], LR[:, hi, 64:128], LR[:, hi, 0:64], start=True, stop=True)
                return ins
            P.c("pe", mm_blk, reads=[LR_b] + ([RAW_b, DEC_b, hg_b] if scalar_decay else []), writes=[bb[0]])
            if scalar_decay:
                blkv = bank[0][:, :].rearrange("p (a b) -> p a b", a=hg)
                P.c("dve", lambda e, blkv=blkv: e.tensor_tensor(out=BLK, in0=blkv, in1=DEC, op=ALU.mult),
                    reads=[bb[0], DEC_b], writes=[BLK_b])
            elif lowrank:
                blkv = bank[0][:, :].rearrange("p (a b) -> p a b", a=hg)
                P.c("dve", lambda e, blkv=blkv: e.tensor_tensor(out=BLK, in0=blkv, in1=MASK.unsqueeze(1).to_broadcast([128, hg, 128]), op=ALU.mult),
                    reads=[bb[0], consts_b], writes=[BLK_b])
            else:
                blkv = bank[0][0:64, 0:hg * 64].rearrange("p (a b) -> p a b", a=hg)
                P.c("dve", lambda e, blkv=blkv: e.tensor_tensor(out=BLK, in0=blkv, in1=MASKI.unsqueeze(1).to_broadcast([64, hg, 64]), op=ALU.mult),
                    reads=[bb[0], consts_b], writes=[BLK_b])
            if SCAN_STOP[0] <= 2:
                return
            if lowrank:
                def mm_nab(e):
                    ins = None
                    for hi in range(hg):
                        if scalar_decay:
                            ins = e.transpose(bank[1][0:64, hi * 64:(hi + 1) * 64], BLK[0:64, hi, 64:128], ident[0:64, 0:64])
                        else:
                            ins = e.matmul(bank[1][0:64, hi * 64:(hi + 1) * 64], LR[:, hi, 64:128], LR[:, hi, 128:192], start=True, stop=True)
                    return ins
                P.c("pe", mm_nab, reads=[LR_b, BLK_b, consts_b], writes=[bb[1]])
                nabv = bank[1][0:64, 0:hg * 64].rearrange("p (a b) -> p a b", a=hg)
                P.c("dve", lambda e, nabv=nabv: e.tensor_tensor(out=PQ[0][:, :, 64:128], in0=nabv, in1=MASKN.unsqueeze(1).to_broadcast([64, hg, 64]), op=ALU.mult),
                    reads=[bb[1], consts_b], writes=[PQ_b[0]])
                P.c("pool", lambda e: e.tensor_copy(out=PQ[0][:, :, 0:64], in_=BLK[0:64, :, 64:128]), reads=[BLK_b], writes=[PQ_b[0]])
                P.c("pool", lambda e: e.tensor_tensor(out=Rt, in0=BLK[0:64, :, 64:128], in1=ident[0:64, 0:64].unsqueeze(1).to_broadcast([64, hg, 64]), op=ALU.add),
                    reads=[BLK_b, consts_b], writes=[R_b])
                for lev in range(1, 6):
                    pa, pb_ = PQ[(lev - 1) % 2], PQ[lev % 2]
                    pa_b, pb_b = PQ_b[(lev - 1) % 2], PQ_b[lev % 2]

                    def mm_sq(e, pa=pa):
                        ins = None
                        for hi in range(hg):
                            e.matmul(bank[2][0:64, hi * 128:hi * 128 + 64], pa[:, hi, 64:128], pa[:, hi, 0:64], start=True, stop=True)
                            ins = e.matmul(bank[2][0:64, hi * 128 + 64:hi * 128 + 128], pa[:, hi, 0:64], pa[:, hi, 64:128], start=True, stop=True)
                        return ins
                    P.c("pe", mm_sq, reads=[pa_b], writes=[bb[2]])
                    pqv = bank[2][0:64, :].rearrange("p (a b) -> p a b", a=hg)
                    P.c("act", lambda e, pb_=pb_, pqv=pqv: e.activation(out=pb_, in_=pqv, func=AF.Copy), reads=[bb[2]], writes=[pb_b])

                    def mm_r(e, pb_=pb_):
                        ins = None
                        for hi in range(hg):
                            ins = e.matmul(bank[1][0:64, hi * 64:(hi + 1) * 64], pb_[:, hi, 64:128], Rt[:, hi, :], start=True, stop=True)
                        return ins
                    P.c("pe", mm_r, reads=[pb_b, R_b], writes=[bb[1]])
                    rupv = bank[1][0:64, 0:hg * 64].rearrange("p (a b) -> p a b", a=hg)
                    P.c("dve", lambda e, rupv=rupv: e.tensor_tensor(out=Rt, in0=rupv, in1=Rt, op=ALU.add), reads=[bb[1], R_b], writes=[R_b])
            if SCAN_STOP[0] <= 3:
                return
            if lowrank:
                def mm_r1(e):
                    ins = None
                    for hi, h in enumerate(heads):
                        e.matmul(bank[4][0:64, hi * dv:(hi + 1) * dv], LRf[:, hi, 64:128], Sf[:, h * dv:(h + 1) * dv], start=True, stop=False)
                        ins = e.matmul(bank[4][0:64, hi * dv:(hi + 1) * dv], BLK[64:128, hi, 64:128], vs[64:128, h * dv:(h + 1) * dv], start=False, stop=True)
                    return ins
                P.c("pe", mm_r1, reads=[LR_b, S_b[g], BLK_b, vs_b[i]], writes=[bb[4]])
                P.c("act", lambda e: e.activation(out=R1s, in_=bank[4][0:64, 0:hg * dv], func=AF.Copy), reads=[bb[4]], writes=[R1s_b])

                def mm_sa(e):
                    ins = None
                    for hi in range(hg):
                        ins = e.matmul(bank[5][0:64, hi * dv:(hi + 1) * dv], Rt[:, hi, :], R1s[:, hi * dv:(hi + 1) * dv], start=True, stop=True)
                    return ins
                P.c("pe", mm_sa, reads=[R_b, R1s_b], writes=[bb[5]])
                P.c("dve", lambda e: e.tensor_copy(out=vs[0:64, gv], in_=bank[5][0:64, 0:hg * dv]), reads=[bb[5]], writes=[vs_b[i]])

                def mm_o(e):
                    ins = None
                    for hi, h in enumerate(heads):
                        e.matmul(bank[6][0:64, hi * dv:(hi + 1) * dv], LRf[:, hi, 0:64], Sf[:, h * dv:(h + 1) * dv], start=True, stop=False)
                        ins = e.matmul(bank[6][0:64, hi * dv:(hi + 1) * dv], BLK[:, hi, 0:64], vs[:, h * dv:(h + 1) * dv], start=False, stop=True)
                    return ins
                P.c("pe", mm_o, reads=[LR_b, S_b[g], BLK_b, vs_b[i]], writes=[bb[6]])
            else:
                def mm_o(e):
                    ins = None
                    for hi, h in enumerate(heads):
                        e.matmul(bank[6][0:64, hi * dv:(hi + 1) * dv], LRf[:, hi, 0:64], Sf[:, h * dv:(h + 1) * dv], start=True, stop=False)
                        ins = e.matmul(bank[6][0:64, hi * dv:(hi + 1) * dv], BLK[:, hi, :], vs[:, h * dv:(h + 1) * dv], start=False, stop=True)
                    return ins
                P.c("pe", mm_o, reads=[LR_b, S_b[g], BLK_b, vs_b[i]], writes=[bb[6]])
            P.c("act", lambda e, ost=ost: e.activation(out=ost[:, gv], in_=bank[6][0:64, 0:hg * dv], func=AF.Copy), reads=[bb[6]], writes=[ost_b])

            def mm_sd(e):
                ins = None
                for hi, h in enumerate(heads):
                    ins = e.matmul(bank[7][0:dk, hi * dv:(hi + 1) * dv], BKh[:, hi * dk:(hi + 1) * dk], vs[:, h * dv:(h + 1) * dv], start=True, stop=True)
                return ins
            P.c("pe", mm_sd, reads=[BKh_b, vs_b[i]], writes=[bb[7]])
            Sg = S[:, gv].rearrange("p (a b) -> p a b", a=hg)
            egend = E[:, :, last:last + 1].to_broadcast([dk, hg, dv])
            P.c("dve", lambda e, Sg=Sg, egend=egend: e.tensor_tensor(out=Sg, in0=Sg, in1=egend, op=ALU.mult), reads=[S_b[g], E_b], writes=[S_b[g]])
            P.c("dve", lambda e: e.tensor_tensor(out=S[:, gv], in0=S[:, gv], in1=bank[7][0:dk, 0:hg * dv], op=ALU.add), reads=[S_b[g], bb[7]], writes=[S_b[g]])
        for g in range(ngroups):
            group_body(g)
        B.store(o_dst[c * C:(c + 1) * C, :], ost, ost_b, writes=[o_dst_b])
        if seg_end and st_out is not None:
            P.c("act", lambda e: e.activation(out=Sst, in_=S, func=AF.Copy), reads=S_b, writes=[Sst_b])
            B.store(st_out[seg], Sst, Sst_b, final=True)

    issue_loads(0)
    for ci in range(nchunks):
        if ci + 1 < nchunks:
            issue_loads(ci + 1)
        chunk_body(ci)
    if standalone:
        P.barrier()
        A.release(m0)


def build_program(dbg=None):
    B = Builder(dbg)
    nc, P, A = B.nc, B.P, B.arena
    dbg = B.dbg
    nlayers = dbg.get("nlayers", 2)

    xT_in = B.din("xT", [D, T])
    cond_in = B.din("cond", [128, NCH])
    W = {}
    for l in range(2):
        p = "l%d_" % l
        W[p + "w_mod"] = B.din(p + "w_mod", [D, 9 * D])
        W[p + "b_mod"] = B.din(p + "b_mod", [128, 9 * NCH])
        W[p + "norms"] = B.din(p + "norms", [128, 3 * NCH])
        for f in ("ffn1", "ffn2"):
            W[p + f + "_wg"] = B.din(p + f + "_wg", [D, DFF])
            W[p + f + "_wu"] = B.din(p + f + "_wu", [D, DFF])
            W[p + f + "_wd"] = B.din(p + f + "_wd", [DFF, D])
        W[p + "w_in"] = B.din(p + "w_in", [D, L0_IN if l == 0 else L1_IN])
        W[p + "w_out"] = B.din(p + "w_out", [D, D])
    fin_norm = B.din("final_norm", [128, NCH])
    yT_out = B.dout("yT", [D, T])
    consts_np = make_consts()
    consts_in = B.din("consts", consts_np.shape)
    flags_in = B.din("flags", [128, 2])
    tmask_in = B.din("tmask", [T, 4])
    rot_in = B.din("rot", [T, 128])
    s0_in = {"gla": B.din("s0_gla", [2, 128, 1024]), "ret": B.din("s0_ret", [2, 128, 1024]),
             "gdn": B.din("s0_gdn", [2, 128, 1024]), "rwkv": B.din("s0_rwkv", [2, 64, 1024])}
    st_out = {"gla": B.dout("st_gla", [2, NSEG, 128, 1024]), "ret": B.dout("st_ret", [2, NSEG, 128, 1024]),
              "gdn": B.dout("st_gdn", [2, NSEG, 128, 1024]), "rwkv": B.dout("st_rwkv", [2, NSEG, 64, 1024])}
    SP_ = {}
    for nm, shp in (("l0_gk_up", [16, 1024]), ("l0_gk_b", [1, 1024]), ("l0_gla_norm", [1, 256]), ("l0_ret_norm", [1, 256]),
                    ("ret_la_f", [T, 512]), ("ret_la_b", [T, 512]),
                    ("l1_conv", [5, 3072]), ("l1_dtb", [1, 16]), ("l1_alog", [1, 16]), ("l1_gdn_norm", [1, 128]),
                    ("l1_mu", [1, 3456]), ("l1_w0", [1, 2048]), ("l1_a0", [1, 2048]), ("l1_w2", [64, 2048]),
                    ("l1_a2", [64, 2048]), ("l1_g2", [128, 1024]), ("l1_kk", [1, 1024]), ("l1_ka", [1, 1024]),
                    ("l1_rk", [1, 1024]), ("l1_lnw", [1, 1024]), ("l1_lnb", [1, 1024])):
        SP_[nm] = B.din(nm, shp)

    ones_bf = A.alloc("ones_bf", [128, 128], BF16)
    ones_b = Buf("ones_bf")
    P.c("pool", lambda e: e.memset(ones_bf, 1.0), writes=[ones_b])
    mods = [A.alloc("mods%d" % l, [128, 9 * NCH]) for l in range(2)]
    mods_b = [Buf("mods%d" % l) for l in range(2)]
    norms = [A.alloc("norms%d" % l, [128, 3 * NCH]) for l in range(2)]
    norms_b = [Buf("norms%d" % l) for l in range(2)]
    modA = [A.alloc("modA%d" % l, [128, 3 * NCH]) for l in range(2)]
    modG = [A.alloc("modG%d" % l, [128, 3 * NCH]) for l in range(2)]
    modc_b = [Buf("modc%d" % l) for l in range(2)]
    finw = A.alloc("finw", [128, NCH])
    finw_b = Buf("finw")
    B.load(finw, fin_norm, finw_b)
    consts = A.alloc("consts", list(consts_np.shape))
    consts_b = Buf("consts")
    B.load(consts, consts_in, consts_b)
    flags = A.alloc("flags", [128, 2])
    flags_b = Buf("flags")
    B.load(flags, flags_in, flags_b)
    ones_f = A.alloc("ones_f", [1, 128])
    ones_f_b = Buf("ones_f")
    P.c("pool", lambda e: e.memset(ones_f, 1.0), writes=[ones_f_b])
    ident = consts[:, CONST_COLS["ident"][0]:CONST_COLS["ident"][0] + 128]

    def phase_mods(l):
        p = "l%d_" % l
        m0 = A.mark()
        cond = A.alloc("cond", [128, NCH])
        cond_b = B.buf("cond")
        scond = A.alloc("scond", [128, NCH], BF16)
        scond_b = Buf("scond")
        bmod = A.alloc("bmod", [128, 9 * NCH])
        bmod_b = B.buf("bmod")
        B.load(cond, cond_in, cond_b)
        B.load(bmod, W[p + "b_mod"], bmod_b)
        B.load(norms[l], W[p + "norms"], norms_b[l])
        P.c("act", lambda e: e.activation(out=scond, in_=cond, func=AF.Silu), reads=[cond_b], writes=[scond_b])
        wsl = Slots(A, "wmod", 3, [128, NCH, 512], BF16)
        wsl.bufs = [B.buf("wmod%d" % i) for i in range(3)]
        wsrc = W[p + "w_mod"].rearrange("(k p) n -> p k n", p=128)
        ps = B.banks[0]
        ps_b = B.bank_b[0]
        ngrp = 9 * D // 512
        for g in range(ngrp):
            wt, wt_b = wsl.next()
            B.load(wt, wsrc[:, :, g * 512:(g + 1) * 512], wt_b, eng="pool")

            def mm(e, wt=wt, g=g):
                ins = None
                for cidx in range(4):
                    col = g * 4 + cidx
                    for k in range(NCH):
                        ins = e.matmul(ps[:, col:col + 1], wt[:, k, cidx * 128:(cidx + 1) * 128],
                                       scond[:, k:k + 1], start=(k == 0), stop=(k == NCH - 1))
                return ins
            P.c("pe", mm, reads=[wt_b, scond_b], writes=[ps_b])
        P.c("dve", lambda e: e.tensor_tensor(out=mods[l], in0=ps[:, 0:9 * NCH], in1=bmod, op=ALU.add),
            reads=[ps_b, bmod_b], writes=[mods_b[l]])
        for i in range(3):
            sc = mods[l][:, (3 * i + 1) * NCH:(3 * i + 2) * NCH]
            g = mods[l][:, (3 * i + 2) * NCH:(3 * i + 3) * NCH]
            gam = norms[l][:, i * NCH:(i + 1) * NCH]
            P.c("dve", lambda e, sc=sc, gam=gam, i=i: e.scalar_tensor_tensor(
                out=modA[l][:, i * NCH:(i + 1) * NCH], in0=sc, scalar=1.0, in1=gam, op0=ALU.add, op1=ALU.mult),
                reads=[mods_b[l], norms_b[l]], writes=[modc_b[l]])
            fac = 1.0 if i == 1 else 0.5
            P.c("dve", lambda e, g=g, i=i, fac=fac: e.tensor_scalar_mul(
                out=modG[l][:, i * NCH:(i + 1) * NCH], in0=g, scalar1=fac),
                reads=[mods_b[l]], writes=[modc_b[l]])
        P.barrier()
        A.release(m0)

    def tt(eng, out, in0, in1, op, r, w):
        return P.c(eng, lambda e: e.tensor_tensor(out=out, in0=in0, in1=in1, op=op), reads=r, writes=w)

    def stt(eng, out, in0, scalar, in1, op0, op1, r, w):
        eng = "dve"
        return P.c(eng, lambda e: e.scalar_tensor_tensor(out=out, in0=in0, scalar=scalar, in1=in1, op0=op0, op1=op1), reads=r, writes=w)

    def tsm(eng, out, in0, s1, r, w):
        return P.c(eng, lambda e: e.tensor_scalar_mul(out=out, in0=in0, scalar1=s1), reads=r, writes=w)

    def tcopy(eng, out, in_, r, w):
        return P.c(eng, lambda e: e.tensor_copy(out=out, in_=in_), reads=r, writes=w)

    def actf(out, in_, func, r, w, scale=1.0, bias=None):
        def fn(e):
            if bias is None:
                return e.activation(out=out, in_=in_, func=func, scale=scale)
            return e.activation(out=out, in_=in_, func=func, scale=scale, bias=bias)
        return P.c("act", fn, reads=r, writes=w)

    def red(eng, out, in_, r, w):
        return P.c(eng, lambda e: e.tensor_reduce(out=out, in_=in_, axis=AX.X, op=ALU.add), reads=r, writes=w)

    def rstd_small(t, r_b, sc, eps):
        actf(t, t, AF.Ln, [r_b], [r_b], scale=sc, bias=eps)
        actf(t, t, AF.Exp, [r_b], [r_b], scale=-0.5)

    xs_scr, _ = B.dscr("xs", [D, T])
    xs_tile_b = [Buf("xs_t%d" % i) for i in range(NT)]
    oT_scr, oT_b = B.dscr("oT", [D, T], BF16)
    SC = {}
    for nm, shp in (("gla_q", [T, 512]), ("gla_k", [T, 512]), ("gla_v", [T, 1024]), ("gla_g", [T, 1024]),
                    ("gla_la_f", [T, 512]), ("gla_la_b", [T, 512]),
                    ("ret_q", [T, 512]), ("ret_k", [T, 512]), ("ret_v", [T, 1024]), ("ret_g", [T, 1024]),
                    ("OA_f", [T, 1024]), ("OA_b", [T, 1024]), ("OB_f", [T, 1024]), ("OB_b", [T, 1024]),
                    ("zq", [T + 4, 3072]), ("gdn_g", [T, 1024]), ("zr", [T + 4, 3456]), ("ab", [T, 32]),
                    ("gdn_q", [T, 1024]), ("gdn_a", [T, 1024]), ("gdn_v", [T, 1024]),
                    ("gdn_la_f", [T, 1024]), ("gdn_la_b", [T, 1024]), ("gdn_k_f", [T, 1024]), ("gdn_k_b", [T, 1024]),
                    ("gdn_b_f", [T, 1024]), ("gdn_b_b", [T, 1024]),
                    ("rw_q", [T, 1024]), ("rw_v", [T, 1024]), ("rw_a", [T, 1024]),
                    ("rw_la_f", [T, 1024]), ("rw_la_b", [T, 1024]), ("rw_k_f", [T, 1024]), ("rw_k_b", [T, 1024]),
                    ("rw_b_f", [T, 1024]), ("rw_b_b", [T, 1024]), ("rw_gate", [T, 1024]), ("rw_bonus", [T, 1024])):
        SC[nm] = B.dscr(nm, shp)
    SC["ret_la_f"] = (SP_["ret_la_f"], Buf("ret_la_f"))
    SC["ret_la_b"] = (SP_["ret_la_b"], Buf("ret_la_b"))

    class TileCtx:
        pass

    def tile_alloc():
        tc = TileCtx()
        tc.x = A.alloc("xtile", [128, NCH, TT])
        tc.x_b = [Buf("xtile%d" % j) for j in range(NCH)]
        tc.x_own = [B.buf("xown%d" % g) for g in range(4)]
        tc.h = A.alloc("htile", [128, NCH, TT], BF16)
        tc.h_b = [Buf("htile%d" % j) for j in range(NCH)]
        tc.act = A.alloc("acttile", [128, NF, TT], BF16)
        tc.act_b = [Buf("act%d" % j) for j in range(NF)]
        tc.sq = A.alloc("sq", [128, 2, TT], BF16)
        tc.sq_b = [Buf("sq0"), Buf("sq1")]
        tc.rstd = A.alloc("rstd", [128, TT])
        tc.rstd_b = Buf("rstd")
        tc.tmp = A.alloc("tmpx", [128, 2, TT])
        tc.tmp_b = [Buf("tmpx0"), Buf("tmpx1")]
        tc.sg = A.alloc("sg", [128, 2, TT])
        tc.sg_b = [Buf("sg0"), Buf("sg1")]
        tc.wgu = Slots(A, "wgu", 2, [128, NCH * 512], BF16)
        tc.wgu.bufs = [B.buf("wgu%d" % i) for i in range(2)]
        tc.wd = Slots(A, "wd", 2, [128, 11, 512], BF16)
        tc.wd.bufs = [B.buf("wd%d" % i) for i in range(2)]
        tc.stg = [A.alloc("stg%d" % i, [128, 512]) for i in range(4)]
        tc.stg_b = [B.buf("stg%d" % i) for i in range(4)]
        tc.stg_i = 0
        tc.rot = A.alloc("rot_t", [128, 128])
        tc.rot_b = B.buf("rot_t")
        tc.rx = A.alloc("rot_x", [128, 4, 128])
        tc.rx_b = Buf("rot_x")
        tc.rtmp = [A.alloc("rot_tmp%d" % i, [128, 4, 64]) for i in range(4)]
        tc.rtmp_b = [Buf("rot_tmp%d" % i) for i in range(4)]
        tc.wtail = A.alloc("wtail", [128, NCH, 32], BF16)
        tc.wtail_b = B.buf("wtail")
        tc.gdT = A.alloc("gdT", [16, 2, TT])
        tc.gdT_b = Buf("gdT")
        tc.gkup = A.alloc("gkup", [16, 2, 512])
        tc.gkb = A.alloc("gkb", [1, 2, 512])
        tc.gk_b = B.buf("gkparams")
        B.dma_in(tc.gkup, SP_["l0_gk_up"].rearrange("k (d c) -> k d c", d=2), tc.gk_b, [tc.gk_b])
        B.dma_in(tc.gkb, SP_["l0_gk_b"].rearrange("k (d c) -> k d c", d=2), tc.gk_b, [tc.gk_b])
        return tc

    def next_stage(tc):
        i = tc.stg_i
        tc.stg_i = (i + 1) % 4
        return tc.stg[i], tc.stg_b[i]

    def x_load(tc, src, t, src_bufs=()):
        srcv = src.rearrange("(j p) t -> p j t", p=128)
        for g in range(4):
            B.dma_in(tc.x[:, 4 * g:4 * g + 4, :], srcv[:, 4 * g:4 * g + 4, t * TT:(t + 1) * TT], tc.x_own[g],
                     tc.x_b[4 * g:4 * g + 4], reads=src_bufs)

    def x_store(tc, dst, t, dst_bufs=(), final=False):
        dstv = dst.rearrange("(j p) t -> p j t", p=128)
        for g in range(4):
            B.dma_out(dstv[:, 4 * g:4 * g + 4, t * TT:(t + 1) * TT], tc.x[:, 4 * g:4 * g + 4, :], tc.x_own[g],
                      tc.x_b[4 * g:4 * g + 4], writes=dst_bufs, final=final)

    def rms_stats(tc):
        ps = B.banks[7]
        ps_b = B.bank_b[7]
        for j in range(NCH):
            s = j % 2
            if j % 2 == 0:
                P.c("act", lambda e, j=j, s=s: e.activation(out=tc.sq[:, s, :], in_=tc.x[:, j, :], func=AF.Square),
                    reads=[tc.x_b[j]], writes=[tc.sq_b[s]])
            else:
                P.c("pool", lambda e, j=j, s=s: e.tensor_tensor(out=tc.sq[:, s, :], in0=tc.x[:, j, :], in1=tc.x[:, j, :], op=ALU.mult),
                    reads=[tc.x_b[j]], writes=[tc.sq_b[s]])
            P.c("pe", lambda e, j=j, s=s: e.matmul(ps[:, :], ones_bf, tc.sq[:, s, :], start=(j == 0), stop=(j == NCH - 1)),
                reads=[tc.sq_b[s], ones_b], writes=[ps_b])
        actf(tc.rstd, ps[:, :], AF.Ln, [ps_b], [tc.rstd_b], scale=1.0 / D, bias=EPS)
        actf(tc.rstd, tc.rstd, AF.Exp, [tc.rstd_b], [tc.rstd_b], scale=-0.5)

    def modulate(tc, l, i):
        rms_stats(tc)
        for j in range(NCH):
            s = j % 2
            a_col = modA[l][:, i * NCH + j:i * NCH + j + 1]
            sh_col = mods[l][:, (3 * i) * NCH + j:(3 * i) * NCH + j + 1]
            stt("dve", tc.tmp[:, s, :], tc.x[:, j, :], a_col, tc.rstd, ALU.mult, ALU.mult,
                [tc.x_b[j], tc.rstd_b, modc_b[l]], [tc.tmp_b[s]])
            actf(tc.h[:, j, :], tc.tmp[:, s, :], AF.Identity, [tc.tmp_b[s], mods_b[l]], [tc.h_b[j]], bias=sh_col)

    def ffn(tc, l, which, i):
        p = "l%d_ffn%d_" % (l, which)
        wg_src = W[p + "wg"].rearrange("(k p) n -> p k n", p=128)
        wu_src = W[p + "wu"].rearrange("(k p) n -> p k n", p=128)
        wd_src = W[p + "wd"].rearrange("(f p) n -> p f n", p=128)
        for g in range(NF // 2):
            wflat, wt_b = tc.wgu.next()
            wt = wflat.rearrange("p (a k c) -> p a k c", a=2, k=NCH)
            B.load(wt[:, 0], wg_src[:, :, g * 256:(g + 1) * 256], wt_b, eng="pool")
            B.load(wt[:, 1], wu_src[:, :, g * 256:(g + 1) * 256], wt_b, eng="pool")
            for ci in range(2):
                f = g * 2 + ci
                pg, pg_b = B.banks[(f % 2) * 2], B.bank_b[(f % 2) * 2]
                pu, pu_b = B.banks[(f % 2) * 2 + 1], B.bank_b[(f % 2) * 2 + 1]

                def mm(e, wt=wt, ci=ci, which_w=0, ps=pg):
                    ins = None
                    for k in range(NCH):
                        ins = e.matmul(ps[:, :], wt[:, which_w, k, ci * 128:(ci + 1) * 128], tc.h[:, k, :],
                                       start=(k == 0), stop=(k == NCH - 1))
                    return ins
                P.c("pe", lambda e, mm=mm, wt=wt, ci=ci, pg=pg: mm(e, wt, ci, 0, pg), reads=[wt_b] + tc.h_b, writes=[pg_b])
                P.c("pe", lambda e, mm=mm, wt=wt, ci=ci, pu=pu: mm(e, wt, ci, 1, pu), reads=[wt_b] + tc.h_b, writes=[pu_b])
                s = f % 2
                actf(tc.sg[:, s, :], pg[:, :], AF.Silu, [pg_b], [tc.sg_b[s]])
                tt("dve", tc.act[:, f, :], pu[:, :], tc.sg[:, s, :], ALU.mult, [pu_b, tc.sg_b[s]], [tc.act_b[f]])
        for dg in range(4):
            pbanks = [4 + q for q in range(4)]
            for part in range(4):
                wt, wt_b = tc.wd.next()
                B.load(wt, wd_src[:, part * 11:(part + 1) * 11, dg * 512:(dg + 1) * 512], wt_b, eng="pool")
                for q in range(4):
                    def mm(e, wt=wt, part=part, q=q):
                        ins = None
                        for fi in range(11):
                            f = part * 11 + fi
                            ins = e.matmul(B.banks[pbanks[q]][:, :], wt[:, fi, q * 128:(q + 1) * 128], tc.act[:, f, :],
                                           start=(f == 0), stop=(f == NF - 1))
                        return ins
                    P.c("pe", mm, reads=[wt_b] + tc.act_b[part * 11:(part + 1) * 11], writes=[B.bank_b[pbanks[q]]])
            for q in range(4):
                j = dg * 4 + q
                gcol = modG[l][:, i * NCH + j:i * NCH + j + 1]
                stt("dve", tc.x[:, j, :], B.banks[pbanks[q]][:, :], gcol, tc.x[:, j, :], ALU.mult, ALU.add,
                    [B.bank_b[pbanks[q]], modc_b[l], tc.x_b[j]], [tc.x_b[j]])

    def final_norm(tc):
        rms_stats(tc)
        for j in range(NCH):
            stt("dve", tc.x[:, j, :], tc.x[:, j, :], finw[:, j:j + 1], tc.rstd, ALU.mult, ALU.mult,
                [tc.x_b[j], tc.rstd_b, finw_b], [tc.x_b[j]])

    def wout(tc, l, t):
        own = B.buf("oTload")
        B.dma_in(tc.act[:, 0:NCH, :], oT_scr.rearrange("(k p) t -> p k t", p=128)[:, :, t * TT:(t + 1) * TT], own,
                 tc.act_b[0:NCH], reads=[oT_b])
        w_src = W["l%d_w_out" % l].rearrange("(k p) n -> p k n", p=128)
        for dg in range(4):
            wflat, wt_b = tc.wgu.next()
            wv = wflat.rearrange("p (k c) -> p k c", k=NCH)
            B.load(wv, w_src[:, :, dg * 512:(dg + 1) * 512], wt_b, eng="pool")
            for q in range(4):
                j = dg * 4 + q
                ps, ps_b = B.banks[q], B.bank_b[q]

                def mm(e, wv=wv, q=q, ps=ps):
                    ins = None
                    for k in range(NCH):
                        ins = e.matmul(ps[:, :], wv[:, k, q * 128:(q + 1) * 128], tc.act[:, k, :], start=(k == 0), stop=(k == NCH - 1))
                    return ins
                P.c("pe", mm, reads=[wt_b] + tc.act_b[0:NCH], writes=[ps_b])
                gcol = modG[l][:, NCH + j:NCH + j + 1]
                stt("dve", tc.x[:, j, :], ps[:, :], gcol, tc.x[:, j, :], ALU.mult, ALU.add,
                    [ps_b, modc_b[l], tc.x_b[j]], [tc.x_b[j]])

    QS = 128 ** -0.5

    def proj(tc, l, t, plan, ncols_tail):
        w_src = W["l%d_w_in" % l].rearrange("(k p) n -> p k n", p=128)
        for cg in range(len(plan)):
            dst, row_off, col_off, kind, scale = plan[cg]
            dst_ap, dst_b = SC[dst]
            wflat, wt_b = tc.wgu.next()
            wv = wflat.rearrange("p (k c) -> p k c", k=NCH)
            B.load(wv, w_src[:, :, cg * 512:(cg + 1) * 512], wt_b, eng="pool")
            for tb in range(4):
                ps, ps_b = B.banks[(cg * 4 + tb) % 4], B.bank_b[(cg * 4 + tb) % 4]
                r0 = t * TT + tb * 128

                def mm(e, wv=wv, tb=tb, ps=ps):
                    ins = None
                    for k in range(NCH):
                        ins = e.matmul(ps[:, :], tc.h[:, k, tb * 128:(tb + 1) * 128], wv[:, k, :], start=(k == 0), stop=(k == NCH - 1))
                    return ins
                P.c("pe", mm, reads=[wt_b] + tc.h_b, writes=[ps_b])
                st, st_b = next_stage(tc)
                if kind == "copy":
                    if (cg + tb) % 2 == 0:
                        actf(st, ps[:, :], AF.Identity, [ps_b], [st_b], scale=scale)
                    else:
                        tsm("dve", st, ps[:, :], scale, [ps_b], [st_b])
                elif kind == "silu":
                    actf(st, ps[:, :], AF.Silu, [ps_b], [st_b])
                else:
                    B.dma_in(tc.rot, rot_in[r0:r0 + 128, :], tc.rot_b, [tc.rot_b])
                    cosB = tc.rot[:, 0:64].unsqueeze(1).to_broadcast([128, 4, 64])
                    sinB = tc.rot[:, 64:128].unsqueeze(1).to_broadcast([128, 4, 64])
                    psv = ps[:, :].rearrange("p (h c) -> p h c", h=4)
                    actf(tc.rx, psv, AF.Identity, [ps_b], [tc.rx_b], scale=scale)
                    sv = st.rearrange("p (h c) -> p h c", h=4)
                    tt("dve", tc.rtmp[0], tc.rx[:, :, 0:64], cosB, ALU.mult, [tc.rx_b, tc.rot_b], [tc.rtmp_b[0]])
                    tt("pool", tc.rtmp[1], tc.rx[:, :, 64:128], sinB, ALU.mult, [tc.rx_b, tc.rot_b], [tc.rtmp_b[1]])
                    tt("dve", sv[:, :, 0:64], tc.rtmp[0], tc.rtmp[1], ALU.subtract, [tc.rtmp_b[0], tc.rtmp_b[1]], [st_b])
                    tt("pool", tc.rtmp[2], tc.rx[:, :, 0:64], sinB, ALU.mult, [tc.rx_b, tc.rot_b], [tc.rtmp_b[2]])
                    tt("dve", tc.rtmp[3], tc.rx[:, :, 64:128], cosB, ALU.mult, [tc.rx_b, tc.rot_b], [tc.rtmp_b[3]])
                    tt("pool", sv[:, :, 64:128], tc.rtmp[2], tc.rtmp[3], ALU.add, [tc.rtmp_b[2], tc.rtmp_b[3]], [st_b])
                B.dma_out(dst_ap[row_off + r0:row_off + r0 + 128, col_off:col_off + 512], st, st_b, [st_b], writes=[dst_b])
        c0 = len(plan) * 512
        if l == 0:
            B.load(tc.wtail, w_src[:, :, c0:c0 + 32], tc.wtail_b, eng="pool")
            for d_ in range(2):
                ps, ps_b = B.banks[d_], B.bank_b[d_]

                def mm(e, d_=d_, ps=ps):
                    ins = None
                    for k in range(NCH):
                        ins = e.matmul(ps[0:16, :], tc.wtail[:, k, d_ * 16:(d_ + 1) * 16], tc.h[:, k, :], start=(k == 0), stop=(k == NCH - 1))
                    return ins
                P.c("pe", mm, reads=[tc.wtail_b] + tc.h_b, writes=[ps_b])
                actf(tc.gdT[:, d_, :], ps[0:16, :], AF.Copy, [ps_b], [tc.gdT_b])
            for d_ in range(2):
                dst_ap, dst_b = SC["gla_la_f" if d_ == 0 else "gla_la_b"]
                for tb in range(4):
                    ps, ps_b = B.banks[2 + (tb % 2)], B.bank_b[2 + (tb % 2)]
                    r0 = t * TT + tb * 128

                    def mm(e, d_=d_, tb=tb, ps=ps):
                        e.matmul(ps[:, :], tc.gdT[0:16, d_, tb * 128:(tb + 1) * 128], tc.gkup[0:16, d_, :], start=True, stop=False)
                        return e.matmul(ps[:, :], ones_f[0:1, 0:128], tc.gkb[0:1, d_, :], start=False, stop=True)
                    P.c("pe", mm, reads=[tc.gdT_b, tc.gk_b, ones_f_b], writes=[ps_b])
                    st, st_b = next_stage(tc)
                    actf(st, ps[:, :], AF.Exp, [ps_b], [st_b], scale=-1.0)
                    actf(st, st, AF.Ln, [st_b], [st_b], bias=1.0)
                    tsm("dve", st, st, -1.0 / 16.0, [st_b], [st_b])
                    B.dma_out(dst_ap[r0:r0 + 128, :], st, st_b, [st_b], writes=[dst_b])
        else:
            wflat, wt_b = tc.wgu.next()
            wv = wflat[:, 0:NCH * 416].rearrange("p (k c) -> p k c", k=NCH)
            B.load(wv, w_src[:, :, c0:c0 + 416], wt_b, eng="pool")
            for tb in range(4):
                ps, ps_b = B.banks[tb % 4], B.bank_b[tb % 4]
                r0 = t * TT + tb * 128

                def mm(e, wv=wv, tb=tb, ps=ps):
                    ins = None
                    for k in range(NCH):
                        ins = e.matmul(ps[:, 0:416], tc.h[:, k, tb * 128:(tb + 1) * 128], wv[:, k, :], start=(k == 0), stop=(k == NCH - 1))
                    return ins
                P.c("pe", mm, reads=[wt_b] + tc.h_b, writes=[ps_b])
                st, st_b = next_stage(tc)
                actf(st[:, 0:416], ps[:, 0:416], AF.Copy, [ps_b], [st_b])
                B.dma_out(SC["zr"][0][2 + r0:2 + r0 + 128, 3072:3456], st[:, 0:384], st_b, [st_b], writes=[SC["zr"][1]])
                B.dma_out(SC["ab"][0][r0:r0 + 128, :], st[:, 384:416], st_b, [st_b], writes=[SC["ab"][1]])

    PLAN0 = [("gla_q", 0, 0, "copy", QS), ("gla_k", 0, 0, "copy", 1.0), ("gla_v", 0, 0, "copy", 1.0), ("gla_v", 0, 512, "copy", 1.0),
             ("gla_g", 0, 0, "silu", 1.0), ("gla_g", 0, 512, "silu", 1.0),
             ("ret_q", 0, 0, "rot", QS), ("ret_k", 0, 0, "rot", 1.0), ("ret_v", 0, 0, "copy", 1.0), ("ret_v", 0, 512, "copy", 1.0),
             ("ret_g", 0, 0, "silu", 1.0), ("ret_g", 0, 512, "silu", 1.0)]
    PLAN1 = [("zq", 2, 512 * i, "copy", 1.0) for i in range(6)] + [("gdn_g", 0, 0, "silu", 1.0), ("gdn_g", 0, 512, "silu", 1.0)] + \
            [("zr", 2, 512 * i, "copy", 1.0) for i in range(6)]

    def post_phase(l):
        B.phase()
        m0 = A.mark()
        specs = [("OA", 4, 256, False, EPS), ("OB", 4, 256, True, EPS)] if l == 0 else \
                [("OA", 8, 128, False, EPS), ("OB", 16, 64, True, 64e-5)]
        oall = A.alloc("oall", [128, D])
        oall_b = Buf("oall")
        oTst = A.alloc("oTst", [128, NCH, 128], BF16)
        oTst_b = B.pbuf()
        Of = [A.alloc("Of%d" % i, [128, 1024]) for i in range(2)]
        Ob = [A.alloc("Ob%d" % i, [128, 1024]) for i in range(2)]
        Gt = [A.alloc("Gt%d" % i, [128, 1024]) for i in range(2)]
        Of_b = [B.pbuf() for i in range(2)]
        Ob_b = [B.pbuf() for i in range(2)]
        Gt_b = [B.pbuf() for i in range(2)]
        sqt = A.alloc("sqt", [128, 1024])
        sqt_b = Buf("sqt")
        ss = A.alloc("ss", [128, 16])
        ss_b = Buf("ss")
        sm = A.alloc("sm", [128, 16])
        sm_b = Buf("sm")
        pown = B.pbuf()
        if l == 0:
            nwA = A.alloc("nwA", [128, 256])
            nwB = A.alloc("nwB", [128, 256])
            B.dma_in(nwA, SP_["l0_gla_norm"].partition_broadcast(128), pown, [pown])
            B.dma_in(nwB, SP_["l0_ret_norm"].partition_broadcast(128), pown, [pown])
            gates = ["gla_g", "ret_g"]
        else:
            nwA = A.alloc("nwA", [128, 128])
            lnw = A.alloc("lnw", [128, 1024])
            lnb = A.alloc("lnb", [128, 1024])
            B.dma_in(nwA, SP_["l1_gdn_norm"].partition_broadcast(128), pown, [pown])
            B.dma_in(lnw, SP_["l1_lnw"].partition_broadcast(128), pown, [pown])
            B.dma_in(lnb, SP_["l1_lnb"].partition_broadcast(128), pown, [pown])
            bon = [A.alloc("bon%d" % i, [128, 1024]) for i in range(2)]
            bon_b = [B.pbuf() for i in range(2)]
            gates = ["gdn_g", "rw_gate"]
        for tb in range(T // 128):
            r0 = tb * 128
            for mi, (onm, Hh, dvv, center, eps) in enumerate(specs):
                i = (tb * 2 + mi) % 2
                B.dma_in(Of[i], SC[onm + "_f"][0][r0:r0 + 128, :], Of_b[i], [Of_b[i]], reads=[SC[onm + "_f"][1]])
                B.dma_in(Ob[i], SC[onm + "_b"][0][r0:r0 + 128, :], Ob_b[i], [Ob_b[i]], reads=[SC[onm + "_b"][1]])
                B.dma_in(Gt[i], SC[gates[mi]][0][r0:r0 + 128, :], Gt_b[i], [Gt_b[i]], reads=[SC[gates[mi]][1]])
                o = Of[i]
                o_b = Of_b[i]
                o3 = o.rearrange("p (h c) -> p h c", h=Hh)
                tt("dve", o, Of[i], Ob[i], ALU.add, [Of_b[i], Ob_b[i]], [o_b])
                if center:
                    red("dve", sm[:, 0:Hh], o3, [o_b], [sm_b])
                    tsm("dve", sm[:, 0:Hh], sm[:, 0:Hh], -1.0 / dvv, [sm_b], [sm_b])
                    tt("pool", o3, o3, sm[:, 0:Hh].unsqueeze(2).to_broadcast([128, Hh, dvv]), ALU.add, [o_b, sm_b], [o_b])
                tt("pool", sqt, o, o, ALU.mult, [o_b], [sqt_b])
                red("dve", ss[:, 0:Hh], sqt.rearrange("p (h c) -> p h c", h=Hh), [sqt_b], [ss_b])
                rstd_small(ss[:, 0:Hh], ss_b, 1.0 / dvv, eps)
                tt("dve", o3, o3, ss[:, 0:Hh].unsqueeze(2).to_broadcast([128, Hh, dvv]), ALU.mult, [o_b, ss_b], [o_b])
                dsto = oall[:, mi * 1024:(mi + 1) * 1024]
                if l == 1 and mi == 1:
                    B.dma_in(bon[0], SC["rw_bonus"][0][r0:r0 + 128, :], bon_b[0], [bon_b[0]], reads=[SC["rw_bonus"][1]])
                    tt("pool", o, o, lnw, ALU.mult, [o_b, pown], [o_b])
                    tt("dve", o, o, lnb, ALU.add, [o_b, pown], [o_b])
                    tt("pool", o, o, bon[0], ALU.add, [o_b, bon_b[0]], [o_b])
                else:
                    nw = nwA if mi == 0 else nwB
                    tt("pool", o3, o3, nw.unsqueeze(1).to_broadcast([128, Hh, dvv]), ALU.mult, [o_b, pown], [o_b])
                tt("dve", dsto, o, Gt[i], ALU.mult, [o_b, Gt_b[i]], [oall_b])
            for bq in range(4):
                ps, ps_b = B.banks[bq], B.bank_b[bq]

                def mmt(e, bq=bq, ps=ps):
                    ins = None
                    for kk_ in range(4):
                        k = bq * 4 + kk_
                        ins = e.transpose(ps[:, kk_ * 128:(kk_ + 1) * 128], oall[:, k * 128:(k + 1) * 128], ident)
                    return ins
                P.c("pe", mmt, reads=[oall_b, consts_b], writes=[ps_b])
                dst = oTst[:, bq * 4:(bq + 1) * 4, :]
                srcv = ps[:, :].rearrange("p (a b) -> p a b", a=4)
                if bq % 2 == 0:
                    actf(dst, srcv, AF.Copy, [ps_b], [oTst_b])
                else:
                    tcopy("dve", dst, srcv, [ps_b], [oTst_b])
            B.dma_out(oT_scr.rearrange("(k p) t -> p k t", p=128)[:, :, r0:r0 + 128], oTst, oTst_b, [oTst_b], writes=[oT_b])
        P.barrier()
        A.release(m0)

    def pre1_gdn():
        B.phase()
        m0 = A.mark()
        own = B.pbuf()
        CW = A.alloc("CW", [128, 5, 3072])
        for j in range(5):
            B.dma_in(CW[:, j, :], SP_["l1_conv"][j:j + 1, :].partition_broadcast(128), own, [own])
        dtb = A.alloc("dtb", [128, 16])
        negA = A.alloc("negA", [128, 16])
        B.dma_in(dtb, SP_["l1_dtb"].partition_broadcast(128), own, [own])
        B.dma_in(negA, SP_["l1_alog"].partition_broadcast(128), own, [own])
        actf(negA, negA, AF.Exp, [own], [own])
        tsm("dve", negA, negA, -1.0, [own], [own])
        Z = [[A.alloc("Z%d_%d" % (s_, j), [128, 1024]) for j in range(5)] for s_ in range(2)]
        Z_b = [[B.pbuf() for j in range(5)] for s_ in range(2)]
        tm = A.alloc("tm", [128, 4]); tm_b = B.pbuf()
        ab = A.alloc("abt", [128, 32]); ab_b = B.pbuf()
        sm16 = [A.alloc("sm16_%d" % i, [128, 16]) for i in range(5)]
        sm_b = Buf("sm16")
        acc = A.alloc("acc", [128, 1024]); acc_b = Buf("acc")
        tmpc = A.alloc("tmpc", [128, 1024]); tmpc_b = Buf("tmpc")
        part_t = [A.alloc("part%d" % i, [128, 1024]) for i in range(3)]
        part_b = [B.pbuf() for i in range(3)]
        sq = A.alloc("sqg", [128, 1024]); sq_b = Buf("sqg")
        ssq = A.alloc("ssq", [128, 8]); ssq_b = Buf("ssq")
        ssk = A.alloc("ssk", [128, 8]); ssk_b = Buf("ssk")
        outs = {nm: (A.alloc("o_" + nm, [128, 1024]), B.pbuf()) for nm in ("la_f", "la_b", "k_f", "k_b", "b_f", "b_b")}
        la, beta, ela, nbe, xsm = sm16
        for tb in range(T // 128):
            r0 = tb * 128
            B.dma_in(tm, tmask_in[r0:r0 + 128, :], tm_b, [tm_b])
            B.dma_in(ab, SC["ab"][0][r0:r0 + 128, :], ab_b, [ab_b], reads=[SC["ab"][1]])
            tt("dve", xsm, ab[:, 0:16], dtb, ALU.add, [ab_b, own], [sm_b])
            actf(xsm, xsm, AF.Exp, [sm_b], [sm_b])
            actf(xsm, xsm, AF.Ln, [sm_b], [sm_b], bias=1.0)
            tt("dve", la, xsm, negA, ALU.mult, [sm_b, own], [sm_b])
            actf(beta, ab[:, 16:32], AF.Sigmoid, [ab_b], [sm_b])
            actf(ela, la, AF.Exp, [sm_b], [sm_b])
            stt("dve", nbe, beta, -1.0, ela, ALU.mult, ALU.mult, [sm_b], [sm_b])
            for part in range(3):
                s_ = (tb * 3 + part) % 2
                for j in range(5):
                    B.dma_in(Z[s_][j], SC["zq"][0][r0 + j:r0 + j + 128, part * 1024:(part + 1) * 1024], Z_b[s_][j], [Z_b[s_][j]],
                             reads=[SC["zq"][1]])
                pc = slice(part * 1024, (part + 1) * 1024)
                tt("dve", acc, Z[s_][2], CW[:, 2, pc], ALU.mult, [Z_b[s_][2], own], [acc_b])
                for j, mi in ((0, 0), (1, 1), (3, 2), (4, 3)):
                    stt("pool", tmpc, Z[s_][j], tm[:, mi:mi + 1], CW[:, j, pc], ALU.mult, ALU.mult, [Z_b[s_][j], tm_b, own], [tmpc_b])
                    tt("dve", acc, acc, tmpc, ALU.add, [acc_b, tmpc_b], [acc_b])
                actf(part_t[part], acc, AF.Silu, [acc_b], [part_b[part]])
            qt, kt, vt = part_t
            q3 = qt.rearrange("p (h c) -> p h c", h=8)
            k3 = kt.rearrange("p (h c) -> p h c", h=8)
            tt("pool", sq, qt, qt, ALU.mult, [part_b[0]], [sq_b])
            red("dve", ssq, sq.rearrange("p (h c) -> p h c", h=8), [sq_b], [ssq_b])
            rstd_small(ssq, ssq_b, 1.0, EPS)
            tsm("dve", ssq, ssq, QS, [ssq_b], [ssq_b])
            tt("dve", q3, q3, ssq.unsqueeze(2).to_broadcast([128, 8, 128]), ALU.mult, [part_b[0], ssq_b], [part_b[0]])
            tt("pool", sq, kt, kt, ALU.mult, [part_b[1]], [sq_b])
            red("dve", ssk, sq.rearrange("p (h c) -> p h c", h=8), [sq_b], [ssk_b])
            rstd_small(ssk, ssk_b, 1.0, EPS)
            tt("dve", k3, k3, ssk.unsqueeze(2).to_broadcast([128, 8, 128]), ALU.mult, [part_b[1], ssk_b], [part_b[1]])
            B.dma_out(SC["gdn_q"][0][r0:r0 + 128, :], qt, part_b[0], [part_b[0]], writes=[SC["gdn_q"][1]])
            B.dma_out(SC["gdn_a"][0][r0:r0 + 128, :], kt, part_b[1], [part_b[1]], writes=[SC["gdn_a"][1]])
            B.dma_out(SC["gdn_v"][0][r0:r0 + 128, :], vt, part_b[2], [part_b[2]], writes=[SC["gdn_v"][1]])
            for di, dn in enumerate(("f", "b")):
                hs = slice(di * 8, (di + 1) * 8)
                t_la, b_la = outs["la_" + dn]
                t_k, b_k = outs["k_" + dn]
                t_b, b_b = outs["b_" + dn]
                tcopy("pool", t_la.rearrange("p (h c) -> p h c", h=8), la[:, hs].unsqueeze(2).to_broadcast([128, 8, 128]), [sm_b], [b_la])
                tt("dve", t_k.rearrange("p (h c) -> p h c", h=8), k3, beta[:, hs].unsqueeze(2).to_broadcast([128, 8, 128]), ALU.mult,
                   [part_b[1], sm_b], [b_k])
                tt("pool", t_b.rearrange("p (h c) -> p h c", h=8), k3, nbe[:, hs].unsqueeze(2).to_broadcast([128, 8, 128]), ALU.mult,
                   [part_b[1], sm_b], [b_b])
                B.dma_out(SC["gdn_la_" + dn][0][r0:r0 + 128, :], t_la, b_la, [b_la], writes=[SC["gdn_la_" + dn][1]])
                B.dma_out(SC["gdn_k_" + dn][0][r0:r0 + 128, :], t_k, b_k, [b_k], writes=[SC["gdn_k_" + dn][1]])
                B.dma_out(SC["gdn_b_" + dn][0][r0:r0 + 128, :], t_b, b_b, [b_b], writes=[SC["gdn_b_" + dn][1]])
        P.barrier()
        A.release(m0)

    def pre1_rwkv():
        B.phase()
        m0 = A.mark()
        own = B.pbuf()
        MU = A.alloc("MU", [128, 3456])
        B.dma_in(MU, SP_["l1_mu"].partition_broadcast(128), own, [own])
        KKw = A.alloc("KKw", [128, 1024]); KAw = A.alloc("KAw", [128, 1024]); RKw = A.alloc("RKw", [128, 1024])
        B.dma_in(KKw, SP_["l1_kk"].partition_broadcast(128), own, [own])
        B.dma_in(KAw, SP_["l1_ka"].partition_broadcast(128), own, [own])
        B.dma_in(RKw, SP_["l1_rk"].partition_broadcast(128), own, [own])
        w0 = A.alloc("w0", [1, 2, 1024]); a0 = A.alloc("a0", [1, 2, 1024])
        B.dma_in(w0, SP_["l1_w0"].rearrange("k (d c) -> k d c", d=2), own, [own])
        B.dma_in(a0, SP_["l1_a0"].rearrange("k (d c) -> k d c", d=2), own, [own])
        w2 = A.alloc("w2", [64, 2, 1024]); a2 = A.alloc("a2", [64, 2, 1024]); g2 = A.alloc("g2", [128, 1024])
        B.dma_in(w2, SP_["l1_w2"].rearrange("k (d c) -> k d c", d=2), own, [own])
        B.dma_in(a2, SP_["l1_a2"].rearrange("k (d c) -> k d c", d=2), own, [own])
        B.dma_in(g2, SP_["l1_g2"], own, [own])
        Zs = [A.alloc("Zs%d" % j, [128, 3456]) for j in range(3)]
        Zs_b = [B.pbuf() for j in range(3)]
        zr = A.alloc("zrt", [128, 3456]); zr_b = B.pbuf()
        tm = A.alloc("tm", [128, 4]); tm_b = B.pbuf()
        th = A.alloc("th", [128, 256]); th_b = Buf("th")
        sgd = A.alloc("sgd", [128, 128]); sgd_b = Buf("sgd")
        LT = A.alloc("LT", [64, 4, 128]); LT_b = Buf("LT")
        sgT = A.alloc("sgT", [128, 128]); sgT_b = Buf("sgT")
        kk = A.alloc("kkt", [128, 1024]); kk_b = Buf("kkt")
        na = A.alloc("nat", [128, 1024]); na_b = B.pbuf()
        sq = A.alloc("sqr", [128, 1024]); sq_b = Buf("sqr")
        ss = A.alloc("ssr", [128, 16]); ss_b = Buf("ssr")
        bs = A.alloc("bsr", [128, 16]); bs_b = Buf("bsr")
        gate = A.alloc("gatet", [128, 1024]); gate_b = B.pbuf()
        bonus = A.alloc("bonust", [128, 1024]); bonus_b = B.pbuf()
        Ad = A.alloc("Adt", [128, 1024]); Ad_b = Buf("Adt")
        u = A.alloc("ut", [128, 1024]); u_b = Buf("ut")
        outs = {nm: (A.alloc("o_" + nm, [128, 1024]), B.pbuf()) for nm in ("la_f", "la_b", "k_f", "k_b", "b_f", "b_b")}
        for tb in range(T // 128):
            r0 = tb * 128
            B.dma_in(tm, tmask_in[r0:r0 + 128, :], tm_b, [tm_b])
            for j in range(3):
                B.dma_in(Zs[j], SC["zr"][0][r0 + 1 + j:r0 + 1 + j + 128, :], Zs_b[j], [Zs_b[j]], reads=[SC["zr"][1]])
            tsm("pool", zr, Zs[0], tm[:, 1:2], [Zs_b[0], tm_b], [zr_b])
            stt("dve", zr, Zs[2], tm[:, 2:3], zr, ALU.mult, ALU.add, [Zs_b[2], tm_b, zr_b], [zr_b])
            stt("pool", zr, zr, 0.5, Zs[1], ALU.mult, ALU.subtract, [zr_b, Zs_b[1]], [zr_b])
            tt("dve", zr, zr, MU, ALU.mult, [zr_b, own], [zr_b])
            tt("pool", zr, zr, Zs[1], ALU.add, [zr_b, Zs_b[1]], [zr_b])
            r_ = zr[:, 0:1024]; kr = zr[:, 1024:2048]; vr = zr[:, 2048:3072]
            rws = dbg.get("rw_stop", 99)
            if rws <= 1:
                break
            actf(th[:, 0:128], zr[:, 3072:3200], AF.Tanh, [zr_b], [th_b])
            tcopy("dve", th[:, 128:256], zr[:, 3200:3328], [zr_b], [th_b])
            actf(sgd, zr[:, 3328:3456], AF.Sigmoid, [zr_b], [sgd_b])
            ps, ps_b = B.banks[0], B.bank_b[0]

            def mmt(e, ps=ps):
                for q in range(4):
                    e.transpose(ps[0:64, q * 128:(q + 1) * 128], th[:, q * 64:(q + 1) * 64], ident)
                return e.transpose(B.banks[1][:, 0:128], sgd, ident)
            P.c("pe", mmt, reads=[th_b, sgd_b, consts_b], writes=[ps_b, B.bank_b[1]])
            tcopy("dve", LT, ps[0:64, :].rearrange("p (a b) -> p a b", a=4), [ps_b], [LT_b])
            actf(sgT, B.banks[1][:, 0:128], AF.Copy, [B.bank_b[1]], [sgT_b])
            if rws <= 2:
                break
            tt("dve", kk, kr, KKw, ALU.mult, [zr_b, own], [kk_b])
            tt("pool", sq, kk, kk, ALU.mult, [kk_b], [sq_b])
            red("dve", ss, sq.rearrange("p (h c) -> p h c", h=16), [sq_b], [ss_b])
            rstd_small(ss, ss_b, 1.0, EPS)
            kk3 = kk.rearrange("p (h c) -> p h c", h=16)
            tt("dve", kk3, kk3, ss.unsqueeze(2).to_broadcast([128, 16, 64]), ALU.mult, [kk_b, ss_b], [kk_b])
            tsm("pool", na, kk, -1.0, [kk_b], [na_b])
            B.dma_out(SC["rw_a"][0][r0:r0 + 128, :], na, na_b, [na_b], writes=[SC["rw_a"][1]])
            B.dma_out(SC["rw_q"][0][r0:r0 + 128, :], r_, zr_b, [zr_b], writes=[SC["rw_q"][1]])
            B.dma_out(SC["rw_v"][0][r0:r0 + 128, :], vr, zr_b, [zr_b], writes=[SC["rw_v"][1]])
            if rws <= 3:
                break
            for half in range(2):
                pg, pg_b = B.banks[2 + half], B.bank_b[2 + half]
                P.c("pe", lambda e, half=half, pg=pg: e.matmul(pg[:, :], sgT, g2[:, half * 512:(half + 1) * 512], start=True, stop=True),
                    reads=[sgT_b, own], writes=[pg_b])
                actf(gate[:, half * 512:(half + 1) * 512], pg[:, :], AF.Copy, [pg_b], [gate_b])
            B.dma_out(SC["rw_gate"][0][r0:r0 + 128, :], gate, gate_b, [gate_b], writes=[SC["rw_gate"][1]])
            if rws <= 4:
                break
            for di, dn in enumerate(("f", "b")):
                t_la, b_la = outs["la_" + dn]
                t_k, b_k = outs["k_" + dn]
                t_b, b_b = outs["b_" + dn]
                for half in range(2):
                    hc = slice(half * 512, (half + 1) * 512)
                    pw, pw_b = B.banks[4 + half], B.bank_b[4 + half]

                    def mmw(e, di=di, hc=hc, pw=pw):
                        e.matmul(pw[:, :], LT[0:64, di, :], w2[0:64, di, hc], start=True, stop=False)
                        return e.matmul(pw[:, :], ones_f[0:1, 0:128], w0[0:1, di, hc], start=False, stop=True)
                    P.c("pe", mmw, reads=[LT_b, own, ones_f_b], writes=[pw_b])
                    actf(t_la[:, hc], pw[:, :], AF.Exp, [pw_b], [b_la], scale=-1.0)
                    pa, pa_b = B.banks[6 + half], B.bank_b[6 + half]

                    def mma(e, di=di, hc=hc, pa=pa):
                        e.matmul(pa[:, :], LT[0:64, 2 + di, :], a2[0:64, di, hc], start=True, stop=False)
                        return e.matmul(pa[:, :], ones_f[0:1, 0:128], a0[0:1, di, hc], start=False, stop=True)
                    P.c("pe", mma, reads=[LT_b, own, ones_f_b], writes=[pa_b])
                    actf(Ad[:, hc], pa[:, :], AF.Sigmoid, [pa_b], [Ad_b])
                actf(t_la, t_la, AF.Ln, [b_la], [b_la], bias=1.0)
                actf(t_la, t_la, AF.Exp, [b_la], [b_la], scale=-1.0, bias=-0.5)
                tsm("dve", t_la, t_la, -1.0, [b_la], [b_la])
                B.dma_out(SC["rw_la_" + dn][0][r0:r0 + 128, :], t_la, b_la, [b_la], writes=[SC["rw_la_" + dn][1]])
                stt("dve", u, Ad, -1.0, KAw, ALU.add, ALU.mult, [Ad_b, own], [u_b])
                tt("pool", u, u, kr, ALU.mult, [u_b, zr_b], [u_b])
                tt("dve", t_k, u, kr, ALU.add, [u_b, zr_b], [b_k])
                B.dma_out(SC["rw_k_" + dn][0][r0:r0 + 128, :], t_k, b_k, [b_k], writes=[SC["rw_k_" + dn][1]])
                tt("pool", t_b, kk, Ad, ALU.mult, [kk_b, Ad_b], [b_b])
                B.dma_out(SC["rw_b_" + dn][0][r0:r0 + 128, :], t_b, b_b, [b_b], writes=[SC["rw_b_" + dn][1]])
                tt("dve", u, r_, t_k, ALU.mult, [zr_b, b_k], [u_b])
                tt("pool", u, u, RKw, ALU.mult, [u_b, own], [u_b])
                red("dve", bs, u.rearrange("p (h c) -> p h c", h=16), [u_b], [bs_b])
                v3 = vr.rearrange("p (h c) -> p h c", h=16)
                bsB = bs.unsqueeze(2).to_broadcast([128, 16, 64])
                if di == 0:
                    tt("dve", bonus.rearrange("p (h c) -> p h c", h=16), v3, bsB, ALU.mult, [zr_b, bs_b], [bonus_b])
                else:
                    tt("dve", u.rearrange("p (h c) -> p h c", h=16), v3, bsB, ALU.mult, [zr_b, bs_b], [u_b])
                    tt("pool", bonus, bonus, u, ALU.add, [bonus_b, u_b], [bonus_b])
            B.dma_out(SC["rw_bonus"][0][r0:r0 + 128, :], bonus, bonus_b, [bonus_b], writes=[SC["rw_bonus"][1]])
            if rws <= 5:
                break
        P.barrier()
        A.release(m0)

    def scans(l):
        specs = []
        if l == 0:
            for di, dn in enumerate(("f", "b")):
                specs.append(("gla", 4, 128, 256, False, dn, di, {"q": "gla_q", "k": "gla_k", "v": "gla_v", "la": "gla_la_" + dn}, "OA_" + dn))
                specs.append(("ret", 4, 128, 256, False, dn, di, {"q": "ret_q", "k": "ret_k", "v": "ret_v", "la": "ret_la_" + dn}, "OB_" + dn))
        else:
            for di, dn in enumerate(("f", "b")):
                specs.append(("gdn", 8, 128, 128, True, dn, di, {"q": "gdn_q", "k": "gdn_k_" + dn, "v": "gdn_v", "la": "gdn_la_" + dn,
                                                                 "a": "gdn_a", "b": "gdn_b_" + dn}, "OA_" + dn))
                specs.append(("rwkv", 16, 64, 64, True, dn, di, {"q": "rw_q", "k": "rw_k_" + dn, "v": "rw_v", "la": "rw_la_" + dn,
                                                                 "a": "rw_a", "b": "rw_b_" + dn}, "OB_" + dn))
        for mixer in sorted(set(sp_[0] for sp_ in specs)):
            m0 = A.mark()
            P.begin_streams(2)
            for nm, H, dk, dv, lowrank, dn, di, srcn, onm in specs:
                if nm != mixer:
                    continue
                P.set_stream(di)
                src = {k_: SC[v_] for k_, v_ in srcn.items()}
                scan_pass(B, nm + dn, H, dk, dv, lowrank, dn, src, s0_in[nm][di], SC[onm][0], SC[onm][1], st_out[nm][di],
                          flags, flags_b, consts, consts_b, scalar_decay=(nm == "gdn"), slot=di,
                          pbanks=(4 * di, 4 * di + 1, 4 * di + 2, 4 * di + 3), standalone=False)
            P.end_streams()
            P.barrier()
            A.release(m0)

    nl = dbg.get("nlayers", 2)
    for l in range(2):
        phase_mods(l)
    m0 = A.mark()
    zt = A.alloc("zerot", [2, 3456])
    zt_b = B.buf("zerot")
    P.c("pool", lambda e: e.memset(zt, 0.0), writes=[zt_b])
    for nm, w_ in (("zq", 3072), ("zr", 3456)):
        B.dma_out(SC[nm][0][0:2, :], zt[:, 0:w_], zt_b, [zt_b], writes=[SC[nm][1]])
        B.dma_out(SC[nm][0][T + 2:T + 4, :], zt[:, 0:w_], zt_b, [zt_b], writes=[SC[nm][1]])
    P.barrier()
    A.release(m0)

    ntiles = dbg.get("ntiles", NT)
    stop = dbg.get("stop", 99)

    def finish():
        P.emit(final_wait_ops=B.stores)
        return B
    if stop <= 0:
        return finish()
    B.phase()
    m0 = A.mark()
    tc = tile_alloc()
    for t in range(ntiles):
        x_load(tc, xT_in, t)
        modulate(tc, 0, 0)
        ffn(tc, 0, 1, 0)
        modulate(tc, 0, 1)
        proj(tc, 0, t, PLAN0, 32)
        x_store(tc, xs_scr, t, dst_bufs=[xs_tile_b[t]])
    P.barrier()
    A.release(m0)
    if stop <= 1:
        return finish()
    scans(0)
    if stop <= 2:
        return finish()
    post_phase(0)
    if stop <= 3:
        return finish()
    B.phase()
    m0 = A.mark()
    tc = tile_alloc()
    for t in range(ntiles):
        x_load(tc, xs_scr, t, src_bufs=[xs_tile_b[t]])
        wout(tc, 0, t)
        modulate(tc, 0, 2)
        ffn(tc, 0, 2, 2)
        modulate(tc, 1, 0)
        ffn(tc, 1, 1, 0)
        modulate(tc, 1, 1)
        proj(tc, 1, t, PLAN1, 416)
        x_store(tc, xs_scr, t, dst_bufs=[xs_tile_b[t]])
    P.barrier()
    A.release(m0)
    if stop <= 4:
        return finish()
    pre1_gdn()
    if stop <= 5:
        return finish()
    pre1_rwkv()
    if stop <= 6:
        return finish()
    scans(1)
    if stop <= 7:
        return finish()
    post_phase(1)
    if stop <= 8:
        return finish()
    B.phase()
    m0 = A.mark()
    tc = tile_alloc()
    for t in range(ntiles):
        x_load(tc, xs_scr, t, src_bufs=[xs_tile_b[t]])
        wout(tc, 1, t)
        modulate(tc, 1, 2)
        ffn(tc, 1, 2, 2)
        final_norm(tc)
        x_store(tc, yT_out, t, final=True)
    A.release(m0)
    P.emit(final_wait_ops=B.stores)
    return B


def fm_vec(v):
    v = np.asarray(v, np.float32)
    return np.ascontiguousarray(v.reshape(-1, 128).T)


def row(v):
    return np.ascontiguousarray(np.asarray(v, np.float32).reshape(1, -1))


def host_weights(inp):
    f32 = lambda a: np.ascontiguousarray(np.asarray(a), dtype=np.float32)
    Wd = {}
    for l in range(2):
        p = "l%d_" % l
        Wd[p + "w_mod"] = f32(inp[p + "w_mod"])
        Wd[p + "b_mod"] = fm_vec(inp[p + "b_mod"])
        Wd[p + "norms"] = np.ascontiguousarray(np.concatenate([fm_vec(inp[p + "norm%d" % i]) for i in (1, 2, 3)], axis=1))
        for f in ("ffn1", "ffn2"):
            for w in ("wg", "wu", "wd"):
                Wd[p + f + "_" + w] = f32(inp[p + f + "_" + w])
        Wd[p + "w_out"] = f32(inp[p + "w_out"])
    perm0 = np.concatenate([np.arange(0, 3072), np.arange(3104, 6176), np.arange(3072, 3104)])
    perm1 = np.concatenate([np.arange(0, 4096), np.arange(4128, 7584), np.arange(4096, 4128)])
    Wd["l0_w_in"] = np.ascontiguousarray(np.asarray(inp["l0_w_in"], np.float32)[:, perm0])
    Wd["l1_w_in"] = np.ascontiguousarray(np.asarray(inp["l1_w_in"], np.float32)[:, perm1])
    Wd["final_norm"] = fm_vec(inp["final_norm"])
    Wd["consts"] = make_consts()
    Wd["l0_gk_up"] = np.ascontiguousarray(np.concatenate([f32(inp["l0_gla_gk_up_fwd"]), f32(inp["l0_gla_gk_up_bwd"])], axis=1))
    Wd["l0_gk_b"] = np.ascontiguousarray(np.concatenate([row(inp["l0_gla_gk_b_fwd"]), row(inp["l0_gla_gk_b_bwd"])], axis=1))
    Wd["l0_gla_norm"] = row(inp["l0_gla_norm"])
    Wd["l0_ret_norm"] = row(inp["l0_ret_norm"])
    for nm, e0 in (("ret_la_f", 5.0), ("ret_la_b", 5.5)):
        h = np.arange(4, dtype=np.float32)
        lg = np.log1p(-np.power(np.float32(2.0), -(np.float32(e0) + h))).astype(np.float32)
        Wd[nm] = np.ascontiguousarray(np.broadcast_to(np.repeat(lg, 128)[None, :], (T, 512)).astype(np.float32))
    Wd["l1_conv"] = f32(inp["l1_gdn_conv"])
    Wd["l1_dtb"] = np.ascontiguousarray(np.concatenate([row(inp["l1_gdn_dt_bias_fwd"]), row(inp["l1_gdn_dt_bias_bwd"])], axis=1))
    Wd["l1_alog"] = np.ascontiguousarray(np.concatenate([row(inp["l1_gdn_A_log_fwd"]), row(inp["l1_gdn_A_log_bwd"])], axis=1))
    Wd["l1_gdn_norm"] = row(inp["l1_gdn_norm"])
    Wd["l1_mu"] = row(inp["l1_rwkv_mu"])
    Wd["l1_w0"] = np.ascontiguousarray(np.concatenate([row(inp["l1_rwkv_w0_fwd"]), row(inp["l1_rwkv_w0_bwd"])], axis=1))
    Wd["l1_a0"] = np.ascontiguousarray(np.concatenate([row(inp["l1_rwkv_a0_fwd"]), row(inp["l1_rwkv_a0_bwd"])], axis=1))
    Wd["l1_w2"] = np.ascontiguousarray(np.concatenate([f32(inp["l1_rwkv_w2_fwd"]), f32(inp["l1_rwkv_w2_bwd"])], axis=1))
    Wd["l1_a2"] = np.ascontiguousarray(np.concatenate([f32(inp["l1_rwkv_a2_fwd"]), f32(inp["l1_rwkv_a2_bwd"])], axis=1))
    Wd["l1_g2"] = f32(inp["l1_rwkv_g2"])
    Wd["l1_kk"] = row(inp["l1_rwkv_k_k"])
    Wd["l1_ka"] = row(inp["l1_rwkv_k_a"])
    Wd["l1_rk"] = row(inp["l1_rwkv_r_k"])
    Wd["l1_lnw"] = row(inp["l1_rwkv_ln_w"])
    Wd["l1_lnb"] = row(inp["l1_rwkv_ln_b"])
    return Wd


INPUT_NAMES = (
    "x_prompt", "x_sample", "c", "c_ctx",
    "state_l0_gla_fwd", "state_l0_gla_bwd", "state_l0_ret_fwd", "state_l0_ret_bwd",
    "state_l1_gdn_fwd", "state_l1_gdn_bwd", "state_l1_rwkv_fwd", "state_l1_rwkv_bwd",
    "l0_w_mod", "l0_b_mod", "l0_norm1", "l0_norm2", "l0_norm3",
    "l0_ffn1_wg", "l0_ffn1_wu", "l0_ffn1_wd", "l0_ffn2_wg", "l0_ffn2_wu", "l0_ffn2_wd", "l0_w_in", "l0_w_out",
    "l0_gla_gk_up_fwd", "l0_gla_gk_b_fwd", "l0_gla_gk_up_bwd", "l0_gla_gk_b_bwd", "l0_gla_norm", "l0_ret_norm",
    "l1_w_mod", "l1_b_mod", "l1_norm1", "l1_norm2", "l1_norm3",
    "l1_ffn1_wg", "l1_ffn1_wu", "l1_ffn1_wd", "l1_ffn2_wg", "l1_ffn2_wu", "l1_ffn2_wd", "l1_w_in", "l1_w_out",
    "l1_gdn_conv", "l1_gdn_A_log_fwd", "l1_gdn_dt_bias_fwd", "l1_gdn_A_log_bwd", "l1_gdn_dt_bias_bwd", "l1_gdn_norm",
    "l1_rwkv_mu", "l1_rwkv_w0_fwd", "l1_rwkv_w2_fwd", "l1_rwkv_a0_fwd", "l1_rwkv_a2_fwd",
    "l1_rwkv_w0_bwd", "l1_rwkv_w2_bwd", "l1_rwkv_a0_bwd", "l1_rwkv_a2_bwd",
    "l1_rwkv_g2", "l1_rwkv_k_k", "l1_rwkv_k_a", "l1_rwkv_r_k", "l1_rwkv_ln_w", "l1_rwkv_ln_b", "final_norm")

ST_NAMES = (("gla", "l0_gla", 4, 128, 256), ("ret", "l0_ret", 4, 128, 256), ("gdn", "l1_gdn", 8, 128, 128), ("rwkv", "l1_rwkv", 16, 64, 64))


def core_inputs(inp, core):
    m = {}
    sample = core < 4
    tpos = np.arange(T)
    if sample:
        x = np.asarray(inp["x_sample"][core], np.float32)
        cond = np.asarray(inp["c"][core], np.float32)
        pos = tpos
        seglen = T
    else:
        x = np.zeros((T, D), np.float32)
        for s in range(4):
            x[s * SEG:(s + 1) * SEG] = np.asarray(inp["x_prompt"][4 * (core - 4) + s], np.float32)
        cond = np.asarray(inp["c_ctx"], np.float32)
        pos = tpos % SEG
        seglen = SEG
    m["xT"] = np.ascontiguousarray(x.T)
    m["cond"] = fm_vec(cond)
    fl = np.zeros((128, 2), np.float32)
    fl[:, 0] = 1.0 if sample else 0.0
    m["flags"] = fl
    tm = np.zeros((T, 4), np.float32)
    for i, sft in enumerate((-2, -1, 1, 2)):
        tm[:, i] = ((pos + sft >= 0) & (pos + sft < seglen)).astype(np.float32)
    m["tmask"] = tm
    rot = np.zeros((T, 128), np.float32)
    if sample:
        inv = (np.float32(10000.0) ** (-np.arange(32, dtype=np.float32) / np.float32(32))).astype(np.float32)
        rowp = (tpos // 64).astype(np.float32)
        colp = (tpos % 64).astype(np.float32)
        ang = np.concatenate([rowp[:, None] * inv[None, :], colp[:, None] * inv[None, :]], axis=1).astype(np.float32)
        rot[:, 0:64] = np.cos(ang)
        rot[:, 64:128] = np.sin(ang)
    else:
        rot[:, 0:64] = 1.0
    m["rot"] = rot
    for nm, key, H, dk, dv in ST_NAMES:
        s0 = np.zeros((2, dk, H * dv), np.float32)
        if sample:
            for di, dn in enumerate(("fwd", "bwd")):
                st = np.asarray(inp["state_%s_%s" % (key, dn)][core], np.float32)
                s0[di] = st.transpose(1, 0, 2).reshape(dk, H * dv)
        m["s0_" + nm] = s0
    return m


_CACHE = {}


def kernel(**inputs):
    if "prog" not in _CACHE:
        _CACHE["prog"] = build_program()
    Bd = _CACHE["prog"]
    Wd = host_weights(inputs)
    in_maps = []
    for core in range(8):
        m = dict(Wd)
        m.update(core_inputs(inputs, core))
        in_maps.append({k: m[k] for k in Bd.inp})
    res = run_bass_kernel_spmd(Bd.nc, in_maps, core_ids=list(range(8)))
    r = res.results
    y_prompt = np.zeros((16, SEG, D), np.float32)
    y_sample = np.zeros((4, T, D), np.float32)
    for core in range(4):
        y_sample[core] = np.asarray(r[core]["yT"]).T
    for core in range(4, 8):
        yt = np.asarray(r[core]["yT"]).T
        for s in range(4):
            y_prompt[4 * (core - 4) + s] = yt[s * SEG:(s + 1) * SEG]
    outs = [y_prompt, y_sample]
    for nm, key, H, dk, dv in ST_NAMES:
        for di in range(2):
            st = np.zeros((16, H, dk, dv), np.float32)
            for core in range(4, 8):
                so = np.asarray(r[core]["st_" + nm])
                for s in range(4):
                    st[4 * (core - 4) + s] = so[di, s].reshape(dk, H, dv).transpose(1, 0, 2)
            outs.append(st)
    return tuple(outs)
```

```python
import contextlib
import numpy as np
import concourse.bass as bass
import concourse.mybir as mybir
from concourse.bass_utils import run_bass_kernel_spmd

F32 = mybir.dt.float32
BF16 = mybir.dt.bfloat16
U8 = mybir.dt.uint8
ALU = mybir.AluOpType
AF = mybir.ActivationFunctionType
AX = mybir.AxisListType

D = 2048
NCH = 16
T = 2048
TT = 512
NT = T // TT
DFF = 5632
NF = DFF // 128
C = 64
NCHUNK = T // C
SEG = 256
NSEG = T // SEG
EPS = 1e-6
L0_IN = 6176
L1_IN = 7584


class Buf:
    __slots__ = ("name", "last_w", "readers", "last_dma", "dma_sem", "dma_cnt")

    def __init__(self, name):
        self.name = name
        self.last_w = None
        self.readers = []
        self.last_dma = None
        self.dma_sem = None
        self.dma_cnt = 0


class Op:
    __slots__ = ("eng", "fn", "deps", "is_dma", "sem", "val", "signal")

    def __init__(self, eng, fn, is_dma):
        self.eng = eng
        self.fn = fn
        self.deps = []
        self.is_dma = is_dma
        self.sem = None
        self.val = 0
        self.signal = is_dma


class Prog:
    ENGS = ("pe", "act", "dve", "pool", "sp")

    def __init__(self, nc):
        self.nc = nc
        self.ops = {e: [] for e in self.ENGS}
        self.nops = 0
        self.dma_bufs = []
        self.barrier_deps = {e: [] for e in self.ENGS}
        self.all_bufs = []
        self.streams = None
        self.cur_stream = None
        self.stream_bdeps = None

    def begin_streams(self, n):
        self.streams = [[] for _ in range(n)]
        self.stream_bdeps = [{e: list(self.barrier_deps[e]) for e in self.ENGS} for _ in range(n)]
        self.barrier_deps = {e: [] for e in self.ENGS}

    def set_stream(self, k):
        self.cur_stream = k

    def end_streams(self):
        lists = self.streams
        n = max(len(l) for l in lists)
        for i in range(n):
            for l in lists:
                if i < len(l):
                    self.ops[l[i].eng].append(l[i])
        self.streams = None
        self.cur_stream = None
        self.stream_bdeps = None

    def _append(self, op):
        if self.cur_stream is not None:
            self.streams[self.cur_stream].append(op)
        else:
            self.ops[op.eng].append(op)
        self.nops += 1

    def buf(self, name):
        b = Buf(name)
        return b

    def _track(self, op, reads, writes):
        deps = op.deps
        for b in reads:
            if b.last_w is not None:
                deps.append(b.last_w)
            b.readers.append(op)
        for b in writes:
            if b.last_w is not None:
                deps.append(b.last_w)
            deps.extend(b.readers)
            b.last_w = op
            b.readers = []
        bdd = self.barrier_deps if self.cur_stream is None else self.stream_bdeps[self.cur_stream]
        bd = bdd[op.eng]
        if bd:
            deps.extend(bd)
            bdd[op.eng] = []

    def c(self, eng, fn, reads=(), writes=()):
        op = Op(eng, fn, False)
        self._track(op, reads, writes)
        self._append(op)
        return op

    def dma(self, eng, out_ap, in_ap, sbuf, reads=(), writes=()):
        def fn(e, out_ap=out_ap, in_ap=in_ap):
            return e.dma_start(out=out_ap, in_=in_ap)
        op = Op(eng, fn, True)
        self._track(op, reads, writes)
        if sbuf.last_dma is not None:
            op.deps.append(sbuf.last_dma)
        sbuf.last_dma = op
        if sbuf.dma_sem is None:
            sbuf.dma_sem = "pending"
            self.dma_bufs.append(sbuf)
        sbuf.dma_cnt += 1
        op.sem = sbuf
        op.val = 16 * sbuf.dma_cnt
        self._append(op)
        return op

    def barrier(self):
        lasts = []
        for e in self.ENGS:
            for op in reversed(self.ops[e]):
                if not op.is_dma:
                    lasts.append(op)
                    break
        for b in self.dma_bufs:
            if b.last_dma is not None:
                lasts.append(b.last_dma)
        for e in self.ENGS:
            self.barrier_deps[e] = list(lasts)

    def emit(self, final_wait_ops=()):
        nc = self.nc
        for e in self.ENGS:
            for op in self.ops[e]:
                for d in op.deps:
                    if d is not op:
                        d.signal = True
        for op in final_wait_ops:
            op.signal = True
        with contextlib.ExitStack() as st:
            esem = {}
            for e in ("pe", "act", "dve", "pool"):
                esem[e] = st.enter_context(nc.semaphore("s_" + e))
            for i, b in enumerate(self.dma_bufs):
                b.dma_sem = st.enter_context(nc.semaphore("d%d_%s" % (i, b.name)))
            for e in self.ENGS:
                k = 0
                for op in self.ops[e]:
                    if op.is_dma:
                        op.sem = op.sem.dma_sem
                    elif op.signal:
                        k += 1
                        op.sem = esem[e]
                        op.val = k
            block = st.enter_context(nc.Block())

            def run(e, eng):
                waited = {}
                for op in self.ops[e]:
                    need = {}
                    for d in op.deps:
                        if d is op or not d.signal:
                            continue
                        s = d.sem
                        if waited.get(s.num, 0) >= d.val:
                            continue
                        if need.get(s.num, (None, 0))[1] < d.val:
                            need[s.num] = (s, d.val)
                    for num, (s, v) in need.items():
                        eng.wait_ge(s, v)
                        waited[num] = v
                    ins = op.fn(eng)
                    if op.signal:
                        ins.then_inc(op.sem, 16 if op.is_dma else 1)
                if e == "sp":
                    for op in final_wait_ops:
                        if waited.get(op.sem.num, 0) < op.val:
                            eng.wait_ge(op.sem, op.val)
                            waited[op.sem.num] = op.val

            @block.tensor
            def _(eng):
                run("pe", eng)

            @block.scalar
            def _(eng):
                run("act", eng)

            @block.vector
            def _(eng):
                run("dve", eng)

            @block.gpsimd
            def _(eng):
                run("pool", eng)

            @block.sync
            def _(eng):
                run("sp", eng)


class Arena:
    def __init__(self, nc, nbytes):
        self.t = nc.alloc_sbuf_tensor("arena", [128, nbytes], U8)
        self.nbytes = nbytes
        self.off = 0
        self.peak = 0

    def mark(self):
        return self.off

    def release(self, m):
        self.off = m

    def alloc(self, name, shape, dt=F32, parts=None):
        esz = 4 if dt == F32 else 2
        n = int(np.prod(shape[1:])) * esz
        o = self.off
        self.off += (n + 63) // 64 * 64
        self.peak = max(self.peak, self.off)
        assert self.off <= self.nbytes, "SBUF arena overflow %s %d" % (name, self.off)
        ap = self.t[0:shape[0], o:o + n].bitcast(dt)
        if len(shape) == 3:
            ap = ap.rearrange("p (a b) -> p a b", a=shape[1])
        elif len(shape) == 4:
            ap = ap.rearrange("p (a b c) -> p a b c", a=shape[1], b=shape[2])
        return ap


class Slots:
    def __init__(self, arena, name, n, shape, dt):
        self.aps = [arena.alloc("%s%d" % (name, i), shape, dt) for i in range(n)]
        self.bufs = [Buf("%s%d" % (name, i)) for i in range(n)]
        self.i = 0
        self.n = n

    def next(self):
        i = self.i
        self.i = (i + 1) % self.n
        return self.aps[i], self.bufs[i]


class Builder:
    def __init__(self, dbg=None):
        self.dbg = dbg or {}
        self.nc = bass.Bass("TRN2", target_bir_lowering=False)
        self.P = Prog(self.nc)
        self.inp = {}
        self.out = {}
        self.stores = []
        self.arena = Arena(self.nc, 206 * 1024)
        nc = self.nc
        self.banks = [nc.alloc_psum_tensor("bank%d" % i, [128, 512], F32) for i in range(8)]
        self.bank_b = [Buf("bank%d" % i) for i in range(8)]
        self.scr = {}
        self.shared = {}
        self.pcount = 0

    def buf(self, key):
        if key not in self.shared:
            self.shared[key] = Buf(key)
        return self.shared[key]

    def phase(self):
        self.pcount = 0

    def pbuf(self):
        b = self.buf("ph%d" % self.pcount)
        self.pcount += 1
        return b

    def dma_in(self, dst_ap, src_ap, owner, writes, reads=(), eng="sp"):
        return self.P.dma(eng, dst_ap, src_ap, owner, reads=reads, writes=writes)

    def dma_out(self, dst_ap, src_ap, owner, reads, writes=(), eng="sp", final=False):
        op = self.P.dma(eng, dst_ap, src_ap, owner, reads=reads, writes=writes)
        if final:
            self.stores.append(op)
        return op

    def din(self, name, shape, dt=F32):
        ap = self.nc.dram_tensor(name, list(shape), dt, kind="ExternalInput").ap()
        self.inp[name] = ap
        return ap

    def dout(self, name, shape, dt=F32):
        ap = self.nc.dram_tensor(name, list(shape), dt, kind="ExternalOutput").ap()
        self.out[name] = ap
        return ap

    def dscr(self, name, shape, dt=F32):
        if name in self.dbg.get("dump", ()):
            ap = self.nc.dram_tensor(name, list(shape), dt, kind="ExternalOutput").ap()
            self.out[name] = ap
        else:
            ap = self.nc.dram_tensor(name, list(shape), dt).ap()
        self.scr[name] = (ap, Buf(name))
        return ap, self.scr[name][1]

    def load(self, dst_ap, src_ap, dst_buf, reads=(), eng="sp"):
        return self.P.dma(eng, dst_ap, src_ap, dst_buf, reads=reads, writes=[dst_buf])

    def store(self, dst_ap, src_ap, src_buf, writes=(), eng="sp", final=False):
        op = self.P.dma(eng, dst_ap, src_ap, src_buf, reads=[src_buf], writes=writes)
        if final:
            self.stores.append(op)
        return op


CONST_COLS = {}
SCAN_STOP = [99]


def make_consts():
    cols = []
    off = 0

    def add(name, arr):
        nonlocal off
        a = np.zeros((128, arr.shape[1]), np.float32)
        a[:arr.shape[0]] = arr
        cols.append(a)
        CONST_COLS[name] = (off, arr.shape[1])
        off += arr.shape[1]
    idx = np.arange(C)
    add("ident", np.eye(128, dtype=np.float32))
    for d in ("f", "b"):
        if d == "f":
            before = idx[:, None] <= idx[None, :]
            sbefore = idx[:, None] < idx[None, :]
        else:
            before = idx[:, None] >= idx[None, :]
            sbefore = idx[:, None] > idx[None, :]
        after = ~before
        U = before.astype(np.float32)
        Us = sbefore.astype(np.float32)
        add("UU_" + d, np.concatenate([U, Us], axis=1))
        add("UR_" + d, np.concatenate([after, after], axis=1).astype(np.float32))
        half = np.concatenate([U, Us], axis=1)
        add("MASK_" + d, np.concatenate([half, half], axis=0))
        add("MASKN_" + d, Us.T.copy())
        add("MASKI_" + d, U)
        add("MNEG_" + d, (np.concatenate([half, half], axis=0) - 1.0) * 30000.0)
    return np.concatenate(cols, axis=1)


def scan_pass(B, name, H, dk, dv, lowrank, d, src, s0_ap, o_dst, o_dst_b, st_out, flags, flags_b,
              consts, consts_b, nchunks=NCHUNK, chunks_per_seg=SEG // C, scalar_decay=False, slot=0, pbanks=None,
              standalone=True):
    nc, P, A = B.nc, B.P, B.arena
    m0 = A.mark()
    hg = 4 if lowrank else 2
    ngroups = H // hg
    W_ = 256 if lowrank else 128
    NP = 128 if lowrank else 64

    def cst(nm, rows, c0=0, c1=None):
        o, n = CONST_COLS[nm]
        c1 = n if c1 is None else c1
        return consts[0:rows, o + c0:o + c1]
    ident = cst("ident", 128)
    UU = cst("UU_" + d, 64)
    UR = cst("UR_" + d, 64) if lowrank else cst("UR_" + d, 64, 0, 64)
    MASK = cst("MASK_" + d, 128)
    MASKN = cst("MASKN_" + d, 64)
    MASKI = cst("MASKI_" + d, 64)
    last = C - 1 if d == "f" else 0

    nb = 2
    la_t = [A.alloc(name + "la%d" % i, [64, H * dk]) for i in range(nb)]
    q_t = [A.alloc(name + "q%d" % i, [64, H * dk]) for i in range(nb)]
    la_b = [B.buf("sc%d_" % slot + "la%d" % i) for i in range(nb)]
    q_b = [B.buf("sc%d_" % slot + "q%d" % i) for i in range(nb)]
    if lowrank:
        a_t = [A.alloc(name + "a%d" % i, [64, H * dk]) for i in range(nb)]
        a_b = [B.buf("sc%d_" % slot + "a%d" % i) for i in range(nb)]
        k0_t = [A.alloc(name + "k0%d" % i, [64, H * dk]) for i in range(nb)]
        k0_b = [B.buf("sc%d_" % slot + "k0%d" % i) for i in range(nb)]
    bk_t = [A.alloc(name + "bk%d" % i, [NP, H * dk]) for i in range(nb)]
    bk_b = [B.buf("sc%d_" % slot + "bk%d" % i) for i in range(nb)]
    vs_t = [A.alloc(name + "vs%d" % i, [NP, H * dv]) for i in range(nb)]
    vs_b = [B.buf("sc%d_" % slot + "vs%d" % i) for i in range(nb)]
    E = A.alloc(name + "E", [dk, hg, W_]); E_b = Buf(name + "E")
    eH = A.alloc(name + "eH", [NP, hg * dk]); eH_b = Buf(name + "eH")
    LRf = A.alloc(name + "LR", [128, hg, W_]); LR_b = Buf(name + "LR")
    LR = LRf[0:dk]
    BKh = A.alloc(name + "BKh", [NP, hg * dk]); BKh_b = Buf(name + "BKh")
    BW = 128 if lowrank else 64
    BLK = A.alloc(name + "BLK", [NP, hg, BW]); BLK_b = Buf(name + "BLK")
    if lowrank:
        PQ = [A.alloc(name + "PQ%d" % i, [64, hg, 128]) for i in range(2)]
        PQ_b = [Buf(name + "PQ%d" % i) for i in range(2)]
        Rt = A.alloc(name + "R", [64, hg, 64]); R_b = Buf(name + "R")
        R1s = A.alloc(name + "R1s", [64, hg * dv]); R1s_b = Buf(name + "R1s")
    if scalar_decay:
        RAW = A.alloc(name + "RAW", [dk, hg, 256]); RAW_b = Buf(name + "RAW")
        DEC = A.alloc(name + "DEC", [128, hg, 128]); DEC_b = Buf(name + "DEC")
        hcol = A.alloc(name + "hcol", [128, hg]); gend = A.alloc(name + "gend", [128, hg]); hg_b = Buf(name + "hgend")
        MNEG = cst("MNEG_" + d, 128)
    Ost = [A.alloc(name + "Ost%d" % i, [64, H * dv]) for i in range(2)]
    Ost_b = [B.buf("sc%d_" % slot + "Ost%d" % i) for i in range(2)]
    Sf = A.alloc(name + "S", [128, H * dv]); S_b = [B.buf("sc%d_" % slot + "S%d" % g) for g in range(ngroups)]
    S = Sf[0:dk]
    Sst = A.alloc(name + "Sst", [dk, H * dv]); Sst_b = B.buf("sc%d_" % slot + "Sst")
    if pbanks is None:
        bank, bb = B.banks, B.bank_b
    else:
        b0, b1, b2, b3 = pbanks
        lmap = [b0, b1, b2, b3, b3, b1, b0, b2]
        bank = [B.banks[m] for m in lmap]
        bb = [B.bank_b[m] for m in lmap]

    if dk < 128:
        P.c("pool", lambda e: e.memset(Sf, 0.0), writes=[S_b[0]])
        P.c("pool", lambda e: e.memset(LRf, 0.0), writes=[LR_b])
    B.load(S, s0_ap, S_b[0])
    for g in range(1, ngroups):
        S_b[g].last_w = S_b[0].last_w

    order = list(range(nchunks)) if d == "f" else list(range(nchunks - 1, -1, -1))

    def issue_loads(ci):
        c = order[ci]
        i = ci % nb
        r0, r1 = c * C, (c + 1) * C
        B.load(la_t[i], src["la"][0][r0:r1, :], la_b[i], reads=[src["la"][1]])
        B.load(q_t[i], src["q"][0][r0:r1, :], q_b[i], reads=[src["q"][1]])
        if lowrank:
            B.load(a_t[i], src["a"][0][r0:r1, :], a_b[i], reads=[src["a"][1]])
            B.load(k0_t[i], src["k"][0][r0:r1, :], k0_b[i], reads=[src["k"][1]])
            B.load(bk_t[i][0:64, :], src["b"][0][r0:r1, :], bk_b[i], reads=[src["b"][1]])
            B.load(bk_t[i][64:128, :], src["k"][0][r0:r1, :], bk_b[i], reads=[src["k"][1]])
            B.load(vs_t[i][64:128, :], src["v"][0][r0:r1, :], vs_b[i], reads=[src["v"][1]])
        else:
            B.load(bk_t[i], src["k"][0][r0:r1, :], bk_b[i], reads=[src["k"][1]])
            B.load(vs_t[i], src["v"][0][r0:r1, :], vs_b[i], reads=[src["v"][1]])

    def chunk_body(ci):
        c = order[ci]
        i = ci % nb
        la, q, bk, vs = la_t[i], q_t[i], bk_t[i], vs_t[i]
        seg_start = (ci % chunks_per_seg == 0) and ci > 0
        seg_end = (ci % chunks_per_seg == chunks_per_seg - 1)
        seg = c // chunks_per_seg
        ost, ost_b = Ost[ci % 2], Ost_b[ci % 2]
        if seg_start:
            for g in range(ngroups):
                gs = slice(g * hg * dv, (g + 1) * hg * dv)
                P.c("pool", lambda e, gs=gs: e.tensor_scalar_mul(out=S[:, gs], in0=S[:, gs], scalar1=flags[0:dk, 0:1]),
                    reads=[S_b[g], flags_b], writes=[S_b[g]])
        def group_body(g):
            heads = list(range(g * hg, (g + 1) * hg))
            gk = slice(g * hg * dk, (g + 1) * hg * dk)
            gv = slice(g * hg * dv, (g + 1) * hg * dv)
            def mm_cum(e):
                ins = None
                for hi, h in enumerate(heads):
                    ins = e.matmul(bank[0][0:dk, hi * 128:(hi + 1) * 128], la[:, h * dk:(h + 1) * dk], UU, start=True, stop=True)
                return ins
            P.c("pe", mm_cum, reads=[la_b[i], consts_b], writes=[bb[0]])

            def mm_h2(e):
                ins = None
                for hi, h in enumerate(heads):
                    ins = e.matmul(bank[1][0:NP, hi * dk:(hi + 1) * dk], UR, la[:, h * dk:(h + 1) * dk], start=True, stop=True)
                return ins
            P.c("pe", mm_h2, reads=[la_b[i], consts_b], writes=[bb[1]])
            cumv = bank[0][0:dk, 0:hg * 128].rearrange("p (a b) -> p a b", a=hg)
            if scalar_decay:
                P.c("act", lambda e: e.activation(out=E[:, :, 0:128], in_=cumv, func=AF.Exp), reads=[bb[0]], writes=[E_b])
            elif lowrank:
                P.c("act", lambda e: e.activation(out=E[:, :, 0:128], in_=cumv, func=AF.Exp), reads=[bb[0]], writes=[E_b])
                P.c("act", lambda e: e.activation(out=E[:, :, 128:192], in_=cumv[:, :, 0:64], func=AF.Exp, scale=-1.0), reads=[bb[0]], writes=[E_b])
                P.c("act", lambda e: e.activation(out=E[:, :, 192:256], in_=cumv[:, :, 0:64], func=AF.Exp, scale=-1.0), reads=[bb[0]], writes=[E_b])
            else:
                P.c("act", lambda e: e.activation(out=E[:, :, 0:64], in_=cumv[:, :, 0:64], func=AF.Exp), reads=[bb[0]], writes=[E_b])
                P.c("act", lambda e: e.activation(out=E[:, :, 64:128], in_=cumv[:, :, 0:64], func=AF.Exp, scale=-1.0), reads=[bb[0]], writes=[E_b])
            P.c("act", lambda e: e.activation(out=eH, in_=bank[1][0:NP, 0:hg * dk], func=AF.Exp), reads=[bb[1]], writes=[eH_b])
            def mm_tr(e):
                ins = None
                for hi, h in enumerate(heads):
                    hs = slice(h * dk, (h + 1) * dk)
                    if lowrank:
                        bnk = bank[2 + hi // 2]
                        o = (hi % 2) * 256
                        e.transpose(bnk[0:dk, o:o + 64], q[:, hs], ident[0:64, 0:64])
                        e.transpose(bnk[0:dk, o + 64:o + 128], a_t[i][:, hs], ident[0:64, 0:64])
                        e.transpose(bnk[0:dk, o + 128:o + 192], bk[0:64, hs], ident[0:64, 0:64])
                        ins = e.transpose(bnk[0:dk, o + 192:o + 256], k0_t[i][:, hs], ident[0:64, 0:64])
                    else:
                        o = hi * 128
                        e.transpose(bank[2][0:dk, o:o + 64], q[:, hs], ident[0:64, 0:64])
                        ins = e.transpose(bank[2][0:dk, o + 64:o + 128], bk[:, hs], ident[0:64, 0:64])
                return ins
            rds = [q_b[i], bk_b[i], consts_b] + ([a_b[i], k0_b[i]] if lowrank else [])
            P.c("pe", mm_tr, reads=rds, writes=[bb[2], bb[3]] if lowrank else [bb[2]])
            if scalar_decay:
                for half in range(2):
                    trv = bank[2 + half][0:dk, :].rearrange("p (a b) -> p a b", a=2)
                    P.c("act", lambda e, half=half, trv=trv: e.activation(out=RAW[:, 2 * half:2 * half + 2, :], in_=trv, func=AF.Copy),
                        reads=[bb[2 + half]], writes=[RAW_b])
                    P.c("dve", lambda e, half=half: e.tensor_tensor(out=LR[:, 2 * half:2 * half + 2, 0:128], in0=RAW[:, 2 * half:2 * half + 2, 0:128], in1=E[:, 2 * half:2 * half + 2, 0:128], op=ALU.mult),
                        reads=[RAW_b, E_b], writes=[LR_b])
                h2v = bank[1][:, 0:hg * dk].rearrange("p (a b) -> p a b", a=hg)
                P.c("act", lambda e, h2v=h2v: e.activation(out=hcol.unsqueeze(2), in_=h2v[:, :, 0:1], func=AF.Copy), reads=[bb[1]], writes=[hg_b])
                P.c("act", lambda e: e.activation(out=gend.unsqueeze(2), in_=cumv[:, :, last:last + 1], func=AF.Copy), reads=[bb[0]], writes=[hg_b])
                P.c("dve", lambda e: e.tensor_tensor(out=hcol, in0=hcol, in1=gend, op=ALU.subtract), reads=[hg_b], writes=[hg_b])
                for hi in range(hg):
                    P.c("act", lambda e, hi=hi: e.activation(out=DEC[:, hi, :], in_=bank[0][:, hi * 128:(hi + 1) * 128], func=AF.Identity, bias=hcol[:, hi:hi + 1]),
                        reads=[bb[0], hg_b], writes=[DEC_b])
                P.c("pool", lambda e: e.tensor_tensor(out=DEC, in0=DEC, in1=MNEG.unsqueeze(1).to_broadcast([128, hg, 128]), op=ALU.add),
                    reads=[DEC_b, consts_b], writes=[DEC_b])
                P.c("act", lambda e: e.activation(out=DEC, in_=DEC, func=AF.Exp), reads=[DEC_b], writes=[DEC_b])
            elif lowrank:
                for half in range(2):
                    trv = bank[2 + half][0:dk, :].rearrange("p (a b) -> p a b", a=2)
                    P.c("dve", lambda e, half=half, trv=trv: e.tensor_tensor(out=LR[:, 2 * half:2 * half + 2, :], in0=trv, in1=E[:, 2 * half:2 * half + 2, :], op=ALU.mult),
                        reads=[bb[2 + half], E_b], writes=[LR_b])
            else:
                trv = bank[2][0:dk, 0:hg * 128].rearrange("p (a b) -> p a b", a=hg)
                P.c("dve", lambda e, trv=trv: e.tensor_tensor(out=LR, in0=trv, in1=E, op=ALU.mult), reads=[bb[2], E_b], writes=[LR_b])
            P.c("pool", lambda e: e.tensor_tensor(out=BKh, in0=bk[:, gk], in1=eH, op=ALU.mult), reads=[bk_b[i], eH_b], writes=[BKh_b])
            if SCAN_STOP[0] <= 1:
                return
            def mm_blk(e):
                ins = None
                for hi in range(hg):
                    if scalar_decay:
                        ins = e.matmul(bank[0][:, hi * 128:(hi + 1) * 128], RAW[:, hi, 128:256], RAW[:, hi, 0:128], start=True, stop=True)
                    elif lowrank:
                        ins = e.matmul(bank[0][:, hi * 128:(hi + 1) * 128], LR[:, hi, 128:256], LR[:, hi, 0:128], start=True, stop=True)
                    else:
                        ins = e.matmul(bank[0][0:64, hi * 64:(hi + 1) * 64], LR[:, hi, 64:128], LR[:, hi, 0:64], start=True, stop=True)
                return ins
            P.c("pe", mm_blk, reads=[LR_b] + ([RAW_b, DEC_b, hg_b] if scalar_decay else []), writes=[bb[0]])
            if scalar_decay:
                blkv = bank[0][:, :].rearrange("p (a b) -> p a b", a=hg)
                P.c("dve", lambda e, blkv=blkv: e.tensor_tensor(out=BLK, in0=blkv, in1=DEC, op=ALU.mult),
                    reads=[bb[0], DEC_b], writes=[BLK_b])
            elif lowrank:
                blkv = bank[0][:, :].rearrange("p (a b) -> p a b", a=hg)
                P.c("dve", lambda e, blkv=blkv: e.tensor_tensor(out=BLK, in0=blkv, in1=MASK.unsqueeze(1).to_broadcast([128, hg, 128]), op=ALU.mult),
                    reads=[bb[0], consts_b], writes=[BLK_b])
            else:
                blkv = bank[0][0:64, 0:hg * 64].rearrange("p (a b) -> p a b", a=hg)
                P.c("dve", lambda e, blkv=blkv: e.tensor_tensor(out=BLK, in0=blkv, in1=MASKI.unsqueeze(1).to_broadcast([64, hg, 64]), op=ALU.mult),
                    reads=[bb[0], consts_b], writes=[BLK_b])
            if SCAN_STOP[0] <= 2:
                return
            if lowrank:
                def mm_nab(e):
                    ins = None
                    for hi in range(hg):
                        if scalar_decay:
                            ins = e.transpose(bank[1][0:64, hi * 64:(hi + 1) * 64], BLK[0:64, hi, 64:128], ident[0:64, 0:64])
                        else:
                            ins = e.matmul(bank[1][0:64, hi * 64:(hi + 1) * 64], LR[:, hi, 64:128], LR[:, hi, 128:192], start=True, stop=True)
                    return ins
                P.c("pe", mm_nab, reads=[LR_b, BLK_b, consts_b], writes=[bb[1]])
                nabv = bank[1][0:64, 0:hg * 64].rearrange("p (a b) -> p a b", a=hg)
                P.c("dve", lambda e, nabv=nabv: e.tensor_tensor(out=PQ[0][:, :, 64:128], in0=nabv, in1=MASKN.unsqueeze(1).to_broadcast([64, hg, 64]), op=ALU.mult),
                    reads=[bb[1], consts_b], writes=[PQ_b[0]])
                P.c("act", lambda e: e.activation(out=PQ[0][:, :, 0:64], in_=BLK[0:64, :, 64:128], func=AF.Copy), reads=[BLK_b], writes=[PQ_b[0]])
                P.c("dve", lambda e: e.tensor_tensor(out=Rt, in0=BLK[0:64, :, 64:128], in1=ident[0:64, 0:64].unsqueeze(1).to_broadcast([64, hg, 64]), op=ALU.add),
                    reads=[BLK_b, consts_b], writes=[R_b])
                for lev in range(1, 6):
                    pa, pb_ = PQ[(lev - 1) % 2], PQ[lev % 2]
                    pa_b, pb_b = PQ_b[(lev - 1) % 2], PQ_b[lev % 2]

                    def mm_sq(e, pa=pa):
                        ins = None
                        for hi in range(hg):
                            e.matmul(bank[2][0:64, hi * 128:hi * 128 + 64], pa[:, hi, 64:128], pa[:, hi, 0:64], start=True, stop=True)
                            ins = e.matmul(bank[2][0:64, hi * 128 + 64:hi * 128 + 128], pa[:, hi, 0:64], pa[:, hi, 64:128], start=True, stop=True)
                        return ins
                    P.c("pe", mm_sq, reads=[pa_b], writes=[bb[2]])
                    pqv = bank[2][0:64, :].rearrange("p (a b) -> p a b", a=hg)
                    P.c("act", lambda e, pb_=pb_, pqv=pqv: e.activation(out=pb_, in_=pqv, func=AF.Copy), reads=[bb[2]], writes=[pb_b])

                    def mm_r(e, pb_=pb_):
                        ins = None
                        for hi in range(hg):
                            ins = e.matmul(bank[1][0:64, hi * 64:(hi + 1) * 64], pb_[:, hi, 64:128], Rt[:, hi, :], start=True, stop=True)
                        return ins
                    P.c("pe", mm_r, reads=[pb_b, R_b], writes=[bb[1]])
                    rupv = bank[1][0:64, 0:hg * 64].rearrange("p (a b) -> p a b", a=hg)
                    P.c("dve", lambda e, rupv=rupv: e.tensor_tensor(out=Rt, in0=rupv, in1=Rt, op=ALU.add), reads=[bb[1], R_b], writes=[R_b])
            if SCAN_STOP[0] <= 3:
                return
            if lowrank:
                def mm_r1(e):
                    ins = None
                    for hi, h in enumerate(heads):
                        e.matmul(bank[4][0:64, hi * dv:(hi + 1) * dv], LRf[:, hi, 64:128], Sf[:, h * dv:(h + 1) * dv], start=True, stop=False)
                        ins = e.matmul(bank[4][0:64, hi * dv:(hi + 1) * dv], BLK[64:128, hi, 64:128], vs[64:128, h * dv:(h + 1) * dv], start=False, stop=True)
                    return ins
                P.c("pe", mm_r1, reads=[LR_b, S_b[g], BLK_b, vs_b[i]], writes=[bb[4]])
                P.c("act", lambda e: e.activation(out=R1s, in_=bank[4][0:64, 0:hg * dv], func=AF.Copy), reads=[bb[4]], writes=[R1s_b])

                def mm_sa(e):
                    ins = None
                    for hi in range(hg):
                        ins = e.matmul(bank[5][0:64, hi * dv:(hi + 1) * dv], Rt[:, hi, :], R1s[:, hi * dv:(hi + 1) * dv], start=True, stop=True)
                    return ins
                P.c("pe", mm_sa, reads=[R_b, R1s_b], writes=[bb[5]])
                P.c("dve", lambda e: e.tensor_copy(out=vs[0:64, gv], in_=bank[5][0:64, 0:hg * dv]), reads=[bb[5]], writes=[vs_b[i]])

                def mm_o(e):
                    ins = None
                    for hi, h in enumerate(heads):
                        e.matmul(bank[6][0:64, hi * dv:(hi + 1) * dv], LRf[:, hi, 0:64], Sf[:, h * dv:(h + 1) * dv], start=True, stop=False)
                        ins = e.matmul(bank[6][0:64, hi * dv:(hi + 1) * dv], BLK[:, hi, 0:64], vs[:, h * dv:(h + 1) * dv], start=False, stop=True)
                    return ins
                P.c("pe", mm_o, reads=[LR_b, S_b[g], BLK_b, vs_b[i]], writes=[bb[6]])
            else:
                def mm_o(e):
                    ins = None
                    for hi, h in enumerate(heads):
                        e.matmul(bank[6][0:64, hi * dv:(hi + 1) * dv], LRf[:, hi, 0:64], Sf[:, h * dv:(h + 1) * dv], start=True, stop=False)
                        ins = e.matmul(bank[6][0:64, hi * dv:(hi + 1) * dv], BLK[:, hi, :], vs[:, h * dv:(h + 1) * dv], start=False, stop=True)
                    return ins
                P.c("pe", mm_o, reads=[LR_b, S_b[g], BLK_b, vs_b[i]], writes=[bb[6]])
            P.c("act", lambda e, ost=ost: e.activation(out=ost[:, gv], in_=bank[6][0:64, 0:hg * dv], func=AF.Copy), reads=[bb[6]], writes=[ost_b])

            def mm_sd(e):
                ins = None
                for hi, h in enumerate(heads):
                    ins = e.matmul(bank[7][0:dk, hi * dv:(hi + 1) * dv], BKh[:, hi * dk:(hi + 1) * dk], vs[:, h * dv:(h + 1) * dv], start=True, stop=True)
                return ins
            P.c("pe", mm_sd, reads=[BKh_b, vs_b[i]], writes=[bb[7]])
            Sg = S[:, gv].rearrange("p (a b) -> p a b", a=hg)
            egend = E[:, :, last:last + 1].to_broadcast([dk, hg, dv])
            P.c("dve", lambda e, Sg=Sg, egend=egend: e.tensor_tensor(out=Sg, in0=Sg, in1=egend, op=ALU.mult), reads=[S_b[g], E_b], writes=[S_b[g]])
            P.c("dve", lambda e: e.tensor_tensor(out=S[:, gv], in0=S[:, gv], in1=bank[7][0:dk, 0:hg * dv], op=ALU.add), reads=[S_b[g], bb[7]], writes=[S_b[g]])
        for g in range(ngroups):
            group_body(g)
        B.store(o_dst[c * C:(c + 1) * C, :], ost, ost_b, writes=[o_dst_b])
        if seg_end and st_out is not None:
            P.c("act", lambda e: e.activation(out=Sst, in_=S, func=AF.Copy), reads=S_b, writes=[Sst_b])
            B.store(st_out[seg], Sst, Sst_b, final=True)

    issue_loads(0)
    for ci in range(nchunks):
        if ci + 1 < nchunks:
            issue_loads(ci + 1)
        chunk_body(ci)
    if standalone:
        P.barrier()
        A.release(m0)


def build_program(dbg=None):
    B = Builder(dbg)
    nc, P, A = B.nc, B.P, B.arena
    dbg = B.dbg
    nlayers = dbg.get("nlayers", 2)

    xT_in = B.din("xT", [D, T])
    cond_in = B.din("cond", [128, NCH])
    W = {}
    for l in range(2):
        p = "l%d_" % l
        W[p + "w_mod"] = B.din(p + "w_mod", [D, 9 * D])
        W[p + "b_mod"] = B.din(p + "b_mod", [128, 9 * NCH])
        W[p + "norms"] = B.din(p + "norms", [128, 3 * NCH])
        for f in ("ffn1", "ffn2"):
            W[p + f + "_wg"] = B.din(p + f + "_wg", [D, DFF])
            W[p + f + "_wu"] = B.din(p + f + "_wu", [D, DFF])
            W[p + f + "_wd"] = B.din(p + f + "_wd", [DFF, D])
        W[p + "w_in"] = B.din(p + "w_in", [D, L0_IN if l == 0 else L1_IN])
        W[p + "w_out"] = B.din(p + "w_out", [D, D])
    fin_norm = B.din("final_norm", [128, NCH])
    yT_out = B.dout("yT", [D, T])
    consts_np = make_consts()
    consts_in = B.din("consts", consts_np.shape)
    flags_in = B.din("flags", [128, 2])
    tmask_in = B.din("tmask", [T, 4])
    rot_in = B.din("rot", [T, 128])
    s0_in = {"gla": B.din("s0_gla", [2, 128, 1024]), "ret": B.din("s0_ret", [2, 128, 1024]),
             "gdn": B.din("s0_gdn", [2, 128, 1024]), "rwkv": B.din("s0_rwkv", [2, 64, 1024])}
    st_out = {"gla": B.dout("st_gla", [2, NSEG, 128, 1024]), "ret": B.dout("st_ret", [2, NSEG, 128, 1024]),
              "gdn": B.dout("st_gdn", [2, NSEG, 128, 1024]), "rwkv": B.dout("st_rwkv", [2, NSEG, 64, 1024])}
    SP_ = {}
    for nm, shp in (("l0_gk_up", [16, 1024]), ("l0_gk_b", [1, 1024]), ("l0_gla_norm", [1, 256]), ("l0_ret_norm", [1, 256]),
                    ("ret_la_f", [T, 512]), ("ret_la_b", [T, 512]),
                    ("l1_conv", [5, 3072]), ("l1_dtb", [1, 16]), ("l1_alog", [1, 16]), ("l1_gdn_norm", [1, 128]),
                    ("l1_mu", [1, 3456]), ("l1_w0", [1, 2048]), ("l1_a0", [1, 2048]), ("l1_w2", [64, 2048]),
                    ("l1_a2", [64, 2048]), ("l1_g2", [128, 1024]), ("l1_kk", [1, 1024]), ("l1_ka", [1, 1024]),
                    ("l1_rk", [1, 1024]), ("l1_lnw", [1, 1024]), ("l1_lnb", [1, 1024])):
        SP_[nm] = B.din(nm, shp)

    ones_bf = A.alloc("ones_bf", [128, 128], BF16)
    ones_b = Buf("ones_bf")
    P.c("pool", lambda e: e.memset(ones_bf, 1.0), writes=[ones_b])
    mods = [A.alloc("mods%d" % l, [128, 9 * NCH]) for l in range(2)]
    mods_b = [Buf("mods%d" % l) for l in range(2)]
    norms = [A.alloc("norms%d" % l, [128, 3 * NCH]) for l in range(2)]
    norms_b = [Buf("norms%d" % l) for l in range(2)]
    modA = [A.alloc("modA%d" % l, [128, 3 * NCH]) for l in range(2)]
    modG = [A.alloc("modG%d" % l, [128, 3 * NCH]) for l in range(2)]
    modc_b = [Buf("modc%d" % l) for l in range(2)]
    finw = A.alloc("finw", [128, NCH])
    finw_b = Buf("finw")
    B.load(finw, fin_norm, finw_b)
    consts = A.alloc("consts", list(consts_np.shape))
    consts_b = Buf("consts")
    B.load(consts, consts_in, consts_b)
    flags = A.alloc("flags", [128, 2])
    flags_b = Buf("flags")
    B.load(flags, flags_in, flags_b)
    ones_f = A.alloc("ones_f", [1, 128])
    ones_f_b = Buf("ones_f")
    P.c("pool", lambda e: e.memset(ones_f, 1.0), writes=[ones_f_b])
    ident = consts[:, CONST_COLS["ident"][0]:CONST_COLS["ident"][0] + 128]

    def phase_mods(l):
        p = "l%d_" % l
        m0 = A.mark()
        cond = A.alloc("cond", [128, NCH])
        cond_b = B.buf("cond")
        scond = A.alloc("scond", [128, NCH], BF16)
        scond_b = Buf("scond")
        bmod = A.alloc("bmod", [128, 9 * NCH])
        bmod_b = B.buf("bmod")
        B.load(cond, cond_in, cond_b)
        B.load(bmod, W[p + "b_mod"], bmod_b)
        B.load(norms[l], W[p + "norms"], norms_b[l])
        P.c("act", lambda e: e.activation(out=scond, in_=cond, func=AF.Silu), reads=[cond_b], writes=[scond_b])
        wsl = Slots(A, "wmod", 3, [128, NCH, 512], BF16)
        wsl.bufs = [B.buf("wmod%d" % i) for i in range(3)]
        wsrc = W[p + "w_mod"].rearrange("(k p) n -> p k n", p=128)
        ps = B.banks[0]
        ps_b = B.bank_b[0]
        ngrp = 9 * D // 512
        for g in range(ngrp):
            wt, wt_b = wsl.next()
            B.load(wt, wsrc[:, :, g * 512:(g + 1) * 512], wt_b, eng="pool")

            def mm(e, wt=wt, g=g):
                ins = None
                for cidx in range(4):
                    col = g * 4 + cidx
                    for k in range(NCH):
                        ins = e.matmul(ps[:, col:col + 1], wt[:, k, cidx * 128:(cidx + 1) * 128],
                                       scond[:, k:k + 1], start=(k == 0), stop=(k == NCH - 1))
                return ins
            P.c("pe", mm, reads=[wt_b, scond_b], writes=[ps_b])
        P.c("dve", lambda e: e.tensor_tensor(out=mods[l], in0=ps[:, 0:9 * NCH], in1=bmod, op=ALU.add),
            reads=[ps_b, bmod_b], writes=[mods_b[l]])
        for i in range(3):
            sc = mods[l][:, (3 * i + 1) * NCH:(3 * i + 2) * NCH]
            g = mods[l][:, (3 * i + 2) * NCH:(3 * i + 3) * NCH]
            gam = norms[l][:, i * NCH:(i + 1) * NCH]
            P.c("dve", lambda e, sc=sc, gam=gam, i=i: e.scalar_tensor_tensor(
                out=modA[l][:, i * NCH:(i + 1) * NCH], in0=sc, scalar=1.0, in1=gam, op0=ALU.add, op1=ALU.mult),
                reads=[mods_b[l], norms_b[l]], writes=[modc_b[l]])
            fac = 1.0 if i == 1 else 0.5
            P.c("dve", lambda e, g=g, i=i, fac=fac: e.tensor_scalar_mul(
                out=modG[l][:, i * NCH:(i + 1) * NCH], in0=g, scalar1=fac),
                reads=[mods_b[l]], writes=[modc_b[l]])
        P.barrier()
        A.release(m0)

    def tt(eng, out, in0, in1, op, r, w):
        return P.c(eng, lambda e: e.tensor_tensor(out=out, in0=in0, in1=in1, op=op), reads=r, writes=w)

    def stt(eng, out, in0, scalar, in1, op0, op1, r, w):
        eng = "dve"
        return P.c(eng, lambda e: e.scalar_tensor_tensor(out=out, in0=in0, scalar=scalar, in1=in1, op0=op0, op1=op1), reads=r, writes=w)

    def tsm(eng, out, in0, s1, r, w):
        return P.c(eng, lambda e: e.tensor_scalar_mul(out=out, in0=in0, scalar1=s1), reads=r, writes=w)

    def tcopy(eng, out, in_, r, w):
        return P.c(eng, lambda e: e.tensor_copy(out=out, in_=in_), reads=r, writes=w)

    def actf(out, in_, func, r, w, scale=1.0, bias=None):
        def fn(e):
            if bias is None:
                return e.activation(out=out, in_=in_, func=func, scale=scale)
            return e.activation(out=out, in_=in_, func=func, scale=scale, bias=bias)
        return P.c("act", fn, reads=r, writes=w)

    def red(eng, out, in_, r, w):
        return P.c(eng, lambda e: e.tensor_reduce(out=out, in_=in_, axis=AX.X, op=ALU.add), reads=r, writes=w)

    def rstd_small(t, r_b, sc, eps):
        actf(t, t, AF.Ln, [r_b], [r_b], scale=sc, bias=eps)
        actf(t, t, AF.Exp, [r_b], [r_b], scale=-0.5)

    xs_scr, _ = B.dscr("xs", [D, T])
    xs_tile_b = [Buf("xs_t%d" % i) for i in range(NT)]
    oT_scr, oT_b = B.dscr("oT", [D, T], BF16)
    SC = {}
    for nm, shp in (("gla_q", [T, 512]), ("gla_k", [T, 512]), ("gla_v", [T, 1024]), ("gla_g", [T, 1024]),
                    ("gla_la_f", [T, 512]), ("gla_la_b", [T, 512]),
                    ("ret_q", [T, 512]), ("ret_k", [T, 512]), ("ret_v", [T, 1024]), ("ret_g", [T, 1024]),
                    ("OA_f", [T, 1024]), ("OA_b", [T, 1024]), ("OB_f", [T, 1024]), ("OB_b", [T, 1024]),
                    ("zq", [T + 4, 3072]), ("gdn_g", [T, 1024]), ("zr", [T + 4, 3456]), ("ab", [T, 32]),
                    ("gdn_q", [T, 1024]), ("gdn_a", [T, 1024]), ("gdn_v", [T, 1024]),
                    ("gdn_la_f", [T, 1024]), ("gdn_la_b", [T, 1024]), ("gdn_k_f", [T, 1024]), ("gdn_k_b", [T, 1024]),
                    ("gdn_b_f", [T, 1024]), ("gdn_b_b", [T, 1024]),
                    ("rw_q", [T, 1024]), ("rw_v", [T, 1024]), ("rw_a", [T, 1024]),
                    ("rw_la_f", [T, 1024]), ("rw_la_b", [T, 1024]), ("rw_k_f", [T, 1024]), ("rw_k_b", [T, 1024]),
                    ("rw_b_f", [T, 1024]), ("rw_b_b", [T, 1024]), ("rw_gate", [T, 1024]), ("rw_bonus", [T, 1024])):
        SC[nm] = B.dscr(nm, shp)
    SC["ret_la_f"] = (SP_["ret_la_f"], Buf("ret_la_f"))
    SC["ret_la_b"] = (SP_["ret_la_b"], Buf("ret_la_b"))

    class TileCtx:
        pass

    def tile_alloc(l0=True):
        tc = TileCtx()
        tc.x = A.alloc("xtile", [128, NCH, TT])
        tc.x_b = [Buf("xtile%d" % j) for j in range(NCH)]
        tc.x_own = [B.buf("xown%d" % g) for g in range(4)]
        tc.h = A.alloc("htile", [128, NCH, TT], BF16)
        tc.h_b = [Buf("htile%d" % j) for j in range(NCH)]
        tc.act = A.alloc("acttile", [128, NF, TT], BF16)
        tc.act_b = [Buf("act%d" % j) for j in range(NF)]
        tc.sq = A.alloc("sq", [128, 2, TT], BF16)
        tc.sq_b = [Buf("sq0"), Buf("sq1")]
        tc.rstd = A.alloc("rstd", [128, TT])
        tc.rstd_b = Buf("rstd")
        tc.tmp = A.alloc("tmpx", [128, 2, TT])
        tc.tmp_b = [Buf("tmpx0"), Buf("tmpx1")]
        tc.sg = A.alloc("sg", [128, 2, TT])
        tc.sg_b = [Buf("sg0"), Buf("sg1")]
        nw = 2 if l0 else 3
        tc.wgu = Slots(A, "wgu", nw, [128, NCH * 512], BF16)
        tc.wgu.bufs = [B.buf("wgu%d" % i) for i in range(nw)]
        tc.wd = Slots(A, "wd", 2, [128, 11, 512], BF16)
        tc.wd.bufs = [B.buf("wd%d" % i) for i in range(2)]
        tc.stg = [A.alloc("stg%d" % i, [128, 512]) for i in range(4)]
        tc.stg_b = [B.buf("stg%d" % i) for i in range(4)]
        tc.stg_i = 0
        if not l0:
            return tc
        tc.rot = A.alloc("rot_t", [128, 128])
        tc.rot_b = B.buf("rot_t")
        tc.rx = A.alloc("rot_x", [128, 4, 128])
        tc.rx_b = Buf("rot_x")
        tc.rtmp = [A.alloc("rot_tmp%d" % i, [128, 4, 64]) for i in range(4)]
        tc.rtmp_b = [Buf("rot_tmp%d" % i) for i in range(4)]
        tc.wtail = A.alloc("wtail", [128, NCH, 32], BF16)
        tc.wtail_b = B.buf("wtail")
        tc.gdT = A.alloc("gdT", [16, 2, TT])
        tc.gdT_b = Buf("gdT")
        tc.gkup = A.alloc("gkup", [16, 2, 512])
        tc.gkb = A.alloc("gkb", [1, 2, 512])
        tc.gk_b = B.buf("gkparams")
        B.dma_in(tc.gkup, SP_["l0_gk_up"].rearrange("k (d c) -> k d c", d=2), tc.gk_b, [tc.gk_b])
        B.dma_in(tc.gkb, SP_["l0_gk_b"].rearrange("k (d c) -> k d c", d=2), tc.gk_b, [tc.gk_b])
        return tc

    def next_stage(tc):
        i = tc.stg_i
        tc.stg_i = (i + 1) % 4
        return tc.stg[i], tc.stg_b[i]

    def x_load(tc, src, t, src_bufs=()):
        srcv = src.rearrange("(j p) t -> p j t", p=128)
        for g in range(4):
            B.dma_in(tc.x[:, 4 * g:4 * g + 4, :], srcv[:, 4 * g:4 * g + 4, t * TT:(t + 1) * TT], tc.x_own[g],
                     tc.x_b[4 * g:4 * g + 4], reads=src_bufs)

    def x_store(tc, dst, t, dst_bufs=(), final=False):
        dstv = dst.rearrange("(j p) t -> p j t", p=128)
        for g in range(4):
            B.dma_out(dstv[:, 4 * g:4 * g + 4, t * TT:(t + 1) * TT], tc.x[:, 4 * g:4 * g + 4, :], tc.x_own[g],
                      tc.x_b[4 * g:4 * g + 4], writes=dst_bufs, final=final)

    def rms_stats(tc):
        ps = B.banks[7]
        ps_b = B.bank_b[7]
        for j in range(NCH):
            s = j % 2
            if j % 2 == 0:
                P.c("act", lambda e, j=j, s=s: e.activation(out=tc.sq[:, s, :], in_=tc.x[:, j, :], func=AF.Square),
                    reads=[tc.x_b[j]], writes=[tc.sq_b[s]])
            else:
                P.c("pool", lambda e, j=j, s=s: e.tensor_tensor(out=tc.sq[:, s, :], in0=tc.x[:, j, :], in1=tc.x[:, j, :], op=ALU.mult),
                    reads=[tc.x_b[j]], writes=[tc.sq_b[s]])
            P.c("pe", lambda e, j=j, s=s: e.matmul(ps[:, :], ones_bf, tc.sq[:, s, :], start=(j == 0), stop=(j == NCH - 1)),
                reads=[tc.sq_b[s], ones_b], writes=[ps_b])
        actf(tc.rstd, ps[:, :], AF.Ln, [ps_b], [tc.rstd_b], scale=1.0 / D, bias=EPS)
        actf(tc.rstd, tc.rstd, AF.Exp, [tc.rstd_b], [tc.rstd_b], scale=-0.5)

    def modulate(tc, l, i):
        rms_stats(tc)
        for j in range(NCH):
            s = j % 2
            a_col = modA[l][:, i * NCH + j:i * NCH + j + 1]
            sh_col = mods[l][:, (3 * i) * NCH + j:(3 * i) * NCH + j + 1]
            stt("dve", tc.tmp[:, s, :], tc.x[:, j, :], a_col, tc.rstd, ALU.mult, ALU.mult,
                [tc.x_b[j], tc.rstd_b, modc_b[l]], [tc.tmp_b[s]])
            actf(tc.h[:, j, :], tc.tmp[:, s, :], AF.Identity, [tc.tmp_b[s], mods_b[l]], [tc.h_b[j]], bias=sh_col)

    def ffn(tc, l, which, i):
        p = "l%d_ffn%d_" % (l, which)
        wg_src = W[p + "wg"].rearrange("(k p) n -> p k n", p=128)
        wu_src = W[p + "wu"].rearrange("(k p) n -> p k n", p=128)
        wd_src = W[p + "wd"].rearrange("(f p) n -> p f n", p=128)
        for g in range(NF // 2):
            wflat, wt_b = tc.wgu.next()
            wt = wflat.rearrange("p (a k c) -> p a k c", a=2, k=NCH)
            B.load(wt[:, 0], wg_src[:, :, g * 256:(g + 1) * 256], wt_b, eng="pool")
            B.load(wt[:, 1], wu_src[:, :, g * 256:(g + 1) * 256], wt_b, eng="pool")
            for ci in range(2):
                f = g * 2 + ci
                pg, pg_b = B.banks[(f % 2) * 2], B.bank_b[(f % 2) * 2]
                pu, pu_b = B.banks[(f % 2) * 2 + 1], B.bank_b[(f % 2) * 2 + 1]

                def mm(e, wt=wt, ci=ci, which_w=0, ps=pg):
                    ins = None
                    for k in range(NCH):
                        ins = e.matmul(ps[:, :], wt[:, which_w, k, ci * 128:(ci + 1) * 128], tc.h[:, k, :],
                                       start=(k == 0), stop=(k == NCH - 1))
                    return ins
                P.c("pe", lambda e, mm=mm, wt=wt, ci=ci, pg=pg: mm(e, wt, ci, 0, pg), reads=[wt_b] + tc.h_b, writes=[pg_b])
                P.c("pe", lambda e, mm=mm, wt=wt, ci=ci, pu=pu: mm(e, wt, ci, 1, pu), reads=[wt_b] + tc.h_b, writes=[pu_b])
                s = f % 2
                actf(tc.sg[:, s, :], pg[:, :], AF.Silu, [pg_b], [tc.sg_b[s]])
                tt("dve", tc.act[:, f, :], pu[:, :], tc.sg[:, s, :], ALU.mult, [pu_b, tc.sg_b[s]], [tc.act_b[f]])
        for dg in range(4):
            pbanks = [4 + q for q in range(4)]
            for part in range(4):
                wt, wt_b = tc.wd.next()
                B.load(wt, wd_src[:, part * 11:(part + 1) * 11, dg * 512:(dg + 1) * 512], wt_b, eng="pool")
                for q in range(4):
                    def mm(e, wt=wt, part=part, q=q):
                        ins = None
                        for fi in range(11):
                            f = part * 11 + fi
                            ins = e.matmul(B.banks[pbanks[q]][:, :], wt[:, fi, q * 128:(q + 1) * 128], tc.act[:, f, :],
                                           start=(f == 0), stop=(f == NF - 1))
                        return ins
                    P.c("pe", mm, reads=[wt_b] + tc.act_b[part * 11:(part + 1) * 11], writes=[B.bank_b[pbanks[q]]])
            for q in range(4):
                j = dg * 4 + q
                gcol = modG[l][:, i * NCH + j:i * NCH + j + 1]
                stt("dve", tc.x[:, j, :], B.banks[pbanks[q]][:, :], gcol, tc.x[:, j, :], ALU.mult, ALU.add,
                    [B.bank_b[pbanks[q]], modc_b[l], tc.x_b[j]], [tc.x_b[j]])

    def final_norm(tc):
        rms_stats(tc)
        for j in range(NCH):
            stt("dve", tc.x[:, j, :], tc.x[:, j, :], finw[:, j:j + 1], tc.rstd, ALU.mult, ALU.mult,
                [tc.x_b[j], tc.rstd_b, finw_b], [tc.x_b[j]])

    def wout(tc, l, t):
        own = B.buf("oTload")
        B.dma_in(tc.act[:, 0:NCH, :], oT_scr.rearrange("(k p) t -> p k t", p=128)[:, :, t * TT:(t + 1) * TT], own,
                 tc.act_b[0:NCH], reads=[oT_b])
        w_src = W["l%d_w_out" % l].rearrange("(k p) n -> p k n", p=128)
        for dg in range(4):
            wflat, wt_b = tc.wgu.next()
            wv = wflat.rearrange("p (k c) -> p k c", k=NCH)
            B.load(wv, w_src[:, :, dg * 512:(dg + 1) * 512], wt_b, eng="pool")
            for q in range(4):
                j = dg * 4 + q
                ps, ps_b = B.banks[q], B.bank_b[q]

                def mm(e, wv=wv, q=q, ps=ps):
                    ins = None
                    for k in range(NCH):
                        ins = e.matmul(ps[:, :], wv[:, k, q * 128:(q + 1) * 128], tc.act[:, k, :], start=(k == 0), stop=(k == NCH - 1))
                    return ins
                P.c("pe", mm, reads=[wt_b] + tc.act_b[0:NCH], writes=[ps_b])
                gcol = modG[l][:, NCH + j:NCH + j + 1]
                stt("dve", tc.x[:, j, :], ps[:, :], gcol, tc.x[:, j, :], ALU.mult, ALU.add,
                    [ps_b, modc_b[l], tc.x_b[j]], [tc.x_b[j]])

    QS = 128 ** -0.5

    def proj(tc, l, t, plan, ncols_tail):
        w_src = W["l%d_w_in" % l].rearrange("(k p) n -> p k n", p=128)
        for cg in range(len(plan)):
            dst, row_off, col_off, kind, scale = plan[cg]
            dst_ap, dst_b = SC[dst]
            wflat, wt_b = tc.wgu.next()
            wv = wflat.rearrange("p (k c) -> p k c", k=NCH)
            B.load(wv, w_src[:, :, cg * 512:(cg + 1) * 512], wt_b, eng="pool")
            for tb in range(4):
                ps, ps_b = B.banks[(cg * 4 + tb) % 4], B.bank_b[(cg * 4 + tb) % 4]
                r0 = t * TT + tb * 128

                def mm(e, wv=wv, tb=tb, ps=ps):
                    ins = None
                    for k in range(NCH):
                        ins = e.matmul(ps[:, :], tc.h[:, k, tb * 128:(tb + 1) * 128], wv[:, k, :], start=(k == 0), stop=(k == NCH - 1))
                    return ins
                P.c("pe", mm, reads=[wt_b] + tc.h_b, writes=[ps_b])
                st, st_b = next_stage(tc)
                if kind == "copy":
                    if (cg + tb) % 2 == 0:
                        actf(st, ps[:, :], AF.Identity, [ps_b], [st_b], scale=scale)
                    else:
                        tsm("dve", st, ps[:, :], scale, [ps_b], [st_b])
                elif kind == "silu":
                    actf(st, ps[:, :], AF.Silu, [ps_b], [st_b])
                else:
                    B.dma_in(tc.rot, rot_in[r0:r0 + 128, :], tc.rot_b, [tc.rot_b])
                    cosB = tc.rot[:, 0:64].unsqueeze(1).to_broadcast([128, 4, 64])
                    sinB = tc.rot[:, 64:128].unsqueeze(1).to_broadcast([128, 4, 64])
                    psv = ps[:, :].rearrange("p (h c) -> p h c", h=4)
                    actf(tc.rx, psv, AF.Identity, [ps_b], [tc.rx_b], scale=scale)
                    sv = st.rearrange("p (h c) -> p h c", h=4)
                    tt("dve", tc.rtmp[0], tc.rx[:, :, 0:64], cosB, ALU.mult, [tc.rx_b, tc.rot_b], [tc.rtmp_b[0]])
                    tt("pool", tc.rtmp[1], tc.rx[:, :, 64:128], sinB, ALU.mult, [tc.rx_b, tc.rot_b], [tc.rtmp_b[1]])
                    tt("dve", sv[:, :, 0:64], tc.rtmp[0], tc.rtmp[1], ALU.subtract, [tc.rtmp_b[0], tc.rtmp_b[1]], [st_b])
                    tt("pool", tc.rtmp[2], tc.rx[:, :, 0:64], sinB, ALU.mult, [tc.rx_b, tc.rot_b], [tc.rtmp_b[2]])
                    tt("dve", tc.rtmp[3], tc.rx[:, :, 64:128], cosB, ALU.mult, [tc.rx_b, tc.rot_b], [tc.rtmp_b[3]])
                    tt("pool", sv[:, :, 64:128], tc.rtmp[2], tc.rtmp[3], ALU.add, [tc.rtmp_b[2], tc.rtmp_b[3]], [st_b])
                B.dma_out(dst_ap[row_off + r0:row_off + r0 + 128, col_off:col_off + 512], st, st_b, [st_b], writes=[dst_b])
        c0 = len(plan) * 512
        if l == 0:
            B.load(tc.wtail, w_src[:, :, c0:c0 + 32], tc.wtail_b, eng="pool")
            for d_ in range(2):
                ps, ps_b = B.banks[d_], B.bank_b[d_]

                def mm(e, d_=d_, ps=ps):
                    ins = None
                    for k in range(NCH):
                        ins = e.matmul(ps[0:16, :], tc.wtail[:, k, d_ * 16:(d_ + 1) * 16], tc.h[:, k, :], start=(k == 0), stop=(k == NCH - 1))
                    return ins
                P.c("pe", mm, reads=[tc.wtail_b] + tc.h_b, writes=[ps_b])
                actf(tc.gdT[:, d_, :], ps[0:16, :], AF.Copy, [ps_b], [tc.gdT_b])
            for d_ in range(2):
                dst_ap, dst_b = SC["gla_la_f" if d_ == 0 else "gla_la_b"]
                for tb in range(4):
                    ps, ps_b = B.banks[2 + (tb % 2)], B.bank_b[2 + (tb % 2)]
                    r0 = t * TT + tb * 128

                    def mm(e, d_=d_, tb=tb, ps=ps):
                        e.matmul(ps[:, :], tc.gdT[0:16, d_, tb * 128:(tb + 1) * 128], tc.gkup[0:16, d_, :], start=True, stop=False)
                        return e.matmul(ps[:, :], ones_f[0:1, 0:128], tc.gkb[0:1, d_, :], start=False, stop=True)
                    P.c("pe", mm, reads=[tc.gdT_b, tc.gk_b, ones_f_b], writes=[ps_b])
                    st, st_b = next_stage(tc)
                    actf(st, ps[:, :], AF.Exp, [ps_b], [st_b], scale=-1.0)
                    actf(st, st, AF.Ln, [st_b], [st_b], bias=1.0)
                    tsm("dve", st, st, -1.0 / 16.0, [st_b], [st_b])
                    B.dma_out(dst_ap[r0:r0 + 128, :], st, st_b, [st_b], writes=[dst_b])
        else:
            wflat, wt_b = tc.wgu.next()
            wv = wflat[:, 0:NCH * 416].rearrange("p (k c) -> p k c", k=NCH)
            B.load(wv, w_src[:, :, c0:c0 + 416], wt_b, eng="pool")
            for tb in range(4):
                ps, ps_b = B.banks[tb % 4], B.bank_b[tb % 4]
                r0 = t * TT + tb * 128

                def mm(e, wv=wv, tb=tb, ps=ps):
                    ins = None
                    for k in range(NCH):
                        ins = e.matmul(ps[:, 0:416], tc.h[:, k, tb * 128:(tb + 1) * 128], wv[:, k, :], start=(k == 0), stop=(k == NCH - 1))
                    return ins
                P.c("pe", mm, reads=[wt_b] + tc.h_b, writes=[ps_b])
                st, st_b = next_stage(tc)
                actf(st[:, 0:416], ps[:, 0:416], AF.Copy, [ps_b], [st_b])
                B.dma_out(SC["zr"][0][2 + r0:2 + r0 + 128, 3072:3456], st[:, 0:384], st_b, [st_b], writes=[SC["zr"][1]])
                B.dma_out(SC["ab"][0][r0:r0 + 128, :], st[:, 384:416], st_b, [st_b], writes=[SC["ab"][1]])

    PLAN0 = [("gla_q", 0, 0, "copy", QS), ("gla_k", 0, 0, "copy", 1.0), ("gla_v", 0, 0, "copy", 1.0), ("gla_v", 0, 512, "copy", 1.0),
             ("gla_g", 0, 0, "silu", 1.0), ("gla_g", 0, 512, "silu", 1.0),
             ("ret_q", 0, 0, "rot", QS), ("ret_k", 0, 0, "rot", 1.0), ("ret_v", 0, 0, "copy", 1.0), ("ret_v", 0, 512, "copy", 1.0),
             ("ret_g", 0, 0, "silu", 1.0), ("ret_g", 0, 512, "silu", 1.0)]
    PLAN1 = [("zq", 2, 512 * i, "copy", 1.0) for i in range(6)] + [("gdn_g", 0, 0, "silu", 1.0), ("gdn_g", 0, 512, "silu", 1.0)] + \
            [("zr", 2, 512 * i, "copy", 1.0) for i in range(6)]

    def post_phase(l):
        B.phase()
        m0 = A.mark()
        specs = [("OA", 4, 256, False, EPS), ("OB", 4, 256, True, EPS)] if l == 0 else \
                [("OA", 8, 128, False, EPS), ("OB", 16, 64, True, 64e-5)]
        oall = A.alloc("oall", [128, D])
        oall_b = Buf("oall")
        oTst = A.alloc("oTst", [128, NCH, 128], BF16)
        oTst_b = B.pbuf()
        Of = [A.alloc("Of%d" % i, [128, 1024]) for i in range(2)]
        Ob = [A.alloc("Ob%d" % i, [128, 1024]) for i in range(2)]
        Gt = [A.alloc("Gt%d" % i, [128, 1024]) for i in range(2)]
        Of_b = [B.pbuf() for i in range(2)]
        Ob_b = [B.pbuf() for i in range(2)]
        Gt_b = [B.pbuf() for i in range(2)]
        sqt = A.alloc("sqt", [128, 1024])
        sqt_b = Buf("sqt")
        ss = A.alloc("ss", [128, 16])
        ss_b = Buf("ss")
        sm = A.alloc("sm", [128, 16])
        sm_b = Buf("sm")
        pown = B.pbuf()
        if l == 0:
            nwA = A.alloc("nwA", [128, 256])
            nwB = A.alloc("nwB", [128, 256])
            B.dma_in(nwA, SP_["l0_gla_norm"].partition_broadcast(128), pown, [pown])
            B.dma_in(nwB, SP_["l0_ret_norm"].partition_broadcast(128), pown, [pown])
            gates = ["gla_g", "ret_g"]
        else:
            nwA = A.alloc("nwA", [128, 128])
            lnw = A.alloc("lnw", [128, 1024])
            lnb = A.alloc("lnb", [128, 1024])
            B.dma_in(nwA, SP_["l1_gdn_norm"].partition_broadcast(128), pown, [pown])
            B.dma_in(lnw, SP_["l1_lnw"].partition_broadcast(128), pown, [pown])
            B.dma_in(lnb, SP_["l1_lnb"].partition_broadcast(128), pown, [pown])
            bon = [A.alloc("bon%d" % i, [128, 1024]) for i in range(2)]
            bon_b = [B.pbuf() for i in range(2)]
            gates = ["gdn_g", "rw_gate"]
        for tb in range(T // 128):
            r0 = tb * 128
            for mi, (onm, Hh, dvv, center, eps) in enumerate(specs):
                i = (tb * 2 + mi) % 2
                B.dma_in(Of[i], SC[onm + "_f"][0][r0:r0 + 128, :], Of_b[i], [Of_b[i]], reads=[SC[onm + "_f"][1]])
                B.dma_in(Ob[i], SC[onm + "_b"][0][r0:r0 + 128, :], Ob_b[i], [Ob_b[i]], reads=[SC[onm + "_b"][1]])
                B.dma_in(Gt[i], SC[gates[mi]][0][r0:r0 + 128, :], Gt_b[i], [Gt_b[i]], reads=[SC[gates[mi]][1]])
                o = Of[i]
                o_b = Of_b[i]
                o3 = o.rearrange("p (h c) -> p h c", h=Hh)
                tt("dve", o, Of[i], Ob[i], ALU.add, [Of_b[i], Ob_b[i]], [o_b])
                if center:
                    red("dve", sm[:, 0:Hh], o3, [o_b], [sm_b])
                    tsm("dve", sm[:, 0:Hh], sm[:, 0:Hh], -1.0 / dvv, [sm_b], [sm_b])
                    tt("pool", o3, o3, sm[:, 0:Hh].unsqueeze(2).to_broadcast([128, Hh, dvv]), ALU.add, [o_b, sm_b], [o_b])
                tt("pool", sqt, o, o, ALU.mult, [o_b], [sqt_b])
                red("dve", ss[:, 0:Hh], sqt.rearrange("p (h c) -> p h c", h=Hh), [sqt_b], [ss_b])
                rstd_small(ss[:, 0:Hh], ss_b, 1.0 / dvv, eps)
                tt("dve", o3, o3, ss[:, 0:Hh].unsqueeze(2).to_broadcast([128, Hh, dvv]), ALU.mult, [o_b, ss_b], [o_b])
                dsto = oall[:, mi * 1024:(mi + 1) * 1024]
                if l == 1 and mi == 1:
                    B.dma_in(bon[0], SC["rw_bonus"][0][r0:r0 + 128, :], bon_b[0], [bon_b[0]], reads=[SC["rw_bonus"][1]])
                    tt("pool", o, o, lnw, ALU.mult, [o_b, pown], [o_b])
                    tt("dve", o, o, lnb, ALU.add, [o_b, pown], [o_b])
                    tt("pool", o, o, bon[0], ALU.add, [o_b, bon_b[0]], [o_b])
                else:
                    nw = nwA if mi == 0 else nwB
                    tt("pool", o3, o3, nw.unsqueeze(1).to_broadcast([128, Hh, dvv]), ALU.mult, [o_b, pown], [o_b])
                tt("dve", dsto, o, Gt[i], ALU.mult, [o_b, Gt_b[i]], [oall_b])
            for bq in range(4):
                ps, ps_b = B.banks[bq], B.bank_b[bq]

                def mmt(e, bq=bq, ps=ps):
                    ins = None
                    for kk_ in range(4):
                        k = bq * 4 + kk_
                        ins = e.transpose(ps[:, kk_ * 128:(kk_ + 1) * 128], oall[:, k * 128:(k + 1) * 128], ident)
                    return ins
                P.c("pe", mmt, reads=[oall_b, consts_b], writes=[ps_b])
                dst = oTst[:, bq * 4:(bq + 1) * 4, :]
                srcv = ps[:, :].rearrange("p (a b) -> p a b", a=4)
                if bq % 2 == 0:
                    actf(dst, srcv, AF.Copy, [ps_b], [oTst_b])
                else:
                    tcopy("dve", dst, srcv, [ps_b], [oTst_b])
            B.dma_out(oT_scr.rearrange("(k p) t -> p k t", p=128)[:, :, r0:r0 + 128], oTst, oTst_b, [oTst_b], writes=[oT_b])
        P.barrier()
        A.release(m0)

    def pre1_gdn():
        B.phase()
        m0 = A.mark()
        own = B.pbuf()
        CW = A.alloc("CW", [128, 5, 3072])
        for j in range(5):
            B.dma_in(CW[:, j, :], SP_["l1_conv"][j:j + 1, :].partition_broadcast(128), own, [own])
        dtb = A.alloc("dtb", [128, 16])
        negA = A.alloc("negA", [128, 16])
        B.dma_in(dtb, SP_["l1_dtb"].partition_broadcast(128), own, [own])
        B.dma_in(negA, SP_["l1_alog"].partition_broadcast(128), own, [own])
        actf(negA, negA, AF.Exp, [own], [own])
        tsm("dve", negA, negA, -1.0, [own], [own])
        Z = [[A.alloc("Z%d_%d" % (s_, j), [128, 1024]) for j in range(5)] for s_ in range(2)]
        Z_b = [[B.pbuf() for j in range(5)] for s_ in range(2)]
        tm = A.alloc("tm", [128, 4]); tm_b = B.pbuf()
        ab = A.alloc("abt", [128, 32]); ab_b = B.pbuf()
        sm16 = [A.alloc("sm16_%d" % i, [128, 16]) for i in range(5)]
        sm_b = Buf("sm16")
        acc = A.alloc("acc", [128, 1024]); acc_b = Buf("acc")
        tmpc = A.alloc("tmpc", [128, 1024]); tmpc_b = Buf("tmpc")
        part_t = [A.alloc("part%d" % i, [128, 1024]) for i in range(3)]
        part_b = [B.pbuf() for i in range(3)]
        sq = A.alloc("sqg", [128, 1024]); sq_b = Buf("sqg")
        ssq = A.alloc("ssq", [128, 8]); ssq_b = Buf("ssq")
        ssk = A.alloc("ssk", [128, 8]); ssk_b = Buf("ssk")
        outs = {nm: (A.alloc("o_" + nm, [128, 1024]), B.pbuf()) for nm in ("la_f", "la_b", "k_f", "k_b", "b_f", "b_b")}
        la, beta, ela, nbe, xsm = sm16
        for tb in range(T // 128):
            r0 = tb * 128
            B.dma_in(tm, tmask_in[r0:r0 + 128, :], tm_b, [tm_b])
            B.dma_in(ab, SC["ab"][0][r0:r0 + 128, :], ab_b, [ab_b], reads=[SC["ab"][1]])
            tt("dve", xsm, ab[:, 0:16], dtb, ALU.add, [ab_b, own], [sm_b])
            actf(xsm, xsm, AF.Exp, [sm_b], [sm_b])
            actf(xsm, xsm, AF.Ln, [sm_b], [sm_b], bias=1.0)
            tt("dve", la, xsm, negA, ALU.mult, [sm_b, own], [sm_b])
            actf(beta, ab[:, 16:32], AF.Sigmoid, [ab_b], [sm_b])
            actf(ela, la, AF.Exp, [sm_b], [sm_b])
            stt("dve", nbe, beta, -1.0, ela, ALU.mult, ALU.mult, [sm_b], [sm_b])
            for part in range(3):
                s_ = (tb * 3 + part) % 2
                for j in range(5):
                    B.dma_in(Z[s_][j], SC["zq"][0][r0 + j:r0 + j + 128, part * 1024:(part + 1) * 1024], Z_b[s_][j], [Z_b[s_][j]],
                             reads=[SC["zq"][1]])
                pc = slice(part * 1024, (part + 1) * 1024)
                tt("dve", acc, Z[s_][2], CW[:, 2, pc], ALU.mult, [Z_b[s_][2], own], [acc_b])
                for j, mi in ((0, 0), (1, 1), (3, 2), (4, 3)):
                    stt("pool", tmpc, Z[s_][j], tm[:, mi:mi + 1], CW[:, j, pc], ALU.mult, ALU.mult, [Z_b[s_][j], tm_b, own], [tmpc_b])
                    tt("dve", acc, acc, tmpc, ALU.add, [acc_b, tmpc_b], [acc_b])
                actf(part_t[part], acc, AF.Silu, [acc_b], [part_b[part]])
            qt, kt, vt = part_t
            q3 = qt.rearrange("p (h c) -> p h c", h=8)
            k3 = kt.rearrange("p (h c) -> p h c", h=8)
            tt("pool", sq, qt, qt, ALU.mult, [part_b[0]], [sq_b])
            red("dve", ssq, sq.rearrange("p (h c) -> p h c", h=8), [sq_b], [ssq_b])
            rstd_small(ssq, ssq_b, 1.0, EPS)
            tsm("dve", ssq, ssq, QS, [ssq_b], [ssq_b])
            tt("dve", q3, q3, ssq.unsqueeze(2).to_broadcast([128, 8, 128]), ALU.mult, [part_b[0], ssq_b], [part_b[0]])
            tt("pool", sq, kt, kt, ALU.mult, [part_b[1]], [sq_b])
            red("dve", ssk, sq.rearrange("p (h c) -> p h c", h=8), [sq_b], [ssk_b])
            rstd_small(ssk, ssk_b, 1.0, EPS)
            tt("dve", k3, k3, ssk.unsqueeze(2).to_broadcast([128, 8, 128]), ALU.mult, [part_b[1], ssk_b], [part_b[1]])
            B.dma_out(SC["gdn_q"][0][r0:r0 + 128, :], qt, part_b[0], [part_b[0]], writes=[SC["gdn_q"][1]])
            B.dma_out(SC["gdn_a"][0][r0:r0 + 128, :], kt, part_b[1], [part_b[1]], writes=[SC["gdn_a"][1]])
            B.dma_out(SC["gdn_v"][0][r0:r0 + 128, :], vt, part_b[2], [part_b[2]], writes=[SC["gdn_v"][1]])
            for di, dn in enumerate(("f", "b")):
                hs = slice(di * 8, (di + 1) * 8)
                t_la, b_la = outs["la_" + dn]
                t_k, b_k = outs["k_" + dn]
                t_b, b_b = outs["b_" + dn]
                tcopy("pool", t_la.rearrange("p (h c) -> p h c", h=8), la[:, hs].unsqueeze(2).to_broadcast([128, 8, 128]), [sm_b], [b_la])
                tt("dve", t_k.rearrange("p (h c) -> p h c", h=8), k3, beta[:, hs].unsqueeze(2).to_broadcast([128, 8, 128]), ALU.mult,
                   [part_b[1], sm_b], [b_k])
                tt("pool", t_b.rearrange("p (h c) -> p h c", h=8), k3, nbe[:, hs].unsqueeze(2).to_broadcast([128, 8, 128]), ALU.mult,
                   [part_b[1], sm_b], [b_b])
                B.dma_out(SC["gdn_la_" + dn][0][r0:r0 + 128, :], t_la, b_la, [b_la], writes=[SC["gdn_la_" + dn][1]])
                B.dma_out(SC["gdn_k_" + dn][0][r0:r0 + 128, :], t_k, b_k, [b_k], writes=[SC["gdn_k_" + dn][1]])
                B.dma_out(SC["gdn_b_" + dn][0][r0:r0 + 128, :], t_b, b_b, [b_b], writes=[SC["gdn_b_" + dn][1]])
        P.barrier()
        A.release(m0)

    def pre1_rwkv():
        B.phase()
        m0 = A.mark()
        own = B.pbuf()
        MU = A.alloc("MU", [128, 3456])
        B.dma_in(MU, SP_["l1_mu"].partition_broadcast(128), own, [own])
        KKw = A.alloc("KKw", [128, 1024]); KAw = A.alloc("KAw", [128, 1024]); RKw = A.alloc("RKw", [128, 1024])
        B.dma_in(KKw, SP_["l1_kk"].partition_broadcast(128), own, [own])
        B.dma_in(KAw, SP_["l1_ka"].partition_broadcast(128), own, [own])
        B.dma_in(RKw, SP_["l1_rk"].partition_broadcast(128), own, [own])
        w0 = A.alloc("w0", [1, 2, 1024]); a0 = A.alloc("a0", [1, 2, 1024])
        B.dma_in(w0, SP_["l1_w0"].rearrange("k (d c) -> k d c", d=2), own, [own])
        B.dma_in(a0, SP_["l1_a0"].rearrange("k (d c) -> k d c", d=2), own, [own])
        w2 = A.alloc("w2", [64, 2, 1024]); a2 = A.alloc("a2", [64, 2, 1024]); g2 = A.alloc("g2", [128, 1024])
        B.dma_in(w2, SP_["l1_w2"].rearrange("k (d c) -> k d c", d=2), own, [own])
        B.dma_in(a2, SP_["l1_a2"].rearrange("k (d c) -> k d c", d=2), own, [own])
        B.dma_in(g2, SP_["l1_g2"], own, [own])
        Zs = [A.alloc("Zs%d" % j, [128, 3456]) for j in range(3)]
        Zs_b = [B.pbuf() for j in range(3)]
        zr = A.alloc("zrt", [128, 3456]); zr_b = B.pbuf()
        tm = A.alloc("tm", [128, 4]); tm_b = B.pbuf()
        th = A.alloc("th", [128, 256]); th_b = Buf("th")
        sgd = A.alloc("sgd", [128, 128]); sgd_b = Buf("sgd")
        LT = A.alloc("LT", [64, 4, 128]); LT_b = Buf("LT")
        sgT = A.alloc("sgT", [128, 128]); sgT_b = Buf("sgT")
        kk = A.alloc("kkt", [128, 1024]); kk_b = Buf("kkt")
        na = A.alloc("nat", [128, 1024]); na_b = B.pbuf()
        sq = A.alloc("sqr", [128, 1024]); sq_b = Buf("sqr")
        ss = A.alloc("ssr", [128, 16]); ss_b = Buf("ssr")
        bs = A.alloc("bsr", [128, 16]); bs_b = Buf("bsr")
        gate = A.alloc("gatet", [128, 1024]); gate_b = B.pbuf()
        bonus = A.alloc("bonust", [128, 1024]); bonus_b = B.pbuf()
        Ad = A.alloc("Adt", [128, 1024]); Ad_b = Buf("Adt")
        u = A.alloc("ut", [128, 1024]); u_b = Buf("ut")
        outs = {nm: (A.alloc("o_" + nm, [128, 1024]), B.pbuf()) for nm in ("la_f", "la_b", "k_f", "k_b", "b_f", "b_b")}
        for tb in range(T // 128):
            r0 = tb * 128
            B.dma_in(tm, tmask_in[r0:r0 + 128, :], tm_b, [tm_b])
            for j in range(3):
                B.dma_in(Zs[j], SC["zr"][0][r0 + 1 + j:r0 + 1 + j + 128, :], Zs_b[j], [Zs_b[j]], reads=[SC["zr"][1]])
            tsm("pool", zr, Zs[0], tm[:, 1:2], [Zs_b[0], tm_b], [zr_b])
            stt("dve", zr, Zs[2], tm[:, 2:3], zr, ALU.mult, ALU.add, [Zs_b[2], tm_b, zr_b], [zr_b])
            stt("pool", zr, zr, 0.5, Zs[1], ALU.mult, ALU.subtract, [zr_b, Zs_b[1]], [zr_b])
            tt("dve", zr, zr, MU, ALU.mult, [zr_b, own], [zr_b])
            tt("pool", zr, zr, Zs[1], ALU.add, [zr_b, Zs_b[1]], [zr_b])
            r_ = zr[:, 0:1024]; kr = zr[:, 1024:2048]; vr = zr[:, 2048:3072]
            rws = dbg.get("rw_stop", 99)
            if rws <= 1:
                break
            actf(th[:, 0:128], zr[:, 3072:3200], AF.Tanh, [zr_b], [th_b])
            tcopy("dve", th[:, 128:256], zr[:, 3200:3328], [zr_b], [th_b])
            actf(sgd, zr[:, 3328:3456], AF.Sigmoid, [zr_b], [sgd_b])
            ps, ps_b = B.banks[0], B.bank_b[0]

            def mmt(e, ps=ps):
                for q in range(4):
                    e.transpose(ps[0:64, q * 128:(q + 1) * 128], th[:, q * 64:(q + 1) * 64], ident)
                return e.transpose(B.banks[1][:, 0:128], sgd, ident)
            P.c("pe", mmt, reads=[th_b, sgd_b, consts_b], writes=[ps_b, B.bank_b[1]])
            tcopy("dve", LT, ps[0:64, :].rearrange("p (a b) -> p a b", a=4), [ps_b], [LT_b])
            actf(sgT, B.banks[1][:, 0:128], AF.Copy, [B.bank_b[1]], [sgT_b])
            if rws <= 2:
                break
            tt("dve", kk, kr, KKw, ALU.mult, [zr_b, own], [kk_b])
            tt("pool", sq, kk, kk, ALU.mult, [kk_b], [sq_b])
            red("dve", ss, sq.rearrange("p (h c) -> p h c", h=16), [sq_b], [ss_b])
            rstd_small(ss, ss_b, 1.0, EPS)
            kk3 = kk.rearrange("p (h c) -> p h c", h=16)
            tt("dve", kk3, kk3, ss.unsqueeze(2).to_broadcast([128, 16, 64]), ALU.mult, [kk_b, ss_b], [kk_b])
            tsm("pool", na, kk, -1.0, [kk_b], [na_b])
            B.dma_out(SC["rw_a"][0][r0:r0 + 128, :], na, na_b, [na_b], writes=[SC["rw_a"][1]])
            B.dma_out(SC["rw_q"][0][r0:r0 + 128, :], r_, zr_b, [zr_b], writes=[SC["rw_q"][1]])
            B.dma_out(SC["rw_v"][0][r0:r0 + 128, :], vr, zr_b, [zr_b], writes=[SC["rw_v"][1]])
            if rws <= 3:
                break
            for half in range(2):
                pg, pg_b = B.banks[2 + half], B.bank_b[2 + half]
                P.c("pe", lambda e, half=half, pg=pg: e.matmul(pg[:, :], sgT, g2[:, half * 512:(half + 1) * 512], start=True, stop=True),
                    reads=[sgT_b, own], writes=[pg_b])
                actf(gate[:, half * 512:(half + 1) * 512], pg[:, :], AF.Copy, [pg_b], [gate_b])
            B.dma_out(SC["rw_gate"][0][r0:r0 + 128, :], gate, gate_b, [gate_b], writes=[SC["rw_gate"][1]])
            if rws <= 4:
                break
            for di, dn in enumerate(("f", "b")):
                t_la, b_la = outs["la_" + dn]
                t_k, b_k = outs["k_" + dn]
                t_b, b_b = outs["b_" + dn]
                for half in range(2):
                    hc = slice(half * 512, (half + 1) * 512)
                    pw, pw_b = B.banks[4 + half], B.bank_b[4 + half]

                    def mmw(e, di=di, hc=hc, pw=pw):
                        e.matmul(pw[:, :], LT[0:64, di, :], w2[0:64, di, hc], start=True, stop=False)
                        return e.matmul(pw[:, :], ones_f[0:1, 0:128], w0[0:1, di, hc], start=False, stop=True)
                    P.c("pe", mmw, reads=[LT_b, own, ones_f_b], writes=[pw_b])
                    actf(t_la[:, hc], pw[:, :], AF.Exp, [pw_b], [b_la], scale=-1.0)
                    pa, pa_b = B.banks[6 + half], B.bank_b[6 + half]

                    def mma(e, di=di, hc=hc, pa=pa):
                        e.matmul(pa[:, :], LT[0:64, 2 + di, :], a2[0:64, di, hc], start=True, stop=False)
                        return e.matmul(pa[:, :], ones_f[0:1, 0:128], a0[0:1, di, hc], start=False, stop=True)
                    P.c("pe", mma, reads=[LT_b, own, ones_f_b], writes=[pa_b])
                    actf(Ad[:, hc], pa[:, :], AF.Sigmoid, [pa_b], [Ad_b])
                actf(t_la, t_la, AF.Ln, [b_la], [b_la], bias=1.0)
                actf(t_la, t_la, AF.Exp, [b_la], [b_la], scale=-1.0, bias=-0.5)
                tsm("dve", t_la, t_la, -1.0, [b_la], [b_la])
                B.dma_out(SC["rw_la_" + dn][0][r0:r0 + 128, :], t_la, b_la, [b_la], writes=[SC["rw_la_" + dn][1]])
                stt("dve", u, Ad, -1.0, KAw, ALU.add, ALU.mult, [Ad_b, own], [u_b])
                tt("pool", u, u, kr, ALU.mult, [u_b, zr_b], [u_b])
                tt("dve", t_k, u, kr, ALU.add, [u_b, zr_b], [b_k])
                B.dma_out(SC["rw_k_" + dn][0][r0:r0 + 128, :], t_k, b_k, [b_k], writes=[SC["rw_k_" + dn][1]])
                tt("pool", t_b, kk, Ad, ALU.mult, [kk_b, Ad_b], [b_b])
                B.dma_out(SC["rw_b_" + dn][0][r0:r0 + 128, :], t_b, b_b, [b_b], writes=[SC["rw_b_" + dn][1]])
                tt("dve", u, r_, t_k, ALU.mult, [zr_b, b_k], [u_b])
                tt("pool", u, u, RKw, ALU.mult, [u_b, own], [u_b])
                red("dve", bs, u.rearrange("p (h c) -> p h c", h=16), [u_b], [bs_b])
                v3 = vr.rearrange("p (h c) -> p h c", h=16)
                bsB = bs.unsqueeze(2).to_broadcast([128, 16, 64])
                if di == 0:
                    tt("dve", bonus.rearrange("p (h c) -> p h c", h=16), v3, bsB, ALU.mult, [zr_b, bs_b], [bonus_b])
                else:
                    tt("dve", u.rearrange("p (h c) -> p h c", h=16), v3, bsB, ALU.mult, [zr_b, bs_b], [u_b])
                    tt("pool", bonus, bonus, u, ALU.add, [bonus_b, u_b], [bonus_b])
            B.dma_out(SC["rw_bonus"][0][r0:r0 + 128, :], bonus, bonus_b, [bonus_b], writes=[SC["rw_bonus"][1]])
            if rws <= 5:
                break
        P.barrier()
        A.release(m0)

    def scans(l):
        specs = []
        if l == 0:
            for di, dn in enumerate(("f", "b")):
                specs.append(("gla", 4, 128, 256, False, dn, di, {"q": "gla_q", "k": "gla_k", "v": "gla_v", "la": "gla_la_" + dn}, "OA_" + dn))
                specs.append(("ret", 4, 128, 256, False, dn, di, {"q": "ret_q", "k": "ret_k", "v": "ret_v", "la": "ret_la_" + dn}, "OB_" + dn))
        else:
            for di, dn in enumerate(("f", "b")):
                specs.append(("gdn", 8, 128, 128, True, dn, di, {"q": "gdn_q", "k": "gdn_k_" + dn, "v": "gdn_v", "la": "gdn_la_" + dn,
                                                                 "a": "gdn_a", "b": "gdn_b_" + dn}, "OA_" + dn))
                specs.append(("rwkv", 16, 64, 64, True, dn, di, {"q": "rw_q", "k": "rw_k_" + dn, "v": "rw_v", "la": "rw_la_" + dn,
                                                                 "a": "rw_a", "b": "rw_b_" + dn}, "OB_" + dn))
        for mixer in sorted(set(sp_[0] for sp_ in specs)):
            m0 = A.mark()
            P.begin_streams(2)
            for nm, H, dk, dv, lowrank, dn, di, srcn, onm in specs:
                if nm != mixer:
                    continue
                P.set_stream(di)
                src = {k_: SC[v_] for k_, v_ in srcn.items()}
                scan_pass(B, nm + dn, H, dk, dv, lowrank, dn, src, s0_in[nm][di], SC[onm][0], SC[onm][1], st_out[nm][di],
                          flags, flags_b, consts, consts_b, scalar_decay=(nm == "gdn"), slot=di,
                          pbanks=(4 * di, 4 * di + 1, 4 * di + 2, 4 * di + 3), standalone=False)
            P.end_streams()
            P.barrier()
            A.release(m0)

    nl = dbg.get("nlayers", 2)
    for l in range(2):
        phase_mods(l)
    m0 = A.mark()
    zt = A.alloc("zerot", [2, 3456])
    zt_b = B.buf("zerot")
    P.c("pool", lambda e: e.memset(zt, 0.0), writes=[zt_b])
    for nm, w_ in (("zq", 3072), ("zr", 3456)):
        B.dma_out(SC[nm][0][0:2, :], zt[:, 0:w_], zt_b, [zt_b], writes=[SC[nm][1]])
        B.dma_out(SC[nm][0][T + 2:T + 4, :], zt[:, 0:w_], zt_b, [zt_b], writes=[SC[nm][1]])
    P.barrier()
    A.release(m0)

    ntiles = dbg.get("ntiles", NT)
    stop = dbg.get("stop", 99)

    def finish():
        P.emit(final_wait_ops=B.stores)
        return B
    if stop <= 0:
        return finish()
    B.phase()
    m0 = A.mark()
    tc = tile_alloc()
    for t in range(ntiles):
        x_load(tc, xT_in, t)
        modulate(tc, 0, 0)
        ffn(tc, 0, 1, 0)
        modulate(tc, 0, 1)
        proj(tc, 0, t, PLAN0, 32)
        x_store(tc, xs_scr, t, dst_bufs=[xs_tile_b[t]])
    P.barrier()
    A.release(m0)
    if stop <= 1:
        return finish()
    scans(0)
    if stop <= 2:
        return finish()
    post_phase(0)
    if stop <= 3:
        return finish()
    B.phase()
    m0 = A.mark()
    tc = tile_alloc(False)
    for t in range(ntiles):
        x_load(tc, xs_scr, t, src_bufs=[xs_tile_b[t]])
        wout(tc, 0, t)
        modulate(tc, 0, 2)
        ffn(tc, 0, 2, 2)
        modulate(tc, 1, 0)
        ffn(tc, 1, 1, 0)
        modulate(tc, 1, 1)
        proj(tc, 1, t, PLAN1, 416)
        x_store(tc, xs_scr, t, dst_bufs=[xs_tile_b[t]])
    P.barrier()
    A.release(m0)
    if stop <= 4:
        return finish()
    pre1_gdn()
    if stop <= 5:
        return finish()
    pre1_rwkv()
    if stop <= 6:
        return finish()
    scans(1)
    if stop <= 7:
        return finish()
    post_phase(1)
    if stop <= 8:
        return finish()
    B.phase()
    m0 = A.mark()
    tc = tile_alloc(False)
    for t in range(ntiles):
        x_load(tc, xs_scr, t, src_bufs=[xs_tile_b[t]])
        wout(tc, 1, t)
        modulate(tc, 1, 2)
        ffn(tc, 1, 2, 2)
        final_norm(tc)
        x_store(tc, yT_out, t, final=True)
    A.release(m0)
    P.emit(final_wait_ops=B.stores)
    return B


def fm_vec(v):
    v = np.asarray(v, np.float32)
    return np.ascontiguousarray(v.reshape(-1, 128).T)


def row(v):
    return np.ascontiguousarray(np.asarray(v, np.float32).reshape(1, -1))


def host_weights(inp):
    f32 = lambda a: np.ascontiguousarray(np.asarray(a), dtype=np.float32)
    Wd = {}
    for l in range(2):
        p = "l%d_" % l
        Wd[p + "w_mod"] = f32(inp[p + "w_mod"])
        Wd[p + "b_mod"] = fm_vec(inp[p + "b_mod"])
        Wd[p + "norms"] = np.ascontiguousarray(np.concatenate([fm_vec(inp[p + "norm%d" % i]) for i in (1, 2, 3)], axis=1))
        for f in ("ffn1", "ffn2"):
            for w in ("wg", "wu", "wd"):
                Wd[p + f + "_" + w] = f32(inp[p + f + "_" + w])
        Wd[p + "w_out"] = f32(inp[p + "w_out"])
    perm0 = np.concatenate([np.arange(0, 3072), np.arange(3104, 6176), np.arange(3072, 3104)])
    perm1 = np.concatenate([np.arange(0, 4096), np.arange(4128, 7584), np.arange(4096, 4128)])
    Wd["l0_w_in"] = np.ascontiguousarray(np.asarray(inp["l0_w_in"], np.float32)[:, perm0])
    Wd["l1_w_in"] = np.ascontiguousarray(np.asarray(inp["l1_w_in"], np.float32)[:, perm1])
    Wd["final_norm"] = fm_vec(inp["final_norm"])
    Wd["consts"] = make_consts()
    Wd["l0_gk_up"] = np.ascontiguousarray(np.concatenate([f32(inp["l0_gla_gk_up_fwd"]), f32(inp["l0_gla_gk_up_bwd"])], axis=1))
    Wd["l0_gk_b"] = np.ascontiguousarray(np.concatenate([row(inp["l0_gla_gk_b_fwd"]), row(inp["l0_gla_gk_b_bwd"])], axis=1))
    Wd["l0_gla_norm"] = row(inp["l0_gla_norm"])
    Wd["l0_ret_norm"] = row(inp["l0_ret_norm"])
    for nm, e0 in (("ret_la_f", 5.0), ("ret_la_b", 5.5)):
        h = np.arange(4, dtype=np.float32)
        lg = np.log1p(-np.power(np.float32(2.0), -(np.float32(e0) + h))).astype(np.float32)
        Wd[nm] = np.ascontiguousarray(np.broadcast_to(np.repeat(lg, 128)[None, :], (T, 512)).astype(np.float32))
    Wd["l1_conv"] = f32(inp["l1_gdn_conv"])
    Wd["l1_dtb"] = np.ascontiguousarray(np.concatenate([row(inp["l1_gdn_dt_bias_fwd"]), row(inp["l1_gdn_dt_bias_bwd"])], axis=1))
    Wd["l1_alog"] = np.ascontiguousarray(np.concatenate([row(inp["l1_gdn_A_log_fwd"]), row(inp["l1_gdn_A_log_bwd"])], axis=1))
    Wd["l1_gdn_norm"] = row(inp["l1_gdn_norm"])
    Wd["l1_mu"] = row(inp["l1_rwkv_mu"])
    Wd["l1_w0"] = np.ascontiguousarray(np.concatenate([row(inp["l1_rwkv_w0_fwd"]), row(inp["l1_rwkv_w0_bwd"])], axis=1))
    Wd["l1_a0"] = np.ascontiguousarray(np.concatenate([row(inp["l1_rwkv_a0_fwd"]), row(inp["l1_rwkv_a0_bwd"])], axis=1))
    Wd["l1_w2"] = np.ascontiguousarray(np.concatenate([f32(inp["l1_rwkv_w2_fwd"]), f32(inp["l1_rwkv_w2_bwd"])], axis=1))
    Wd["l1_a2"] = np.ascontiguousarray(np.concatenate([f32(inp["l1_rwkv_a2_fwd"]), f32(inp["l1_rwkv_a2_bwd"])], axis=1))
    Wd["l1_g2"] = f32(inp["l1_rwkv_g2"])
    Wd["l1_kk"] = row(inp["l1_rwkv_k_k"])
    Wd["l1_ka"] = row(inp["l1_rwkv_k_a"])
    Wd["l1_rk"] = row(inp["l1_rwkv_r_k"])
    Wd["l1_lnw"] = row(inp["l1_rwkv_ln_w"])
    Wd["l1_lnb"] = row(inp["l1_rwkv_ln_b"])
    return Wd


INPUT_NAMES = (
    "x_prompt", "x_sample", "c", "c_ctx",
    "state_l0_gla_fwd", "state_l0_gla_bwd", "state_l0_ret_fwd", "state_l0_ret_bwd",
    "state_l1_gdn_fwd", "state_l1_gdn_bwd", "state_l1_rwkv_fwd", "state_l1_rwkv_bwd",
    "l0_w_mod", "l0_b_mod", "l0_norm1", "l0_norm2", "l0_norm3",
    "l0_ffn1_wg", "l0_ffn1_wu", "l0_ffn1_wd", "l0_ffn2_wg", "l0_ffn2_wu", "l0_ffn2_wd", "l0_w_in", "l0_w_out",
    "l0_gla_gk_up_fwd", "l0_gla_gk_b_fwd", "l0_gla_gk_up_bwd", "l0_gla_gk_b_bwd", "l0_gla_norm", "l0_ret_norm",
    "l1_w_mod", "l1_b_mod", "l1_norm1", "l1_norm2", "l1_norm3",
    "l1_ffn1_wg", "l1_ffn1_wu", "l1_ffn1_wd", "l1_ffn2_wg", "l1_ffn2_wu", "l1_ffn2_wd", "l1_w_in", "l1_w_out",
    "l1_gdn_conv", "l1_gdn_A_log_fwd", "l1_gdn_dt_bias_fwd", "l1_gdn_A_log_bwd", "l1_gdn_dt_bias_bwd", "l1_gdn_norm",
    "l1_rwkv_mu", "l1_rwkv_w0_fwd", "l1_rwkv_w2_fwd", "l1_rwkv_a0_fwd", "l1_rwkv_a2_fwd",
    "l1_rwkv_w0_bwd", "l1_rwkv_w2_bwd", "l1_rwkv_a0_bwd", "l1_rwkv_a2_bwd",
    "l1_rwkv_g2", "l1_rwkv_k_k", "l1_rwkv_k_a", "l1_rwkv_r_k", "l1_rwkv_ln_w", "l1_rwkv_ln_b", "final_norm")

ST_NAMES = (("gla", "l0_gla", 4, 128, 256), ("ret", "l0_ret", 4, 128, 256), ("gdn", "l1_gdn", 8, 128, 128), ("rwkv", "l1_rwkv", 16, 64, 64))


def core_inputs(inp, core):
    m = {}
    sample = core < 4
    tpos = np.arange(T)
    if sample:
        x = np.asarray(inp["x_sample"][core], np.float32)
        cond = np.asarray(inp["c"][core], np.float32)
        pos = tpos
        seglen = T
    else:
        x = np.zeros((T, D), np.float32)
        for s in range(4):
            x[s * SEG:(s + 1) * SEG] = np.asarray(inp["x_prompt"][4 * (core - 4) + s], np.float32)
        cond = np.asarray(inp["c_ctx"], np.float32)
        pos = tpos % SEG
        seglen = SEG
    m["xT"] = np.ascontiguousarray(x.T)
    m["cond"] = fm_vec(cond)
    fl = np.zeros((128, 2), np.float32)
    fl[:, 0] = 1.0 if sample else 0.0
    m["flags"] = fl
    tm = np.zeros((T, 4), np.float32)
    for i, sft in enumerate((-2, -1, 1, 2)):
        tm[:, i] = ((pos + sft >= 0) & (pos + sft < seglen)).astype(np.float32)
    m["tmask"] = tm
    rot = np.zeros((T, 128), np.float32)
    if sample:
        inv = (np.float32(10000.0) ** (-np.arange(32, dtype=np.float32) / np.float32(32))).astype(np.float32)
        rowp = (tpos // 64).astype(np.float32)
        colp = (tpos % 64).astype(np.float32)
        ang = np.concatenate([rowp[:, None] * inv[None, :], colp[:, None] * inv[None, :]], axis=1).astype(np.float32)
        rot[:, 0:64] = np.cos(ang)
        rot[:, 64:128] = np.sin(ang)
    else:
        rot[:, 0:64] = 1.0
    m["rot"] = rot
    for nm, key, H, dk, dv in ST_NAMES:
        s0 = np.zeros((2, dk, H * dv), np.float32)
        if sample:
            for di, dn in enumerate(("fwd", "bwd")):
                st = np.asarray(inp["state_%s_%s" % (key, dn)][core], np.float32)
                s0[di] = st.transpose(1, 0, 2).reshape(dk, H * dv)
        m["s0_" + nm] = s0
    return m


_CACHE = {}


def kernel(**inputs):
    if "prog" not in _CACHE:
        _CACHE["prog"] = build_program()
    Bd = _CACHE["prog"]
    Wd = host_weights(inputs)
    in_maps = []
    for core in range(8):
        m = dict(Wd)
        m.update(core_inputs(inputs, core))
        in_maps.append({k: m[k] for k in Bd.inp})
    res = run_bass_kernel_spmd(Bd.nc, in_maps, core_ids=list(range(8)))
    r = res.results
    y_prompt = np.zeros((16, SEG, D), np.float32)
    y_sample = np.zeros((4, T, D), np.float32)
    for core in range(4):
        y_sample[core] = np.asarray(r[core]["yT"]).T
    for core in range(4, 8):
        yt = np.asarray(r[core]["yT"]).T
        for s in range(4):
            y_prompt[4 * (core - 4) + s] = yt[s * SEG:(s + 1) * SEG]
    outs = [y_prompt, y_sample]
    for nm, key, H, dk, dv in ST_NAMES:
        for di in range(2):
            st = np.zeros((16, H, dk, dv), np.float32)
            for core in range(4, 8):
                so = np.asarray(r[core]["st_" + nm])
                for s in range(4):
                    st[4 * (core - 4) + s] = so[di, s].reshape(dk, H, dv).transpose(1, 0, 2)
            outs.append(st)
    return tuple(outs)
```

```python
import contextlib
import numpy as np
import concourse.bass as bass
import concourse.mybir as mybir
from concourse.bass_utils import run_bass_kernel_spmd

F32 = mybir.dt.float32
BF16 = mybir.dt.bfloat16
U8 = mybir.dt.uint8
ALU = mybir.AluOpType
AF = mybir.ActivationFunctionType
AX = mybir.AxisListType

D = 2048
NCH = 16
T = 2048
TT = 512
NT = T // TT
DFF = 5632
NF = DFF // 128
C = 64
NCHUNK = T // C
SEG = 256
NSEG = T // SEG
EPS = 1e-6
L0_IN = 6176
L1_IN = 7584


class Buf:
    __slots__ = ("name", "last_w", "readers", "last_dma", "dma_sem", "dma_cnt")

    def __init__(self, name):
        self.name = name
        self.last_w = None
        self.readers = []
        self.last_dma = None
        self.dma_sem = None
        self.dma_cnt = 0


class Op:
    __slots__ = ("eng", "fn", "deps", "is_dma", "sem", "val", "signal")

    def __init__(self, eng, fn, is_dma):
        self.eng = eng
        self.fn = fn
        self.deps = []
        self.is_dma = is_dma
        self.sem = None
        self.val = 0
        self.signal = is_dma


class Prog:
    ENGS = ("pe", "act", "dve", "pool", "sp")

    def __init__(self, nc):
        self.nc = nc
        self.ops = {e: [] for e in self.ENGS}
        self.nops = 0
        self.dma_bufs = []
        self.barrier_deps = {e: [] for e in self.ENGS}
        self.all_bufs = []
        self.streams = None
        self.cur_stream = None
        self.stream_bdeps = None

    def begin_streams(self, n):
        self.streams = [[] for _ in range(n)]
        self.stream_bdeps = [{e: list(self.barrier_deps[e]) for e in self.ENGS} for _ in range(n)]
        self.barrier_deps = {e: [] for e in self.ENGS}

    def set_stream(self, k):
        self.cur_stream = k

    def end_streams(self):
        lists = self.streams
        n = max(len(l) for l in lists)
        for i in range(n):
            for l in lists:
                if i < len(l):
                    self.ops[l[i].eng].append(l[i])
        self.streams = None
        self.cur_stream = None
        self.stream_bdeps = None

    def _append(self, op):
        if self.cur_stream is not None:
            self.streams[self.cur_stream].append(op)
        else:
            self.ops[op.eng].append(op)
        self.nops += 1

    def buf(self, name):
        b = Buf(name)
        return b

    def _track(self, op, reads, writes):
        deps = op.deps
        for b in reads:
            if b.last_w is not None:
                deps.append(b.last_w)
            b.readers.append(op)
        for b in writes:
            if b.last_w is not None:
                deps.append(b.last_w)
            deps.extend(b.readers)
            b.last_w = op
            b.readers = []
        bdd = self.barrier_deps if self.cur_stream is None else self.stream_bdeps[self.cur_stream]
        bd = bdd[op.eng]
        if bd:
            deps.extend(bd)
            bdd[op.eng] = []

    def c(self, eng, fn, reads=(), writes=()):
        op = Op(eng, fn, False)
        self._track(op, reads, writes)
        self._append(op)
        return op

    def dma(self, eng, out_ap, in_ap, sbuf, reads=(), writes=()):
        def fn(e, out_ap=out_ap, in_ap=in_ap):
            return e.dma_start(out=out_ap, in_=in_ap)
        op = Op(eng, fn, True)
        self._track(op, reads, writes)
        if sbuf.last_dma is not None:
            op.deps.append(sbuf.last_dma)
        sbuf.last_dma = op
        if sbuf.dma_sem is None:
            sbuf.dma_sem = "pending"
            self.dma_bufs.append(sbuf)
        sbuf.dma_cnt += 1
        op.sem = sbuf
        op.val = 16 * sbuf.dma_cnt
        self._append(op)
        return op

    def barrier(self):
        lasts = []
        for e in self.ENGS:
            for op in reversed(self.ops[e]):
                if not op.is_dma:
                    lasts.append(op)
                    break
        for b in self.dma_bufs:
            if b.last_dma is not None:
                lasts.append(b.last_dma)
        for e in self.ENGS:
            self.barrier_deps[e] = list(lasts)

    def emit(self, final_wait_ops=()):
        nc = self.nc
        for e in self.ENGS:
            for op in self.ops[e]:
                for d in op.deps:
                    if d is not op:
                        d.signal = True
        for op in final_wait_ops:
            op.signal = True
        with contextlib.ExitStack() as st:
            esem = {}
            for e in ("pe", "act", "dve", "pool"):
                esem[e] = st.enter_context(nc.semaphore("s_" + e))
            for i, b in enumerate(self.dma_bufs):
                b.dma_sem = st.enter_context(nc.semaphore("d%d_%s" % (i, b.name)))
            for e in self.ENGS:
                k = 0
                for op in self.ops[e]:
                    if op.is_dma:
                        op.sem = op.sem.dma_sem
                    elif op.signal:
                        k += 1
                        op.sem = esem[e]
                        op.val = k
            block = st.enter_context(nc.Block())

            def run(e, eng):
                waited = {}
                for op in self.ops[e]:
                    need = {}
                    for d in op.deps:
                        if d is op or not d.signal:
                            continue
                        s = d.sem
                        if waited.get(s.num, 0) >= d.val:
                            continue
                        if need.get(s.num, (None, 0))[1] < d.val:
                            need[s.num] = (s, d.val)
                    for num, (s, v) in need.items():
                        eng.wait_ge(s, v)
                        waited[num] = v
                    ins = op.fn(eng)
                    if op.signal:
                        ins.then_inc(op.sem, 16 if op.is_dma else 1)
                if e == "sp":
                    for op in final_wait_ops:
                        if waited.get(op.sem.num, 0) < op.val:
                            eng.wait_ge(op.sem, op.val)
                            waited[op.sem.num] = op.val

            @block.tensor
            def _(eng):
                run("pe", eng)

            @block.scalar
            def _(eng):
                run("act", eng)

            @block.vector
            def _(eng):
                run("dve", eng)

            @block.gpsimd
            def _(eng):
                run("pool", eng)

            @block.sync
            def _(eng):
                run("sp", eng)


class Arena:
    def __init__(self, nc, nbytes):
        self.t = nc.alloc_sbuf_tensor("arena", [128, nbytes], U8)
        self.nbytes = nbytes
        self.off = 0
        self.peak = 0

    def mark(self):
        return self.off

    def release(self, m):
        self.off = m

    def alloc(self, name, shape, dt=F32, parts=None):
        esz = 4 if dt == F32 else 2
        n = int(np.prod(shape[1:])) * esz
        o = self.off
        self.off += (n + 63) // 64 * 64
        self.peak = max(self.peak, self.off)
        assert self.off <= self.nbytes, "SBUF arena overflow %s %d" % (name, self.off)
        ap = self.t[0:shape[0], o:o + n].bitcast(dt)
        if len(shape) == 3:
            ap = ap.rearrange("p (a b) -> p a b", a=shape[1])
        elif len(shape) == 4:
            ap = ap.rearrange("p (a b c) -> p a b c", a=shape[1], b=shape[2])
        return ap


class Slots:
    def __init__(self, arena, name, n, shape, dt):
        self.aps = [arena.alloc("%s%d" % (name, i), shape, dt) for i in range(n)]
        self.bufs = [Buf("%s%d" % (name, i)) for i in range(n)]
        self.i = 0
        self.n = n

    def next(self):
        i = self.i
        self.i = (i + 1) % self.n
        return self.aps[i], self.bufs[i]


class Builder:
    def __init__(self, dbg=None):
        self.dbg = dbg or {}
        self.nc = bass.Bass("TRN2", target_bir_lowering=False)
        self.P = Prog(self.nc)
        self.inp = {}
        self.out = {}
        self.stores = []
        self.arena = Arena(self.nc, 206 * 1024)
        nc = self.nc
        self.banks = [nc.alloc_psum_tensor("bank%d" % i, [128, 512], F32) for i in range(8)]
        self.bank_b = [Buf("bank%d" % i) for i in range(8)]
        self.scr = {}
        self.shared = {}
        self.pcount = 0

    def buf(self, key):
        if key not in self.shared:
            self.shared[key] = Buf(key)
        return self.shared[key]

    def phase(self):
        self.pcount = 0

    def pbuf(self):
        b = self.buf("ph%d" % self.pcount)
        self.pcount += 1
        return b

    def dma_in(self, dst_ap, src_ap, owner, writes, reads=(), eng="sp"):
        return self.P.dma(eng, dst_ap, src_ap, owner, reads=reads, writes=writes)

    def dma_out(self, dst_ap, src_ap, owner, reads, writes=(), eng="sp", final=False):
        op = self.P.dma(eng, dst_ap, src_ap, owner, reads=reads, writes=writes)
        if final:
            self.stores.append(op)
        return op

    def din(self, name, shape, dt=F32):
        ap = self.nc.dram_tensor(name, list(shape), dt, kind="ExternalInput").ap()
        self.inp[name] = ap
        return ap

    def dout(self, name, shape, dt=F32):
        ap = self.nc.dram_tensor(name, list(shape), dt, kind="ExternalOutput").ap()
        self.out[name] = ap
        return ap

    def dscr(self, name, shape, dt=F32):
        if name in self.dbg.get("dump", ()):
            ap = self.nc.dram_tensor(name, list(shape), dt, kind="ExternalOutput").ap()
            self.out[name] = ap
        else:
            ap = self.nc.dram_tensor(name, list(shape), dt).ap()
        self.scr[name] = (ap, Buf(name))
        return ap, self.scr[name][1]

    def load(self, dst_ap, src_ap, dst_buf, reads=(), eng="sp"):
        return self.P.dma(eng, dst_ap, src_ap, dst_buf, reads=reads, writes=[dst_buf])

    def store(self, dst_ap, src_ap, src_buf, writes=(), eng="sp", final=False):
        op = self.P.dma(eng, dst_ap, src_ap, src_buf, reads=[src_buf], writes=writes)
        if final:
            self.stores.append(op)
        return op


CONST_COLS = {}
SCAN_STOP = [99]


def make_consts():
    cols = []
    off = 0

    def add(name, arr):
        nonlocal off
        a = np.zeros((128, arr.shape[1]), np.float32)
        a[:arr.shape[0]] = arr
        cols.append(a)
        CONST_COLS[name] = (off, arr.shape[1])
        off += arr.shape[1]
    idx = np.arange(C)
    add("ident", np.eye(128, dtype=np.float32))
    for d in ("f", "b"):
        if d == "f":
            before = idx[:, None] <= idx[None, :]
            sbefore = idx[:, None] < idx[None, :]
        else:
            before = idx[:, None] >= idx[None, :]
            sbefore = idx[:, None] > idx[None, :]
        after = ~before
        U = before.astype(np.float32)
        Us = sbefore.astype(np.float32)
        add("UU_" + d, np.concatenate([U, Us], axis=1))
        add("UR_" + d, np.concatenate([after, after], axis=1).astype(np.float32))
        half = np.concatenate([U, Us], axis=1)
        add("MASK_" + d, np.concatenate([half, half], axis=0))
        add("MASKN_" + d, Us.T.copy())
        add("MASKI_" + d, U)
        add("MNEG_" + d, (np.concatenate([half, half], axis=0) - 1.0) * 30000.0)
    return np.concatenate(cols, axis=1)


def scan_pass(B, name, H, dk, dv, lowrank, d, src, s0_ap, o_dst, o_dst_b, st_out, flags, flags_b,
              consts, consts_b, nchunks=NCHUNK, chunks_per_seg=SEG // C, scalar_decay=False, slot=0, pbanks=None,
              standalone=True):
    nc, P, A = B.nc, B.P, B.arena
    m0 = A.mark()
    hg = 4 if lowrank else 2
    ngroups = H // hg
    W_ = 256 if lowrank else 128
    NP = 128 if lowrank else 64

    def cst(nm, rows, c0=0, c1=None):
        o, n = CONST_COLS[nm]
        c1 = n if c1 is None else c1
        return consts[0:rows, o + c0:o + c1]
    ident = cst("ident", 128)
    UU = cst("UU_" + d, 64)
    UR = cst("UR_" + d, 64) if lowrank else cst("UR_" + d, 64, 0, 64)
    MASK = cst("MASK_" + d, 128)
    MASKN = cst("MASKN_" + d, 64)
    MASKI = cst("MASKI_" + d, 64)
    last = C - 1 if d == "f" else 0

    nb = 2
    la_t = [A.alloc(name + "la%d" % i, [64, H * dk]) for i in range(nb)]
    q_t = [A.alloc(name + "q%d" % i, [64, H * dk]) for i in range(nb)]
    la_b = [B.buf("sc%d_" % slot + "la%d" % i) for i in range(nb)]
    q_b = [B.buf("sc%d_" % slot + "q%d" % i) for i in range(nb)]
    if lowrank:
        a_t = [A.alloc(name + "a%d" % i, [64, H * dk]) for i in range(nb)]
        a_b = [B.buf("sc%d_" % slot + "a%d" % i) for i in range(nb)]
        k0_t = [A.alloc(name + "k0%d" % i, [64, H * dk]) for i in range(nb)]
        k0_b = [B.buf("sc%d_" % slot + "k0%d" % i) for i in range(nb)]
    bk_t = [A.alloc(name + "bk%d" % i, [NP, H * dk]) for i in range(nb)]
    bk_b = [B.buf("sc%d_" % slot + "bk%d" % i) for i in range(nb)]
    vs_t = [A.alloc(name + "vs%d" % i, [NP, H * dv]) for i in range(nb)]
    vs_b = [B.buf("sc%d_" % slot + "vs%d" % i) for i in range(nb)]
    E = A.alloc(name + "E", [dk, hg, W_]); E_b = Buf(name + "E")
    eH = A.alloc(name + "eH", [NP, hg * dk]); eH_b = Buf(name + "eH")
    LRf = A.alloc(name + "LR", [128, hg, W_]); LR_b = Buf(name + "LR")
    LR = LRf[0:dk]
    BKh = A.alloc(name + "BKh", [NP, hg * dk]); BKh_b = Buf(name + "BKh")
    BW = 128 if lowrank else 64
    BLK = A.alloc(name + "BLK", [NP, hg, BW]); BLK_b = Buf(name + "BLK")
    if lowrank:
        PQ = [A.alloc(name + "PQ%d" % i, [64, hg, 128]) for i in range(2)]
        PQ_b = [Buf(name + "PQ%d" % i) for i in range(2)]
        Rt = A.alloc(name + "R", [64, hg, 64]); R_b = Buf(name + "R")
        R1s = A.alloc(name + "R1s", [64, hg * dv]); R1s_b = Buf(name + "R1s")
    if scalar_decay:
        RAW = A.alloc(name + "RAW", [dk, hg, 256]); RAW_b = Buf(name + "RAW")
        DEC = A.alloc(name + "DEC", [128, hg, 128]); DEC_b = Buf(name + "DEC")
        hcol = A.alloc(name + "hcol", [128, hg]); gend = A.alloc(name + "gend", [128, hg]); hg_b = Buf(name + "hgend")
        MNEG = cst("MNEG_" + d, 128)
    Ost = [A.alloc(name + "Ost%d" % i, [64, H * dv]) for i in range(2)]
    Ost_b = [B.buf("sc%d_" % slot + "Ost%d" % i) for i in range(2)]
    Sf = A.alloc(name + "S", [128, H * dv]); S_b = [B.buf("sc%d_" % slot + "S%d" % g) for g in range(ngroups)]
    S = Sf[0:dk]
    Sst = A.alloc(name + "Sst", [dk, H * dv]); Sst_b = B.buf("sc%d_" % slot + "Sst")
    if pbanks is None:
        bank, bb = B.banks, B.bank_b
    else:
        b0, b1, b2, b3 = pbanks
        lmap = [b0, b1, b2, b3, b3, b1, b0, b2]
        bank = [B.banks[m] for m in lmap]
        bb = [B.bank_b[m] for m in lmap]

    if dk < 128:
        P.c("pool", lambda e: e.memset(Sf, 0.0), writes=[S_b[0]])
        P.c("pool", lambda e: e.memset(LRf, 0.0), writes=[LR_b])
    B.load(S, s0_ap, S_b[0])
    for g in range(1, ngroups):
        S_b[g].last_w = S_b[0].last_w

    order = list(range(nchunks)) if d == "f" else list(range(nchunks - 1, -1, -1))

    def issue_loads(ci):
        c = order[ci]
        i = ci % nb
        r0, r1 = c * C, (c + 1) * C
        B.load(la_t[i], src["la"][0][r0:r1, :], la_b[i], reads=[src["la"][1]])
        B.load(q_t[i], src["q"][0][r0:r1, :], q_b[i], reads=[src["q"][1]])
        if lowrank:
            B.load(a_t[i], src["a"][0][r0:r1, :], a_b[i], reads=[src["a"][1]])
            B.load(k0_t[i], src["k"][0][r0:r1, :], k0_b[i], reads=[src["k"][1]])
            B.load(bk_t[i][0:64, :], src["b"][0][r0:r1, :], bk_b[i], reads=[src["b"][1]])
            B.load(bk_t[i][64:128, :], src["k"][0][r0:r1, :], bk_b[i], reads=[src["k"][1]])
            B.load(vs_t[i][64:128, :], src["v"][0][r0:r1, :], vs_b[i], reads=[src["v"][1]])
        else:
            B.load(bk_t[i], src["k"][0][r0:r1, :], bk_b[i], reads=[src["k"][1]])
            B.load(vs_t[i], src["v"][0][r0:r1, :], vs_b[i], reads=[src["v"][1]])

    def chunk_body(ci):
        c = order[ci]
        i = ci % nb
        la, q, bk, vs = la_t[i], q_t[i], bk_t[i], vs_t[i]
        seg_start = (ci % chunks_per_seg == 0) and ci > 0
        seg_end = (ci % chunks_per_seg == chunks_per_seg - 1)
        seg = c // chunks_per_seg
        ost, ost_b = Ost[ci % 2], Ost_b[ci % 2]
        if seg_start:
            for g in range(ngroups):
                gs = slice(g * hg * dv, (g + 1) * hg * dv)
                P.c("pool", lambda e, gs=gs: e.tensor_scalar_mul(out=S[:, gs], in0=S[:, gs], scalar1=flags[0:dk, 0:1]),
                    reads=[S_b[g], flags_b], writes=[S_b[g]])
        def group_body(g):
            heads = list(range(g * hg, (g + 1) * hg))
            gk = slice(g * hg * dk, (g + 1) * hg * dk)
            gv = slice(g * hg * dv, (g + 1) * hg * dv)
            def mm_cum(e):
                ins = None
                for hi, h in enumerate(heads):
                    ins = e.matmul(bank[0][0:dk, hi * 128:(hi + 1) * 128], la[:, h * dk:(h + 1) * dk], UU, start=True, stop=True)
                return ins
            P.c("pe", mm_cum, reads=[la_b[i], consts_b], writes=[bb[0]])

            def mm_h2(e):
                ins = None
                for hi, h in enumerate(heads):
                    ins = e.matmul(bank[1][0:NP, hi * dk:(hi + 1) * dk], UR, la[:, h * dk:(h + 1) * dk], start=True, stop=True)
                return ins
            P.c("pe", mm_h2, reads=[la_b[i], consts_b], writes=[bb[1]])
            cumv = bank[0][0:dk, 0:hg * 128].rearrange("p (a b) -> p a b", a=hg)
            if scalar_decay:
                P.c("act", lambda e: e.activation(out=E[:, :, 0:128], in_=cumv, func=AF.Exp), reads=[bb[0]], writes=[E_b])
            elif lowrank:
                P.c("act", lambda e: e.activation(out=E[:, :, 0:128], in_=cumv, func=AF.Exp), reads=[bb[0]], writes=[E_b])
                P.c("act", lambda e: e.activation(out=E[:, :, 128:192], in_=cumv[:, :, 0:64], func=AF.Exp, scale=-1.0), reads=[bb[0]], writes=[E_b])
                P.c("act", lambda e: e.activation(out=E[:, :, 192:256], in_=cumv[:, :, 0:64], func=AF.Exp, scale=-1.0), reads=[bb[0]], writes=[E_b])
            else:
                P.c("act", lambda e: e.activation(out=E[:, :, 0:64], in_=cumv[:, :, 0:64], func=AF.Exp), reads=[bb[0]], writes=[E_b])
                P.c("act", lambda e: e.activation(out=E[:, :, 64:128], in_=cumv[:, :, 0:64], func=AF.Exp, scale=-1.0), reads=[bb[0]], writes=[E_b])
            P.c("act", lambda e: e.activation(out=eH, in_=bank[1][0:NP, 0:hg * dk], func=AF.Exp), reads=[bb[1]], writes=[eH_b])
            def mm_tr(e):
                ins = None
                for hi, h in enumerate(heads):
                    hs = slice(h * dk, (h + 1) * dk)
                    if lowrank:
                        bnk = bank[2 + hi // 2]
                        o = (hi % 2) * 256
                        e.transpose(bnk[0:dk, o:o + 64], q[:, hs], ident[0:64, 0:64])
                        e.transpose(bnk[0:dk, o + 64:o + 128], a_t[i][:, hs], ident[0:64, 0:64])
                        e.transpose(bnk[0:dk, o + 128:o + 192], bk[0:64, hs], ident[0:64, 0:64])
                        ins = e.transpose(bnk[0:dk, o + 192:o + 256], k0_t[i][:, hs], ident[0:64, 0:64])
                    else:
                        o = hi * 128
                        e.transpose(bank[2][0:dk, o:o + 64], q[:, hs], ident[0:64, 0:64])
                        ins = e.transpose(bank[2][0:dk, o + 64:o + 128], bk[:, hs], ident[0:64, 0:64])
                return ins
            rds = [q_b[i], bk_b[i], consts_b] + ([a_b[i], k0_b[i]] if lowrank else [])
            P.c("pe", mm_tr, reads=rds, writes=[bb[2], bb[3]] if lowrank else [bb[2]])
            if scalar_decay:
                for half in range(2):
                    trv = bank[2 + half][0:dk, :].rearrange("p (a b) -> p a b", a=2)
                    P.c("act", lambda e, half=half, trv=trv: e.activation(out=RAW[:, 2 * half:2 * half + 2, :], in_=trv, func=AF.Copy),
                        reads=[bb[2 + half]], writes=[RAW_b])
                    P.c("dve", lambda e, half=half: e.tensor_tensor(out=LR[:, 2 * half:2 * half + 2, 0:128], in0=RAW[:, 2 * half:2 * half + 2, 0:128], in1=E[:, 2 * half:2 * half + 2, 0:128], op=ALU.mult),
                        reads=[RAW_b, E_b], writes=[LR_b])
                h2v = bank[1][:, 0:hg * dk].rearrange("p (a b) -> p a b", a=hg)
                P.c("act", lambda e, h2v=h2v: e.activation(out=hcol.unsqueeze(2), in_=h2v[:, :, 0:1], func=AF.Copy), reads=[bb[1]], writes=[hg_b])
                P.c("act", lambda e: e.activation(out=gend.unsqueeze(2), in_=cumv[:, :, last:last + 1], func=AF.Copy), reads=[bb[0]], writes=[hg_b])
                P.c("dve", lambda e: e.tensor_tensor(out=hcol, in0=hcol, in1=gend, op=ALU.subtract), reads=[hg_b], writes=[hg_b])
                for hi in range(hg):
                    P.c("act", lambda e, hi=hi: e.activation(out=DEC[:, hi, :], in_=bank[0][:, hi * 128:(hi + 1) * 128], func=AF.Identity, bias=hcol[:, hi:hi + 1]),
                        reads=[bb[0], hg_b], writes=[DEC_b])
                P.c("pool", lambda e: e.tensor_tensor(out=DEC, in0=DEC, in1=MNEG.unsqueeze(1).to_broadcast([128, hg, 128]), op=ALU.add),
                    reads=[DEC_b, consts_b], writes=[DEC_b])
                P.c("act", lambda e: e.activation(out=DEC, in_=DEC, func=AF.Exp), reads=[DEC_b], writes=[DEC_b])
            elif lowrank:
                for half in range(2):
                    trv = bank[2 + half][0:dk, :].rearrange("p (a b) -> p a b", a=2)
                    P.c("dve", lambda e, half=half, trv=trv: e.tensor_tensor(out=LR[:, 2 * half:2 * half + 2, :], in0=trv, in1=E[:, 2 * half:2 * half + 2, :], op=ALU.mult),
                        reads=[bb[2 + half], E_b], writes=[LR_b])
            else:
                trv = bank[2][0:dk, 0:hg * 128].rearrange("p (a b) -> p a b", a=hg)
                P.c("dve", lambda e, trv=trv: e.tensor_tensor(out=LR, in0=trv, in1=E, op=ALU.mult), reads=[bb[2], E_b], writes=[LR_b])
            P.c("pool", lambda e: e.tensor_tensor(out=BKh, in0=bk[:, gk], in1=eH, op=ALU.mult), reads=[bk_b[i], eH_b], writes=[BKh_b])
            if SCAN_STOP[0] <= 1:
                return
            def mm_blk(e):
                ins = None
                for hi in range(hg):
                    if scalar_decay:
                        ins = e.matmul(bank[0][:, hi * 128:(hi + 1) * 128], RAW[:, hi, 128:256], RAW[:, hi, 0:128], start=True, stop=True)
                    elif lowrank:
                        ins = e.matmul(bank[0][:, hi * 128:(hi + 1) * 128], LR[:, hi, 128:256], LR[:, hi, 0:128], start=True, stop=True)
                    else:
                        ins = e.matmul(bank[0][0:64, hi * 64:(hi + 1) * 64], LR[:, hi, 64:128], LR[:, hi, 0:64], start=True, stop=True)
                return ins
            P.c("pe", mm_blk, reads=[LR_b] + ([RAW_b, DEC_b, hg_b] if scalar_decay else []), writes=[bb[0]])
            if scalar_decay:
                blkv = bank[0][:, :].rearrange("p (a b) -> p a b", a=hg)
                P.c("dve", lambda e, blkv=blkv: e.tensor_tensor(out=BLK, in0=blkv, in1=DEC, op=ALU.mult),
                    reads=[bb[0], DEC_b], writes=[BLK_b])
            elif lowrank:
                blkv = bank[0][:, :].rearrange("p (a b) -> p a b", a=hg)
                P.c("dve", lambda e, blkv=blkv: e.tensor_tensor(out=BLK, in0=blkv, in1=MASK.unsqueeze(1).to_broadcast([128, hg, 128]), op=ALU.mult),
                    reads=[bb[0], consts_b], writes=[BLK_b])
            else:
                blkv = bank[0][0:64, 0:hg * 64].rearrange("p (a b) -> p a b", a=hg)
                P.c("dve", lambda e, blkv=blkv: e.tensor_tensor(out=BLK, in0=blkv, in1=MASKI.unsqueeze(1).to_broadcast([64, hg, 64]), op=ALU.mult),
                    reads=[bb[0], consts_b], writes=[BLK_b])
            if SCAN_STOP[0] <= 2:
                return
            if lowrank:
                def mm_nab(e):
                    ins = None
                    for hi in range(hg):
                        if scalar_decay:
                            ins = e.transpose(bank[1][0:64, hi * 64:(hi + 1) * 64], BLK[0:64, hi, 64:128], ident[0:64, 0:64])
                        else:
                            ins = e.matmul(bank[1][0:64, hi * 64:(hi + 1) * 64], LR[:, hi, 64:128], LR[:, hi, 128:192], start=True, stop=True)
                    return ins
                P.c("pe", mm_nab, reads=[LR_b, BLK_b, consts_b], writes=[bb[1]])
                nabv = bank[1][0:64, 0:hg * 64].rearrange("p (a b) -> p a b", a=hg)
                P.c("dve", lambda e, nabv=nabv: e.tensor_tensor(out=PQ[0][:, :, 64:128], in0=nabv, in1=MASKN.unsqueeze(1).to_broadcast([64, hg, 64]), op=ALU.mult),
                    reads=[bb[1], consts_b], writes=[PQ_b[0]])
                P.c("act", lambda e: e.activation(out=PQ[0][:, :, 0:64], in_=BLK[0:64, :, 64:128], func=AF.Copy), reads=[BLK_b], writes=[PQ_b[0]])
                P.c("dve", lambda e: e.tensor_tensor(out=Rt, in0=BLK[0:64, :, 64:128], in1=ident[0:64, 0:64].unsqueeze(1).to_broadcast([64, hg, 64]), op=ALU.add),
                    reads=[BLK_b, consts_b], writes=[R_b])
                for lev in range(1, 6):
                    pa, pb_ = PQ[(lev - 1) % 2], PQ[lev % 2]
                    pa_b, pb_b = PQ_b[(lev - 1) % 2], PQ_b[lev % 2]

                    def mm_sq(e, pa=pa):
                        ins = None
                        for hi in range(hg):
                            e.matmul(bank[2][0:64, hi * 128:hi * 128 + 64], pa[:, hi, 64:128], pa[:, hi, 0:64], start=True, stop=True)
                            ins = e.matmul(bank[2][0:64, hi * 128 + 64:hi * 128 + 128], pa[:, hi, 0:64], pa[:, hi, 64:128], start=True, stop=True)
                        return ins
                    P.c("pe", mm_sq, reads=[pa_b], writes=[bb[2]])
                    pqv = bank[2][0:64, :].rearrange("p (a b) -> p a b", a=hg)
                    P.c("act", lambda e, pb_=pb_, pqv=pqv: e.activation(out=pb_, in_=pqv, func=AF.Copy), reads=[bb[2]], writes=[pb_b])

                    def mm_r(e, pb_=pb_):
                        ins = None
                        for hi in range(hg):
                            ins = e.matmul(bank[1][0:64, hi * 64:(hi + 1) * 64], pb_[:, hi, 64:128], Rt[:, hi, :], start=True, stop=True)
                        return ins
                    P.c("pe", mm_r, reads=[pb_b, R_b], writes=[bb[1]])
                    rupv = bank[1][0:64, 0:hg * 64].rearrange("p (a b) -> p a b", a=hg)
                    P.c("dve", lambda e, rupv=rupv: e.tensor_tensor(out=Rt, in0=rupv, in1=Rt, op=ALU.add), reads=[bb[1], R_b], writes=[R_b])
            if SCAN_STOP[0] <= 3:
                return
            if lowrank:
                def mm_r1(e):
                    ins = None
                    for hi, h in enumerate(heads):
                        e.matmul(bank[4][0:64, hi * dv:(hi + 1) * dv], LRf[:, hi, 64:128], Sf[:, h * dv:(h + 1) * dv], start=True, stop=False)
                        ins = e.matmul(bank[4][0:64, hi * dv:(hi + 1) * dv], BLK[64:128, hi, 64:128], vs[64:128, h * dv:(h + 1) * dv], start=False, stop=True)
                    return ins
                P.c("pe", mm_r1, reads=[LR_b, S_b[g], BLK_b, vs_b[i]], writes=[bb[4]])
                P.c("act", lambda e: e.activation(out=R1s, in_=bank[4][0:64, 0:hg * dv], func=AF.Copy), reads=[bb[4]], writes=[R1s_b])

                def mm_sa(e):
                    ins = None
                    for hi in range(hg):
                        ins = e.matmul(bank[5][0:64, hi * dv:(hi + 1) * dv], Rt[:, hi, :], R1s[:, hi * dv:(hi + 1) * dv], start=True, stop=True)
                    return ins
                P.c("pe", mm_sa, reads=[R_b, R1s_b], writes=[bb[5]])
                P.c("dve", lambda e: e.tensor_copy(out=vs[0:64, gv], in_=bank[5][0:64, 0:hg * dv]), reads=[bb[5]], writes=[vs_b[i]])

                def mm_o(e):
                    ins = None
                    for hi, h in enumerate(heads):
                        e.matmul(bank[6][0:64, hi * dv:(hi + 1) * dv], LRf[:, hi, 0:64], Sf[:, h * dv:(h + 1) * dv], start=True, stop=False)
                        ins = e.matmul(bank[6][0:64, hi * dv:(hi + 1) * dv], BLK[:, hi, 0:64], vs[:, h * dv:(h + 1) * dv], start=False, stop=True)
                    return ins
                P.c("pe", mm_o, reads=[LR_b, S_b[g], BLK_b, vs_b[i]], writes=[bb[6]])
            else:
                def mm_o(e):
                    ins = None
                    for hi, h in enumerate(heads):
                        e.matmul(bank[6][0:64, hi * dv:(hi + 1) * dv], LRf[:, hi, 0:64], Sf[:, h * dv:(h + 1) * dv], start=True, stop=False)
                        ins = e.matmul(bank[6][0:64, hi * dv:(hi + 1) * dv], BLK[:, hi, :], vs[:, h * dv:(h + 1) * dv], start=False, stop=True)
                    return ins
                P.c("pe", mm_o, reads=[LR_b, S_b[g], BLK_b, vs_b[i]], writes=[bb[6]])
            P.c("act", lambda e, ost=ost: e.activation(out=ost[:, gv], in_=bank[6][0:64, 0:hg * dv], func=AF.Copy), reads=[bb[6]], writes=[ost_b])

            def mm_sd(e):
                ins = None
                for hi, h in enumerate(heads):
                    ins = e.matmul(bank[7][0:dk, hi * dv:(hi + 1) * dv], BKh[:, hi * dk:(hi + 1) * dk], vs[:, h * dv:(h + 1) * dv], start=True, stop=True)
                return ins
            P.c("pe", mm_sd, reads=[BKh_b, vs_b[i]], writes=[bb[7]])
            Sg = S[:, gv].rearrange("p (a b) -> p a b", a=hg)
            egend = E[:, :, last:last + 1].to_broadcast([dk, hg, dv])
            P.c("dve", lambda e, Sg=Sg, egend=egend: e.tensor_tensor(out=Sg, in0=Sg, in1=egend, op=ALU.mult), reads=[S_b[g], E_b], writes=[S_b[g]])
            P.c("dve", lambda e: e.tensor_tensor(out=S[:, gv], in0=S[:, gv], in1=bank[7][0:dk, 0:hg * dv], op=ALU.add), reads=[S_b[g], bb[7]], writes=[S_b[g]])
        for g in range(ngroups):
            group_body(g)
        B.store(o_dst[c * C:(c + 1) * C, :], ost, ost_b, writes=[o_dst_b])
        if seg_end and st_out is not None:
            P.c("act", lambda e: e.activation(out=Sst, in_=S, func=AF.Copy), reads=S_b, writes=[Sst_b])
            B.store(st_out[seg], Sst, Sst_b, final=True)

    issue_loads(0)
    for ci in range(nchunks):
        if ci + 1 < nchunks:
            issue_loads(ci + 1)
        chunk_body(ci)
    if standalone:
        P.barrier()
        A.release(m0)


def build_program(dbg=None):
    B = Builder(dbg)
    nc, P, A = B.nc, B.P, B.arena
    dbg = B.dbg
    nlayers = dbg.get("nlayers", 2)

    xT_in = B.din("xT", [D, T])
    cond_in = B.din("cond", [128, NCH])
    W = {}
    for l in range(2):
        p = "l%d_" % l
        W[p + "w_mod"] = B.din(p + "w_mod", [D, 9 * D])
        W[p + "b_mod"] = B.din(p + "b_mod", [128, 9 * NCH])
        W[p + "norms"] = B.din(p + "norms", [128, 3 * NCH])
        for f in ("ffn1", "ffn2"):
            W[p + f + "_wg"] = B.din(p + f + "_wg", [D, DFF])
            W[p + f + "_wu"] = B.din(p + f + "_wu", [D, DFF])
            W[p + f + "_wd"] = B.din(p + f + "_wd", [DFF, D])
        W[p + "w_in"] = B.din(p + "w_in", [D, L0_IN if l == 0 else L1_IN])
        W[p + "w_out"] = B.din(p + "w_out", [D, D])
    fin_norm = B.din("final_norm", [128, NCH])
    yT_out = B.dout("yT", [D, T])
    consts_np = make_consts()
    consts_in = B.din("consts", consts_np.shape)
    flags_in = B.din("flags", [128, 2])
    tmask_in = B.din("tmask", [T, 4])
    rot_in = B.din("rot", [T, 128])
    s0_in = {"gla": B.din("s0_gla", [2, 128, 1024]), "ret": B.din("s0_ret", [2, 128, 1024]),
             "gdn": B.din("s0_gdn", [2, 128, 1024]), "rwkv": B.din("s0_rwkv", [2, 64, 1024])}
    st_out = {"gla": B.dout("st_gla", [2, NSEG, 128, 1024]), "ret": B.dout("st_ret", [2, NSEG, 128, 1024]),
              "gdn": B.dout("st_gdn", [2, NSEG, 128, 1024]), "rwkv": B.dout("st_rwkv", [2, NSEG, 64, 1024])}
    SP_ = {}
    for nm, shp in (("l0_gk_up", [16, 1024]), ("l0_gk_b", [1, 1024]), ("l0_gla_norm", [1, 256]), ("l0_ret_norm", [1, 256]),
                    ("ret_la_f", [T, 512]), ("ret_la_b", [T, 512]),
                    ("l1_conv", [5, 3072]), ("l1_dtb", [1, 16]), ("l1_alog", [1, 16]), ("l1_gdn_norm", [1, 128]),
                    ("l1_mu", [1, 3456]), ("l1_w0", [1, 2048]), ("l1_a0", [1, 2048]), ("l1_w2", [64, 2048]),
                    ("l1_a2", [64, 2048]), ("l1_g2", [128, 1024]), ("l1_kk", [1, 1024]), ("l1_ka", [1, 1024]),
                    ("l1_rk", [1, 1024]), ("l1_lnw", [1, 1024]), ("l1_lnb", [1, 1024])):
        SP_[nm] = B.din(nm, shp)

    ones_bf = A.alloc("ones_bf", [128, 128], BF16)
    ones_b = Buf("ones_bf")
    P.c("pool", lambda e: e.memset(ones_bf, 1.0), writes=[ones_b])
    mods = [A.alloc("mods%d" % l, [128, 9 * NCH]) for l in range(2)]
    mods_b = [Buf("mods%d" % l) for l in range(2)]
    norms = [A.alloc("norms%d" % l, [128, 3 * NCH]) for l in range(2)]
    norms_b = [Buf("norms%d" % l) for l in range(2)]
    modA = [A.alloc("modA%d" % l, [128, 3 * NCH]) for l in range(2)]
    modG = [A.alloc("modG%d" % l, [128, 3 * NCH]) for l in range(2)]
    modc_b = [Buf("modc%d" % l) for l in range(2)]
    finw = A.alloc("finw", [128, NCH])
    finw_b = Buf("finw")
    B.load(finw, fin_norm, finw_b)
    consts = A.alloc("consts", list(consts_np.shape))
    consts_b = Buf("consts")
    B.load(consts, consts_in, consts_b)
    flags = A.alloc("flags", [128, 2])
    flags_b = Buf("flags")
    B.load(flags, flags_in, flags_b)
    ones_f = A.alloc("ones_f", [1, 128])
    ones_f_b = Buf("ones_f")
    P.c("pool", lambda e: e.memset(ones_f, 1.0), writes=[ones_f_b])
    ident = consts[:, CONST_COLS["ident"][0]:CONST_COLS["ident"][0] + 128]

    def phase_mods(l):
        p = "l%d_" % l
        m0 = A.mark()
        cond = A.alloc("cond", [128, NCH])
        cond_b = B.buf("cond")
        scond = A.alloc("scond", [128, NCH], BF16)
        scond_b = Buf("scond")
        bmod = A.alloc("bmod", [128, 9 * NCH])
        bmod_b = B.buf("bmod")
        B.load(cond, cond_in, cond_b)
        B.load(bmod, W[p + "b_mod"], bmod_b)
        B.load(norms[l], W[p + "norms"], norms_b[l])
        P.c("act", lambda e: e.activation(out=scond, in_=cond, func=AF.Silu), reads=[cond_b], writes=[scond_b])
        wsl = Slots(A, "wmod", 3, [128, NCH, 512], BF16)
        wsl.bufs = [B.buf("wmod%d" % i) for i in range(3)]
        wsrc = W[p + "w_mod"].rearrange("(k p) n -> p k n", p=128)
        ps = B.banks[0]
        ps_b = B.bank_b[0]
        ngrp = 9 * D // 512
        for g in range(ngrp):
            wt, wt_b = wsl.next()
            B.load(wt, wsrc[:, :, g * 512:(g + 1) * 512], wt_b, eng="pool")

            def mm(e, wt=wt, g=g):
                ins = None
                for cidx in range(4):
                    col = g * 4 + cidx
                    for k in range(NCH):
                        ins = e.matmul(ps[:, col:col + 1], wt[:, k, cidx * 128:(cidx + 1) * 128],
                                       scond[:, k:k + 1], start=(k == 0), stop=(k == NCH - 1))
                return ins
            P.c("pe", mm, reads=[wt_b, scond_b], writes=[ps_b])
        P.c("dve", lambda e: e.tensor_tensor(out=mods[l], in0=ps[:, 0:9 * NCH], in1=bmod, op=ALU.add),
            reads=[ps_b, bmod_b], writes=[mods_b[l]])
        for i in range(3):
            sc = mods[l][:, (3 * i + 1) * NCH:(3 * i + 2) * NCH]
            g = mods[l][:, (3 * i + 2) * NCH:(3 * i + 3) * NCH]
            gam = norms[l][:, i * NCH:(i + 1) * NCH]
            P.c("dve", lambda e, sc=sc, gam=gam, i=i: e.scalar_tensor_tensor(
                out=modA[l][:, i * NCH:(i + 1) * NCH], in0=sc, scalar=1.0, in1=gam, op0=ALU.add, op1=ALU.mult),
                reads=[mods_b[l], norms_b[l]], writes=[modc_b[l]])
            fac = 1.0 if i == 1 else 0.5
            P.c("dve", lambda e, g=g, i=i, fac=fac: e.tensor_scalar_mul(
                out=modG[l][:, i * NCH:(i + 1) * NCH], in0=g, scalar1=fac),
                reads=[mods_b[l]], writes=[modc_b[l]])
        P.barrier()
        A.release(m0)

    def tt(eng, out, in0, in1, op, r, w):
        return P.c(eng, lambda e: e.tensor_tensor(out=out, in0=in0, in1=in1, op=op), reads=r, writes=w)

    def stt(eng, out, in0, scalar, in1, op0, op1, r, w):
        eng = "dve"
        return P.c(eng, lambda e: e.scalar_tensor_tensor(out=out, in0=in0, scalar=scalar, in1=in1, op0=op0, op1=op1), reads=r, writes=w)

    def tsm(eng, out, in0, s1, r, w):
        return P.c(eng, lambda e: e.tensor_scalar_mul(out=out, in0=in0, scalar1=s1), reads=r, writes=w)

    def tcopy(eng, out, in_, r, w):
        return P.c(eng, lambda e: e.tensor_copy(out=out, in_=in_), reads=r, writes=w)

    def actf(out, in_, func, r, w, scale=1.0, bias=None):
        def fn(e):
            if bias is None:
                return e.activation(out=out, in_=in_, func=func, scale=scale)
            return e.activation(out=out, in_=in_, func=func, scale=scale, bias=bias)
        return P.c("act", fn, reads=r, writes=w)

    def red(eng, out, in_, r, w):
        return P.c(eng, lambda e: e.tensor_reduce(out=out, in_=in_, axis=AX.X, op=ALU.add), reads=r, writes=w)

    def rstd_small(t, r_b, sc, eps):
        actf(t, t, AF.Ln, [r_b], [r_b], scale=sc, bias=eps)
        actf(t, t, AF.Exp, [r_b], [r_b], scale=-0.5)

    xs_scr, _ = B.dscr("xs", [D, T])
    xs_tile_b = [Buf("xs_t%d" % i) for i in range(NT)]
    oT_scr, oT_b = B.dscr("oT", [D, T], BF16)
    SC = {}
    for nm, shp in (("gla_q", [T, 512]), ("gla_k", [T, 512]), ("gla_v", [T, 1024]), ("gla_g", [T, 1024]),
                    ("gla_la_f", [T, 512]), ("gla_la_b", [T, 512]),
                    ("ret_q", [T, 512]), ("ret_k", [T, 512]), ("ret_v", [T, 1024]), ("ret_g", [T, 1024]),
                    ("OA_f", [T, 1024]), ("OA_b", [T, 1024]), ("OB_f", [T, 1024]), ("OB_b", [T, 1024]),
                    ("zq", [T + 4, 3072]), ("gdn_g", [T, 1024]), ("zr", [T + 4, 3456]), ("ab", [T, 32]),
                    ("gdn_q", [T, 1024]), ("gdn_a", [T, 1024]), ("gdn_v", [T, 1024]),
                    ("gdn_la_f", [T, 1024]), ("gdn_la_b", [T, 1024]), ("gdn_k_f", [T, 1024]), ("gdn_k_b", [T, 1024]),
                    ("gdn_b_f", [T, 1024]), ("gdn_b_b", [T, 1024]),
                    ("rw_q", [T, 1024]), ("rw_v", [T, 1024]), ("rw_a", [T, 1024]),
                    ("rw_la_f", [T, 1024]), ("rw_la_b", [T, 1024]), ("rw_k_f", [T, 1024]), ("rw_k_b", [T, 1024]),
                    ("rw_b_f", [T, 1024]), ("rw_b_b", [T, 1024]), ("rw_gate", [T, 1024]), ("rw_bonus", [T, 1024])):
        SC[nm] = B.dscr(nm, shp)
    SC["ret_la_f"] = (SP_["ret_la_f"], Buf("ret_la_f"))
    SC["ret_la_b"] = (SP_["ret_la_b"], Buf("ret_la_b"))

    class TileCtx:
        pass

    def tile_alloc(l0=True):
        tc = TileCtx()
        tc.x = A.alloc("xtile", [128, NCH, TT])
        tc.x_b = [Buf("xtile%d" % j) for j in range(NCH)]
        tc.x_own = [B.buf("xown%d" % g) for g in range(4)]
        tc.h = A.alloc("htile", [128, NCH, TT], BF16)
        tc.h_b = [Buf("htile%d" % j) for j in range(NCH)]
        tc.act = A.alloc("acttile", [128, NF, TT], BF16)
        tc.act_b = [Buf("act%d" % j) for j in range(NF)]
        tc.sq = A.alloc("sq", [128, 2, TT], BF16)
        tc.sq_b = [Buf("sq0"), Buf("sq1")]
        tc.rstd = A.alloc("rstd", [128, TT])
        tc.rstd_b = Buf("rstd")
        tc.tmp = A.alloc("tmpx", [128, 2, TT])
        tc.tmp_b = [Buf("tmpx0"), Buf("tmpx1")]
        tc.sg = A.alloc("sg", [128, 2, TT])
        tc.sg_b = [Buf("sg0"), Buf("sg1")]
        nw = 2 if l0 else 3
        tc.wgu = Slots(A, "wgu", nw, [128, NCH * 512], BF16)
        tc.wgu.bufs = [B.buf("wgu%d" % i) for i in range(nw)]
        nwd = 2 if l0 else 3
        tc.wd = Slots(A, "wd", nwd, [128, 11, 512], BF16)
        tc.wd.bufs = [B.buf("wd%d" % i) for i in range(nwd)]
        tc.stg = [A.alloc("stg%d" % i, [128, 512]) for i in range(4)]
        tc.stg_b = [B.buf("stg%d" % i) for i in range(4)]
        tc.stg_i = 0
        if not l0:
            return tc
        tc.rot = A.alloc("rot_t", [128, 128])
        tc.rot_b = B.buf("rot_t")
        tc.rx = A.alloc("rot_x", [128, 4, 128])
        tc.rx_b = Buf("rot_x")
        tc.rtmp = [A.alloc("rot_tmp%d" % i, [128, 4, 64]) for i in range(4)]
        tc.rtmp_b = [Buf("rot_tmp%d" % i) for i in range(4)]
        tc.wtail = A.alloc("wtail", [128, NCH, 32], BF16)
        tc.wtail_b = B.buf("wtail")
        tc.gdT = A.alloc("gdT", [16, 2, TT])
        tc.gdT_b = Buf("gdT")
        tc.gkup = A.alloc("gkup", [16, 2, 512])
        tc.gkb = A.alloc("gkb", [1, 2, 512])
        tc.gk_b = B.buf("gkparams")
        B.dma_in(tc.gkup, SP_["l0_gk_up"].rearrange("k (d c) -> k d c", d=2), tc.gk_b, [tc.gk_b])
        B.dma_in(tc.gkb, SP_["l0_gk_b"].rearrange("k (d c) -> k d c", d=2), tc.gk_b, [tc.gk_b])
        return tc

    def next_stage(tc):
        i = tc.stg_i
        tc.stg_i = (i + 1) % 4
        return tc.stg[i], tc.stg_b[i]

    def x_load(tc, src, t, src_bufs=()):
        srcv = src.rearrange("(j p) t -> p j t", p=128)
        for g in range(4):
            B.dma_in(tc.x[:, 4 * g:4 * g + 4, :], srcv[:, 4 * g:4 * g + 4, t * TT:(t + 1) * TT], tc.x_own[g],
                     tc.x_b[4 * g:4 * g + 4], reads=src_bufs)

    def x_store(tc, dst, t, dst_bufs=(), final=False):
        dstv = dst.rearrange("(j p) t -> p j t", p=128)
        for g in range(4):
            B.dma_out(dstv[:, 4 * g:4 * g + 4, t * TT:(t + 1) * TT], tc.x[:, 4 * g:4 * g + 4, :], tc.x_own[g],
                      tc.x_b[4 * g:4 * g + 4], writes=dst_bufs, final=final)

    def rms_stats(tc):
        ps = B.banks[7]
        ps_b = B.bank_b[7]
        for j in range(NCH):
            s = j % 2
            if j % 2 == 0:
                P.c("act", lambda e, j=j, s=s: e.activation(out=tc.sq[:, s, :], in_=tc.x[:, j, :], func=AF.Square),
                    reads=[tc.x_b[j]], writes=[tc.sq_b[s]])
            else:
                P.c("pool", lambda e, j=j, s=s: e.tensor_tensor(out=tc.sq[:, s, :], in0=tc.x[:, j, :], in1=tc.x[:, j, :], op=ALU.mult),
                    reads=[tc.x_b[j]], writes=[tc.sq_b[s]])
            P.c("pe", lambda e, j=j, s=s: e.matmul(ps[:, :], ones_bf, tc.sq[:, s, :], start=(j == 0), stop=(j == NCH - 1)),
                reads=[tc.sq_b[s], ones_b], writes=[ps_b])
        actf(tc.rstd, ps[:, :], AF.Ln, [ps_b], [tc.rstd_b], scale=1.0 / D, bias=EPS)
        actf(tc.rstd, tc.rstd, AF.Exp, [tc.rstd_b], [tc.rstd_b], scale=-0.5)

    def modulate(tc, l, i):
        rms_stats(tc)
        for j in range(NCH):
            s = j % 2
            a_col = modA[l][:, i * NCH + j:i * NCH + j + 1]
            sh_col = mods[l][:, (3 * i) * NCH + j:(3 * i) * NCH + j + 1]
            stt("dve", tc.tmp[:, s, :], tc.x[:, j, :], a_col, tc.rstd, ALU.mult, ALU.mult,
                [tc.x_b[j], tc.rstd_b, modc_b[l]], [tc.tmp_b[s]])
            actf(tc.h[:, j, :], tc.tmp[:, s, :], AF.Identity, [tc.tmp_b[s], mods_b[l]], [tc.h_b[j]], bias=sh_col)

    def ffn(tc, l, which, i):
        p = "l%d_ffn%d_" % (l, which)
        wg_src = W[p + "wg"].rearrange("(k p) n -> p k n", p=128)
        wu_src = W[p + "wu"].rearrange("(k p) n -> p k n", p=128)
        wd_src = W[p + "wd"].rearrange("(f p) n -> p f n", p=128)
        for g in range(NF // 2):
            wflat, wt_b = tc.wgu.next()
            wt = wflat.rearrange("p (a k c) -> p a k c", a=2, k=NCH)
            B.load(wt[:, 0], wg_src[:, :, g * 256:(g + 1) * 256], wt_b, eng="pool")
            B.load(wt[:, 1], wu_src[:, :, g * 256:(g + 1) * 256], wt_b, eng="pool")
            for ci in range(2):
                f = g * 2 + ci
                pg, pg_b = B.banks[(f % 2) * 2], B.bank_b[(f % 2) * 2]
                pu, pu_b = B.banks[(f % 2) * 2 + 1], B.bank_b[(f % 2) * 2 + 1]

                def mm(e, wt=wt, ci=ci, which_w=0, ps=pg):
                    ins = None
                    for k in range(NCH):
                        ins = e.matmul(ps[:, :], wt[:, which_w, k, ci * 128:(ci + 1) * 128], tc.h[:, k, :],
                                       start=(k == 0), stop=(k == NCH - 1))
                    return ins
                P.c("pe", lambda e, mm=mm, wt=wt, ci=ci, pg=pg: mm(e, wt, ci, 0, pg), reads=[wt_b] + tc.h_b, writes=[pg_b])
                P.c("pe", lambda e, mm=mm, wt=wt, ci=ci, pu=pu: mm(e, wt, ci, 1, pu), reads=[wt_b] + tc.h_b, writes=[pu_b])
                s = f % 2
                actf(tc.sg[:, s, :], pg[:, :], AF.Silu, [pg_b], [tc.sg_b[s]])
                tt("dve", tc.act[:, f, :], pu[:, :], tc.sg[:, s, :], ALU.mult, [pu_b, tc.sg_b[s]], [tc.act_b[f]])
        for dg in range(4):
            pbanks = [4 + q for q in range(4)]
            for part in range(4):
                wt, wt_b = tc.wd.next()
                B.load(wt, wd_src[:, part * 11:(part + 1) * 11, dg * 512:(dg + 1) * 512], wt_b, eng="pool")
                for q in range(4):
                    def mm(e, wt=wt, part=part, q=q):
                        ins = None
                        for fi in range(11):
                            f = part * 11 + fi
                            ins = e.matmul(B.banks[pbanks[q]][:, :], wt[:, fi, q * 128:(q + 1) * 128], tc.act[:, f, :],
                                           start=(f == 0), stop=(f == NF - 1))
                        return ins
                    P.c("pe", mm, reads=[wt_b] + tc.act_b[part * 11:(part + 1) * 11], writes=[B.bank_b[pbanks[q]]])
            for q in range(4):
                j = dg * 4 + q
                gcol = modG[l][:, i * NCH + j:i * NCH + j + 1]
                stt("dve", tc.x[:, j, :], B.banks[pbanks[q]][:, :], gcol, tc.x[:, j, :], ALU.mult, ALU.add,
                    [B.bank_b[pbanks[q]], modc_b[l], tc.x_b[j]], [tc.x_b[j]])

    def final_norm(tc):
        rms_stats(tc)
        for j in range(NCH):
            stt("dve", tc.x[:, j, :], tc.x[:, j, :], finw[:, j:j + 1], tc.rstd, ALU.mult, ALU.mult,
                [tc.x_b[j], tc.rstd_b, finw_b], [tc.x_b[j]])

    def wout(tc, l, t):
        own = B.buf("oTload")
        B.dma_in(tc.act[:, 0:NCH, :], oT_scr.rearrange("(k p) t -> p k t", p=128)[:, :, t * TT:(t + 1) * TT], own,
                 tc.act_b[0:NCH], reads=[oT_b])
        w_src = W["l%d_w_out" % l].rearrange("(k p) n -> p k n", p=128)
        for dg in range(4):
            wflat, wt_b = tc.wgu.next()
            wv = wflat.rearrange("p (k c) -> p k c", k=NCH)
            B.load(wv, w_src[:, :, dg * 512:(dg + 1) * 512], wt_b, eng="pool")
            for q in range(4):
                j = dg * 4 + q
                ps, ps_b = B.banks[q], B.bank_b[q]

                def mm(e, wv=wv, q=q, ps=ps):
                    ins = None
                    for k in range(NCH):
                        ins = e.matmul(ps[:, :], wv[:, k, q * 128:(q + 1) * 128], tc.act[:, k, :], start=(k == 0), stop=(k == NCH - 1))
                    return ins
                P.c("pe", mm, reads=[wt_b] + tc.act_b[0:NCH], writes=[ps_b])
                gcol = modG[l][:, NCH + j:NCH + j + 1]
                stt("dve", tc.x[:, j, :], ps[:, :], gcol, tc.x[:, j, :], ALU.mult, ALU.add,
                    [ps_b, modc_b[l], tc.x_b[j]], [tc.x_b[j]])

    QS = 128 ** -0.5

    def proj(tc, l, t, plan, ncols_tail):
        w_src = W["l%d_w_in" % l].rearrange("(k p) n -> p k n", p=128)
        for cg in range(len(plan)):
            dst, row_off, col_off, kind, scale = plan[cg]
            dst_ap, dst_b = SC[dst]
            wflat, wt_b = tc.wgu.next()
            wv = wflat.rearrange("p (k c) -> p k c", k=NCH)
            B.load(wv, w_src[:, :, cg * 512:(cg + 1) * 512], wt_b, eng="pool")
            for tb in range(4):
                ps, ps_b = B.banks[(cg * 4 + tb) % 4], B.bank_b[(cg * 4 + tb) % 4]
                r0 = t * TT + tb * 128

                def mm(e, wv=wv, tb=tb, ps=ps):
                    ins = None
                    for k in range(NCH):
                        ins = e.matmul(ps[:, :], tc.h[:, k, tb * 128:(tb + 1) * 128], wv[:, k, :], start=(k == 0), stop=(k == NCH - 1))
                    return ins
                P.c("pe", mm, reads=[wt_b] + tc.h_b, writes=[ps_b])
                st, st_b = next_stage(tc)
                if kind == "copy":
                    if (cg + tb) % 2 == 0:
                        actf(st, ps[:, :], AF.Identity, [ps_b], [st_b], scale=scale)
                    else:
                        tsm("dve", st, ps[:, :], scale, [ps_b], [st_b])
                elif kind == "silu":
                    actf(st, ps[:, :], AF.Silu, [ps_b], [st_b])
                else:
                    B.dma_in(tc.rot, rot_in[r0:r0 + 128, :], tc.rot_b, [tc.rot_b])
                    cosB = tc.rot[:, 0:64].unsqueeze(1).to_broadcast([128, 4, 64])
                    sinB = tc.rot[:, 64:128].unsqueeze(1).to_broadcast([128, 4, 64])
                    psv = ps[:, :].rearrange("p (h c) -> p h c", h=4)
                    actf(tc.rx, psv, AF.Identity, [ps_b], [tc.rx_b], scale=scale)
                    sv = st.rearrange("p (h c) -> p h c", h=4)
                    tt("dve", tc.rtmp[0], tc.rx[:, :, 0:64], cosB, ALU.mult, [tc.rx_b, tc.rot_b], [tc.rtmp_b[0]])
                    tt("pool", tc.rtmp[1], tc.rx[:, :, 64:128], sinB, ALU.mult, [tc.rx_b, tc.rot_b], [tc.rtmp_b[1]])
                    tt("dve", sv[:, :, 0:64], tc.rtmp[0], tc.rtmp[1], ALU.subtract, [tc.rtmp_b[0], tc.rtmp_b[1]], [st_b])
                    tt("pool", tc.rtmp[2], tc.rx[:, :, 0:64], sinB, ALU.mult, [tc.rx_b, tc.rot_b], [tc.rtmp_b[2]])
                    tt("dve", tc.rtmp[3], tc.rx[:, :, 64:128], cosB, ALU.mult, [tc.rx_b, tc.rot_b], [tc.rtmp_b[3]])
                    tt("pool", sv[:, :, 64:128], tc.rtmp[2], tc.rtmp[3], ALU.add, [tc.rtmp_b[2], tc.rtmp_b[3]], [st_b])
                B.dma_out(dst_ap[row_off + r0:row_off + r0 + 128, col_off:col_off + 512], st, st_b, [st_b], writes=[dst_b])
        c0 = len(plan) * 512
        if l == 0:
            B.load(tc.wtail, w_src[:, :, c0:c0 + 32], tc.wtail_b, eng="pool")
            for d_ in range(2):
                ps, ps_b = B.banks[d_], B.bank_b[d_]

                def mm(e, d_=d_, ps=ps):
                    ins = None
                    for k in range(NCH):
                        ins = e.matmul(ps[0:16, :], tc.wtail[:, k, d_ * 16:(d_ + 1) * 16], tc.h[:, k, :], start=(k == 0), stop=(k == NCH - 1))
                    return ins
                P.c("pe", mm, reads=[tc.wtail_b] + tc.h_b, writes=[ps_b])
                actf(tc.gdT[:, d_, :], ps[0:16, :], AF.Copy, [ps_b], [tc.gdT_b])
            for d_ in range(2):
                dst_ap, dst_b = SC["gla_la_f" if d_ == 0 else "gla_la_b"]
                for tb in range(4):
                    ps, ps_b = B.banks[2 + (tb % 2)], B.bank_b[2 + (tb % 2)]
                    r0 = t * TT + tb * 128

                    def mm(e, d_=d_, tb=tb, ps=ps):
                        e.matmul(ps[:, :], tc.gdT[0:16, d_, tb * 128:(tb + 1) * 128], tc.gkup[0:16, d_, :], start=True, stop=False)
                        return e.matmul(ps[:, :], ones_f[0:1, 0:128], tc.gkb[0:1, d_, :], start=False, stop=True)
                    P.c("pe", mm, reads=[tc.gdT_b, tc.gk_b, ones_f_b], writes=[ps_b])
                    st, st_b = next_stage(tc)
                    actf(st, ps[:, :], AF.Exp, [ps_b], [st_b], scale=-1.0)
                    actf(st, st, AF.Ln, [st_b], [st_b], bias=1.0)
                    tsm("dve", st, st, -1.0 / 16.0, [st_b], [st_b])
                    B.dma_out(dst_ap[r0:r0 + 128, :], st, st_b, [st_b], writes=[dst_b])
        else:
            wflat, wt_b = tc.wgu.next()
            wv = wflat[:, 0:NCH * 416].rearrange("p (k c) -> p k c", k=NCH)
            B.load(wv, w_src[:, :, c0:c0 + 416], wt_b, eng="pool")
            for tb in range(4):
                ps, ps_b = B.banks[tb % 4], B.bank_b[tb % 4]
                r0 = t * TT + tb * 128

                def mm(e, wv=wv, tb=tb, ps=ps):
                    ins = None
                    for k in range(NCH):
                        ins = e.matmul(ps[:, 0:416], tc.h[:, k, tb * 128:(tb + 1) * 128], wv[:, k, :], start=(k == 0), stop=(k == NCH - 1))
                    return ins
                P.c("pe", mm, reads=[wt_b] + tc.h_b, writes=[ps_b])
                st, st_b = next_stage(tc)
                actf(st[:, 0:416], ps[:, 0:416], AF.Copy, [ps_b], [st_b])
                B.dma_out(SC["zr"][0][2 + r0:2 + r0 + 128, 3072:3456], st[:, 0:384], st_b, [st_b], writes=[SC["zr"][1]])
                B.dma_out(SC["ab"][0][r0:r0 + 128, :], st[:, 384:416], st_b, [st_b], writes=[SC["ab"][1]])

    PLAN0 = [("gla_q", 0, 0, "copy", QS), ("gla_k", 0, 0, "copy", 1.0), ("gla_v", 0, 0, "copy", 1.0), ("gla_v", 0, 512, "copy", 1.0),
             ("gla_g", 0, 0, "silu", 1.0), ("gla_g", 0, 512, "silu", 1.0),
             ("ret_q", 0, 0, "rot", QS), ("ret_k", 0, 0, "rot", 1.0), ("ret_v", 0, 0, "copy", 1.0), ("ret_v", 0, 512, "copy", 1.0),
             ("ret_g", 0, 0, "silu", 1.0), ("ret_g", 0, 512, "silu", 1.0)]
    PLAN1 = [("zq", 2, 512 * i, "copy", 1.0) for i in range(6)] + [("gdn_g", 0, 0, "silu", 1.0), ("gdn_g", 0, 512, "silu", 1.0)] + \
            [("zr", 2, 512 * i, "copy", 1.0) for i in range(6)]

    def post_phase(l):
        B.phase()
        m0 = A.mark()
        specs = [("OA", 4, 256, False, EPS), ("OB", 4, 256, True, EPS)] if l == 0 else \
                [("OA", 8, 128, False, EPS), ("OB", 16, 64, True, 64e-5)]
        oall = A.alloc("oall", [128, D])
        oall_b = Buf("oall")
        oTst = A.alloc("oTst", [128, NCH, 128], BF16)
        oTst_b = B.pbuf()
        Of = [A.alloc("Of%d" % i, [128, 1024]) for i in range(2)]
        Ob = [A.alloc("Ob%d" % i, [128, 1024]) for i in range(2)]
        Gt = [A.alloc("Gt%d" % i, [128, 1024]) for i in range(2)]
        Of_b = [B.pbuf() for i in range(2)]
        Ob_b = [B.pbuf() for i in range(2)]
        Gt_b = [B.pbuf() for i in range(2)]
        sqt = A.alloc("sqt", [128, 1024])
        sqt_b = Buf("sqt")
        ss = A.alloc("ss", [128, 16])
        ss_b = Buf("ss")
        sm = A.alloc("sm", [128, 16])
        sm_b = Buf("sm")
        pown = B.pbuf()
        if l == 0:
            nwA = A.alloc("nwA", [128, 256])
            nwB = A.alloc("nwB", [128, 256])
            B.dma_in(nwA, SP_["l0_gla_norm"].partition_broadcast(128), pown, [pown])
            B.dma_in(nwB, SP_["l0_ret_norm"].partition_broadcast(128), pown, [pown])
            gates = ["gla_g", "ret_g"]
        else:
            nwA = A.alloc("nwA", [128, 128])
            lnw = A.alloc("lnw", [128, 1024])
            lnb = A.alloc("lnb", [128, 1024])
            B.dma_in(nwA, SP_["l1_gdn_norm"].partition_broadcast(128), pown, [pown])
            B.dma_in(lnw, SP_["l1_lnw"].partition_broadcast(128), pown, [pown])
            B.dma_in(lnb, SP_["l1_lnb"].partition_broadcast(128), pown, [pown])
            bon = [A.alloc("bon%d" % i, [128, 1024]) for i in range(2)]
            bon_b = [B.pbuf() for i in range(2)]
            gates = ["gdn_g", "rw_gate"]
        for tb in range(T // 128):
            r0 = tb * 128
            for mi, (onm, Hh, dvv, center, eps) in enumerate(specs):
                i = (tb * 2 + mi) % 2
                B.dma_in(Of[i], SC[onm + "_f"][0][r0:r0 + 128, :], Of_b[i], [Of_b[i]], reads=[SC[onm + "_f"][1]])
                B.dma_in(Ob[i], SC[onm + "_b"][0][r0:r0 + 128, :], Ob_b[i], [Ob_b[i]], reads=[SC[onm + "_b"][1]])
                B.dma_in(Gt[i], SC[gates[mi]][0][r0:r0 + 128, :], Gt_b[i], [Gt_b[i]], reads=[SC[gates[mi]][1]])
                o = Of[i]
                o_b = Of_b[i]
                o3 = o.rearrange("p (h c) -> p h c", h=Hh)
                tt("dve", o, Of[i], Ob[i], ALU.add, [Of_b[i], Ob_b[i]], [o_b])
                if center:
                    red("dve", sm[:, 0:Hh], o3, [o_b], [sm_b])
                    tsm("dve", sm[:, 0:Hh], sm[:, 0:Hh], -1.0 / dvv, [sm_b], [sm_b])
                    tt("pool", o3, o3, sm[:, 0:Hh].unsqueeze(2).to_broadcast([128, Hh, dvv]), ALU.add, [o_b, sm_b], [o_b])
                tt("pool", sqt, o, o, ALU.mult, [o_b], [sqt_b])
                red("dve", ss[:, 0:Hh], sqt.rearrange("p (h c) -> p h c", h=Hh), [sqt_b], [ss_b])
                rstd_small(ss[:, 0:Hh], ss_b, 1.0 / dvv, eps)
                tt("dve", o3, o3, ss[:, 0:Hh].unsqueeze(2).to_broadcast([128, Hh, dvv]), ALU.mult, [o_b, ss_b], [o_b])
                dsto = oall[:, mi * 1024:(mi + 1) * 1024]
                if l == 1 and mi == 1:
                    B.dma_in(bon[0], SC["rw_bonus"][0][r0:r0 + 128, :], bon_b[0], [bon_b[0]], reads=[SC["rw_bonus"][1]])
                    tt("pool", o, o, lnw, ALU.mult, [o_b, pown], [o_b])
                    tt("dve", o, o, lnb, ALU.add, [o_b, pown], [o_b])
                    tt("pool", o, o, bon[0], ALU.add, [o_b, bon_b[0]], [o_b])
                else:
                    nw = nwA if mi == 0 else nwB
                    tt("pool", o3, o3, nw.unsqueeze(1).to_broadcast([128, Hh, dvv]), ALU.mult, [o_b, pown], [o_b])
                tt("dve", dsto, o, Gt[i], ALU.mult, [o_b, Gt_b[i]], [oall_b])
            for bq in range(4):
                ps, ps_b = B.banks[bq], B.bank_b[bq]

                def mmt(e, bq=bq, ps=ps):
                    ins = None
                    for kk_ in range(4):
                        k = bq * 4 + kk_
                        ins = e.transpose(ps[:, kk_ * 128:(kk_ + 1) * 128], oall[:, k * 128:(k + 1) * 128], ident)
                    return ins
                P.c("pe", mmt, reads=[oall_b, consts_b], writes=[ps_b])
                dst = oTst[:, bq * 4:(bq + 1) * 4, :]
                srcv = ps[:, :].rearrange("p (a b) -> p a b", a=4)
                if bq % 2 == 0:
                    actf(dst, srcv, AF.Copy, [ps_b], [oTst_b])
                else:
                    tcopy("dve", dst, srcv, [ps_b], [oTst_b])
            B.dma_out(oT_scr.rearrange("(k p) t -> p k t", p=128)[:, :, r0:r0 + 128], oTst, oTst_b, [oTst_b], writes=[oT_b])
        P.barrier()
        A.release(m0)

    def pre1_gdn():
        B.phase()
        m0 = A.mark()
        own = B.pbuf()
        CW = A.alloc("CW", [128, 5, 3072])
        for j in range(5):
            B.dma_in(CW[:, j, :], SP_["l1_conv"][j:j + 1, :].partition_broadcast(128), own, [own])
        dtb = A.alloc("dtb", [128, 16])
        negA = A.alloc("negA", [128, 16])
        B.dma_in(dtb, SP_["l1_dtb"].partition_broadcast(128), own, [own])
        B.dma_in(negA, SP_["l1_alog"].partition_broadcast(128), own, [own])
        actf(negA, negA, AF.Exp, [own], [own])
        tsm("dve", negA, negA, -1.0, [own], [own])
        Z = [[A.alloc("Z%d_%d" % (s_, j), [128, 1024]) for j in range(5)] for s_ in range(2)]
        Z_b = [[B.pbuf() for j in range(5)] for s_ in range(2)]
        tm = A.alloc("tm", [128, 4]); tm_b = B.pbuf()
        ab = A.alloc("abt", [128, 32]); ab_b = B.pbuf()
        sm16 = [A.alloc("sm16_%d" % i, [128, 16]) for i in range(5)]
        sm_b = Buf("sm16")
        acc = A.alloc("acc", [128, 1024]); acc_b = Buf("acc")
        tmpc = A.alloc("tmpc", [128, 1024]); tmpc_b = Buf("tmpc")
        part_t = [A.alloc("part%d" % i, [128, 1024]) for i in range(3)]
        part_b = [B.pbuf() for i in range(3)]
        sq = A.alloc("sqg", [128, 1024]); sq_b = Buf("sqg")
        ssq = A.alloc("ssq", [128, 8]); ssq_b = Buf("ssq")
        ssk = A.alloc("ssk", [128, 8]); ssk_b = Buf("ssk")
        outs = {nm: (A.alloc("o_" + nm, [128, 1024]), B.pbuf()) for nm in ("la_f", "la_b", "k_f", "k_b", "b_f", "b_b")}
        la, beta, ela, nbe, xsm = sm16
        for tb in range(T // 128):
            r0 = tb * 128
            B.dma_in(tm, tmask_in[r0:r0 + 128, :], tm_b, [tm_b])
            B.dma_in(ab, SC["ab"][0][r0:r0 + 128, :], ab_b, [ab_b], reads=[SC["ab"][1]])
            tt("dve", xsm, ab[:, 0:16], dtb, ALU.add, [ab_b, own], [sm_b])
            actf(xsm, xsm, AF.Exp, [sm_b], [sm_b])
            actf(xsm, xsm, AF.Ln, [sm_b], [sm_b], bias=1.0)
            tt("dve", la, xsm, negA, ALU.mult, [sm_b, own], [sm_b])
            actf(beta, ab[:, 16:32], AF.Sigmoid, [ab_b], [sm_b])
            actf(ela, la, AF.Exp, [sm_b], [sm_b])
            stt("dve", nbe, beta, -1.0, ela, ALU.mult, ALU.mult, [sm_b], [sm_b])
            for part in range(3):
                s_ = (tb * 3 + part) % 2
                for j in range(5):
                    B.dma_in(Z[s_][j], SC["zq"][0][r0 + j:r0 + j + 128, part * 1024:(part + 1) * 1024], Z_b[s_][j], [Z_b[s_][j]],
                             reads=[SC["zq"][1]])
                pc = slice(part * 1024, (part + 1) * 1024)
                tt("dve", acc, Z[s_][2], CW[:, 2, pc], ALU.mult, [Z_b[s_][2], own], [acc_b])
                for j, mi in ((0, 0), (1, 1), (3, 2), (4, 3)):
                    stt("pool", tmpc, Z[s_][j], tm[:, mi:mi + 1], CW[:, j, pc], ALU.mult, ALU.mult, [Z_b[s_][j], tm_b, own], [tmpc_b])
                    tt("dve", acc, acc, tmpc, ALU.add, [acc_b, tmpc_b], [acc_b])
                actf(part_t[part], acc, AF.Silu, [acc_b], [part_b[part]])
            qt, kt, vt = part_t
            q3 = qt.rearrange("p (h c) -> p h c", h=8)
            k3 = kt.rearrange("p (h c) -> p h c", h=8)
            tt("pool", sq, qt, qt, ALU.mult, [part_b[0]], [sq_b])
            red("dve", ssq, sq.rearrange("p (h c) -> p h c", h=8), [sq_b], [ssq_b])
            rstd_small(ssq, ssq_b, 1.0, EPS)
            tsm("dve", ssq, ssq, QS, [ssq_b], [ssq_b])
            tt("dve", q3, q3, ssq.unsqueeze(2).to_broadcast([128, 8, 128]), ALU.mult, [part_b[0], ssq_b], [part_b[0]])
            tt("pool", sq, kt, kt, ALU.mult, [part_b[1]], [sq_b])
            red("dve", ssk, sq.rearrange("p (h c) -> p h c", h=8), [sq_b], [ssk_b])
            rstd_small(ssk, ssk_b, 1.0, EPS)
            tt("dve", k3, k3, ssk.unsqueeze(2).to_broadcast([128, 8, 128]), ALU.mult, [part_b[1], ssk_b], [part_b[1]])
            B.dma_out(SC["gdn_q"][0][r0:r0 + 128, :], qt, part_b[0], [part_b[0]], writes=[SC["gdn_q"][1]])
            B.dma_out(SC["gdn_a"][0][r0:r0 + 128, :], kt, part_b[1], [part_b[1]], writes=[SC["gdn_a"][1]])
            B.dma_out(SC["gdn_v"][0][r0:r0 + 128, :], vt, part_b[2], [part_b[2]], writes=[SC["gdn_v"][1]])
            for di, dn in enumerate(("f", "b")):
                hs = slice(di * 8, (di + 1) * 8)
                t_la, b_la = outs["la_" + dn]
                t_k, b_k = outs["k_" + dn]
                t_b, b_b = outs["b_" + dn]
                tcopy("pool", t_la.rearrange("p (h c) -> p h c", h=8), la[:, hs].unsqueeze(2).to_broadcast([128, 8, 128]), [sm_b], [b_la])
                tt("dve", t_k.rearrange("p (h c) -> p h c", h=8), k3, beta[:, hs].unsqueeze(2).to_broadcast([128, 8, 128]), ALU.mult,
                   [part_b[1], sm_b], [b_k])
                tt("pool", t_b.rearrange("p (h c) -> p h c", h=8), k3, nbe[:, hs].unsqueeze(2).to_broadcast([128, 8, 128]), ALU.mult,
                   [part_b[1], sm_b], [b_b])
                B.dma_out(SC["gdn_la_" + dn][0][r0:r0 + 128, :], t_la, b_la, [b_la], writes=[SC["gdn_la_" + dn][1]])
                B.dma_out(SC["gdn_k_" + dn][0][r0:r0 + 128, :], t_k, b_k, [b_k], writes=[SC["gdn_k_" + dn][1]])
                B.dma_out(SC["gdn_b_" + dn][0][r0:r0 + 128, :], t_b, b_b, [b_b], writes=[SC["gdn_b_" + dn][1]])
        P.barrier()
        A.release(m0)

    def pre1_rwkv():
        B.phase()
        m0 = A.mark()
        own = B.pbuf()
        MU = A.alloc("MU", [128, 3456])
        B.dma_in(MU, SP_["l1_mu"].partition_broadcast(128), own, [own])
        KKw = A.alloc("KKw", [128, 1024]); KAw = A.alloc("KAw", [128, 1024]); RKw = A.alloc("RKw", [128, 1024])
        B.dma_in(KKw, SP_["l1_kk"].partition_broadcast(128), own, [own])
        B.dma_in(KAw, SP_["l1_ka"].partition_broadcast(128), own, [own])
        B.dma_in(RKw, SP_["l1_rk"].partition_broadcast(128), own, [own])
        w0 = A.alloc("w0", [1, 2, 1024]); a0 = A.alloc("a0", [1, 2, 1024])
        B.dma_in(w0, SP_["l1_w0"].rearrange("k (d c) -> k d c", d=2), own, [own])
        B.dma_in(a0, SP_["l1_a0"].rearrange("k (d c) -> k d c", d=2), own, [own])
        w2 = A.alloc("w2", [64, 2, 1024]); a2 = A.alloc("a2", [64, 2, 1024]); g2 = A.alloc("g2", [128, 1024])
        B.dma_in(w2, SP_["l1_w2"].rearrange("k (d c) -> k d c", d=2), own, [own])
        B.dma_in(a2, SP_["l1_a2"].rearrange("k (d c) -> k d c", d=2), own, [own])
        B.dma_in(g2, SP_["l1_g2"], own, [own])
        Zs = [A.alloc("Zs%d" % j, [128, 3456]) for j in range(3)]
        Zs_b = [B.pbuf() for j in range(3)]
        zr = A.alloc("zrt", [128, 3456]); zr_b = B.pbuf()
        tm = A.alloc("tm", [128, 4]); tm_b = B.pbuf()
        th = A.alloc("th", [128, 256]); th_b = Buf("th")
        sgd = A.alloc("sgd", [128, 128]); sgd_b = Buf("sgd")
        LT = A.alloc("LT", [64, 4, 128]); LT_b = Buf("LT")
        sgT = A.alloc("sgT", [128, 128]); sgT_b = Buf("sgT")
        kk = A.alloc("kkt", [128, 1024]); kk_b = Buf("kkt")
        na = A.alloc("nat", [128, 1024]); na_b = B.pbuf()
        sq = A.alloc("sqr", [128, 1024]); sq_b = Buf("sqr")
        ss = A.alloc("ssr", [128, 16]); ss_b = Buf("ssr")
        bs = A.alloc("bsr", [128, 16]); bs_b = Buf("bsr")
        gate = A.alloc("gatet", [128, 1024]); gate_b = B.pbuf()
        bonus = A.alloc("bonust", [128, 1024]); bonus_b = B.pbuf()
        Ad = A.alloc("Adt", [128, 1024]); Ad_b = Buf("Adt")
        u = A.alloc("ut", [128, 1024]); u_b = Buf("ut")
        outs = {nm: (A.alloc("o_" + nm, [128, 1024]), B.pbuf()) for nm in ("la_f", "la_b", "k_f", "k_b", "b_f", "b_b")}
        for tb in range(T // 128):
            r0 = tb * 128
            B.dma_in(tm, tmask_in[r0:r0 + 128, :], tm_b, [tm_b])
            for j in range(3):
                B.dma_in(Zs[j], SC["zr"][0][r0 + 1 + j:r0 + 1 + j + 128, :], Zs_b[j], [Zs_b[j]], reads=[SC["zr"][1]])
            tsm("pool", zr, Zs[0], tm[:, 1:2], [Zs_b[0], tm_b], [zr_b])
            stt("dve", zr, Zs[2], tm[:, 2:3], zr, ALU.mult, ALU.add, [Zs_b[2], tm_b, zr_b], [zr_b])
            stt("pool", zr, zr, 0.5, Zs[1], ALU.mult, ALU.subtract, [zr_b, Zs_b[1]], [zr_b])
            tt("dve", zr, zr, MU, ALU.mult, [zr_b, own], [zr_b])
            tt("pool", zr, zr, Zs[1], ALU.add, [zr_b, Zs_b[1]], [zr_b])
            r_ = zr[:, 0:1024]; kr = zr[:, 1024:2048]; vr = zr[:, 2048:3072]
            rws = dbg.get("rw_stop", 99)
            if rws <= 1:
                break
            actf(th[:, 0:128], zr[:, 3072:3200], AF.Tanh, [zr_b], [th_b])
            tcopy("dve", th[:, 128:256], zr[:, 3200:3328], [zr_b], [th_b])
            actf(sgd, zr[:, 3328:3456], AF.Sigmoid, [zr_b], [sgd_b])
            ps, ps_b = B.banks[0], B.bank_b[0]

            def mmt(e, ps=ps):
                for q in range(4):
                    e.transpose(ps[0:64, q * 128:(q + 1) * 128], th[:, q * 64:(q + 1) * 64], ident)
                return e.transpose(B.banks[1][:, 0:128], sgd, ident)
            P.c("pe", mmt, reads=[th_b, sgd_b, consts_b], writes=[ps_b, B.bank_b[1]])
            tcopy("dve", LT, ps[0:64, :].rearrange("p (a b) -> p a b", a=4), [ps_b], [LT_b])
            actf(sgT, B.banks[1][:, 0:128], AF.Copy, [B.bank_b[1]], [sgT_b])
            if rws <= 2:
                break
            tt("dve", kk, kr, KKw, ALU.mult, [zr_b, own], [kk_b])
            tt("pool", sq, kk, kk, ALU.mult, [kk_b], [sq_b])
            red("dve", ss, sq.rearrange("p (h c) -> p h c", h=16), [sq_b], [ss_b])
            rstd_small(ss, ss_b, 1.0, EPS)
            kk3 = kk.rearrange("p (h c) -> p h c", h=16)
            tt("dve", kk3, kk3, ss.unsqueeze(2).to_broadcast([128, 16, 64]), ALU.mult, [kk_b, ss_b], [kk_b])
            tsm("pool", na, kk, -1.0, [kk_b], [na_b])
            B.dma_out(SC["rw_a"][0][r0:r0 + 128, :], na, na_b, [na_b], writes=[SC["rw_a"][1]])
            B.dma_out(SC["rw_q"][0][r0:r0 + 128, :], r_, zr_b, [zr_b], writes=[SC["rw_q"][1]])
            B.dma_out(SC["rw_v"][0][r0:r0 + 128, :], vr, zr_b, [zr_b], writes=[SC["rw_v"][1]])
            if rws <= 3:
                break
            for half in range(2):
                pg, pg_b = B.banks[2 + half], B.bank_b[2 + half]
                P.c("pe", lambda e, half=half, pg=pg: e.matmul(pg[:, :], sgT, g2[:, half * 512:(half + 1) * 512], start=True, stop=True),
                    reads=[sgT_b, own], writes=[pg_b])
                actf(gate[:, half * 512:(half + 1) * 512], pg[:, :], AF.Copy, [pg_b], [gate_b])
            B.dma_out(SC["rw_gate"][0][r0:r0 + 128, :], gate, gate_b, [gate_b], writes=[SC["rw_gate"][1]])
            if rws <= 4:
                break
            for di, dn in enumerate(("f", "b")):
                t_la, b_la = outs["la_" + dn]
                t_k, b_k = outs["k_" + dn]
                t_b, b_b = outs["b_" + dn]
                for half in range(2):
                    hc = slice(half * 512, (half + 1) * 512)
                    pw, pw_b = B.banks[4 + half], B.bank_b[4 + half]

                    def mmw(e, di=di, hc=hc, pw=pw):
                        e.matmul(pw[:, :], LT[0:64, di, :], w2[0:64, di, hc], start=True, stop=False)
                        return e.matmul(pw[:, :], ones_f[0:1, 0:128], w0[0:1, di, hc], start=False, stop=True)
                    P.c("pe", mmw, reads=[LT_b, own, ones_f_b], writes=[pw_b])
                    actf(t_la[:, hc], pw[:, :], AF.Exp, [pw_b], [b_la], scale=-1.0)
                    pa, pa_b = B.banks[6 + half], B.bank_b[6 + half]

                    def mma(e, di=di, hc=hc, pa=pa):
                        e.matmul(pa[:, :], LT[0:64, 2 + di, :], a2[0:64, di, hc], start=True, stop=False)
                        return e.matmul(pa[:, :], ones_f[0:1, 0:128], a0[0:1, di, hc], start=False, stop=True)
                    P.c("pe", mma, reads=[LT_b, own, ones_f_b], writes=[pa_b])
                    actf(Ad[:, hc], pa[:, :], AF.Sigmoid, [pa_b], [Ad_b])
                actf(t_la, t_la, AF.Ln, [b_la], [b_la], bias=1.0)
                actf(t_la, t_la, AF.Exp, [b_la], [b_la], scale=-1.0, bias=-0.5)
                tsm("dve", t_la, t_la, -1.0, [b_la], [b_la])
                B.dma_out(SC["rw_la_" + dn][0][r0:r0 + 128, :], t_la, b_la, [b_la], writes=[SC["rw_la_" + dn][1]])
                stt("dve", u, Ad, -1.0, KAw, ALU.add, ALU.mult, [Ad_b, own], [u_b])
                tt("pool", u, u, kr, ALU.mult, [u_b, zr_b], [u_b])
                tt("dve", t_k, u, kr, ALU.add, [u_b, zr_b], [b_k])
                B.dma_out(SC["rw_k_" + dn][0][r0:r0 + 128, :], t_k, b_k, [b_k], writes=[SC["rw_k_" + dn][1]])
                tt("pool", t_b, kk, Ad, ALU.mult, [kk_b, Ad_b], [b_b])
                B.dma_out(SC["rw_b_" + dn][0][r0:r0 + 128, :], t_b, b_b, [b_b], writes=[SC["rw_b_" + dn][1]])
                tt("dve", u, r_, t_k, ALU.mult, [zr_b, b_k], [u_b])
                tt("pool", u, u, RKw, ALU.mult, [u_b, own], [u_b])
                red("dve", bs, u.rearrange("p (h c) -> p h c", h=16), [u_b], [bs_b])
                v3 = vr.rearrange("p (h c) -> p h c", h=16)
                bsB = bs.unsqueeze(2).to_broadcast([128, 16, 64])
                if di == 0:
                    tt("dve", bonus.rearrange("p (h c) -> p h c", h=16), v3, bsB, ALU.mult, [zr_b, bs_b], [bonus_b])
                else:
                    tt("dve", u.rearrange("p (h c) -> p h c", h=16), v3, bsB, ALU.mult, [zr_b, bs_b], [u_b])
                    tt("pool", bonus, bonus, u, ALU.add, [bonus_b, u_b], [bonus_b])
            B.dma_out(SC["rw_bonus"][0][r0:r0 + 128, :], bonus, bonus_b, [bonus_b], writes=[SC["rw_bonus"][1]])
            if rws <= 5:
                break
        P.barrier()
        A.release(m0)

    def scans(l):
        specs = []
        if l == 0:
            for di, dn in enumerate(("f", "b")):
                specs.append(("gla", 4, 128, 256, False, dn, di, {"q": "gla_q", "k": "gla_k", "v": "gla_v", "la": "gla_la_" + dn}, "OA_" + dn))
                specs.append(("ret", 4, 128, 256, False, dn, di, {"q": "ret_q", "k": "ret_k", "v": "ret_v", "la": "ret_la_" + dn}, "OB_" + dn))
        else:
            for di, dn in enumerate(("f", "b")):
                specs.append(("gdn", 8, 128, 128, True, dn, di, {"q": "gdn_q", "k": "gdn_k_" + dn, "v": "gdn_v", "la": "gdn_la_" + dn,
                                                                 "a": "gdn_a", "b": "gdn_b_" + dn}, "OA_" + dn))
                specs.append(("rwkv", 16, 64, 64, True, dn, di, {"q": "rw_q", "k": "rw_k_" + dn, "v": "rw_v", "la": "rw_la_" + dn,
                                                                 "a": "rw_a", "b": "rw_b_" + dn}, "OB_" + dn))
        for mixer in sorted(set(sp_[0] for sp_ in specs)):
            m0 = A.mark()
            P.begin_streams(2)
            for nm, H, dk, dv, lowrank, dn, di, srcn, onm in specs:
                if nm != mixer:
                    continue
                P.set_stream(di)
                src = {k_: SC[v_] for k_, v_ in srcn.items()}
                scan_pass(B, nm + dn, H, dk, dv, lowrank, dn, src, s0_in[nm][di], SC[onm][0], SC[onm][1], st_out[nm][di],
                          flags, flags_b, consts, consts_b, scalar_decay=(nm == "gdn"), slot=di,
                          pbanks=(4 * di, 4 * di + 1, 4 * di + 2, 4 * di + 3), standalone=False)
            P.end_streams()
            P.barrier()
            A.release(m0)

    nl = dbg.get("nlayers", 2)
    for l in range(2):
        phase_mods(l)
    m0 = A.mark()
    zt = A.alloc("zerot", [2, 3456])
    zt_b = B.buf("zerot")
    P.c("pool", lambda e: e.memset(zt, 0.0), writes=[zt_b])
    for nm, w_ in (("zq", 3072), ("zr", 3456)):
        B.dma_out(SC[nm][0][0:2, :], zt[:, 0:w_], zt_b, [zt_b], writes=[SC[nm][1]])
        B.dma_out(SC[nm][0][T + 2:T + 4, :], zt[:, 0:w_], zt_b, [zt_b], writes=[SC[nm][1]])
    P.barrier()
    A.release(m0)

    ntiles = dbg.get("ntiles", NT)
    stop = dbg.get("stop", 99)

    def finish():
        P.emit(final_wait_ops=B.stores)
        return B
    if stop <= 0:
        return finish()
    B.phase()
    m0 = A.mark()
    tc = tile_alloc()
    for t in range(ntiles):
        x_load(tc, xT_in, t)
        modulate(tc, 0, 0)
        ffn(tc, 0, 1, 0)
        modulate(tc, 0, 1)
        proj(tc, 0, t, PLAN0, 32)
        x_store(tc, xs_scr, t, dst_bufs=[xs_tile_b[t]])
    P.barrier()
    A.release(m0)
    if stop <= 1:
        return finish()
    scans(0)
    if stop <= 2:
        return finish()
    post_phase(0)
    if stop <= 3:
        return finish()
    B.phase()
    m0 = A.mark()
    tc = tile_alloc(False)
    for t in range(ntiles):
        x_load(tc, xs_scr, t, src_bufs=[xs_tile_b[t]])
        wout(tc, 0, t)
        modulate(tc, 0, 2)
        ffn(tc, 0, 2, 2)
        modulate(tc, 1, 0)
        ffn(tc, 1, 1, 0)
        modulate(tc, 1, 1)
        proj(tc, 1, t, PLAN1, 416)
        x_store(tc, xs_scr, t, dst_bufs=[xs_tile_b[t]])
    P.barrier()
    A.release(m0)
    if stop <= 4:
        return finish()
    pre1_gdn()
    if stop <= 5:
        return finish()
    pre1_rwkv()
    if stop <= 6:
        return finish()
    scans(1)
    if stop <= 7:
        return finish()
    post_phase(1)
    if stop <= 8:
        return finish()
    B.phase()
    m0 = A.mark()
    tc = tile_alloc(False)
    for t in range(ntiles):
        x_load(tc, xs_scr, t, src_bufs=[xs_tile_b[t]])
        wout(tc, 1, t)
        modulate(tc, 1, 2)
        ffn(tc, 1, 2, 2)
        final_norm(tc)
        x_store(tc, yT_out, t, final=True)
    A.release(m0)
    P.emit(final_wait_ops=B.stores)
    return B


def fm_vec(v):
    v = np.asarray(v, np.float32)
    return np.ascontiguousarray(v.reshape(-1, 128).T)


def row(v):
    return np.ascontiguousarray(np.asarray(v, np.float32).reshape(1, -1))


def host_weights(inp):
    f32 = lambda a: np.ascontiguousarray(np.asarray(a), dtype=np.float32)
    Wd = {}
    for l in range(2):
        p = "l%d_" % l
        Wd[p + "w_mod"] = f32(inp[p + "w_mod"])
        Wd[p + "b_mod"] = fm_vec(inp[p + "b_mod"])
        Wd[p + "norms"] = np.ascontiguousarray(np.concatenate([fm_vec(inp[p + "norm%d" % i]) for i in (1, 2, 3)], axis=1))
        for f in ("ffn1", "ffn2"):
            for w in ("wg", "wu", "wd"):
                Wd[p + f + "_" + w] = f32(inp[p + f + "_" + w])
        Wd[p + "w_out"] = f32(inp[p + "w_out"])
    perm0 = np.concatenate([np.arange(0, 3072), np.arange(3104, 6176), np.arange(3072, 3104)])
    perm1 = np.concatenate([np.arange(0, 4096), np.arange(4128, 7584), np.arange(4096, 4128)])
    Wd["l0_w_in"] = np.ascontiguousarray(np.asarray(inp["l0_w_in"], np.float32)[:, perm0])
    Wd["l1_w_in"] = np.ascontiguousarray(np.asarray(inp["l1_w_in"], np.float32)[:, perm1])
    Wd["final_norm"] = fm_vec(inp["final_norm"])
    Wd["consts"] = make_consts()
    Wd["l0_gk_up"] = np.ascontiguousarray(np.concatenate([f32(inp["l0_gla_gk_up_fwd"]), f32(inp["l0_gla_gk_up_bwd"])], axis=1))
    Wd["l0_gk_b"] = np.ascontiguousarray(np.concatenate([row(inp["l0_gla_gk_b_fwd"]), row(inp["l0_gla_gk_b_bwd"])], axis=1))
    Wd["l0_gla_norm"] = row(inp["l0_gla_norm"])
    Wd["l0_ret_norm"] = row(inp["l0_ret_norm"])
    for nm, e0 in (("ret_la_f", 5.0), ("ret_la_b", 5.5)):
        h = np.arange(4, dtype=np.float32)
        lg = np.log1p(-np.power(np.float32(2.0), -(np.float32(e0) + h))).astype(np.float32)
        Wd[nm] = np.ascontiguousarray(np.broadcast_to(np.repeat(lg, 128)[None, :], (T, 512)).astype(np.float32))
    Wd["l1_conv"] = f32(inp["l1_gdn_conv"])
    Wd["l1_dtb"] = np.ascontiguousarray(np.concatenate([row(inp["l1_gdn_dt_bias_fwd"]), row(inp["l1_gdn_dt_bias_bwd"])], axis=1))
    Wd["l1_alog"] = np.ascontiguousarray(np.concatenate([row(inp["l1_gdn_A_log_fwd"]), row(inp["l1_gdn_A_log_bwd"])], axis=1))
    Wd["l1_gdn_norm"] = row(inp["l1_gdn_norm"])
    Wd["l1_mu"] = row(inp["l1_rwkv_mu"])
    Wd["l1_w0"] = np.ascontiguousarray(np.concatenate([row(inp["l1_rwkv_w0_fwd"]), row(inp["l1_rwkv_w0_bwd"])], axis=1))
    Wd["l1_a0"] = np.ascontiguousarray(np.concatenate([row(inp["l1_rwkv_a0_fwd"]), row(inp["l1_rwkv_a0_bwd"])], axis=1))
    Wd["l1_w2"] = np.ascontiguousarray(np.concatenate([f32(inp["l1_rwkv_w2_fwd"]), f32(inp["l1_rwkv_w2_bwd"])], axis=1))
    Wd["l1_a2"] = np.ascontiguousarray(np.concatenate([f32(inp["l1_rwkv_a2_fwd"]), f32(inp["l1_rwkv_a2_bwd"])], axis=1))
    Wd["l1_g2"] = f32(inp["l1_rwkv_g2"])
    Wd["l1_kk"] = row(inp["l1_rwkv_k_k"])
    Wd["l1_ka"] = row(inp["l1_rwkv_k_a"])
    Wd["l1_rk"] = row(inp["l1_rwkv_r_k"])
    Wd["l1_lnw"] = row(inp["l1_rwkv_ln_w"])
    Wd["l1_lnb"] = row(inp["l1_rwkv_ln_b"])
    return Wd


INPUT_NAMES = (
    "x_prompt", "x_sample", "c", "c_ctx",
    "state_l0_gla_fwd", "state_l0_gla_bwd", "state_l0_ret_fwd", "state_l0_ret_bwd",
    "state_l1_gdn_fwd", "state_l1_gdn_bwd", "state_l1_rwkv_fwd", "state_l1_rwkv_bwd",
    "l0_w_mod", "l0_b_mod", "l0_norm1", "l0_norm2", "l0_norm3",
    "l0_ffn1_wg", "l0_ffn1_wu", "l0_ffn1_wd", "l0_ffn2_wg", "l0_ffn2_wu", "l0_ffn2_wd", "l0_w_in", "l0_w_out",
    "l0_gla_gk_up_fwd", "l0_gla_gk_b_fwd", "l0_gla_gk_up_bwd", "l0_gla_gk_b_bwd", "l0_gla_norm", "l0_ret_norm",
    "l1_w_mod", "l1_b_mod", "l1_norm1", "l1_norm2", "l1_norm3",
    "l1_ffn1_wg", "l1_ffn1_wu", "l1_ffn1_wd", "l1_ffn2_wg", "l1_ffn2_wu", "l1_ffn2_wd", "l1_w_in", "l1_w_out",
    "l1_gdn_conv", "l1_gdn_A_log_fwd", "l1_gdn_dt_bias_fwd", "l1_gdn_A_log_bwd", "l1_gdn_dt_bias_bwd", "l1_gdn_norm",
    "l1_rwkv_mu", "l1_rwkv_w0_fwd", "l1_rwkv_w2_fwd", "l1_rwkv_a0_fwd", "l1_rwkv_a2_fwd",
    "l1_rwkv_w0_bwd", "l1_rwkv_w2_bwd", "l1_rwkv_a0_bwd", "l1_rwkv_a2_bwd",
    "l1_rwkv_g2", "l1_rwkv_k_k", "l1_rwkv_k_a", "l1_rwkv_r_k", "l1_rwkv_ln_w", "l1_rwkv_ln_b", "final_norm")

ST_NAMES = (("gla", "l0_gla", 4, 128, 256), ("ret", "l0_ret", 4, 128, 256), ("gdn", "l1_gdn", 8, 128, 128), ("rwkv", "l1_rwkv", 16, 64, 64))


def core_inputs(inp, core):
    m = {}
    sample = core < 4
    tpos = np.arange(T)
    if sample:
        x = np.asarray(inp["x_sample"][core], np.float32)
        cond = np.asarray(inp["c"][core], np.float32)
        pos = tpos
        seglen = T
    else:
        x = np.zeros((T, D), np.float32)
        for s in range(4):
            x[s * SEG:(s + 1) * SEG] = np.asarray(inp["x_prompt"][4 * (core - 4) + s], np.float32)
        cond = np.asarray(inp["c_ctx"], np.float32)
        pos = tpos % SEG
        seglen = SEG
    m["xT"] = np.ascontiguousarray(x.T)
    m["cond"] = fm_vec(cond)
    fl = np.zeros((128, 2), np.float32)
    fl[:, 0] = 1.0 if sample else 0.0
    m["flags"] = fl
    tm = np.zeros((T, 4), np.float32)
    for i, sft in enumerate((-2, -1, 1, 2)):
        tm[:, i] = ((pos + sft >= 0) & (pos + sft < seglen)).astype(np.float32)
    m["tmask"] = tm
    rot = np.zeros((T, 128), np.float32)
    if sample:
        inv = (np.float32(10000.0) ** (-np.arange(32, dtype=np.float32) / np.float32(32))).astype(np.float32)
        rowp = (tpos // 64).astype(np.float32)
        colp = (tpos % 64).astype(np.float32)
        ang = np.concatenate([rowp[:, None] * inv[None, :], colp[:, None] * inv[None, :]], axis=1).astype(np.float32)
        rot[:, 0:64] = np.cos(ang)
        rot[:, 64:128] = np.sin(ang)
    else:
        rot[:, 0:64] = 1.0
    m["rot"] = rot
    for nm, key, H, dk, dv in ST_NAMES:
        s0 = np.zeros((2, dk, H * dv), np.float32)
        if sample:
            for di, dn in enumerate(("fwd", "bwd")):
                st = np.asarray(inp["state_%s_%s" % (key, dn)][core], np.float32)
                s0[di] = st.transpose(1, 0, 2).reshape(dk, H * dv)
        m["s0_" + nm] = s0
    return m


_CACHE = {}


def kernel(**inputs):
    if "prog" not in _CACHE:
        _CACHE["prog"] = build_program()
    Bd = _CACHE["prog"]
    Wd = host_weights(inputs)
    in_maps = []
    for core in range(8):
        m = dict(Wd)
        m.update(core_inputs(inputs, core))
        in_maps.append({k: m[k] for k in Bd.inp})
    res = run_bass_kernel_spmd(Bd.nc, in_maps, core_ids=list(range(8)))
    r = res.results
    y_prompt = np.zeros((16, SEG, D), np.float32)
    y_sample = np.zeros((4, T, D), np.float32)
    for core in range(4):
        y_sample[core] = np.asarray(r[core]["yT"]).T
    for core in range(4, 8):
        yt = np.asarray(r[core]["yT"]).T
        for s in range(4):
            y_prompt[4 * (core - 4) + s] = yt[s * SEG:(s + 1) * SEG]
    outs = [y_prompt, y_sample]
    for nm, key, H, dk, dv in ST_NAMES:
        for di in range(2):
            st = np.zeros((16, H, dk, dv), np.float32)
            for core in range(4, 8):
                so = np.asarray(r[core]["st_" + nm])
                for s in range(4):
                    st[4 * (core - 4) + s] = so[di, s].reshape(dk, H, dv).transpose(1, 0, 2)
            outs.append(st)
    return tuple(outs)
```
